# Optimizing a Trainium2 kernel written in Bass

```python
import jax, jax.numpy as jnp
from jax import lax
import numpy as np

D_MODEL = 2048
BATCH = 4
SEQ = 4096
DEPTH = 2

GRID_W = 64
CTX_LEN = 256
D_FF = 5632
N_MOD = 9
LN_EPS = 1e-6

RWKV_HEADS = 16
RWKV_HEAD_DIM = 64
RWKV_W = RWKV_HEADS * RWKV_HEAD_DIM
W_LORA = 64
A_LORA = 64
G_LORA = 160
RWKV_IN = 3 * RWKV_W + W_LORA + A_LORA + G_LORA
RWKV_GN_EPS = 64e-5

MLA_HEADS = 16
QK_NOPE = 128
QK_ROPE = 64
V_HEAD = 128
Q_LORA = 512
KV_LORA = 256
MLA_IN = Q_LORA + KV_LORA + QK_ROPE
MLA_W = MLA_HEADS * V_HEAD
ROPE_BASE = 10000.0
Q_BLOCK = 128

LRU_W = 1024
LRU_BLOCKS = 8
LRU_BW = LRU_W // LRU_BLOCKS
CONV_W = 4
LRU_C = 8.0
LRU_IN = 2 * LRU_W

N_BRANCH = 3
GATE_IN = N_BRANCH * D_MODEL
IN_COLS = RWKV_IN + MLA_IN + LRU_IN + GATE_IN

kernel_name = 'hybrid_rwkv7_mla_rglru_macaron_dit'


def layer_norm(x, g, b):
    xf = x.astype(jnp.float32)
    mu = xf.mean(-1, keepdims=True)
    var = jnp.square(xf - mu).mean(-1, keepdims=True)
    return ((xf - mu) * lax.rsqrt(var + LN_EPS) * g.astype(jnp.float32) + b.astype(jnp.float32)).astype(x.dtype)


def rms_norm(x, g):
    xf = x.astype(jnp.float32)
    y = xf * lax.rsqrt(jnp.mean(xf * xf, -1, keepdims=True) + LN_EPS) * g.astype(jnp.float32)
    return y.astype(x.dtype)


def swiglu(h, w_gate, w_up, w_down):
    return (jax.nn.silu(h @ w_gate) * (h @ w_up)) @ w_down


def axial_rope(n_tok):
    t = jnp.arange(n_tok, dtype=jnp.int32)
    row = (t // GRID_W).astype(jnp.float32)
    col = (t % GRID_W).astype(jnp.float32)
    n_pairs = QK_ROPE // 4
    inv = ROPE_BASE ** (-jnp.arange(n_pairs, dtype=jnp.float32) / n_pairs)
    ang = jnp.concatenate([row[:, None] * inv, col[:, None] * inv], axis=-1)
    return jnp.cos(ang), jnp.sin(ang)


def apply_rope(x, cos, sin):
    xf = x.astype(jnp.float32)
    x1, x2 = xf[..., 0::2], xf[..., 1::2]
    out = jnp.stack([x1 * cos - x2 * sin, x1 * sin + x2 * cos], axis=-1)
    return out.reshape(x.shape).astype(x.dtype)


def shift_latent_grid(f, rows):
    b, s, ch = f.shape
    g = f.reshape(b, rows, GRID_W, 4, ch // 4)
    left = jnp.pad(g[:, :, :-1, 0], ((0, 0), (0, 0), (1, 0), (0, 0)))
    right = jnp.pad(g[:, :, 1:, 1], ((0, 0), (0, 0), (0, 1), (0, 0)))
    up = jnp.pad(g[:, :-1, :, 2], ((0, 0), (1, 0), (0, 0), (0, 0)))
    down = jnp.pad(g[:, 1:, :, 3], ((0, 0), (0, 1), (0, 0), (0, 0)))
    return jnp.stack([left, right, up, down], axis=3).reshape(b, s, ch)


def shift_context(f):
    b, t, ch = f.shape
    g = f.reshape(b, t, 4, ch // 4)
    prev = jnp.pad(g[:, :-1], ((0, 0), (1, 0), (0, 0), (0, 0)))
    nxt = jnp.pad(g[:, 1:], ((0, 0), (0, 1), (0, 0), (0, 0)))
    return jnp.stack([prev[:, :, 0], nxt[:, :, 1], prev[:, :, 2], nxt[:, :, 3]], axis=2).reshape(b, t, ch)


def rwkv7_features(f, lp):
    b, t, _ = f.shape
    c0 = 3 * RWKV_W
    r, k, v, wd, ad, gd = jnp.split(f.astype(jnp.float32), [RWKV_W, 2 * RWKV_W, c0, c0 + W_LORA, c0 + W_LORA + A_LORA], axis=-1)
    heads = lambda z: z.reshape(b, t, RWKV_HEADS, RWKV_HEAD_DIM)
    kk = heads(k * lp['rwkv_k_k'])
    kk = kk / jnp.maximum(jnp.linalg.norm(kk, axis=-1, keepdims=True), 1e-12)
    per_dir = []
    for d in range(2):
        w_log = -jax.nn.softplus(-(lp['rwkv_w0'][d] + jnp.tanh(wd) @ lp['rwkv_w_up'][d])) - 0.5
        decay = jnp.exp(-jnp.exp(w_log))
        a = jax.nn.sigmoid(lp['rwkv_a0'][d] + ad @ lp['rwkv_a_up'][d])
        k_mod = k * (1.0 + (a - 1.0) * lp['rwkv_k_a'])
        per_dir.append((heads(decay), heads(a), heads(k_mod)))
    g = jax.nn.sigmoid(gd) @ lp['rwkv_g_up']
    return heads(r), heads(v), kk, per_dir, g


def rwkv7_scan(r, decay, k, v, kk, a, s0, reverse):
    def step(s, inp):
        r_t, w_t, k_t, v_t, kk_t, a_t = inp
        sa = jnp.einsum('bhvk,bhk->bhv', s, kk_t)
        s = s * w_t[:, :, None, :] - sa[..., None] * (kk_t * a_t)[:, :, None, :] + v_t[..., None] * k_t[:, :, None, :]
        return s, jnp.einsum('bhvk,bhk->bhv', s, r_t)
    xs = tuple(jnp.moveaxis(z, 1, 0) for z in (r, decay, k, v, kk, a))
    s_fin, ys = lax.scan(step, s0, xs, reverse=reverse)
    return jnp.moveaxis(ys, 0, 1), s_fin


def rwkv7_bidir(feat, s0_pair):
    r, v, kk, per_dir, _ = feat
    ys, finals = [], []
    for d, (decay, a, k_mod) in enumerate(per_dir):
        y, s_fin = rwkv7_scan(r, decay, k_mod, v, kk, a, s0_pair[d], d == 1)
        ys.append(y)
        finals.append(s_fin)
    return ys[0] + ys[1], (finals[0], finals[1])


def rwkv7_readout(y, feat, lp):
    r, v, _, per_dir, g = feat
    b, t = y.shape[:2]
    mu = y.mean(-1, keepdims=True)
    var = jnp.square(y - mu).mean(-1, keepdims=True)
    yn = ((y - mu) * lax.rsqrt(var + RWKV_GN_EPS)).reshape(b, t, RWKV_W) * lp['rwkv_lnx_g'] + lp['rwkv_lnx_b']
    k_bonus = 0.5 * (per_dir[0][2] + per_dir[1][2])
    r_k = lp['rwkv_r_k'].reshape(RWKV_HEADS, RWKV_HEAD_DIM)
    bonus = jnp.sum(r * k_bonus * r_k, axis=-1, keepdims=True) * v
    return (yn + bonus.reshape(b, t, RWKV_W)) * g


def mla_project(f, lp, cos_sin):
    b, t, _ = f.shape
    q_lat, kv_lat, k_rope = jnp.split(f, [Q_LORA, Q_LORA + KV_LORA], axis=-1)
    q = (rms_norm(q_lat, lp['mla_q_norm']) @ lp['mla_w_uq']).reshape(b, t, MLA_HEADS, QK_NOPE + QK_ROPE)
    q_nope, q_rope = q[..., :QK_NOPE], q[..., QK_NOPE:]
    kv_c = rms_norm(kv_lat, lp['mla_kv_norm'])
    k_nope = (kv_c @ lp['mla_w_uk']).reshape(b, t, MLA_HEADS, QK_NOPE)
    v = (kv_c @ lp['mla_w_uv']).reshape(b, t, MLA_HEADS, V_HEAD)
    if cos_sin is not None:
        cos, sin = cos_sin
        q_rope = apply_rope(q_rope, cos[None, :, None], sin[None, :, None])
        k_rope = apply_rope(k_rope, cos[None], sin[None])
    return q_nope, q_rope, k_nope, k_rope, v


def mla_attend(q_nope, q_rope, k_nope, k_rope, v):
    scale = (QK_NOPE + QK_ROPE) ** -0.5
    s = jnp.einsum('bqhd,bkhd->bhqk', q_nope, k_nope) + jnp.einsum('bqhd,bkd->bhqk', q_rope, k_rope)
    p = jax.nn.softmax(s.astype(jnp.float32) * scale, axis=-1).astype(v.dtype)
    return jnp.einsum('bhqk,bkhd->bqhd', p, v)


def mla_latent_blocks(q_nope, q_rope, k_nope, k_rope, v):
    b, s = q_nope.shape[:2]
    nb = s // Q_BLOCK
    to_blocks = lambda z: jnp.moveaxis(z.reshape((b, nb, Q_BLOCK) + z.shape[2:]), 1, 0)
    o = lax.map(lambda qs: mla_attend(qs[0], qs[1], k_nope, k_rope, v), (to_blocks(q_nope), to_blocks(q_rope)))
    return jnp.moveaxis(o, 0, 1).reshape(b, s, MLA_W)


def centred_dwconv(x, w, bias):
    t = x.shape[1]
    left = CONV_W // 2
    xp = jnp.pad(x, ((0, 0), (left, CONV_W - 1 - left), (0, 0)))
    y = bias + xp[:, 0:t] * w[0]
    for j in range(1, CONV_W):
        y = y + xp[:, j:j + t] * w[j]
    return y


def rglru_coeffs(xc, lp, d):
    b, t, _ = xc.shape
    xf = xc.astype(jnp.float32)
    xh = xf.reshape(b, t, LRU_BLOCKS, LRU_BW)
    gate_r = jax.nn.sigmoid(jnp.einsum('btnd,nde->btne', xh, lp['lru_wa'][d]).reshape(b, t, LRU_W) + lp['lru_ba'][d])
    gate_i = jax.nn.sigmoid(jnp.einsum('btnd,nde->btne', xh, lp['lru_wx'][d]).reshape(b, t, LRU_W) + lp['lru_bx'][d])
    log_a = -LRU_C * gate_r * jax.nn.softplus(-lp['lru_lambda'][d])
    a = jnp.exp(log_a)
    inp = jnp.sqrt(-jnp.expm1(2.0 * log_a)) * gate_i * xf
    return a, inp


def linear_scan(a, b_in, h0, reverse):
    def combine(e1, e2):
        return e1[0] * e2[0], e2[0] * e1[1] + e2[1]
    a_cum, h = lax.associative_scan(combine, (a, b_in), axis=1, reverse=reverse)
    h = h + a_cum * h0[:, None, :]
    h_fin = h[:, 0] if reverse else h[:, -1]
    return h, h_fin


def gated_merge(gates, y_rw, y_mla, y_lru, lp):
    g = jax.nn.sigmoid(gates + lp['gate_b'])
    g_rw, g_mla, g_lru = jnp.split(g, N_BRANCH, axis=-1)
    y = g_rw * (y_rw @ lp['proj_rwkv']) + g_mla * (y_mla @ lp['proj_mla']) + g_lru * (y_lru @ lp['proj_lru'])
    return y @ lp['w_out']


def hybrid_mixer(hc, hl, rows, cos_sin, lp, ctx_out):
    dt = hl.dtype
    batch = hl.shape[0]
    cuts = [RWKV_IN, RWKV_IN + MLA_IN, RWKV_IN + MLA_IN + LRU_IN]
    rw_c, mla_c, lru_c, gate_c = jnp.split(hc @ lp['w_in'], cuts, axis=-1)
    rw_l, mla_l, lru_l, gate_l = jnp.split(hl @ lp['w_in'], cuts, axis=-1)

    mu = lp['rwkv_mu']
    feat_c = rwkv7_features(rw_c + (shift_context(rw_c) - rw_c) * mu, lp)
    feat_l = rwkv7_features(rw_l + (shift_latent_grid(rw_l, rows) - rw_l) * mu, lp)
    s0 = jnp.zeros((batch, RWKV_HEADS, RWKV_HEAD_DIM, RWKV_HEAD_DIM), jnp.float32)
    y_rw_c, s_fin_c = rwkv7_bidir(feat_c, (s0, s0))
    y_rw_l, _ = rwkv7_bidir(feat_l, s_fin_c)
    rw_out_l = rwkv7_readout(y_rw_l, feat_l, lp).astype(dt)

    qn_c, qr_c, kn_c, kr_c, v_c = mla_project(mla_c, lp, None)
    qn_l, qr_l, kn_l, kr_l, v_l = mla_project(mla_l, lp, cos_sin)
    kn_all = jnp.concatenate([kn_c, kn_l], axis=1)
    kr_all = jnp.concatenate([kr_c, kr_l], axis=1)
    v_all = jnp.concatenate([v_c, v_l], axis=1)
    mla_out_l = mla_latent_blocks(qn_l, qr_l, kn_all, kr_all, v_all)

    xr_c, gb_c = jnp.split(lru_c, 2, axis=-1)
    xr_l, gb_l = jnp.split(lru_l, 2, axis=-1)
    xr_c = centred_dwconv(xr_c, lp['lru_conv_w'], lp['lru_conv_b'])
    xr_l = centred_dwconv(xr_l, lp['lru_conv_w'], lp['lru_conv_b'])
    h0 = jnp.zeros((batch, LRU_W), jnp.float32)
    hs_c, hs_l = [], []
    for d in range(2):
        rev = d == 1
        h_c, h_fin_c = linear_scan(*rglru_coeffs(xr_c, lp, d), h0, rev)
        h_l, _ = linear_scan(*rglru_coeffs(xr_l, lp, d), h_fin_c, rev)
        hs_c.append(h_c)
        hs_l.append(h_l)
    lru_out_l = ((hs_l[0] + hs_l[1]) * jax.nn.gelu(gb_l.astype(jnp.float32))).astype(dt)

    out_l = gated_merge(gate_l, rw_out_l, mla_out_l, lru_out_l, lp)
    if not ctx_out:
        return None, out_l
    rw_out_c = rwkv7_readout(y_rw_c, feat_c, lp).astype(dt)
    mla_out_c = mla_attend(qn_c, qr_c, kn_c, kr_c, v_c).reshape(batch, -1, MLA_W)
    lru_out_c = ((hs_c[0] + hs_c[1]) * jax.nn.gelu(gb_c.astype(jnp.float32))).astype(dt)
    out_c = gated_merge(gate_c, rw_out_c, mla_out_c, lru_out_c, lp)
    return out_c, out_l


def ffn_sublayer(s, mod, j, lp, alpha):
    shift, scale, gate = mod
    h = s * (1.0 + scale) + shift
    y = swiglu(h, lp['ffn_w_gate'][j], lp['ffn_w_up'][j], lp['ffn_w_down'][j])
    return layer_norm(alpha * s + 0.5 * gate * y, lp['ln_g'][2 * j], lp['ln_b'][2 * j])


def setup_inputs(seed: int = 0) -> dict:
    key = jax.random.key(seed)
    ks = iter(jax.random.split(key, 64))
    f32 = jnp.float32
    beta = (8 * DEPTH) ** -0.25
    L = DEPTH

    def nrm(shape, scale):
        return scale * jax.random.normal(next(ks), shape, f32)

    def unif(shape, lo, hi):
        return jax.random.uniform(next(ks), shape, f32, lo, hi)

    lam_u = unif((L, 2, LRU_W), 0.9, 0.999)
    return {
        'x': nrm((BATCH, SEQ, D_MODEL), 1.0),
        'c': nrm((BATCH, D_MODEL), 1.0),
        'ctx': nrm((BATCH, CTX_LEN, D_MODEL), 1.0),
        'c_ctx': nrm((D_MODEL,), 1.0),
        'ada_w': nrm((L, D_MODEL, N_MOD * D_MODEL), 0.5 * D_MODEL ** -0.5),
        'ada_b': nrm((L, N_MOD * D_MODEL), 0.02),
        'ln_g': 1.0 + nrm((L, 3, D_MODEL), 0.02),
        'ln_b': nrm((L, 3, D_MODEL), 0.02),
        'ffn_w_gate': nrm((L, 2, D_MODEL, D_FF), D_MODEL ** -0.5),
        'ffn_w_up': nrm((L, 2, D_MODEL, D_FF), D_MODEL ** -0.5),
        'ffn_w_down': nrm((L, 2, D_FF, D_MODEL), beta * D_FF ** -0.5),
        'w_in': nrm((L, D_MODEL, IN_COLS), D_MODEL ** -0.5),
        'gate_b': nrm((L, GATE_IN), 0.02),
        'rwkv_mu': unif((L, RWKV_IN), 0.0, 1.0),
        'rwkv_w0': unif((L, 2, RWKV_W), -6.0, 1.0),
        'rwkv_w_up': nrm((L, 2, W_LORA, RWKV_W), 0.5 * W_LORA ** -0.5),
        'rwkv_a0': nrm((L, 2, RWKV_W), 0.1),
        'rwkv_a_up': nrm((L, 2, A_LORA, RWKV_W), A_LORA ** -0.5),
        'rwkv_g_up': nrm((L, G_LORA, RWKV_W), G_LORA ** -0.5),
        'rwkv_k_k': 0.85 + nrm((L, RWKV_W), 0.02),
        'rwkv_k_a': 1.0 + nrm((L, RWKV_W), 0.02),
        'rwkv_r_k': nrm((L, RWKV_W), 0.1),
        'rwkv_lnx_g': 1.0 + nrm((L, RWKV_W), 0.02),
        'rwkv_lnx_b': nrm((L, RWKV_W), 0.02),
        'mla_q_norm': 1.0 + nrm((L, Q_LORA), 0.02),
        'mla_kv_norm': 1.0 + nrm((L, KV_LORA), 0.02),
        'mla_w_uq': nrm((L, Q_LORA, MLA_HEADS * (QK_NOPE + QK_ROPE)), Q_LORA ** -0.5),
        'mla_w_uk': nrm((L, KV_LORA, MLA_HEADS * QK_NOPE), KV_LORA ** -0.5),
        'mla_w_uv': nrm((L, KV_LORA, MLA_HEADS * V_HEAD), KV_LORA ** -0.5),
        'lru_conv_w': nrm((L, CONV_W, LRU_W), CONV_W ** -0.5),
        'lru_conv_b': nrm((L, LRU_W), 0.02),
        'lru_wa': nrm((L, 2, LRU_BLOCKS, LRU_BW, LRU_BW), LRU_BW ** -0.5),
        'lru_ba': nrm((L, 2, LRU_W), 0.02),
        'lru_wx': nrm((L, 2, LRU_BLOCKS, LRU_BW, LRU_BW), LRU_BW ** -0.5),
        'lru_bx': nrm((L, 2, LRU_W), 0.02),
        'lru_lambda': jnp.log(lam_u) - jnp.log1p(-lam_u),
        'proj_rwkv': nrm((L, RWKV_W, D_MODEL), RWKV_W ** -0.5),
        'proj_mla': nrm((L, MLA_W, D_MODEL), MLA_W ** -0.5),
        'proj_lru': nrm((L, LRU_W, D_MODEL), LRU_W ** -0.5),
        'w_out': nrm((L, D_MODEL, D_MODEL), beta * D_MODEL ** -0.5),
    }


def reference(x, c, ctx, c_ctx, ada_w, ada_b, ln_g, ln_b, ffn_w_gate, ffn_w_up, ffn_w_down, w_in, gate_b,
              rwkv_mu, rwkv_w0, rwkv_w_up, rwkv_a0, rwkv_a_up, rwkv_g_up, rwkv_k_k, rwkv_k_a, rwkv_r_k,
              rwkv_lnx_g, rwkv_lnx_b, mla_q_norm, mla_kv_norm, mla_w_uq, mla_w_uk, mla_w_uv,
              lru_conv_w, lru_conv_b, lru_wa, lru_ba, lru_wx, lru_bx, lru_lambda,
              proj_rwkv, proj_mla, proj_lru, w_out):
    alpha = (2 * DEPTH) ** 0.25
    n_tok = x.shape[1]
    rows = n_tok // GRID_W
    cos_sin = axial_rope(n_tok)
    silu_c = jax.nn.silu(c)[:, None, :]
    silu_cc = jax.nn.silu(c_ctx)[None, None, :]
    xl, xc = x, ctx
    for l in range(DEPTH):
        last = l == DEPTH - 1
        lp = {
            'ln_g': ln_g[l], 'ln_b': ln_b[l],
            'ffn_w_gate': ffn_w_gate[l], 'ffn_w_up': ffn_w_up[l], 'ffn_w_down': ffn_w_down[l],
            'w_in': w_in[l], 'gate_b': gate_b[l],
            'rwkv_mu': rwkv_mu[l], 'rwkv_w0': rwkv_w0[l], 'rwkv_w_up': rwkv_w_up[l],
            'rwkv_a0': rwkv_a0[l], 'rwkv_a_up': rwkv_a_up[l], 'rwkv_g_up': rwkv_g_up[l],
            'rwkv_k_k': rwkv_k_k[l], 'rwkv_k_a': rwkv_k_a[l], 'rwkv_r_k': rwkv_r_k[l],
            'rwkv_lnx_g': rwkv_lnx_g[l], 'rwkv_lnx_b': rwkv_lnx_b[l],
            'mla_q_norm': mla_q_norm[l], 'mla_kv_norm': mla_kv_norm[l],
            'mla_w_uq': mla_w_uq[l], 'mla_w_uk': mla_w_uk[l], 'mla_w_uv': mla_w_uv[l],
            'lru_conv_w': lru_conv_w[l], 'lru_conv_b': lru_conv_b[l],
            'lru_wa': lru_wa[l], 'lru_ba': lru_ba[l], 'lru_wx': lru_wx[l], 'lru_bx': lru_bx[l],
            'lru_lambda': lru_lambda[l],
            'proj_rwkv': proj_rwkv[l], 'proj_mla': proj_mla[l], 'proj_lru': proj_lru[l], 'w_out': w_out[l],
        }
        mod_l = jnp.split(silu_c @ ada_w[l] + ada_b[l], N_MOD, axis=-1)
        mod_c = jnp.split(silu_cc @ ada_w[l] + ada_b[l], N_MOD, axis=-1)

        xl = ffn_sublayer(xl, mod_l[0:3], 0, lp, alpha)
        xc = ffn_sublayer(xc, mod_c[0:3], 0, lp, alpha)

        hl = xl * (1.0 + mod_l[4]) + mod_l[3]
        hc = xc * (1.0 + mod_c[4]) + mod_c[3]
        out_c, out_l = hybrid_mixer(hc, hl, rows, cos_sin, lp, not last)
        xl = layer_norm(alpha * xl + mod_l[5] * out_l, lp['ln_g'][1], lp['ln_b'][1])
        if not last:
            xc = layer_norm(alpha * xc + mod_c[5] * out_c, lp['ln_g'][1], lp['ln_b'][1])

        xl = ffn_sublayer(xl, mod_l[6:9], 1, lp, alpha)
        if not last:
            xc = ffn_sublayer(xc, mod_c[6:9], 1, lp, alpha)
    return xl
```

```python
import contextlib
import math
import numpy as np
import concourse.bass as bass
import concourse.mybir as mybir
from concourse.bass_utils import run_bass_kernel_spmd

F32 = mybir.dt.float32
BF16 = mybir.dt.bfloat16
AF = mybir.ActivationFunctionType
ALU = mybir.AluOpType

D = 2048
KC = 16
DFF = 5632
FC = 44
NMOD = 9
CTX = 256
GW = 64
RW_W = 1024
RWKV_IN = 3360
MLA_IN = 832
LRU_W = 1024
IN_COLS = 12384
ALPHA = 4.0 ** 0.25
LN_EPS = 1e-6
GN_EPS = 64e-5
C0 = math.exp(-0.5)
ATT_SCALE = 192.0 ** -0.5
CH = 64

SEGS = []
for i in range(8):
    SEGS.append((128 * i, 128))
for i in range(8):
    SEGS.append((1024 + 128 * i, 128))
for i in range(8):
    SEGS.append((2048 + 128 * i, 128))
SEGS.append((3072, 64))
SEGS.append((3136, 64))
SEGS.append((3200, 128))
SEGS.append((3328, 32))
for i in range(4):
    SEGS.append((3360 + 128 * i, 128))
for i in range(2):
    SEGS.append((3872 + 128 * i, 128))
SEGS.append((4128, 64))
for i in range(8):
    SEGS.append((4192 + 128 * i, 128))
for i in range(8):
    SEGS.append((5216 + 128 * i, 128))
for i in range(48):
    SEGS.append((6240 + 128 * i, 128))
NSEG = len(SEGS)
S_R, S_K, S_V, S_WD, S_AD, S_GD, S_Q, S_KV, S_KR, S_XR, S_GB, S_GATE = 0, 8, 16, 24, 25, 26, 28, 32, 34, 35, 43, 51
NRW = 28


class Tile:
    __slots__ = ("t", "name", "last_w", "readers", "dsem", "osem", "kind")

    def __init__(self, t, name, kind="sbuf"):
        self.t = t
        self.name = name
        self.kind = kind
        self.last_w = None
        self.readers = []
        self.dsem = None
        self.osem = None

    def __getitem__(self, idx):
        return self.t[idx]

    def sub(self, name):
        return Tile(self.t, self.name + "." + name, self.kind)


class Op:
    __slots__ = ("eng", "fn", "deps", "is_dma", "sem", "val", "need_sig", "osize")


SEM_EPOCH = 24000
SMALL_OP = 512
import os
DEBUG_INS = os.environ.get("DEBUG_INS")


class Prog:
    ENGS = ("pe", "dve", "act", "pool", "sp")

    def __init__(self, nc):
        self.nc = nc
        self.ops = []
        self.stack = contextlib.ExitStack()
        self.scopes = []
        self.free_sems = []
        self.fence = None
        self.nsem = 0
        self.dummy = Tile(self.stack.enter_context(nc.sbuf_tensor("fence_dummy", [1, 8], F32)), "dummy")

    def new_sem(self, name):
        self.nsem += 1
        return self.stack.enter_context(self.nc.semaphore(name + "_%d" % self.nsem))

    def _reg(self, tl):
        tl.last_w = self.fence
        if self.scopes:
            self.scopes[-1][1].append(tl)
        return tl

    def sbuf(self, name, shape, dtype=F32):
        st = self.scopes[-1][0] if self.scopes else self.stack
        nm = name + "_%d" % len(self.ops)
        t = st.enter_context(self.nc.sbuf_tensor(nm, list(shape), dtype))
        return self._reg(Tile(t, name))

    def psum(self, name, shape, dtype=F32):
        t = self.stack.enter_context(self.nc.psum_tensor(name, list(shape), dtype))
        return Tile(t, name, "psum")

    def dram(self, name, shape, dtype=F32, kind="Internal"):
        t = self.nc.dram_tensor(name, list(shape), dtype, kind=kind).ap()
        return Tile(t, name, "dram")

    @contextlib.contextmanager
    def scope(self):
        st = contextlib.ExitStack()
        self.scopes.append((st, []))
        try:
            yield
        finally:
            _, tiles = self.scopes.pop()
            if tiles:
                d = self.dummy
                self.fence = self.op("dve", "memset", d[:], 0.0, reads=[], writes=[d] + tiles)
                for tl in tiles:
                    for s in (tl.dsem, tl.osem):
                        if s is not None and s[1] < SEM_EPOCH:
                            self.free_sems.append(s)
            st.close()

    def _record(self, eng, fn, reads, writes, is_dma):
        i = len(self.ops)
        deps = set()
        pr = [t for t in reads if t.kind == "psum"]
        if pr:
            writes = list(writes) + [t for t in pr if t not in writes]
            reads = [t for t in reads if t.kind != "psum"]
        for t in reads:
            if t.last_w is not None:
                deps.add(t.last_w)
        for t in writes:
            if t.last_w is not None:
                deps.add(t.last_w)
            deps.update(t.readers)
        for t in reads:
            t.readers.append(i)
        for t in writes:
            t.last_w = i
            t.readers = []
        o = Op()
        o.eng = eng
        o.fn = fn
        o.deps = deps
        o.is_dma = is_dma
        o.sem = None
        o.val = 0
        o.need_sig = is_dma
        o.osize = 1 << 30
        self.ops.append(o)
        return i

    def op(self, eng, name, *args, reads=(), writes=(), **kw):
        def fn(e, name=name, args=args, kw=kw):
            return getattr(e, name)(*args, **kw)

        i = self._record(eng, fn, reads, writes, False)
        if eng != "pe":
            oap = kw.get("out", args[0] if args else None)
            try:
                sz = 1
                for v in oap.shape[1:]:
                    sz *= int(v)
                self.ops[i].osize = 0 if name == "tensor_tensor_scan" else sz
            except Exception:
                pass
        return i

    def _get_sem(self, cur, name):
        if cur is not None and cur[1] < SEM_EPOCH:
            return cur
        if self.free_sems:
            return self.free_sems.pop()
        return [self.new_sem(name), 0]

    def dma(self, out_ap, in_ap, reads=(), writes=(), q="sp", **kw):
        def fn(e, out_ap=out_ap, in_ap=in_ap, kw=kw):
            return e.dma_start(out=out_ap, in_=in_ap, **kw)

        i = self._record(q, fn, reads, writes, True)
        if writes[0].kind == "dram":
            st = reads[0]
            st.osem = self._get_sem(st.osem, "o")
            s = st.osem
        else:
            st = writes[0]
            st.dsem = self._get_sem(st.dsem, "d")
            s = st.dsem
        s[1] += 16
        self.ops[i].sem = s[0]
        self.ops[i].val = s[1]
        return i

    def emit(self):
        nc = self.nc
        ops = self.ops
        for o in ops:
            for d in o.deps:
                od = ops[d]
                if not od.is_dma and (od.eng != o.eng or od.osize < SMALL_OP):
                    od.need_sig = True
        esem = {e: [] for e in self.ENGS}
        ecnt = {e: 0 for e in self.ENGS}
        for o in ops:
            if o.is_dma or not o.need_sig:
                continue
            c = ecnt[o.eng]
            ep = c // SEM_EPOCH
            if ep >= len(esem[o.eng]):
                esem[o.eng].append(self.new_sem("e_" + o.eng))
            o.sem = esem[o.eng][ep]
            o.val = c % SEM_EPOCH + 1
            ecnt[o.eng] = c + 1
        per_eng = {e: [] for e in self.ENGS}
        for i, o in enumerate(ops):
            per_eng[o.eng].append(i)
        fw = {}
        for o in ops:
            if o.is_dma:
                k = id(o.sem)
                if k not in fw or fw[k][1] < o.val:
                    fw[k] = (o.sem, o.val)

        def run(engname, e):
            waited = {}
            for i in per_eng[engname]:
                o = ops[i]
                for d in sorted(o.deps):
                    od = ops[d]
                    if not od.is_dma and od.eng == engname and od.osize >= SMALL_OP:
                        continue
                    k = id(od.sem)
                    if waited.get(k, 0) >= od.val:
                        continue
                    e.wait_ge(od.sem, od.val)
                    waited[k] = od.val
                ins = o.fn(e)
                if DEBUG_INS is not None and DEBUG_INS in str(getattr(ins.ins, "name", "")):
                    print("DEBUG_INS", i, engname, str(ins)[:600])
                if o.need_sig:
                    ins.then_inc(o.sem, 16 if o.is_dma else 1)
            if engname == "sp":
                for s, v in fw.values():
                    e.wait_ge(s, v)

        with nc.Block() as block:
            @block.tensor
            def _(e):
                run("pe", e)

            @block.vector
            def _(e):
                run("dve", e)

            @block.scalar
            def _(e):
                run("act", e)

            @block.gpsimd
            def _(e):
                run("pool", e)

            @block.sync
            def _(e):
                run("sp", e)
        self.stack.close()


class G:
    pass


def fm(v):
    v = np.asarray(v, np.float32)
    n = v.shape[-1] // 128
    w = v.reshape(v.shape[:-1] + (n, 128))
    return np.ascontiguousarray(np.moveaxis(w, -1, 0))


def host_consts(SEQ):
    c = {}
    c["ident"] = np.eye(128, dtype=np.float32)
    c["ones"] = np.ones((128, 128), np.float32)
    blk = np.zeros((128, 128), np.float32)
    blk[:64, :64] = 1
    blk[64:, 64:] = 1
    c["blk"] = blk
    a = np.arange(64)
    strict_f = (a[:, None] < a[None, :]).astype(np.float32)
    incl_f = (a[:, None] <= a[None, :]).astype(np.float32)
    mA = np.zeros((2, 128, 192), np.float32)
    mXY = np.zeros((2, 128, 256), np.float32)
    for d in range(2):
        st = strict_f if d == 0 else strict_f.T
        inc = incl_f if d == 0 else incl_f.T
        for h in range(2):
            rs = slice(64 * h, 64 * h + 64)
            mA[d, rs, 0:64] = inc
            mA[d, rs, 64:128] = st
            mA[d, rs, 128:192] = inc
            mXY[d, rs, 64 * h:64 * h + 64] = st
            mXY[d, rs, 128 + 64 * h:128 + 64 * h + 64] = st.T
    c["maskA"] = mA
    c["maskXY"] = mXY
    Lm = 1024
    rm = np.ones((2, 128, Lm), np.float32)
    rm[0, :, 0::64] = 0
    rm[1, :, 63::64] = 0
    c["rm"] = rm
    t = np.arange(SEQ)
    row = (t // GW).astype(np.float32)
    col = (t % GW).astype(np.float32)
    inv = (10000.0 ** (-np.arange(16, dtype=np.float32) / 16)).astype(np.float32)
    ang = np.concatenate([row[:, None] * inv, col[:, None] * inv], axis=-1).astype(np.float32)
    c["cos"] = np.ascontiguousarray(np.repeat(np.cos(ang), 2, axis=1).T.astype(np.float32))
    c["sin"] = np.ascontiguousarray(np.repeat(np.sin(ang), 2, axis=1).T.astype(np.float32))
    J = np.zeros((64, 64), np.float32)
    for i in range(32):
        J[2 * i + 1, 2 * i] = -1.0
        J[2 * i, 2 * i + 1] = 1.0
    c["Jm"] = J
    return c


def seg_mu(mu_l):
    full = np.zeros((128, NRW), np.float32)
    dirm = np.zeros((128, 4, NRW), np.float32)
    for si in range(NRW):
        c0, w = SEGS[si]
        full[:w, si] = mu_l[c0:c0 + w]
        for p in range(w):
            q = (c0 + p) // 840
            dirm[p, q, si] = mu_l[c0 + p]
    return full, dirm


def seg_dirs(si):
    c0, w = SEGS[si]
    return sorted(set((c0 + p) // 840 for p in range(w)))


SMALL_SPECS = None


def host_small(inp, L):
    s = {}
    s["adab"] = np.stack([fm(inp["ada_b"][l]) for l in range(L)], 1)
    s["lng"] = np.stack([fm(inp["ln_g"][l]) for l in range(L)], 1)
    s["lnb"] = np.stack([fm(inp["ln_b"][l]) for l in range(L)], 1)
    s["gateb"] = np.stack([fm(inp["gate_b"][l]) for l in range(L)], 1)
    mus = [seg_mu(np.asarray(inp["rwkv_mu"][l], np.float32)) for l in range(L)]
    s["mu"] = np.stack([m[0] for m in mus], 1)
    s["mud"] = np.stack([m[1] for m in mus], 1)
    s["w0"] = np.stack([fm(inp["rwkv_w0"][l]) for l in range(L)], 1)
    s["a0"] = np.stack([fm(inp["rwkv_a0"][l]) for l in range(L)], 1)
    for nm, key in (("kk_k", "rwkv_k_k"), ("k_a", "rwkv_k_a"), ("r_k", "rwkv_r_k"), ("lnxg", "rwkv_lnx_g"), ("lnxb", "rwkv_lnx_b")):
        s[nm] = np.stack([fm(inp[key][l]) for l in range(L)], 1)
    s["qng"] = np.stack([fm(inp["mla_q_norm"][l]) for l in range(L)], 1)
    s["kvg"] = np.stack([fm(inp["mla_kv_norm"][l]) for l in range(L)], 1)
    s["convw"] = np.stack([fm(inp["lru_conv_w"][l]) for l in range(L)], 1)
    s["convb"] = np.stack([fm(inp["lru_conv_b"][l]) for l in range(L)], 1)
    for nm, key in (("ba", "lru_ba"), ("bx", "lru_bx"), ("lam", "lru_lambda")):
        s[nm] = np.stack([fm(inp[key][l]) for l in range(L)], 1)
    return {k: np.ascontiguousarray(v, np.float32) for k, v in s.items()}


BIG_W = ["ada_w", "ffn_w_gate", "ffn_w_up", "ffn_w_down", "w_in", "rwkv_w_up", "rwkv_a_up", "rwkv_g_up",
         "mla_w_uq", "mla_w_uk", "mla_w_uv", "lru_wa", "lru_wx", "proj_rwkv", "proj_mla", "proj_lru", "w_out"]


def setup(nc, SEQ, L, shapes, ext_in=(), ext_out=()):
    g = G()
    g.P = Prog(nc)
    g.SEQ = SEQ
    g.T = CTX + SEQ
    g.R = SEQ // GW
    g.L = L
    g.d = {}
    P = g.P
    for nm, shp in shapes.items():
        g.d[nm] = P.dram(nm, shp, F32, "ExternalInput")
    g.ps = [P.psum("ps%d" % i, [128, 512], F32) for i in range(8)]
    g.ext_in = set(ext_in)
    g.ext_out = set(ext_out)
    return g


def scratch(g, name, shape, dtype=F32):
    if name in g.d:
        return g.d[name]
    kind = "Internal"
    if name in g.ext_in:
        kind = "ExternalInput"
    if name in g.ext_out:
        kind = "ExternalOutput"
    g.d[name] = g.P.dram(name, shape, dtype, kind)
    return g.d[name]


def dbg(g, name, tile, ap, shape):
    if name in g.ext_out:
        d = scratch(g, name, shape)
        g.P.dma(d[:], ap, reads=[tile], writes=[d])


def load_const(g, name, shape, src=None, q="sp"):
    P = g.P
    t = P.sbuf(name, shape, F32)
    src = src if src is not None else g.d[name][:]
    P.dma(t[:], src, reads=[g.d[name]], writes=[t], q=q)
    return t


def load_consts(g, which):
    P = g.P
    g.c = {}
    shp = {"ident": [128, 128], "ones": [128, 128], "blk": [128, 128], "Jm": [64, 64]}
    for nm in which:
        if nm in shp:
            g.c[nm] = load_const(g, nm, shp[nm])
    if "maskA" in which:
        g.c["maskA"] = [load_const(g, "maskA", [128, 192], g.d["maskA"][d]) for d in range(2)]
        g.c["maskXY"] = [load_const(g, "maskXY", [128, 256], g.d["maskXY"][d]) for d in range(2)]
        g.c["rm"] = [load_const(g, "rm", [128, 1024], g.d["rm"][d]) for d in range(2)]


def load_small(g, names):
    g.s = {}
    for nm in names:
        shp = list(g.d[nm].t.shape)
        g.s[nm] = load_const(g, nm, shp)


def stage_mod(g, l):
    P = g.P
    pm = g.ps[0]
    with P.scope():
        wts = [P.sbuf("adw%d" % i, [128, KC, 512], F32) for i in range(2)]
        aw = g.d["ada_w"]
        for grp in range(36):
            wt = wts[grp % 2]
            P.dma(wt[:], aw[l, :, grp * 512:(grp + 1) * 512].rearrange("(kc p) n -> p kc n", p=128),
                  reads=[aw], writes=[wt], q="sp" if grp % 2 == 0 else "act")
            for jj in range(4):
                j = grp * 4 + jj
                for kc in range(KC):
                    P.op("pe", "matmul", pm[:, 2 * j:2 * j + 2], lhsT=wt[:, kc, jj * 128:(jj + 1) * 128], rhs=g.cs[:, kc, :],
                         start=(kc == 0), stop=(kc == KC - 1), reads=[wt, g.cs], writes=[pm])
        adab = g.s["adab"]
        P.op("dve", "tensor_tensor", out=g.mod[:], in0=pm[:, 0:288].rearrange("p (j m) -> p j m", m=2),
             in1=adab[:, l, :].unsqueeze(2).to_broadcast([128, 144, 2]), op=ALU.add, reads=[pm, adab], writes=[g.mod])
        P.op("dve", "tensor_scalar", out=g.onep[:], in0=g.mod[:], scalar1=1.0, scalar2=None, op0=ALU.add, reads=[g.mod], writes=[g.onep])
        P.op("dve", "tensor_scalar", out=g.hg[:], in0=g.mod[:], scalar1=0.5, scalar2=None, op0=ALU.mult, reads=[g.mod], writes=[g.hg])


def init_mod(g):
    P = g.P
    g.cs = P.sbuf("cs", [128, KC, 2], F32)
    P.dma(g.cs[:], g.d["cvec"][:], reads=[g.d["cvec"]], writes=[g.cs])
    P.op("act", "activation", out=g.cs[:], in_=g.cs[:], func=AF.Silu, reads=[g.cs], writes=[g.cs])
    g.mod = P.sbuf("mod", [128, 144, 2], F32)
    g.onep = P.sbuf("onep", [128, 144, 2], F32)
    g.hg = P.sbuf("hg", [128, 144, 2], F32)


def rsqrt_(P, out, in_, eps, tiles, scale=1.0):
    P.op("act", "activation", out=out, in_=in_, func=AF.Sqrt, bias=eps, scale=scale, reads=tiles, writes=tiles)
    P.op("dve", "reciprocal", out=out, in_=out, reads=tiles, writes=tiles)


def layer_norm(g, z, NT, l, idx):
    P = g.P
    s1 = g.ps[6]
    s2 = g.ps[7]
    ones = g.c["ones"]
    with P.scope():
        sq = [P.sbuf("lnsq%d" % i, [128, NT], F32) for i in range(2)]
        mean = P.sbuf("lnmean", [128, NT], F32)
        rstd = P.sbuf("lnrstd", [128, NT], F32)
        for dc in range(KC):
            P.op("pe", "matmul", s1[:, :NT], lhsT=ones[:], rhs=z[:, dc, :], start=(dc == 0), stop=(dc == KC - 1), reads=[ones, z], writes=[s1])
            q = sq[dc % 2]
            P.op("act", "activation", out=q[:], in_=z[:, dc, :], func=AF.Square, reads=[z], writes=[q])
            P.op("pe", "matmul", s2[:, :NT], lhsT=ones[:], rhs=q[:], start=(dc == 0), stop=(dc == KC - 1), reads=[ones, q], writes=[s2])
        P.op("act", "mul", out=mean[:], in_=s1[:, :NT], mul=1.0 / D, reads=[s1], writes=[mean])
        P.op("dve", "tensor_tensor", out=rstd[:], in0=mean[:], in1=mean[:], op=ALU.mult, reads=[mean], writes=[rstd])
        P.op("dve", "scalar_tensor_tensor", out=rstd[:], in0=s2[:, :NT], scalar=1.0 / D, in1=rstd[:], op0=ALU.mult, op1=ALU.subtract, reads=[s2, rstd], writes=[rstd])
        rsqrt_(P, rstd[:], rstd[:], LN_EPS, [rstd])
        lng = g.s["lng"]
        lnb = g.s["lnb"]
        for dc in range(KC):
            P.op("dve", "tensor_tensor", out=z[:, dc, :], in0=z[:, dc, :], in1=mean[:], op=ALU.subtract, reads=[z, mean], writes=[z])
            P.op("dve", "tensor_tensor", out=z[:, dc, :], in0=z[:, dc, :], in1=rstd[:], op=ALU.mult, reads=[z, rstd], writes=[z])
            P.op("dve", "tensor_scalar", out=z[:, dc, :], in0=z[:, dc, :], scalar1=lng[:, l, idx, dc:dc + 1], scalar2=lnb[:, l, idx, dc:dc + 1],
                 op0=ALU.mult, op1=ALU.add, reads=[z, lng, lnb], writes=[z])


def modulate(g, out, s, NT, m, sc_slot, sh_slot):
    P = g.P
    for kc in range(KC):
        P.op("dve", "tensor_scalar", out=out[:, kc, :], in0=s[:, kc, :], scalar1=g.onep[:, sc_slot * 16 + kc, m:m + 1],
             scalar2=g.mod[:, sh_slot * 16 + kc, m:m + 1], op0=ALU.mult, op1=ALU.add, reads=[s, g.onep, g.mod], writes=[out])


def ffn_block(g, s, NT, m, l, j, slots, ln_idx):
    P = g.P
    sh, sc, gt = slots
    wgd, wud, wdd = g.d["ffn_w_gate"], g.d["ffn_w_up"], g.d["ffn_w_down"]
    with P.scope():
        AT = P.sbuf("ffa", [128, FC, NT], BF16)
        with P.scope():
            h = P.sbuf("ffh", [128, KC, NT], BF16)
            modulate(g, h, s, NT, m, sc, sh)
            P.op("act", "mul", out=s[:], in_=s[:], mul=ALPHA, reads=[s], writes=[s])
            wgs = [P.sbuf("wg%d" % i, [128, KC, 128], BF16) for i in range(3)]
            wus = [P.sbuf("wu%d" % i, [128, KC, 128], BF16) for i in range(3)]
            sgs = [P.sbuf("sg%d" % i, [128, NT], F32) for i in range(2)]
            for fc in range(FC):
                wg = wgs[fc % 3]
                wu = wus[fc % 3]
                P.dma(wg[:], wgd[l, j, :, fc * 128:(fc + 1) * 128].rearrange("(kc p) n -> p kc n", p=128), reads=[wgd], writes=[wg], q="pool")
                P.dma(wu[:], wud[l, j, :, fc * 128:(fc + 1) * 128].rearrange("(kc p) n -> p kc n", p=128), reads=[wud], writes=[wu], q="pool")
                pg = g.ps[(2 * fc) % 4]
                pu = g.ps[(2 * fc + 1) % 4]
                for kc in range(KC):
                    P.op("pe", "matmul", pg[:, :NT], lhsT=wg[:, kc, :], rhs=h[:, kc, :], start=(kc == 0), stop=(kc == KC - 1), reads=[wg, h], writes=[pg])
                for kc in range(KC):
                    P.op("pe", "matmul", pu[:, :NT], lhsT=wu[:, kc, :], rhs=h[:, kc, :], start=(kc == 0), stop=(kc == KC - 1), reads=[wu, h], writes=[pu])
                sg = sgs[fc % 2]
                P.op("act", "activation", out=sg[:], in_=pg[:, :NT], func=AF.Silu, reads=[pg], writes=[sg])
                P.op("dve", "tensor_tensor", out=AT[:, fc, :], in0=sg[:], in1=pu[:, :NT], op=ALU.mult, reads=[sg, pu], writes=[AT])
        with P.scope():
            wds = [P.sbuf("wd%d" % i, [128, FC, 128], BF16) for i in range(2)]
            for dc in range(KC):
                wd = wds[dc % 2]
                P.dma(wd[:], wdd[l, j, :, dc * 128:(dc + 1) * 128].rearrange("(fc p) n -> p fc n", p=128), reads=[wdd], writes=[wd], q="pool")
                py = g.ps[4 + dc % 2]
                for fc in range(FC):
                    P.op("pe", "matmul", py[:, :NT], lhsT=wd[:, fc, :], rhs=AT[:, fc, :], start=(fc == 0), stop=(fc == FC - 1), reads=[wd, AT], writes=[py])
                P.op("dve", "scalar_tensor_tensor", out=s[:, dc, :], in0=py[:, :NT], scalar=g.hg[:, gt * 16 + dc, m:m + 1], in1=s[:, dc, :],
                     op0=ALU.mult, op1=ALU.add, reads=[py, s, g.hg], writes=[s])
    layer_norm(g, s, NT, l, ln_idx)


def token_tiles(g, with_ctx=True):
    tl = []
    if with_ctx:
        tl.append((0, CTX, 1))
    t0 = CTX
    while t0 < g.T:
        nt = min(512, g.T - t0)
        tl.append((t0, nt, 0))
        t0 += nt
    return tl


def fview(d):
    return d[:].rearrange("(kc p) t -> p kc t", p=128)


def stage_a(g, l, src, do_ffn=True):
    P = g.P
    XL = scratch(g, "XL", [D, g.T])
    F = [scratch(g, "F%d" % i, [128, g.T]) for i in range(NSEG)]
    win = g.d["w_in"]
    gateb = g.s["gateb"]
    for (t0, NT, m) in token_tiles(g):
        with P.scope():
            s = P.sbuf("s", [128, KC, NT], F32)
            P.dma(s[:], fview(src)[:, :, t0:t0 + NT], reads=[src], writes=[s])
            if do_ffn:
                ffn_block(g, s, NT, m, l, 0, (0, 1, 2), 0)
            P.dma(fview(XL)[:, :, t0:t0 + NT], s[:], reads=[s], writes=[XL])
            with P.scope():
                hl = P.sbuf("hl", [128, KC, NT], BF16)
                modulate(g, hl, s, NT, m, 4, 3)
                wts = [P.sbuf("wi%d" % i, [128, KC, 128], BF16) for i in range(3)]
                stg = [P.sbuf("stg%d" % i, [128, NT], F32) for i in range(3)]
                for si, (c0, w) in enumerate(SEGS):
                    wt = wts[si % 3]
                    P.dma(wt[:, :, :w], win[l, :, c0:c0 + w].rearrange("(kc p) n -> p kc n", p=128), reads=[win], writes=[wt], q="pool")
                    pp = g.ps[si % 4]
                    for kc in range(KC):
                        P.op("pe", "matmul", pp[:w, :NT], lhsT=wt[:, kc, :w], rhs=hl[:, kc, :], start=(kc == 0), stop=(kc == KC - 1), reads=[wt, hl], writes=[pp])
                    st = stg[si % 3]
                    if si >= S_GATE:
                        P.op("act", "activation", out=st[:w, :], in_=pp[:w, :NT], func=AF.Sigmoid, bias=gateb[:, l, si - S_GATE:si - S_GATE + 1],
                             reads=[pp, gateb], writes=[st])
                    elif si % 2 == 0:
                        P.op("act", "copy", out=st[:w, :], in_=pp[:w, :NT], reads=[pp], writes=[st])
                    else:
                        P.op("dve", "tensor_copy", out=st[:w, :], in_=pp[:w, :NT], reads=[pp], writes=[st])
                    P.dma(F[si][:w, t0:t0 + NT], st[:w, :], reads=[st], writes=[F[si]])


def init_derived(g):
    P = g.P
    s = g.s
    for nm, src in (("omm", "mu"), ("omka", "k_a")):
        t = P.sbuf(nm, list(g.d[src].t.shape), F32)
        P.op("dve", "tensor_scalar", out=t[:], in0=s[src][:], scalar1=-1.0, scalar2=1.0, op0=ALU.mult, op1=ALU.add, reads=[s[src]], writes=[t])
        s[nm] = t
    s["clam"] = make_clam(g, s["lam"], list(g.d["lam"].t.shape))
    return
    s["clam"] = t


def make_clam(g, lam, shape):
    P = g.P
    e = P.sbuf("clam_e", shape, F32)
    t = P.sbuf("clam", shape, F32)
    P.op("act", "activation", out=e[:], in_=lam[:], func=AF.Exp, scale=-1.0, reads=[lam], writes=[e])
    nt = 12
    P.op("dve", "tensor_scalar", out=t[:], in0=e[:], scalar1=((-1.0) ** (nt + 1)) / nt, scalar2=((-1.0) ** nt) / (nt - 1), op0=ALU.mult, op1=ALU.add, reads=[e], writes=[t])
    for k in range(nt - 2, 0, -1):
        P.op("dve", "tensor_tensor", out=t[:], in0=t[:], in1=e[:], op=ALU.mult, reads=[t, e], writes=[t])
        P.op("dve", "tensor_scalar", out=t[:], in0=t[:], scalar1=((-1.0) ** (k + 1)) / k, scalar2=None, op0=ALU.add, reads=[t], writes=[t])
    P.op("dve", "scalar_tensor_tensor", out=t[:], in0=t[:], scalar=-8.0, in1=e[:], op0=ALU.mult, op1=ALU.mult, reads=[t, e], writes=[t])
    return t


def blocks_of(g):
    bl = [(0, CTX)]
    t0 = CTX
    while t0 < g.T:
        n = min(512, g.T - t0)
        bl.append((t0, n))
        t0 += n
    return bl


def stage_lru(g, l):
    P = g.P
    T = g.T
    F = [scratch(g, "F%d" % i, [128, T]) for i in range(NSEG)]
    YL = scratch(g, "YL", [LRU_W, T], BF16)
    cw, cb, ba, bx, clam = g.s["convw"], g.s["convb"], g.s["ba"], g.s["bx"], g.s["clam"]
    wad, wxd = g.d["lru_wa"], g.d["lru_wx"]
    for n in range(8):
        with P.scope():
            xr = P.sbuf("lxr", [128, T], F32)
            gb = P.sbuf("lgb", [128, T], F32)
            xc = P.sbuf("lxc", [128, T], F32)
            hs = P.sbuf("lhs", [128, T], F32)
            gr = P.sbuf("lgr", [128, T], F32)
            gi = P.sbuf("lgi", [128, T], F32)
            tmp = P.sbuf("ltmp", [128, T], F32)
            yb = P.sbuf("lyb", [128, T], BF16)
            P.dma(xr[:], F[S_XR + n][:, :], reads=[F[S_XR + n]], writes=[xr])
            P.dma(gb[:], F[S_GB + n][:, :], reads=[F[S_GB + n]], writes=[gb], q="act")
            for (a, b) in ((0, CTX), (CTX, T)):
                P.op("dve", "tensor_scalar", out=xc[:, a:b], in0=xr[:, a:b], scalar1=cw[:, l, 2, n:n + 1], scalar2=cb[:, l, n:n + 1],
                     op0=ALU.mult, op1=ALU.add, reads=[xr, cw, cb], writes=[xc])
                for (tap, off) in ((0, -2), (1, -1), (3, 1)):
                    if off < 0:
                        o_, i_ = xc[:, a - off:b], xr[:, a:b + off]
                    else:
                        o_, i_ = xc[:, a:b - off], xr[:, a + off:b]
                    P.op("dve", "scalar_tensor_tensor", out=o_, in0=i_, scalar=cw[:, l, tap, n:n + 1], in1=o_, op0=ALU.mult, op1=ALU.add,
                         reads=[xr, xc, cw], writes=[xc])
            if n == 0:
                dbg(g, "dbg_xc", xc, xc[:], [128, T])
            for d in range(2):
                wa = P.sbuf("lwa%d" % d, [128, 128], F32)
                wx = P.sbuf("lwx%d" % d, [128, 128], F32)
                P.dma(wa[:], wad[l, d, n], reads=[wad], writes=[wa])
                P.dma(wx[:], wxd[l, d, n], reads=[wxd], writes=[wx], q="act")
                for bi, (t0, nb) in enumerate(blocks_of(g)):
                    p1 = g.ps[(2 * bi) % 4]
                    p2 = g.ps[(2 * bi + 1) % 4]
                    P.op("pe", "matmul", p1[:, :nb], lhsT=wa[:], rhs=xc[:, t0:t0 + nb], start=True, stop=True, reads=[wa, xc], writes=[p1])
                    P.op("pe", "matmul", p2[:, :nb], lhsT=wx[:], rhs=xc[:, t0:t0 + nb], start=True, stop=True, reads=[wx, xc], writes=[p2])
                    P.op("act", "activation", out=gr[:, t0:t0 + nb], in_=p1[:, :nb], func=AF.Sigmoid, bias=ba[:, l, d, n:n + 1], reads=[p1, ba], writes=[gr])
                    P.op("act", "activation", out=gi[:, t0:t0 + nb], in_=p2[:, :nb], func=AF.Sigmoid, bias=bx[:, l, d, n:n + 1], reads=[p2, bx], writes=[gi])
                P.op("dve", "tensor_scalar", out=gr[:], in0=gr[:], scalar1=clam[:, l, d, n:n + 1], scalar2=0.25, op0=ALU.mult, op1=ALU.mult, reads=[gr, clam], writes=[gr])
                P.op("dve", "tensor_scalar", out=tmp[:], in0=gr[:], scalar1=1.0 / 7, scalar2=1.0, op0=ALU.mult, op1=ALU.add, reads=[gr], writes=[tmp])
                for kk_ in (6, 5, 4, 3, 2):
                    P.op("dve", "tensor_tensor", out=tmp[:], in0=tmp[:], in1=gr[:], op=ALU.mult, reads=[tmp, gr], writes=[tmp])
                    P.op("dve", "tensor_scalar", out=tmp[:], in0=tmp[:], scalar1=1.0 / kk_, scalar2=1.0, op0=ALU.mult, op1=ALU.add, reads=[tmp], writes=[tmp])
                P.op("dve", "scalar_tensor_tensor", out=tmp[:], in0=tmp[:], scalar=-1.0, in1=gr[:], op0=ALU.mult, op1=ALU.mult, reads=[tmp, gr], writes=[tmp])
                P.op("dve", "tensor_scalar", out=gr[:], in0=tmp[:], scalar1=-1.0, scalar2=1.0, op0=ALU.mult, op1=ALU.add, reads=[tmp], writes=[gr])
                P.op("dve", "tensor_tensor", out=gr[:], in0=gr[:], in1=gr[:], op=ALU.mult, reads=[gr], writes=[gr])
                P.op("dve", "scalar_tensor_tensor", out=xr[:], in0=tmp[:], scalar=-1.0, in1=tmp[:], op0=ALU.mult, op1=ALU.mult, reads=[tmp], writes=[xr])
                P.op("dve", "scalar_tensor_tensor", out=tmp[:], in0=tmp[:], scalar=2.0, in1=xr[:], op0=ALU.mult, op1=ALU.add, reads=[tmp, xr], writes=[tmp])
                P.op("dve", "scalar_tensor_tensor", out=tmp[:], in0=gr[:], scalar=1.0, in1=tmp[:], op0=ALU.add, op1=ALU.mult, reads=[gr, tmp], writes=[tmp])
                P.op("dve", "tensor_tensor", out=gr[:], in0=gr[:], in1=gr[:], op=ALU.mult, reads=[gr], writes=[gr])
                P.op("dve", "scalar_tensor_tensor", out=tmp[:], in0=gr[:], scalar=1.0, in1=tmp[:], op0=ALU.add, op1=ALU.mult, reads=[gr, tmp], writes=[tmp])
                P.op("act", "activation", out=tmp[:], in_=tmp[:], func=AF.Sqrt, reads=[tmp], writes=[tmp])
                P.op("dve", "tensor_tensor", out=tmp[:], in0=tmp[:], in1=gi[:], op=ALU.mult, reads=[tmp, gi], writes=[tmp])
                P.op("dve", "tensor_tensor", out=tmp[:], in0=tmp[:], in1=xc[:], op=ALU.mult, reads=[tmp, xc], writes=[tmp])
                if n == 0:
                    dbg(g, "dbg_a%d" % d, gr, gr[:], [128, T])
                    dbg(g, "dbg_b%d" % d, tmp, tmp[:], [128, T])
                hd = hs if d == 0 else gi
                if d == 0:
                    P.op("dve", "tensor_tensor_scan", out=hd[:, :], data0=gr[:, :], data1=tmp[:, :], initial=0.0, op0=ALU.mult, op1=ALU.add,
                         reads=[gr, tmp], writes=[hd])
                else:
                    P.op("dve", "tensor_tensor_scan", out=hd[:, 0:CTX][:, ::-1], data0=gr[:, 0:CTX][:, ::-1], data1=tmp[:, 0:CTX][:, ::-1], initial=0.0,
                         op0=ALU.mult, op1=ALU.add, reads=[gr, tmp], writes=[hd])
                    P.op("dve", "scalar_tensor_tensor", out=tmp[:, T - 1:T], in0=gr[:, T - 1:T], scalar=hd[:, 0:1], in1=tmp[:, T - 1:T], op0=ALU.mult, op1=ALU.add,
                         reads=[gr, hd, tmp], writes=[tmp])
                    P.op("dve", "tensor_tensor_scan", out=hd[:, CTX:T][:, ::-1], data0=gr[:, CTX:T][:, ::-1], data1=tmp[:, CTX:T][:, ::-1], initial=0.0,
                         op0=ALU.mult, op1=ALU.add, reads=[gr, tmp, hd], writes=[hd])
                    P.op("dve", "tensor_tensor", out=hs[:], in0=hs[:], in1=hd[:], op=ALU.add, reads=[hs, hd], writes=[hs])
            if n == 0:
                dbg(g, "dbg_hs", hs, hs[:], [128, T])
            P.op("dve", "tensor_tensor", out=tmp[:], in0=gb[:], in1=gb[:], op=ALU.mult, reads=[gb], writes=[tmp])
            P.op("dve", "tensor_scalar", out=tmp[:], in0=tmp[:], scalar1=0.044715, scalar2=1.0, op0=ALU.mult, op1=ALU.add, reads=[tmp], writes=[tmp])
            P.op("dve", "tensor_tensor", out=tmp[:], in0=tmp[:], in1=gb[:], op=ALU.mult, reads=[tmp, gb], writes=[tmp])
            P.op("act", "activation", out=tmp[:], in_=tmp[:], func=AF.Sigmoid, scale=1.5957691216057308, reads=[tmp], writes=[tmp])
            P.op("dve", "tensor_tensor", out=tmp[:], in0=tmp[:], in1=gb[:], op=ALU.mult, reads=[tmp, gb], writes=[tmp])
            P.op("dve", "tensor_tensor", out=yb[:], in0=tmp[:], in1=hs[:], op=ALU.mult, reads=[tmp, hs], writes=[yb])
            P.dma(YL[n * 128:(n + 1) * 128, :], yb[:], reads=[yb], writes=[YL])


def stage_rwkv_mix(g, l):
    P = g.P
    T = g.T
    R = g.R
    F = [scratch(g, "F%d" % i, [128, T]) for i in range(NSEG)]
    omm, mud = g.s["omm"], g.s["mud"]
    with P.scope():
        fs = [P.sbuf("mxf%d" % i, [128, T], F32) for i in range(2)]
        os_ = [P.sbuf("mxo%d" % i, [128, T], F32) for i in range(2)]
        for si in range(NRW):
            c0, w = SEGS[si]
            f = fs[si % 2]
            o = os_[si % 2]
            P.dma(f[:w, :], F[si][:w, :], reads=[F[si]], writes=[f], q="sp" if si % 2 == 0 else "act")
            P.op("dve", "tensor_scalar", out=o[:w, :], in0=f[:w, :], scalar1=omm[:w, l, si:si + 1], scalar2=None, op0=ALU.mult, reads=[f, omm], writes=[o])
            fl = f[:w, CTX:T].rearrange("p (r c) -> p r c", c=GW)
            ol = o[:w, CTX:T].rearrange("p (r c) -> p r c", c=GW)
            for q in seg_dirs(si):
                sc = mud[:w, l, q, si:si + 1]
                if q == 0:
                    pairs = [(ol[:, :, 1:GW], fl[:, :, 0:GW - 1]), (o[:w, 1:CTX], f[:w, 0:CTX - 1])]
                elif q == 1:
                    pairs = [(ol[:, :, 0:GW - 1], fl[:, :, 1:GW]), (o[:w, 0:CTX - 1], f[:w, 1:CTX])]
                elif q == 2:
                    pairs = [(ol[:, 1:R, :], fl[:, 0:R - 1, :]), (o[:w, 1:CTX], f[:w, 0:CTX - 1])]
                else:
                    pairs = [(ol[:, 0:R - 1, :], fl[:, 1:R, :]), (o[:w, 0:CTX - 1], f[:w, 1:CTX])]
                for (oo, ii) in pairs:
                    P.op("dve", "scalar_tensor_tensor", out=oo, in0=ii, scalar=sc, in1=oo, op0=ALU.mult, op1=ALU.add, reads=[f, o, mud], writes=[o])
            P.dma(F[si][:w, :], o[:w, :], reads=[o], writes=[F[si]])


def rwkv_stream(g, l, hp, d, bankA, bankB, blocks, tr, YS):
    P = g.P
    F = [scratch(g, "F%d" % i, [128, g.T]) for i in range(NSEG)]
    ident, blk = g.c["ident"], g.c["blk"]
    maskA, maskXY, rm = g.c["maskA"][d], g.c["maskXY"][d], g.c["rm"][d]
    s = g.s
    pA = pTM = pW0 = pSP = bankA
    pXY = pR = pU = pY = bankB
    allA = [bankA]
    A_, TM_, W0_, SP_ = bankA[:, 0:192], bankA[:, 192:384], bankA[:, 384:448], bankA[:, 448:512]
    XY_, R_, U_, Y_ = bankB[:, 0:256], bankB[:, 256:384], bankB[:, 384:448], bankB[:, 448:512]
    P.op("dve", "memset", bankB[:, 0:256], 0.0, reads=[], writes=[pXY])
    nm = "_%d_%d" % (hp, d)
    ST = P.sbuf("ST" + nm, [128, 64], F32)
    P.op("dve", "memset", ST[:], 0.0, reads=[], writes=[ST])
    wup = P.sbuf("wup" + nm, [64, 128], F32)
    aup = P.sbuf("aup" + nm, [64, 128], F32)
    P.dma(wup[:], g.d["rwkv_w_up"][l, d, :, hp * 128:(hp + 1) * 128], reads=[g.d["rwkv_w_up"]], writes=[wup])
    P.dma(aup[:], g.d["rwkv_a_up"][l, d, :, hp * 128:(hp + 1) * 128], reads=[g.d["rwkv_a_up"]], writes=[aup])
    LM = 512
    rt, al, be, kt, vv, Ep, Yb = [P.sbuf(n_ + nm, [128, LM], F32) for n_ in ("rt", "al", "be", "kt", "vv", "Ep", "Yb")]
    XYs = [P.sbuf("XYs%d" % i + nm, [128, 256], F32) for i in range(2)]
    Rs = [P.sbuf("Rs%d" % i + nm, [128, 128], F32) for i in range(2)]
    Am = P.sbuf("Am" + nm, [128, 192], F32)
    TMs = P.sbuf("TMs" + nm, [128, 192], F32)
    W0s = P.sbuf("W0s" + nm, [128, 64], F32)
    Us = P.sbuf("Us" + nm, [128, 64], F32)
    r_, k_, twd, adt, sig, Lc, t1, t2, a_, kk_, t3 = tr
    w0, a0, kkk, ka, omka = s["w0"], s["a0"], s["kk_k"], s["k_a"], s["omka"]
    H2 = (slice(0, 64), slice(64, 128))
    for (t0, Lb) in blocks:
        ts_ = slice(t0, t0 + Lb)
        P.dma(r_[:, :Lb], F[S_R + hp][:, ts_], reads=[F[S_R + hp]], writes=[r_])
        P.dma(k_[:, :Lb], F[S_K + hp][:, ts_], reads=[F[S_K + hp]], writes=[k_], q="act")
        P.dma(vv[:, :Lb], F[S_V + hp][:, ts_], reads=[F[S_V + hp]], writes=[vv])
        P.dma(twd[0:64, :Lb], F[S_WD][0:64, ts_], reads=[F[S_WD]], writes=[twd], q="act")
        P.dma(adt[0:64, :Lb], F[S_AD][0:64, ts_], reads=[F[S_AD]], writes=[adt])
        P.op("act", "activation", out=twd[0:64, :Lb], in_=twd[0:64, :Lb], func=AF.Tanh, reads=[twd], writes=[twd])
        P.op("pe", "matmul", bankA[:, :Lb], lhsT=wup[:], rhs=twd[0:64, :Lb], start=True, stop=True, reads=[wup, twd], writes=allA)
        P.op("act", "activation", out=sig[:, :Lb], in_=bankA[:, :Lb], func=AF.Sigmoid, bias=w0[:, l, d, hp:hp + 1], reads=allA + [w0], writes=[sig])
        P.op("pe", "matmul", bankA[:, :Lb], lhsT=aup[:], rhs=adt[0:64, :Lb], start=True, stop=True, reads=[aup, adt] + allA, writes=allA)
        P.op("act", "activation", out=a_[:, :Lb], in_=bankA[:, :Lb], func=AF.Sigmoid, bias=a0[:, l, d, hp:hp + 1], reads=allA + [a0], writes=[a_])
        if d == 0:
            P.op("dve", "tensor_tensor_scan", out=Lc[:, :Lb], data0=rm[:, :Lb], data1=sig[:, :Lb], initial=0.0, op0=ALU.mult, op1=ALU.add,
                 reads=[rm, sig], writes=[Lc])
        else:
            P.op("dve", "tensor_tensor_scan", out=Lc[:, :Lb][:, ::-1], data0=rm[:, :Lb][:, ::-1], data1=sig[:, :Lb][:, ::-1], initial=0.0,
                 op0=ALU.mult, op1=ALU.add, reads=[rm, sig], writes=[Lc])
        P.op("act", "activation", out=Ep[:, :Lb], in_=Lc[:, :Lb], func=AF.Exp, scale=-C0, reads=[Lc], writes=[Ep])
        P.op("act", "activation", out=t1[:, :Lb], in_=Lc[:, :Lb], func=AF.Exp, scale=C0, reads=[Lc], writes=[t1])
        P.op("dve", "tensor_tensor", out=t2[:, :Lb], in0=Lc[:, :Lb], in1=sig[:, :Lb], op=ALU.subtract, reads=[Lc, sig], writes=[t2])
        P.op("act", "activation", out=t2[:, :Lb], in_=t2[:, :Lb], func=AF.Exp, scale=-C0, reads=[t2], writes=[t2])
        P.op("dve", "tensor_scalar", out=kk_[:, :Lb], in0=k_[:, :Lb], scalar1=kkk[:, l, hp:hp + 1], scalar2=None, op0=ALU.mult, reads=[k_, kkk], writes=[kk_])
        P.op("dve", "tensor_tensor", out=t3[:, :Lb], in0=kk_[:, :Lb], in1=kk_[:, :Lb], op=ALU.mult, reads=[kk_], writes=[t3])
        P.op("pe", "matmul", bankA[:, :Lb], lhsT=blk[:], rhs=t3[:, :Lb], start=True, stop=True, reads=[blk, t3] + allA, writes=allA)
        P.op("act", "activation", out=t3[:, :Lb], in_=bankA[:, :Lb], func=AF.Sqrt, reads=allA, writes=[t3])
        P.op("dve", "tensor_scalar", out=t3[:, :Lb], in0=t3[:, :Lb], scalar1=1e-12, scalar2=None, op0=ALU.max, reads=[t3], writes=[t3])
        P.op("dve", "reciprocal", out=t3[:, :Lb], in_=t3[:, :Lb], reads=[t3], writes=[t3])
        P.op("dve", "tensor_tensor", out=kk_[:, :Lb], in0=kk_[:, :Lb], in1=t3[:, :Lb], op=ALU.mult, reads=[kk_, t3], writes=[kk_])
        P.op("dve", "tensor_scalar", out=t3[:, :Lb], in0=a_[:, :Lb], scalar1=ka[:, l, hp:hp + 1], scalar2=omka[:, l, hp:hp + 1], op0=ALU.mult, op1=ALU.add,
             reads=[a_, ka, omka], writes=[t3])
        P.op("dve", "tensor_tensor", out=t3[:, :Lb], in0=t3[:, :Lb], in1=k_[:, :Lb], op=ALU.mult, reads=[t3, k_], writes=[t3])
        P.op("dve", "tensor_tensor", out=kt[:, :Lb], in0=t3[:, :Lb], in1=t1[:, :Lb], op=ALU.mult, reads=[t3, t1], writes=[kt])
        P.op("dve", "tensor_tensor", out=rt[:, :Lb], in0=r_[:, :Lb], in1=Ep[:, :Lb], op=ALU.mult, reads=[r_, Ep], writes=[rt])
        P.op("dve", "scalar_tensor_tensor", out=al[:, :Lb], in0=kk_[:, :Lb], scalar=-1.0, in1=t2[:, :Lb], op0=ALU.mult, op1=ALU.mult, reads=[kk_, t2], writes=[al])
        P.op("dve", "tensor_tensor", out=be[:, :Lb], in0=kk_[:, :Lb], in1=a_[:, :Lb], op=ALU.mult, reads=[kk_, a_], writes=[be])
        P.op("dve", "tensor_tensor", out=be[:, :Lb], in0=be[:, :Lb], in1=t1[:, :Lb], op=ALU.mult, reads=[be, t1], writes=[be])
        if hp == 0 and d == 1 and t0 == CTX:
            for nm_, tl_ in (("dbg_sig", sig), ("dbg_L", Lc), ("dbg_Ep", Ep), ("dbg_Em", t1), ("dbg_rt", rt), ("dbg_al", al), ("dbg_be", be), ("dbg_kt", kt), ("dbg_kk", kk_)):
                dbg(g, nm_, tl_, tl_[:, :Lb], [128, Lb])
        yield
        nch = Lb // CH
        for c in (range(nch) if d == 0 else range(nch - 1, -1, -1)):
            cs = slice(c * CH, (c + 1) * CH)
            for h in range(2):
                hs = H2[h]
                for j, src in enumerate((be, kt, vv)):
                    P.op("pe", "matmul", TM_[hs, 64 * j:64 * j + 64], lhsT=src[hs, cs], rhs=ident[hs, hs], start=True, stop=True, reads=[src, ident], writes=[pTM])
            P.op("act", "copy", out=TMs[:], in_=TM_, reads=[pTM], writes=[TMs])
            for h in range(2):
                hs = H2[h]
                P.op("pe", "matmul", XY_[hs, 64 * h:64 * h + 64], lhsT=be[hs, cs], rhs=al[hs, cs], start=True, stop=True, reads=[be, al], writes=[pXY])
                P.op("pe", "matmul", XY_[hs, 128 + 64 * h:192 + 64 * h], lhsT=al[hs, cs], rhs=be[hs, cs], start=True, stop=True, reads=[be, al], writes=[pXY])
                P.op("pe", "matmul", A_[hs, 0:64], lhsT=be[hs, cs], rhs=rt[hs, cs], start=True, stop=True, reads=[be, rt], writes=[pA])
                P.op("pe", "matmul", A_[hs, 64:128], lhsT=kt[hs, cs], rhs=al[hs, cs], start=True, stop=True, reads=[kt, al], writes=[pA])
                P.op("pe", "matmul", A_[hs, 128:192], lhsT=kt[hs, cs], rhs=rt[hs, cs], start=True, stop=True, reads=[kt, rt], writes=[pA])
            cur = 0
            P.op("dve", "tensor_tensor", out=XYs[cur][:], in0=XY_, in1=maskXY[:], op=ALU.mult, reads=[pXY, maskXY], writes=[XYs[cur]])
            P.op("dve", "tensor_tensor", out=Am[:], in0=A_, in1=maskA[:], op=ALU.mult, reads=[pA, maskA], writes=[Am])
            rc = 0
            P.op("dve", "tensor_tensor", out=Rs[rc][:], in0=XYs[cur][:, 0:128], in1=ident[:], op=ALU.add, reads=[XYs[cur], ident], writes=[Rs[rc]])
            yield
            for lvl in range(5):
                X, Y = XYs[cur][:, 0:128], XYs[cur][:, 128:256]
                nxt = 1 - cur
                if lvl < 4:
                    P.op("pe", "matmul", XY_[:, 0:128], lhsT=Y, rhs=X, start=True, stop=True, reads=[XYs[cur]], writes=[pXY])
                P.op("pe", "matmul", XY_[:, 128:256], lhsT=X, rhs=Y, start=True, stop=True, reads=[XYs[cur]], writes=[pXY])
                if lvl < 4:
                    P.op("act", "copy", out=XYs[nxt][:], in_=XY_, reads=[pXY], writes=[XYs[nxt]])
                else:
                    P.op("act", "copy", out=XYs[nxt][:, 128:256], in_=XY_[:, 128:256], reads=[pXY], writes=[XYs[nxt]])
                yield
                P.op("pe", "matmul", R_, lhsT=XYs[nxt][:, 128:256], rhs=Rs[rc][:], start=True, stop=True, reads=[XYs[nxt], Rs[rc]], writes=[pR])
                P.op("dve", "tensor_tensor", out=Rs[1 - rc][:], in0=Rs[rc][:], in1=R_, op=ALU.add, reads=[Rs[rc], pR], writes=[Rs[1 - rc]])
                rc = 1 - rc
                cur = nxt
                yield
            Rf = Rs[rc]
            for h in range(2):
                hs = H2[h]
                P.op("pe", "matmul", W0_[hs, :], lhsT=al[hs, cs], rhs=ST[hs, :], start=True, stop=False, reads=[al, ST], writes=[pW0])
                P.op("pe", "matmul", W0_[hs, :], lhsT=Am[hs, 64:128], rhs=TMs[hs, 128:192], start=False, stop=True, reads=[Am, TMs], writes=[pW0])
            P.op("act", "copy", out=W0s[:], in_=W0_, reads=[pW0], writes=[W0s])
            yield
            P.op("pe", "matmul", U_, lhsT=Rf[:], rhs=W0s[:], start=True, stop=True, reads=[Rf, W0s], writes=[pU])
            P.op("dve", "tensor_copy", out=Us[:], in_=U_, reads=[pU], writes=[Us])
            yield
            for h in range(2):
                hs = H2[h]
                P.op("pe", "matmul", Y_[hs, :], lhsT=ST[hs, :], rhs=rt[hs, cs], start=True, stop=False, reads=[ST, rt], writes=[pY])
                P.op("pe", "matmul", Y_[hs, :], lhsT=Us[hs, :], rhs=Am[hs, 0:64], start=False, stop=False, reads=[Us, Am], writes=[pY])
                P.op("pe", "matmul", Y_[hs, :], lhsT=TMs[hs, 128:192], rhs=Am[hs, 128:192], start=False, stop=True, reads=[TMs, Am], writes=[pY])
            P.op("act", "copy", out=Yb[:, cs], in_=Y_, reads=[pY], writes=[Yb])
            if hp == 0 and d == 1 and t0 == CTX and c == 0:
                for nm_, tl_, w_ in (("dbg_Am", Am, 192), ("dbg_Us", Us, 64), ("dbg_TMs", TMs, 192), ("dbg_W0s", W0s, 64), ("dbg_Rf", Rf, 128), ("dbg_ST", ST, 64), ("dbg_XY", XYs[cur], 256)):
                    dbg(g, nm_, tl_, tl_[:, :w_], [128, w_])
                dbg(g, "dbg_Yb", Yb, Yb[:, 0:128], [128, 128])
            for h in range(2):
                hs = H2[h]
                P.op("pe", "matmul", SP_[hs, :], lhsT=TMs[hs, 0:64], rhs=Us[hs, :], start=True, stop=False, reads=[TMs, Us], writes=[pSP])
                P.op("pe", "matmul", SP_[hs, :], lhsT=TMs[hs, 64:128], rhs=TMs[hs, 128:192], start=False, stop=True, reads=[TMs], writes=[pSP])
            pc = Ep[:, c * CH + CH - 1:c * CH + CH] if d == 0 else Ep[:, c * CH:c * CH + 1]
            P.op("dve", "tensor_scalar", out=ST[:], in0=ST[:], scalar1=pc, scalar2=None, op0=ALU.mult, reads=[ST, Ep], writes=[ST])
            P.op("dve", "scalar_tensor_tensor", out=ST[:], in0=SP_, scalar=pc, in1=ST[:], op0=ALU.mult, op1=ALU.add, reads=[pSP, Ep, ST], writes=[ST])
            yield
        P.dma(YS[d][hp][:, ts_], Yb[:, :Lb], reads=[Yb], writes=[YS[d][hp]])
        yield


def stage_rwkv_scan(g, l):
    P = g.P
    T = g.T
    YS = [[scratch(g, "YS%d_%d" % (d, hp), [128, T]) for hp in range(8)] for d in range(2)]
    lat = [(CTX + i * 512, min(512, T - CTX - i * 512)) for i in range((T - CTX + 511) // 512)]
    bl = [[(0, CTX)] + lat, [(0, CTX)] + lat[::-1]]
    for pair in range(4):
        with P.scope():
            tr = [P.sbuf("rtr%d" % i, [128, 512], F32) for i in range(11)]
            gens = []
            k = 0
            for hp in (2 * pair, 2 * pair + 1):
                for d in range(2):
                    gens.append(rwkv_stream(g, l, hp, d, g.ps[2 * k], g.ps[2 * k + 1], bl[d], tr, YS))
                    k += 1
            while gens:
                for ge in list(gens):
                    try:
                        next(ge)
                    except StopIteration:
                        gens.remove(ge)


def stage_rwkv_out(g, l):
    P = g.P
    T = g.T
    F = [scratch(g, "F%d" % i, [128, T]) for i in range(NSEG)]
    YS = [[scratch(g, "YS%d_%d" % (d, hp), [128, T]) for hp in range(8)] for d in range(2)]
    YR = scratch(g, "YR", [RW_W, T], BF16)
    s = g.s
    blk = g.c["blk"]
    a0, ka, omka, rk, lg, lb = s["a0"], s["k_a"], s["omka"], s["r_k"], s["lnxg"], s["lnxb"]
    for hp in range(8):
        with P.scope():
            aups = []
            for d in range(2):
                t = P.sbuf("roa%d" % d, [64, 128], F32)
                P.dma(t[:], g.d["rwkv_a_up"][l, d, :, hp * 128:(hp + 1) * 128], reads=[g.d["rwkv_a_up"]], writes=[t])
                aups.append(t)
            gu0 = P.sbuf("rogu0", [128, 128], F32)
            gu1 = P.sbuf("rogu1", [32, 128], F32)
            P.dma(gu0[:], g.d["rwkv_g_up"][l, 0:128, hp * 128:(hp + 1) * 128], reads=[g.d["rwkv_g_up"]], writes=[gu0])
            P.dma(gu1[:], g.d["rwkv_g_up"][l, 128:160, hp * 128:(hp + 1) * 128], reads=[g.d["rwkv_g_up"]], writes=[gu1])
            names = ("y", "yr", "r", "k", "v", "ad", "g0", "g1", "t1", "t2", "t3")
            tl = {n_: [P.sbuf("ro_%s%d" % (n_, i), [128, 512], F32) for i in range(2)] for n_ in names}
            ob = [P.sbuf("ro_ob%d" % i, [128, 512], BF16) for i in range(2)]
            for bi, (t0, Lb) in enumerate(blocks_of(g)):
                ts_ = slice(t0, t0 + Lb)
                y, yr, r_, k_, v_, ad, g0, g1, t1, t2, t3 = [tl[n_][bi % 2] for n_ in names]
                o = ob[bi % 2]
                P.dma(y[:, :Lb], YS[0][hp][:, ts_], reads=[YS[0][hp]], writes=[y])
                P.dma(yr[:, :Lb], YS[1][hp][:, ts_], reads=[YS[1][hp]], writes=[yr], q="act")
                P.dma(r_[:, :Lb], F[S_R + hp][:, ts_], reads=[F[S_R + hp]], writes=[r_])
                P.dma(k_[:, :Lb], F[S_K + hp][:, ts_], reads=[F[S_K + hp]], writes=[k_], q="act")
                P.dma(v_[:, :Lb], F[S_V + hp][:, ts_], reads=[F[S_V + hp]], writes=[v_])
                P.dma(ad[0:64, :Lb], F[S_AD][0:64, ts_], reads=[F[S_AD]], writes=[ad], q="act")
                P.dma(g0[:, :Lb], F[S_GD][:, ts_], reads=[F[S_GD]], writes=[g0])
                P.dma(g1[0:32, :Lb], F[S_GD + 1][0:32, ts_], reads=[F[S_GD + 1]], writes=[g1], q="act")
                p1, p2, p3, p4 = [g.ps[(4 * bi + i) % 8] for i in range(4)]
                P.op("dve", "tensor_tensor", out=y[:, :Lb], in0=y[:, :Lb], in1=yr[:, :Lb], op=ALU.add, reads=[y, yr], writes=[y])
                P.op("pe", "matmul", p1[:, :Lb], lhsT=blk[:], rhs=y[:, :Lb], start=True, stop=True, reads=[blk, y], writes=[p1])
                P.op("dve", "scalar_tensor_tensor", out=y[:, :Lb], in0=p1[:, :Lb], scalar=-1.0 / 64, in1=y[:, :Lb], op0=ALU.mult, op1=ALU.add, reads=[p1, y], writes=[y])
                P.op("act", "activation", out=t1[:, :Lb], in_=y[:, :Lb], func=AF.Square, reads=[y], writes=[t1])
                P.op("pe", "matmul", p2[:, :Lb], lhsT=blk[:], rhs=t1[:, :Lb], start=True, stop=True, reads=[blk, t1], writes=[p2])
                P.op("act", "activation", out=t1[:, :Lb], in_=p2[:, :Lb], func=AF.Sqrt, bias=GN_EPS, scale=1.0 / 64, reads=[p2], writes=[t1])
                P.op("dve", "reciprocal", out=t1[:, :Lb], in_=t1[:, :Lb], reads=[t1], writes=[t1])
                P.op("dve", "tensor_tensor", out=y[:, :Lb], in0=y[:, :Lb], in1=t1[:, :Lb], op=ALU.mult, reads=[y, t1], writes=[y])
                P.op("dve", "tensor_scalar", out=y[:, :Lb], in0=y[:, :Lb], scalar1=lg[:, l, hp:hp + 1], scalar2=lb[:, l, hp:hp + 1], op0=ALU.mult, op1=ALU.add,
                     reads=[y, lg, lb], writes=[y])
                for d in range(2):
                    P.op("pe", "matmul", p3[:, :Lb], lhsT=aups[d][:], rhs=ad[0:64, :Lb], start=True, stop=True, reads=[aups[d], ad], writes=[p3])
                    tt = t2 if d == 0 else t3
                    P.op("act", "activation", out=tt[:, :Lb], in_=p3[:, :Lb], func=AF.Sigmoid, bias=a0[:, l, d, hp:hp + 1], reads=[p3, a0], writes=[tt])
                P.op("dve", "tensor_tensor", out=t2[:, :Lb], in0=t2[:, :Lb], in1=t3[:, :Lb], op=ALU.add, reads=[t2, t3], writes=[t2])
                P.op("dve", "tensor_scalar", out=t2[:, :Lb], in0=t2[:, :Lb], scalar1=ka[:, l, hp:hp + 1], scalar2=None, op0=ALU.mult, reads=[t2, ka], writes=[t2])
                P.op("dve", "tensor_scalar", out=t3[:, :Lb], in0=k_[:, :Lb], scalar1=omka[:, l, hp:hp + 1], scalar2=2.0, op0=ALU.mult, op1=ALU.mult, reads=[k_, omka], writes=[t3])
                P.op("dve", "tensor_tensor", out=t2[:, :Lb], in0=t2[:, :Lb], in1=k_[:, :Lb], op=ALU.mult, reads=[t2, k_], writes=[t2])
                P.op("dve", "tensor_tensor", out=t2[:, :Lb], in0=t2[:, :Lb], in1=t3[:, :Lb], op=ALU.add, reads=[t2, t3], writes=[t2])
                P.op("dve", "tensor_tensor", out=t2[:, :Lb], in0=t2[:, :Lb], in1=r_[:, :Lb], op=ALU.mult, reads=[t2, r_], writes=[t2])
                P.op("dve", "tensor_scalar", out=t2[:, :Lb], in0=t2[:, :Lb], scalar1=rk[:, l, hp:hp + 1], scalar2=0.5, op0=ALU.mult, op1=ALU.mult, reads=[t2, rk], writes=[t2])
                P.op("pe", "matmul", p4[:, :Lb], lhsT=blk[:], rhs=t2[:, :Lb], start=True, stop=True, reads=[blk, t2], writes=[p4])
                P.op("dve", "tensor_tensor", out=t2[:, :Lb], in0=v_[:, :Lb], in1=p4[:, :Lb], op=ALU.mult, reads=[v_, p4], writes=[t2])
                P.op("dve", "tensor_tensor", out=y[:, :Lb], in0=y[:, :Lb], in1=t2[:, :Lb], op=ALU.add, reads=[y, t2], writes=[y])
                P.op("act", "activation", out=g0[:, :Lb], in_=g0[:, :Lb], func=AF.Sigmoid, reads=[g0], writes=[g0])
                P.op("act", "activation", out=g1[0:32, :Lb], in_=g1[0:32, :Lb], func=AF.Sigmoid, reads=[g1], writes=[g1])
                P.op("pe", "matmul", p1[:, :Lb], lhsT=gu0[:], rhs=g0[:, :Lb], start=True, stop=False, reads=[gu0, g0], writes=[p1])
                P.op("pe", "matmul", p1[:, :Lb], lhsT=gu1[0:32, :], rhs=g1[0:32, :Lb], start=False, stop=True, reads=[gu1, g1], writes=[p1])
                P.op("dve", "tensor_tensor", out=o[:, :Lb], in0=y[:, :Lb], in1=p1[:, :Lb], op=ALU.mult, reads=[y, p1], writes=[o])
                P.dma(YR[hp * 128:(hp + 1) * 128, ts_], o[:, :Lb], reads=[o], writes=[YR])


def stage_mla(g, l, ctx_out):
    P = g.P
    T = g.T
    F = [scratch(g, "F%d" % i, [128, T]) for i in range(NSEG)]
    YM = scratch(g, "YM", [2048, T], BF16)
    ones, ident, Jm = g.c["ones"], g.c["ident"], g.c["Jm"]
    qng, kvg = g.s["qng"], g.s["kvg"]
    blocks = blocks_of(g)
    NKC = T // 128
    with P.scope():
        cos = load_const(g, "cos", [64, g.SEQ])
        sin = load_const(g, "sin", [64, g.SEQ], q="act")
        qn = P.sbuf("qn", [128, 4, T], BF16)
        kvn = P.sbuf("kvn", [128, 2, T], BF16)
        kr = P.sbuf("kr", [64, T], BF16)
        with P.scope():
            xqs = [P.sbuf("xq%d" % i, [128, 4, 512], F32) for i in range(2)]
            xks = [P.sbuf("xk%d" % i, [128, 2, 512], F32) for i in range(2)]
            xrs = [P.sbuf("xr%d" % i, [64, 512], F32) for i in range(2)]
            sqs = [P.sbuf("msq%d" % i, [128, 512], F32) for i in range(2)]
            rs1 = P.sbuf("mrs1", [128, 512], F32)
            rs2 = P.sbuf("mrs2", [128, 512], F32)
            m1 = P.sbuf("mm1", [64, 512], F32)
            m2 = P.sbuf("mm2", [64, 512], F32)
            for bi, (t0, n) in enumerate(blocks):
                xq, xk, xr = xqs[bi % 2], xks[bi % 2], xrs[bi % 2]
                for i in range(4):
                    P.dma(xq[:, i, :n], F[S_Q + i][:, t0:t0 + n], reads=[F[S_Q + i]], writes=[xq], q="sp" if i % 2 == 0 else "act")
                for i in range(2):
                    P.dma(xk[:, i, :n], F[S_KV + i][:, t0:t0 + n], reads=[F[S_KV + i]], writes=[xk], q="sp" if i % 2 == 0 else "act")
                P.dma(xr[0:64, :n], F[S_KR][0:64, t0:t0 + n], reads=[F[S_KR]], writes=[xr])
                pa, pb, pj = g.ps[0], g.ps[1], g.ps[2]
                for (x_, nchunk, pp, rs, gam, dst) in ((xq, 4, pa, rs1, qng, qn), (xk, 2, pb, rs2, kvg, kvn)):
                    for i in range(nchunk):
                        sq = sqs[i % 2]
                        P.op("act", "activation", out=sq[:, :n], in_=x_[:, i, :n], func=AF.Square, reads=[x_], writes=[sq])
                        P.op("pe", "matmul", pp[:, :n], lhsT=ones[:], rhs=sq[:, :n], start=(i == 0), stop=(i == nchunk - 1), reads=[ones, sq], writes=[pp])
                    P.op("act", "activation", out=rs[:, :n], in_=pp[:, :n], func=AF.Sqrt, bias=LN_EPS, scale=1.0 / (128 * nchunk), reads=[pp], writes=[rs])
                    P.op("dve", "reciprocal", out=rs[:, :n], in_=rs[:, :n], reads=[rs], writes=[rs])
                    for i in range(nchunk):
                        P.op("dve", "scalar_tensor_tensor", out=dst[:, i, t0:t0 + n], in0=x_[:, i, :n], scalar=gam[:, l, i:i + 1], in1=rs[:, :n],
                             op0=ALU.mult, op1=ALU.mult, reads=[x_, gam, rs], writes=[dst])
                if t0 < CTX:
                    P.op("act", "copy", out=kr[0:64, t0:t0 + n], in_=xr[0:64, :n], reads=[xr], writes=[kr])
                else:
                    P.op("pe", "matmul", pj[0:64, :n], lhsT=Jm[:], rhs=xr[0:64, :n], start=True, stop=True, reads=[Jm, xr], writes=[pj])
                    P.op("dve", "tensor_tensor", out=m1[:, :n], in0=xr[0:64, :n], in1=cos[:, t0 - CTX:t0 - CTX + n], op=ALU.mult, reads=[xr, cos], writes=[m1])
                    P.op("dve", "tensor_tensor", out=m2[:, :n], in0=pj[0:64, :n], in1=sin[:, t0 - CTX:t0 - CTX + n], op=ALU.mult, reads=[pj, sin], writes=[m2])
                    P.op("dve", "tensor_tensor", out=kr[0:64, t0:t0 + n], in0=m1[:, :n], in1=m2[:, :n], op=ALU.add, reads=[m1, m2], writes=[kr])
        wuq, wuk, wuv = g.d["mla_w_uq"], g.d["mla_w_uk"], g.d["mla_w_uv"]
        for hd in range(16):
            with P.scope():
                wq = P.sbuf("wq", [128, 4, 192], BF16)
                wk = P.sbuf("wk", [128, 2, 128], BF16)
                wv = P.sbuf("wv", [128, 2, 128], BF16)
                P.dma(wq[:], wuq[l, :, hd * 192:(hd + 1) * 192].rearrange("(kc p) n -> p kc n", p=128), reads=[wuq], writes=[wq], q="pool")
                P.dma(wk[:], wuk[l, :, hd * 128:(hd + 1) * 128].rearrange("(kc p) n -> p kc n", p=128), reads=[wuk], writes=[wk], q="pool")
                P.dma(wv[:], wuv[l, :, hd * 128:(hd + 1) * 128].rearrange("(kc p) n -> p kc n", p=128), reads=[wuv], writes=[wv], q="pool")
                Kn = P.sbuf("Kn", [128, T], BF16)
                Qn = P.sbuf("Qn", [128, T], BF16)
                Qr = P.sbuf("Qr", [64, T], BF16)
                Va = P.sbuf("Va", [128, NKC, 132], BF16)
                P.op("dve", "memset", Va[:], 1.0, reads=[], writes=[Va])
                xq32 = P.sbuf("xq32", [64, 512], F32)
                m1 = P.sbuf("hm1", [64, 512], F32)
                m2 = P.sbuf("hm2", [64, 512], F32)
                for bi, (t0, n) in enumerate(blocks):
                    pk, pq, pr, pj = g.ps[0], g.ps[1], g.ps[2], g.ps[3]
                    for kc in range(2):
                        P.op("pe", "matmul", pk[:, :n], lhsT=wk[:, kc, :], rhs=kvn[:, kc, t0:t0 + n], start=(kc == 0), stop=(kc == 1), reads=[wk, kvn], writes=[pk])
                    P.op("act", "copy", out=Kn[:, t0:t0 + n], in_=pk[:, :n], reads=[pk], writes=[Kn])
                    if t0 < CTX and not ctx_out:
                        continue
                    for kc in range(4):
                        P.op("pe", "matmul", pq[:, :n], lhsT=wq[:, kc, 0:128], rhs=qn[:, kc, t0:t0 + n], start=(kc == 0), stop=(kc == 3), reads=[wq, qn], writes=[pq])
                    P.op("dve", "tensor_copy", out=Qn[:, t0:t0 + n], in_=pq[:, :n], reads=[pq], writes=[Qn])
                    for kc in range(4):
                        P.op("pe", "matmul", pr[0:64, :n], lhsT=wq[:, kc, 128:192], rhs=qn[:, kc, t0:t0 + n], start=(kc == 0), stop=(kc == 3), reads=[wq, qn], writes=[pr])
                    if t0 < CTX:
                        P.op("act", "copy", out=Qr[0:64, t0:t0 + n], in_=pr[0:64, :n], reads=[pr], writes=[Qr])
                    else:
                        P.op("act", "copy", out=xq32[:, :n], in_=pr[0:64, :n], reads=[pr], writes=[xq32])
                        P.op("pe", "matmul", pj[0:64, :n], lhsT=Jm[:], rhs=xq32[:, :n], start=True, stop=True, reads=[Jm, xq32], writes=[pj])
                        P.op("dve", "tensor_tensor", out=m1[:, :n], in0=xq32[:, :n], in1=cos[:, t0 - CTX:t0 - CTX + n], op=ALU.mult, reads=[xq32, cos], writes=[m1])
                        P.op("dve", "tensor_tensor", out=m2[:, :n], in0=pj[0:64, :n], in1=sin[:, t0 - CTX:t0 - CTX + n], op=ALU.mult, reads=[pj, sin], writes=[m2])
                        P.op("dve", "tensor_tensor", out=Qr[0:64, t0:t0 + n], in0=m1[:, :n], in1=m2[:, :n], op=ALU.add, reads=[m1, m2], writes=[Qr])
                for tc_ in range(NKC):
                    pv = g.ps[2 + tc_ % 2]
                    for kc in range(2):
                        P.op("pe", "matmul", pv[:, 0:128], lhsT=kvn[:, kc, tc_ * 128:(tc_ + 1) * 128], rhs=wv[:, kc, :], start=(kc == 0), stop=(kc == 1),
                             reads=[kvn, wv], writes=[pv])
                    if tc_ % 2 == 0:
                        P.op("act", "copy", out=Va[:, tc_, 0:128], in_=pv[:, 0:128], reads=[pv], writes=[Va])
                    else:
                        P.op("dve", "tensor_copy", out=Va[:, tc_, 0:128], in_=pv[:, 0:128], reads=[pv], writes=[Va])
                pts = [P.sbuf("PT%d" % i, [128, 512], BF16) for i in range(3)]
                stgs = [P.sbuf("ostg%d" % i, [128, 512], BF16) for i in range(2)]
                o32s = [P.sbuf("o32%d" % i, [128, 128], F32) for i in range(2)]
                recs = [P.sbuf("rec%d" % i, [128, 1], F32) for i in range(2)]
                qblocks = [b_ for b_ in blocks if (b_[0] >= CTX or ctx_out)]
                for qi, (q0, nq) in enumerate(qblocks):
                    keys = list(range(CTX // 128)) if q0 < CTX else list(range(NKC))
                    nsub = nq // 128
                    accs = [g.ps[4 + qs][:, 0:129] for qs in range(nsub)]
                    acct = [g.ps[4 + qs] for qs in range(nsub)]
                    for ki, kc in enumerate(keys):
                        S = g.ps[ki % 2]
                        P.op("pe", "matmul", S[:, :nq], lhsT=Kn[:, kc * 128:(kc + 1) * 128], rhs=Qn[:, q0:q0 + nq], start=True, stop=False, reads=[Kn, Qn], writes=[S])
                        P.op("pe", "matmul", S[:, :nq], lhsT=kr[0:64, kc * 128:(kc + 1) * 128], rhs=Qr[0:64, q0:q0 + nq], start=False, stop=True, reads=[kr, Qr], writes=[S])
                        PT = pts[ki % 3]
                        P.op("act", "activation", out=PT[:, :nq], in_=S[:, :nq], func=AF.Exp, scale=ATT_SCALE, reads=[S], writes=[PT])
                        for qs in range(nsub):
                            P.op("pe", "matmul", accs[qs], lhsT=PT[:, qs * 128:(qs + 1) * 128], rhs=Va[:, kc, 0:129], start=(ki == 0), stop=(ki == len(keys) - 1),
                                 reads=[PT, Va], writes=[acct[qs]])
                    stg = stgs[qi % 2]
                    for qs in range(nsub):
                        rec, o32 = recs[qs % 2], o32s[qs % 2]
                        P.op("dve", "reciprocal", out=rec[:], in_=accs[qs][:, 128:129], reads=[acct[qs]], writes=[rec])
                        P.op("dve", "tensor_scalar", out=o32[:], in0=accs[qs][:, 0:128], scalar1=rec[:, 0:1], scalar2=None, op0=ALU.mult, reads=[acct[qs], rec], writes=[o32])
                        pT = g.ps[2 + qs % 2]
                        P.op("pe", "transpose", pT[:, 0:128], o32[:], ident[:], reads=[o32, ident], writes=[pT])
                        P.op("act", "copy", out=stg[:, qs * 128:(qs + 1) * 128], in_=pT[:, 0:128], reads=[pT], writes=[stg])
                    P.dma(YM[hd * 128:(hd + 1) * 128, q0:q0 + nq], stg[:, :nq], reads=[stg], writes=[YM])


def stage_c(g, l, dst, last):
    P = g.P
    T = g.T
    XL = scratch(g, "XL", [D, T])
    F = [scratch(g, "F%d" % i, [128, T]) for i in range(NSEG)]
    YR = scratch(g, "YR", [RW_W, T], BF16)
    YM = scratch(g, "YM", [2048, T], BF16)
    YL = scratch(g, "YL", [LRU_W, T], BF16)
    prw, pml, plr, wout = g.d["proj_rwkv"], g.d["proj_mla"], g.d["proj_lru"], g.d["w_out"]
    for (t0, NT, m) in token_tiles(g, with_ctx=not last):
        with P.scope():
            s = P.sbuf("cs_", [128, KC, NT], F32)
            P.dma(s[:], fview(XL)[:, :, t0:t0 + NT], reads=[XL], writes=[s])
            P.op("act", "mul", out=s[:], in_=s[:], mul=ALPHA, reads=[s], writes=[s])
            with P.scope():
                yr = P.sbuf("cyr", [128, 8, NT], BF16)
                ym = P.sbuf("cym", [128, 16, NT], BF16)
                yl = P.sbuf("cyl", [128, 8, NT], BF16)
                P.dma(yr[:], YR[:].rearrange("(kc p) t -> p kc t", p=128)[:, :, t0:t0 + NT], reads=[YR], writes=[yr])
                P.dma(ym[:], YM[:].rearrange("(kc p) t -> p kc t", p=128)[:, :, t0:t0 + NT], reads=[YM], writes=[ym], q="act")
                P.dma(yl[:], YL[:].rearrange("(kc p) t -> p kc t", p=128)[:, :, t0:t0 + NT], reads=[YL], writes=[yl])
                ybf = P.sbuf("cyb", [128, KC, NT], BF16)
                gts = [P.sbuf("cgt%d" % i, [128, 3, NT], F32) for i in range(2)]
                pws = [P.sbuf("cpw%d" % i, [128, 32, 128], BF16) for i in range(2)]
                t1 = P.sbuf("ct1", [128, NT], F32)
                t2 = P.sbuf("ct2", [128, NT], F32)
                for dc in range(KC):
                    pw, gt = pws[dc % 2], gts[dc % 2]
                    cs_ = slice(dc * 128, (dc + 1) * 128)
                    P.dma(pw[:, 0:8, :], prw[l, :, cs_].rearrange("(kc p) n -> p kc n", p=128), reads=[prw], writes=[pw], q="pool")
                    P.dma(pw[:, 8:24, :], pml[l, :, cs_].rearrange("(kc p) n -> p kc n", p=128), reads=[pml], writes=[pw], q="pool")
                    P.dma(pw[:, 24:32, :], plr[l, :, cs_].rearrange("(kc p) n -> p kc n", p=128), reads=[plr], writes=[pw], q="pool")
                    for bi in range(3):
                        P.dma(gt[:, bi, :], F[S_GATE + 16 * bi + dc][:, t0:t0 + NT], reads=[F[S_GATE + 16 * bi + dc]], writes=[gt], q="sp" if bi != 1 else "act")
                    p1, p2, p3 = g.ps[0], g.ps[1], g.ps[2]
                    for kc in range(8):
                        P.op("pe", "matmul", p1[:, :NT], lhsT=pw[:, kc, :], rhs=yr[:, kc, :], start=(kc == 0), stop=(kc == 7), reads=[pw, yr], writes=[p1])
                    for kc in range(16):
                        P.op("pe", "matmul", p2[:, :NT], lhsT=pw[:, 8 + kc, :], rhs=ym[:, kc, :], start=(kc == 0), stop=(kc == 15), reads=[pw, ym], writes=[p2])
                    for kc in range(8):
                        P.op("pe", "matmul", p3[:, :NT], lhsT=pw[:, 24 + kc, :], rhs=yl[:, kc, :], start=(kc == 0), stop=(kc == 7), reads=[pw, yl], writes=[p3])
                    P.op("dve", "tensor_tensor", out=t1[:], in0=gt[:, 0, :], in1=p1[:, :NT], op=ALU.mult, reads=[gt, p1], writes=[t1])
                    P.op("dve", "tensor_tensor", out=t2[:], in0=gt[:, 1, :], in1=p2[:, :NT], op=ALU.mult, reads=[gt, p2], writes=[t2])
                    P.op("dve", "tensor_tensor", out=t1[:], in0=t1[:], in1=t2[:], op=ALU.add, reads=[t1, t2], writes=[t1])
                    P.op("dve", "tensor_tensor", out=t2[:], in0=gt[:, 2, :], in1=p3[:, :NT], op=ALU.mult, reads=[gt, p3], writes=[t2])
                    P.op("dve", "tensor_tensor", out=ybf[:, dc, :], in0=t1[:], in1=t2[:], op=ALU.add, reads=[t1, t2], writes=[ybf])
                wos = [P.sbuf("cwo%d" % i, [128, KC, 128], BF16) for i in range(2)]
                for dc in range(KC):
                    wo = wos[dc % 2]
                    P.dma(wo[:], wout[l, :, dc * 128:(dc + 1) * 128].rearrange("(kc p) n -> p kc n", p=128), reads=[wout], writes=[wo], q="pool")
                    po = g.ps[4 + dc % 2]
                    for kc in range(KC):
                        P.op("pe", "matmul", po[:, :NT], lhsT=wo[:, kc, :], rhs=ybf[:, kc, :], start=(kc == 0), stop=(kc == KC - 1), reads=[wo, ybf], writes=[po])
                    P.op("dve", "scalar_tensor_tensor", out=s[:, dc, :], in0=po[:, :NT], scalar=g.mod[:, 5 * 16 + dc, m:m + 1], in1=s[:, dc, :],
                         op0=ALU.mult, op1=ALU.add, reads=[po, s, g.mod], writes=[s])
            layer_norm(g, s, NT, l, 1)
            ffn_block(g, s, NT, m, l, 1, (6, 7, 8), 2)
            if last:
                P.dma(fview(dst)[:, :, t0 - CTX:t0 - CTX + NT], s[:], reads=[s], writes=[dst])
            else:
                P.dma(fview(dst)[:, :, t0:t0 + NT], s[:], reads=[s], writes=[dst])


SMALL_NAMES = ["adab", "lng", "lnb", "gateb", "mu", "mud", "w0", "a0", "kk_k", "k_a", "r_k", "lnxg", "lnxb", "qng", "kvg", "convw", "convb", "ba", "bx", "lam"]
CONST_NAMES = ["ident", "ones", "blk", "Jm", "maskA"]


def build_program(SEQ, L, shapes, stages=None, ext_in=(), ext_out=()):
    nc = bass.Bass("TRN2", target_bir_lowering=False)
    g = setup(nc, SEQ, L, shapes, ext_in=ext_in, ext_out=ext_out)
    load_consts(g, CONST_NAMES)
    load_small(g, SMALL_NAMES)
    init_derived(g)
    init_mod(g)
    T = g.T
    xcur = g.d["xT"]
    XN = scratch(g, "XN", [D, T])
    yT = g.P.dram("yT", [D, SEQ], F32, "ExternalOutput")
    g.d["yT"] = yT
    for l in range(L):
        last = l == L - 1
        stage_mod(g, l)
        stage_a(g, l, xcur)
        stage_lru(g, l)
        stage_rwkv_mix(g, l)
        stage_rwkv_scan(g, l)
        stage_rwkv_out(g, l)
        stage_mla(g, l, ctx_out=not last)
        stage_c(g, l, yT if last else XN, last)
        xcur = XN
    g.P.emit()
    return nc, g


def host_inputs(inp, b, SEQ, L):
    x = np.asarray(inp["x"][b], np.float32)
    cx = np.asarray(inp["ctx"][b], np.float32)
    d = {}
    d["xT"] = np.ascontiguousarray(np.concatenate([cx, x], 0).T)
    d["cvec"] = np.ascontiguousarray(np.stack([fm(inp["c"][b]), fm(inp["c_ctx"])], -1))
    d.update(host_consts(SEQ))
    d.update(host_small(inp, L))
    for k in BIG_W:
        d[k] = np.ascontiguousarray(np.asarray(inp[k], np.float32))
    return d


_CACHE = {}


def kernel(**inputs):
    B, SEQ, _ = inputs["x"].shape
    L = inputs["w_in"].shape[0]
    per_core = [host_inputs(inputs, b, SEQ, L) for b in range(B)]
    shapes = {k: list(v.shape) for k, v in per_core[0].items()}
    key = (SEQ, L)
    if key not in _CACHE:
        _CACHE[key] = build_program(SEQ, L, shapes)
    nc, g = _CACHE[key]
    res = run_bass_kernel_spmd(nc, per_core, core_ids=list(range(B)))
    out = np.stack([np.ascontiguousarray(res.results[b]["yT"].T) for b in range(B)], 0)
    return out.astype(np.float32)
```

```python
import contextlib
import math
import numpy as np
import concourse.bass as bass
import concourse.mybir as mybir
from concourse.bass_utils import run_bass_kernel_spmd

F32 = mybir.dt.float32
BF16 = mybir.dt.bfloat16
AF = mybir.ActivationFunctionType
ALU = mybir.AluOpType

D = 2048
KC = 16
DFF = 5632
FC = 44
NMOD = 9
CTX = 256
GW = 64
RW_W = 1024
RWKV_IN = 3360
MLA_IN = 832
LRU_W = 1024
IN_COLS = 12384
ALPHA = 4.0 ** 0.25
LN_EPS = 1e-6
GN_EPS = 64e-5
C0 = math.exp(-0.5)
ATT_SCALE = 192.0 ** -0.5
CH = 64

SEGS = []
for i in range(8):
    SEGS.append((128 * i, 128))
for i in range(8):
    SEGS.append((1024 + 128 * i, 128))
for i in range(8):
    SEGS.append((2048 + 128 * i, 128))
SEGS.append((3072, 64))
SEGS.append((3136, 64))
SEGS.append((3200, 128))
SEGS.append((3328, 32))
for i in range(4):
    SEGS.append((3360 + 128 * i, 128))
for i in range(2):
    SEGS.append((3872 + 128 * i, 128))
SEGS.append((4128, 64))
for i in range(8):
    SEGS.append((4192 + 128 * i, 128))
for i in range(8):
    SEGS.append((5216 + 128 * i, 128))
for i in range(48):
    SEGS.append((6240 + 128 * i, 128))
NSEG = len(SEGS)
S_R, S_K, S_V, S_WD, S_AD, S_GD, S_Q, S_KV, S_KR, S_XR, S_GB, S_GATE = 0, 8, 16, 24, 25, 26, 28, 32, 34, 35, 43, 51
NRW = 28


class Tile:
    __slots__ = ("t", "name", "last_w", "readers", "dsem", "osem", "kind")

    def __init__(self, t, name, kind="sbuf"):
        self.t = t
        self.name = name
        self.kind = kind
        self.last_w = None
        self.readers = []
        self.dsem = None
        self.osem = None

    def __getitem__(self, idx):
        return self.t[idx]

    def sub(self, name):
        return Tile(self.t, self.name + "." + name, self.kind)


class Op:
    __slots__ = ("eng", "fn", "deps", "is_dma", "sem", "val", "need_sig", "osize")


SEM_EPOCH = 24000
SMALL_OP = 512
import os
DEBUG_INS = os.environ.get("DEBUG_INS")


class Prog:
    ENGS = ("pe", "dve", "act", "pool", "sp")

    def __init__(self, nc):
        self.nc = nc
        self.ops = []
        self.stack = contextlib.ExitStack()
        self.scopes = []
        self.free_sems = []
        self.fence = None
        self.nsem = 0
        self.dummy = Tile(self.stack.enter_context(nc.sbuf_tensor("fence_dummy", [1, 8], F32)), "dummy")

    def new_sem(self, name):
        self.nsem += 1
        return self.stack.enter_context(self.nc.semaphore(name + "_%d" % self.nsem))

    def _reg(self, tl):
        tl.last_w = self.fence
        if self.scopes:
            self.scopes[-1][1].append(tl)
        return tl

    def sbuf(self, name, shape, dtype=F32):
        st = self.scopes[-1][0] if self.scopes else self.stack
        nm = name + "_%d" % len(self.ops)
        t = st.enter_context(self.nc.sbuf_tensor(nm, list(shape), dtype))
        return self._reg(Tile(t, name))

    def psum(self, name, shape, dtype=F32):
        t = self.stack.enter_context(self.nc.psum_tensor(name, list(shape), dtype))
        return Tile(t, name, "psum")

    def dram(self, name, shape, dtype=F32, kind="Internal"):
        t = self.nc.dram_tensor(name, list(shape), dtype, kind=kind).ap()
        return Tile(t, name, "dram")

    @contextlib.contextmanager
    def scope(self):
        st = contextlib.ExitStack()
        self.scopes.append((st, []))
        try:
            yield
        finally:
            _, tiles = self.scopes.pop()
            if tiles:
                d = self.dummy
                self.fence = self.op("dve", "memset", d[:], 0.0, reads=[], writes=[d] + tiles)
                for tl in tiles:
                    for s in (tl.dsem, tl.osem):
                        if s is not None and s[1] < SEM_EPOCH:
                            self.free_sems.append(s)
            st.close()

    def _record(self, eng, fn, reads, writes, is_dma):
        i = len(self.ops)
        deps = set()
        pr = [t for t in reads if t.kind == "psum"]
        if pr:
            writes = list(writes) + [t for t in pr if t not in writes]
            reads = [t for t in reads if t.kind != "psum"]
        for t in reads:
            if t.last_w is not None:
                deps.add(t.last_w)
        for t in writes:
            if t.last_w is not None:
                deps.add(t.last_w)
            deps.update(t.readers)
        for t in reads:
            t.readers.append(i)
        for t in writes:
            t.last_w = i
            t.readers = []
        o = Op()
        o.eng = eng
        o.fn = fn
        o.deps = deps
        o.is_dma = is_dma
        o.sem = None
        o.val = 0
        o.need_sig = is_dma
        o.osize = 1 << 30
        self.ops.append(o)
        return i

    def op(self, eng, name, *args, reads=(), writes=(), **kw):
        def fn(e, name=name, args=args, kw=kw):
            return getattr(e, name)(*args, **kw)

        i = self._record(eng, fn, reads, writes, False)
        if eng != "pe":
            oap = kw.get("out", args[0] if args else None)
            try:
                sz = 1
                for v in oap.shape[1:]:
                    sz *= int(v)
                self.ops[i].osize = 0 if name == "tensor_tensor_scan" else sz
            except Exception:
                pass
        return i

    def _get_sem(self, cur, name):
        if cur is not None and cur[1] < SEM_EPOCH:
            return cur
        if self.free_sems:
            return self.free_sems.pop()
        return [self.new_sem(name), 0]

    def dma(self, out_ap, in_ap, reads=(), writes=(), q="sp", **kw):
        def fn(e, out_ap=out_ap, in_ap=in_ap, kw=kw):
            return e.dma_start(out=out_ap, in_=in_ap, **kw)

        i = self._record(q, fn, reads, writes, True)
        if writes[0].kind == "dram":
            st = reads[0]
            st.osem = self._get_sem(st.osem, "o")
            s = st.osem
        else:
            st = writes[0]
            st.dsem = self._get_sem(st.dsem, "d")
            s = st.dsem
        s[1] += 16
        self.ops[i].sem = s[0]
        self.ops[i].val = s[1]
        return i

    def dma_dd(self, out_ap, in_ap, reads=(), writes=(), q="pool", nslots=8, **kw):
        def fn(e, out_ap=out_ap, in_ap=in_ap, kw=kw):
            return e.dma_start(out=out_ap, in_=in_ap, **kw)

        if not hasattr(self, "dd_slots"):
            self.dd_slots = [[self.new_sem("dd"), 0, None] for _ in range(nslots)]
            self.dd_i = 0
        slot = self.dd_slots[self.dd_i % nslots]
        self.dd_i += 1
        if slot[1] >= SEM_EPOCH:
            slot[0] = self.new_sem("dd")
            slot[1] = 0
        i = self._record(q, fn, reads, writes, True)
        if slot[2] is not None:
            self.ops[i].deps.add(slot[2])
        slot[1] += 16
        slot[2] = i
        self.ops[i].sem = slot[0]
        self.ops[i].val = slot[1]
        return i

    def emit(self):
        nc = self.nc
        ops = self.ops
        for o in ops:
            for d in o.deps:
                od = ops[d]
                if not od.is_dma and (od.eng != o.eng or od.osize < SMALL_OP):
                    od.need_sig = True
        esem = {e: [] for e in self.ENGS}
        ecnt = {e: 0 for e in self.ENGS}
        for o in ops:
            if o.is_dma or not o.need_sig:
                continue
            c = ecnt[o.eng]
            ep = c // SEM_EPOCH
            if ep >= len(esem[o.eng]):
                esem[o.eng].append(self.new_sem("e_" + o.eng))
            o.sem = esem[o.eng][ep]
            o.val = c % SEM_EPOCH + 1
            ecnt[o.eng] = c + 1
        per_eng = {e: [] for e in self.ENGS}
        for i, o in enumerate(ops):
            per_eng[o.eng].append(i)
        fw = {}
        for o in ops:
            if o.is_dma:
                k = id(o.sem)
                if k not in fw or fw[k][1] < o.val:
                    fw[k] = (o.sem, o.val)

        def run(engname, e):
            waited = {}
            for i in per_eng[engname]:
                o = ops[i]
                for d in sorted(o.deps):
                    od = ops[d]
                    if not od.is_dma and od.eng == engname and od.osize >= SMALL_OP:
                        continue
                    k = id(od.sem)
                    if waited.get(k, 0) >= od.val:
                        continue
                    e.wait_ge(od.sem, od.val)
                    waited[k] = od.val
                ins = o.fn(e)
                if DEBUG_INS is not None and DEBUG_INS in str(getattr(ins.ins, "name", "")):
                    print("DEBUG_INS", i, engname, str(ins)[:600])
                if o.need_sig:
                    ins.then_inc(o.sem, 16 if o.is_dma else 1)
            if engname == "sp":
                for s, v in fw.values():
                    e.wait_ge(s, v)

        with nc.Block() as block:
            @block.tensor
            def _(e):
                run("pe", e)

            @block.vector
            def _(e):
                run("dve", e)

            @block.scalar
            def _(e):
                run("act", e)

            @block.gpsimd
            def _(e):
                run("pool", e)

            @block.sync
            def _(e):
                run("sp", e)
        self.stack.close()


class G:
    pass


def fm(v):
    v = np.asarray(v, np.float32)
    n = v.shape[-1] // 128
    w = v.reshape(v.shape[:-1] + (n, 128))
    return np.ascontiguousarray(np.moveaxis(w, -1, 0))


def host_consts(SEQ):
    c = {}
    c["ident"] = np.eye(128, dtype=np.float32)
    c["ones"] = np.ones((128, 128), np.float32)
    blk = np.zeros((128, 128), np.float32)
    blk[:64, :64] = 1
    blk[64:, 64:] = 1
    c["blk"] = blk
    a = np.arange(64)
    strict_f = (a[:, None] < a[None, :]).astype(np.float32)
    incl_f = (a[:, None] <= a[None, :]).astype(np.float32)
    mA = np.zeros((2, 128, 192), np.float32)
    mXY = np.zeros((2, 128, 256), np.float32)
    for d in range(2):
        st = strict_f if d == 0 else strict_f.T
        inc = incl_f if d == 0 else incl_f.T
        for h in range(2):
            rs = slice(64 * h, 64 * h + 64)
            mA[d, rs, 0:64] = inc
            mA[d, rs, 64:128] = st
            mA[d, rs, 128:192] = inc
            mXY[d, rs, 64 * h:64 * h + 64] = st
            mXY[d, rs, 128 + 64 * h:128 + 64 * h + 64] = st.T
    c["maskA"] = mA
    c["maskXY"] = mXY
    Lm = 1024
    rm = np.ones((2, 128, Lm), np.float32)
    rm[0, :, 0::64] = 0
    rm[1, :, 63::64] = 0
    c["rm"] = rm
    t = np.arange(SEQ)
    row = (t // GW).astype(np.float32)
    col = (t % GW).astype(np.float32)
    inv = (10000.0 ** (-np.arange(16, dtype=np.float32) / 16)).astype(np.float32)
    ang = np.concatenate([row[:, None] * inv, col[:, None] * inv], axis=-1).astype(np.float32)
    c["cos"] = np.ascontiguousarray(np.repeat(np.cos(ang), 2, axis=1).T.astype(np.float32))
    c["sin"] = np.ascontiguousarray(np.repeat(np.sin(ang), 2, axis=1).T.astype(np.float32))
    J = np.zeros((64, 64), np.float32)
    for i in range(32):
        J[2 * i + 1, 2 * i] = -1.0
        J[2 * i, 2 * i + 1] = 1.0
    c["Jm"] = J
    return c


def seg_mu(mu_l):
    full = np.zeros((128, NRW), np.float32)
    dirm = np.zeros((128, 4, NRW), np.float32)
    for si in range(NRW):
        c0, w = SEGS[si]
        full[:w, si] = mu_l[c0:c0 + w]
        for p in range(w):
            q = (c0 + p) // 840
            dirm[p, q, si] = mu_l[c0 + p]
    return full, dirm


def seg_dirs(si):
    c0, w = SEGS[si]
    return sorted(set((c0 + p) // 840 for p in range(w)))


SMALL_SPECS = None


def host_small(inp, L):
    s = {}
    s["adab"] = np.stack([fm(inp["ada_b"][l]) for l in range(L)], 1)
    s["lng"] = np.stack([fm(inp["ln_g"][l]) for l in range(L)], 1)
    s["lnb"] = np.stack([fm(inp["ln_b"][l]) for l in range(L)], 1)
    s["gateb"] = np.stack([fm(inp["gate_b"][l]) for l in range(L)], 1)
    mus = [seg_mu(np.asarray(inp["rwkv_mu"][l], np.float32)) for l in range(L)]
    s["mu"] = np.stack([m[0] for m in mus], 1)
    s["mud"] = np.stack([m[1] for m in mus], 1)
    s["w0"] = np.stack([fm(inp["rwkv_w0"][l]) for l in range(L)], 1)
    s["a0"] = np.stack([fm(inp["rwkv_a0"][l]) for l in range(L)], 1)
    for nm, key in (("kk_k", "rwkv_k_k"), ("k_a", "rwkv_k_a"), ("r_k", "rwkv_r_k"), ("lnxg", "rwkv_lnx_g"), ("lnxb", "rwkv_lnx_b")):
        s[nm] = np.stack([fm(inp[key][l]) for l in range(L)], 1)
    s["qng"] = np.stack([fm(inp["mla_q_norm"][l]) for l in range(L)], 1)
    s["kvg"] = np.stack([fm(inp["mla_kv_norm"][l]) for l in range(L)], 1)
    s["convw"] = np.stack([fm(inp["lru_conv_w"][l]) for l in range(L)], 1)
    s["convb"] = np.stack([fm(inp["lru_conv_b"][l]) for l in range(L)], 1)
    for nm, key in (("ba", "lru_ba"), ("bx", "lru_bx"), ("lam", "lru_lambda")):
        s[nm] = np.stack([fm(inp[key][l]) for l in range(L)], 1)
    return {k: np.ascontiguousarray(v, np.float32) for k, v in s.items()}


BIG_W = ["ada_w", "ffn_w_gate", "ffn_w_up", "ffn_w_down", "w_in", "rwkv_w_up", "rwkv_a_up", "rwkv_g_up",
         "mla_w_uq", "mla_w_uk", "mla_w_uv", "lru_wa", "lru_wx", "proj_rwkv", "proj_mla", "proj_lru", "w_out"]


def setup(nc, SEQ, L, shapes, ext_in=(), ext_out=()):
    g = G()
    g.P = Prog(nc)
    g.SEQ = SEQ
    g.T = CTX + SEQ
    g.R = SEQ // GW
    g.L = L
    g.d = {}
    P = g.P
    for nm, shp in shapes.items():
        g.d[nm] = P.dram(nm, shp, F32, "ExternalInput")
    g.ps = [P.psum("ps%d" % i, [128, 512], F32) for i in range(8)]
    g.ext_in = set(ext_in)
    g.ext_out = set(ext_out)
    g.wc = {}
    return g


def scratch(g, name, shape, dtype=F32):
    if name in g.d:
        return g.d[name]
    kind = "Internal"
    if name in g.ext_in:
        kind = "ExternalInput"
    if name in g.ext_out:
        kind = "ExternalOutput"
    g.d[name] = g.P.dram(name, shape, dtype, kind)
    return g.d[name]


def dbg(g, name, tile, ap, shape):
    if name in g.ext_out:
        d = scratch(g, name, shape)
        g.P.dma(d[:], ap, reads=[tile], writes=[d])


def load_const(g, name, shape, src=None, q="sp"):
    P = g.P
    t = P.sbuf(name, shape, F32)
    src = src if src is not None else g.d[name][:]
    P.dma(t[:], src, reads=[g.d[name]], writes=[t], q=q)
    return t


def load_consts(g, which):
    P = g.P
    g.c = {}
    shp = {"ident": [128, 128], "ones": [128, 128], "blk": [128, 128], "Jm": [64, 64]}
    for nm in which:
        if nm in shp:
            g.c[nm] = load_const(g, nm, shp[nm])
    if "maskA" in which:
        g.c["maskA"] = [load_const(g, "maskA", [128, 192], g.d["maskA"][d]) for d in range(2)]
        g.c["maskXY"] = [load_const(g, "maskXY", [128, 256], g.d["maskXY"][d]) for d in range(2)]
        g.c["rm"] = [load_const(g, "rm", [128, 1024], g.d["rm"][d]) for d in range(2)]


def load_small(g, names):
    g.s = {}
    for nm in names:
        shp = list(g.d[nm].t.shape)
        g.s[nm] = load_const(g, nm, shp)


def wc_jobs(g, name, idx, K, col_list):
    W = g.d[name]
    KCn = K // 128
    sc = g.P.dram("WC_%s_%s" % (name, "_".join(map(str, idx))), [len(col_list), 128, KCn * 128], BF16)
    jobs = []
    for t, (c0, w) in enumerate(col_list):
        trk = Tile(sc.t, sc.name + "_%d" % t, "dram")
        dst = sc.t[t].rearrange("p (kc n) -> p kc n", n=128)[:, :, :w]
        src = W[tuple(idx) + (slice(None), slice(c0, c0 + w))].rearrange("(kc p) n -> p kc n", p=128)
        g.wc[(name,) + tuple(idx) + (c0,)] = (trk, dst)
        jobs.append((dst, src, W, trk))
    return jobs


def wc_emit(g, jobs):
    for (dst, src, W, trk) in jobs:
        g.P.dma_dd(dst, src, reads=[W], writes=[trk])


def wload(g, dst_ap, dst_tile, name, idx, c0):
    trk, ap = g.wc[(name,) + tuple(idx) + (c0,)]
    g.P.dma(dst_ap, ap, reads=[trk], writes=[dst_tile], q="sp")


def precast_layer(g, l):
    cols128 = lambda n: [(i * 128, 128) for i in range(n // 128)]
    for j in range(2):
        gj = wc_jobs(g, "ffn_w_gate", (l, j), D, cols128(DFF))
        uj = wc_jobs(g, "ffn_w_up", (l, j), D, cols128(DFF))
        dj = wc_jobs(g, "ffn_w_down", (l, j), DFF, cols128(D))
        ffn = [x for pair in zip(gj, uj) for x in pair] + dj
        if j == 0:
            wc_emit(g, ffn)
            wc_emit(g, wc_jobs(g, "w_in", (l,), D, list(SEGS)))
            pr = wc_jobs(g, "proj_rwkv", (l,), RW_W, cols128(D))
            pm = wc_jobs(g, "proj_mla", (l,), 2048, cols128(D))
            pl = wc_jobs(g, "proj_lru", (l,), LRU_W, cols128(D))
            wc_emit(g, [x for tr_ in zip(pr, pm, pl) for x in tr_])
            wc_emit(g, wc_jobs(g, "w_out", (l,), D, cols128(D)))
        else:
            wc_emit(g, ffn)


def stage_mod(g, l):
    P = g.P
    pm = g.ps[0]
    with P.scope():
        wts = [P.sbuf("adw%d" % i, [128, KC, 512], F32) for i in range(2)]
        aw = g.d["ada_w"]
        for grp in range(36):
            wt = wts[grp % 2]
            P.dma(wt[:], aw[l, :, grp * 512:(grp + 1) * 512].rearrange("(kc p) n -> p kc n", p=128),
                  reads=[aw], writes=[wt], q="sp" if grp % 2 == 0 else "act")
            for jj in range(4):
                j = grp * 4 + jj
                for kc in range(KC):
                    P.op("pe", "matmul", pm[:, 2 * j:2 * j + 2], lhsT=wt[:, kc, jj * 128:(jj + 1) * 128], rhs=g.cs[:, kc, :],
                         start=(kc == 0), stop=(kc == KC - 1), reads=[wt, g.cs], writes=[pm])
        adab = g.s["adab"]
        P.op("dve", "tensor_tensor", out=g.mod[:], in0=pm[:, 0:288].rearrange("p (j m) -> p j m", m=2),
             in1=adab[:, l, :].unsqueeze(2).to_broadcast([128, 144, 2]), op=ALU.add, reads=[pm, adab], writes=[g.mod])
        P.op("dve", "tensor_scalar", out=g.onep[:], in0=g.mod[:], scalar1=1.0, scalar2=None, op0=ALU.add, reads=[g.mod], writes=[g.onep])
        P.op("dve", "tensor_scalar", out=g.hg[:], in0=g.mod[:], scalar1=0.5, scalar2=None, op0=ALU.mult, reads=[g.mod], writes=[g.hg])


def init_mod(g):
    P = g.P
    g.cs = P.sbuf("cs", [128, KC, 2], F32)
    P.dma(g.cs[:], g.d["cvec"][:], reads=[g.d["cvec"]], writes=[g.cs])
    P.op("act", "activation", out=g.cs[:], in_=g.cs[:], func=AF.Silu, reads=[g.cs], writes=[g.cs])
    g.mod = P.sbuf("mod", [128, 144, 2], F32)
    g.onep = P.sbuf("onep", [128, 144, 2], F32)
    g.hg = P.sbuf("hg", [128, 144, 2], F32)


def rsqrt_(P, out, in_, eps, tiles, scale=1.0):
    P.op("act", "activation", out=out, in_=in_, func=AF.Sqrt, bias=eps, scale=scale, reads=tiles, writes=tiles)
    P.op("dve", "reciprocal", out=out, in_=out, reads=tiles, writes=tiles)


def layer_norm(g, z, NT, l, idx):
    P = g.P
    s1 = g.ps[6]
    s2 = g.ps[7]
    ones = g.c["ones"]
    with P.scope():
        sq = [P.sbuf("lnsq%d" % i, [128, NT], F32) for i in range(2)]
        mean = P.sbuf("lnmean", [128, NT], F32)
        rstd = P.sbuf("lnrstd", [128, NT], F32)
        for dc in range(KC):
            P.op("pe", "matmul", s1[:, :NT], lhsT=ones[:], rhs=z[:, dc, :], start=(dc == 0), stop=(dc == KC - 1), reads=[ones, z], writes=[s1])
            q = sq[dc % 2]
            P.op("act", "activation", out=q[:], in_=z[:, dc, :], func=AF.Square, reads=[z], writes=[q])
            P.op("pe", "matmul", s2[:, :NT], lhsT=ones[:], rhs=q[:], start=(dc == 0), stop=(dc == KC - 1), reads=[ones, q], writes=[s2])
        P.op("act", "mul", out=mean[:], in_=s1[:, :NT], mul=1.0 / D, reads=[s1], writes=[mean])
        P.op("dve", "tensor_tensor", out=rstd[:], in0=mean[:], in1=mean[:], op=ALU.mult, reads=[mean], writes=[rstd])
        P.op("dve", "scalar_tensor_tensor", out=rstd[:], in0=s2[:, :NT], scalar=1.0 / D, in1=rstd[:], op0=ALU.mult, op1=ALU.subtract, reads=[s2, rstd], writes=[rstd])
        rsqrt_(P, rstd[:], rstd[:], LN_EPS, [rstd])
        lng = g.s["lng"]
        lnb = g.s["lnb"]
        for dc in range(KC):
            P.op("dve", "tensor_tensor", out=z[:, dc, :], in0=z[:, dc, :], in1=mean[:], op=ALU.subtract, reads=[z, mean], writes=[z])
            P.op("dve", "tensor_tensor", out=z[:, dc, :], in0=z[:, dc, :], in1=rstd[:], op=ALU.mult, reads=[z, rstd], writes=[z])
            P.op("dve", "tensor_scalar", out=z[:, dc, :], in0=z[:, dc, :], scalar1=lng[:, l, idx, dc:dc + 1], scalar2=lnb[:, l, idx, dc:dc + 1],
                 op0=ALU.mult, op1=ALU.add, reads=[z, lng, lnb], writes=[z])


def modulate(g, out, s, NT, m, sc_slot, sh_slot):
    P = g.P
    for kc in range(KC):
        P.op("dve", "tensor_scalar", out=out[:, kc, :], in0=s[:, kc, :], scalar1=g.onep[:, sc_slot * 16 + kc, m:m + 1],
             scalar2=g.mod[:, sh_slot * 16 + kc, m:m + 1], op0=ALU.mult, op1=ALU.add, reads=[s, g.onep, g.mod], writes=[out])


def ffn_block(g, s, NT, m, l, j, slots, ln_idx):
    P = g.P
    sh, sc, gt = slots
    wgd, wud, wdd = g.d["ffn_w_gate"], g.d["ffn_w_up"], g.d["ffn_w_down"]
    with P.scope():
        AT = P.sbuf("ffa", [128, FC, NT], BF16)
        with P.scope():
            h = P.sbuf("ffh", [128, KC, NT], BF16)
            modulate(g, h, s, NT, m, sc, sh)
            P.op("act", "mul", out=s[:], in_=s[:], mul=ALPHA, reads=[s], writes=[s])
            wgs = [P.sbuf("wg%d" % i, [128, KC, 128], BF16) for i in range(3)]
            wus = [P.sbuf("wu%d" % i, [128, KC, 128], BF16) for i in range(3)]
            sgs = [P.sbuf("sg%d" % i, [128, NT], F32) for i in range(2)]
            for fc in range(FC):
                wg = wgs[fc % 3]
                wu = wus[fc % 3]
                wload(g, wg[:], wg, "ffn_w_gate", (l, j), fc * 128)
                wload(g, wu[:], wu, "ffn_w_up", (l, j), fc * 128)
                pg = g.ps[(2 * fc) % 4]
                pu = g.ps[(2 * fc + 1) % 4]
                for kc in range(KC):
                    P.op("pe", "matmul", pg[:, :NT], lhsT=wg[:, kc, :], rhs=h[:, kc, :], start=(kc == 0), stop=(kc == KC - 1), reads=[wg, h], writes=[pg])
                for kc in range(KC):
                    P.op("pe", "matmul", pu[:, :NT], lhsT=wu[:, kc, :], rhs=h[:, kc, :], start=(kc == 0), stop=(kc == KC - 1), reads=[wu, h], writes=[pu])
                sg = sgs[fc % 2]
                P.op("act", "activation", out=sg[:], in_=pg[:, :NT], func=AF.Silu, reads=[pg], writes=[sg])
                P.op("dve", "tensor_tensor", out=AT[:, fc, :], in0=sg[:], in1=pu[:, :NT], op=ALU.mult, reads=[sg, pu], writes=[AT])
        with P.scope():
            wds = [P.sbuf("wd%d" % i, [128, FC, 128], BF16) for i in range(2)]
            for dc in range(KC):
                wd = wds[dc % 2]
                wload(g, wd[:], wd, "ffn_w_down", (l, j), dc * 128)
                py = g.ps[4 + dc % 2]
                for fc in range(FC):
                    P.op("pe", "matmul", py[:, :NT], lhsT=wd[:, fc, :], rhs=AT[:, fc, :], start=(fc == 0), stop=(fc == FC - 1), reads=[wd, AT], writes=[py])
                P.op("dve", "scalar_tensor_tensor", out=s[:, dc, :], in0=py[:, :NT], scalar=g.hg[:, gt * 16 + dc, m:m + 1], in1=s[:, dc, :],
                     op0=ALU.mult, op1=ALU.add, reads=[py, s, g.hg], writes=[s])
    layer_norm(g, s, NT, l, ln_idx)


def token_tiles(g, with_ctx=True):
    tl = []
    if with_ctx:
        tl.append((0, CTX, 1))
    t0 = CTX
    while t0 < g.T:
        nt = min(512, g.T - t0)
        tl.append((t0, nt, 0))
        t0 += nt
    return tl


def fview(d):
    return d[:].rearrange("(kc p) t -> p kc t", p=128)


def stage_a(g, l, src, do_ffn=True):
    P = g.P
    XL = scratch(g, "XL", [D, g.T])
    F = [scratch(g, "F%d" % i, [128, g.T]) for i in range(NSEG)]
    win = g.d["w_in"]
    gateb = g.s["gateb"]
    for (t0, NT, m) in token_tiles(g):
        with P.scope():
            s = P.sbuf("s", [128, KC, NT], F32)
            P.dma(s[:], fview(src)[:, :, t0:t0 + NT], reads=[src], writes=[s])
            if do_ffn:
                ffn_block(g, s, NT, m, l, 0, (0, 1, 2), 0)
            P.dma(fview(XL)[:, :, t0:t0 + NT], s[:], reads=[s], writes=[XL])
            with P.scope():
                hl = P.sbuf("hl", [128, KC, NT], BF16)
                modulate(g, hl, s, NT, m, 4, 3)
                wts = [P.sbuf("wi%d" % i, [128, KC, 128], BF16) for i in range(3)]
                stg = [P.sbuf("stg%d" % i, [128, NT], F32) for i in range(3)]
                for si, (c0, w) in enumerate(SEGS):
                    wt = wts[si % 3]
                    wload(g, wt[:, :, :w], wt, "w_in", (l,), c0)
                    pp = g.ps[si % 4]
                    for kc in range(KC):
                        P.op("pe", "matmul", pp[:w, :NT], lhsT=wt[:, kc, :w], rhs=hl[:, kc, :], start=(kc == 0), stop=(kc == KC - 1), reads=[wt, hl], writes=[pp])
                    st = stg[si % 3]
                    if si >= S_GATE:
                        P.op("act", "activation", out=st[:w, :], in_=pp[:w, :NT], func=AF.Sigmoid, bias=gateb[:, l, si - S_GATE:si - S_GATE + 1],
                             reads=[pp, gateb], writes=[st])
                    elif si % 2 == 0:
                        P.op("act", "copy", out=st[:w, :], in_=pp[:w, :NT], reads=[pp], writes=[st])
                    else:
                        P.op("dve", "tensor_copy", out=st[:w, :], in_=pp[:w, :NT], reads=[pp], writes=[st])
                    P.dma(F[si][:w, t0:t0 + NT], st[:w, :], reads=[st], writes=[F[si]])


def init_derived(g):
    P = g.P
    s = g.s
    for nm, src in (("omm", "mu"), ("omka", "k_a")):
        t = P.sbuf(nm, list(g.d[src].t.shape), F32)
        P.op("dve", "tensor_scalar", out=t[:], in0=s[src][:], scalar1=-1.0, scalar2=1.0, op0=ALU.mult, op1=ALU.add, reads=[s[src]], writes=[t])
        s[nm] = t
    s["clam"] = make_clam(g, s["lam"], list(g.d["lam"].t.shape))
    return
    s["clam"] = t


def make_clam(g, lam, shape):
    P = g.P
    e = P.sbuf("clam_e", shape, F32)
    t = P.sbuf("clam", shape, F32)
    P.op("act", "activation", out=e[:], in_=lam[:], func=AF.Exp, scale=-1.0, reads=[lam], writes=[e])
    nt = 12
    P.op("dve", "tensor_scalar", out=t[:], in0=e[:], scalar1=((-1.0) ** (nt + 1)) / nt, scalar2=((-1.0) ** nt) / (nt - 1), op0=ALU.mult, op1=ALU.add, reads=[e], writes=[t])
    for k in range(nt - 2, 0, -1):
        P.op("dve", "tensor_tensor", out=t[:], in0=t[:], in1=e[:], op=ALU.mult, reads=[t, e], writes=[t])
        P.op("dve", "tensor_scalar", out=t[:], in0=t[:], scalar1=((-1.0) ** (k + 1)) / k, scalar2=None, op0=ALU.add, reads=[t], writes=[t])
    P.op("dve", "scalar_tensor_tensor", out=t[:], in0=t[:], scalar=-8.0, in1=e[:], op0=ALU.mult, op1=ALU.mult, reads=[t, e], writes=[t])
    return t


def blocks_of(g):
    bl = [(0, CTX)]
    t0 = CTX
    while t0 < g.T:
        n = min(512, g.T - t0)
        bl.append((t0, n))
        t0 += n
    return bl


def stage_lru(g, l):
    P = g.P
    T = g.T
    F = [scratch(g, "F%d" % i, [128, T]) for i in range(NSEG)]
    YL = scratch(g, "YL", [LRU_W, T], BF16)
    cw, cb, ba, bx, clam = g.s["convw"], g.s["convb"], g.s["ba"], g.s["bx"], g.s["clam"]
    wad, wxd = g.d["lru_wa"], g.d["lru_wx"]
    for n in range(8):
        with P.scope():
            xr = P.sbuf("lxr", [128, T], F32)
            gb = P.sbuf("lgb", [128, T], F32)
            xc = P.sbuf("lxc", [128, T], F32)
            hs = P.sbuf("lhs", [128, T], F32)
            gr = P.sbuf("lgr", [128, T], F32)
            gi = P.sbuf("lgi", [128, T], F32)
            tmp = P.sbuf("ltmp", [128, T], F32)
            yb = P.sbuf("lyb", [128, T], BF16)
            P.dma(xr[:], F[S_XR + n][:, :], reads=[F[S_XR + n]], writes=[xr])
            P.dma(gb[:], F[S_GB + n][:, :], reads=[F[S_GB + n]], writes=[gb], q="act")
            for (a, b) in ((0, CTX), (CTX, T)):
                P.op("dve", "tensor_scalar", out=xc[:, a:b], in0=xr[:, a:b], scalar1=cw[:, l, 2, n:n + 1], scalar2=cb[:, l, n:n + 1],
                     op0=ALU.mult, op1=ALU.add, reads=[xr, cw, cb], writes=[xc])
                for (tap, off) in ((0, -2), (1, -1), (3, 1)):
                    if off < 0:
                        o_, i_ = xc[:, a - off:b], xr[:, a:b + off]
                    else:
                        o_, i_ = xc[:, a:b - off], xr[:, a + off:b]
                    P.op("dve", "scalar_tensor_tensor", out=o_, in0=i_, scalar=cw[:, l, tap, n:n + 1], in1=o_, op0=ALU.mult, op1=ALU.add,
                         reads=[xr, xc, cw], writes=[xc])
            if n == 0:
                dbg(g, "dbg_xc", xc, xc[:], [128, T])
            for d in range(2):
                wa = P.sbuf("lwa%d" % d, [128, 128], F32)
                wx = P.sbuf("lwx%d" % d, [128, 128], F32)
                P.dma(wa[:], wad[l, d, n], reads=[wad], writes=[wa])
                P.dma(wx[:], wxd[l, d, n], reads=[wxd], writes=[wx], q="act")
                for bi, (t0, nb) in enumerate(blocks_of(g)):
                    p1 = g.ps[(2 * bi) % 4]
                    p2 = g.ps[(2 * bi + 1) % 4]
                    P.op("pe", "matmul", p1[:, :nb], lhsT=wa[:], rhs=xc[:, t0:t0 + nb], start=True, stop=True, reads=[wa, xc], writes=[p1])
                    P.op("pe", "matmul", p2[:, :nb], lhsT=wx[:], rhs=xc[:, t0:t0 + nb], start=True, stop=True, reads=[wx, xc], writes=[p2])
                    P.op("act", "activation", out=gr[:, t0:t0 + nb], in_=p1[:, :nb], func=AF.Sigmoid, bias=ba[:, l, d, n:n + 1], reads=[p1, ba], writes=[gr])
                    P.op("act", "activation", out=gi[:, t0:t0 + nb], in_=p2[:, :nb], func=AF.Sigmoid, bias=bx[:, l, d, n:n + 1], reads=[p2, bx], writes=[gi])
                P.op("dve", "tensor_scalar", out=gr[:], in0=gr[:], scalar1=clam[:, l, d, n:n + 1], scalar2=0.25, op0=ALU.mult, op1=ALU.mult, reads=[gr, clam], writes=[gr])
                P.op("dve", "tensor_scalar", out=tmp[:], in0=gr[:], scalar1=1.0 / 7, scalar2=1.0, op0=ALU.mult, op1=ALU.add, reads=[gr], writes=[tmp])
                for kk_ in (6, 5, 4, 3, 2):
                    P.op("dve", "tensor_tensor", out=tmp[:], in0=tmp[:], in1=gr[:], op=ALU.mult, reads=[tmp, gr], writes=[tmp])
                    P.op("dve", "tensor_scalar", out=tmp[:], in0=tmp[:], scalar1=1.0 / kk_, scalar2=1.0, op0=ALU.mult, op1=ALU.add, reads=[tmp], writes=[tmp])
                P.op("dve", "scalar_tensor_tensor", out=tmp[:], in0=tmp[:], scalar=-1.0, in1=gr[:], op0=ALU.mult, op1=ALU.mult, reads=[tmp, gr], writes=[tmp])
                P.op("dve", "tensor_scalar", out=gr[:], in0=tmp[:], scalar1=-1.0, scalar2=1.0, op0=ALU.mult, op1=ALU.add, reads=[tmp], writes=[gr])
                P.op("dve", "tensor_tensor", out=gr[:], in0=gr[:], in1=gr[:], op=ALU.mult, reads=[gr], writes=[gr])
                P.op("dve", "scalar_tensor_tensor", out=xr[:], in0=tmp[:], scalar=-1.0, in1=tmp[:], op0=ALU.mult, op1=ALU.mult, reads=[tmp], writes=[xr])
                P.op("dve", "scalar_tensor_tensor", out=tmp[:], in0=tmp[:], scalar=2.0, in1=xr[:], op0=ALU.mult, op1=ALU.add, reads=[tmp, xr], writes=[tmp])
                P.op("dve", "scalar_tensor_tensor", out=tmp[:], in0=gr[:], scalar=1.0, in1=tmp[:], op0=ALU.add, op1=ALU.mult, reads=[gr, tmp], writes=[tmp])
                P.op("dve", "tensor_tensor", out=gr[:], in0=gr[:], in1=gr[:], op=ALU.mult, reads=[gr], writes=[gr])
                P.op("dve", "scalar_tensor_tensor", out=tmp[:], in0=gr[:], scalar=1.0, in1=tmp[:], op0=ALU.add, op1=ALU.mult, reads=[gr, tmp], writes=[tmp])
                P.op("act", "activation", out=tmp[:], in_=tmp[:], func=AF.Sqrt, reads=[tmp], writes=[tmp])
                P.op("dve", "tensor_tensor", out=tmp[:], in0=tmp[:], in1=gi[:], op=ALU.mult, reads=[tmp, gi], writes=[tmp])
                P.op("dve", "tensor_tensor", out=tmp[:], in0=tmp[:], in1=xc[:], op=ALU.mult, reads=[tmp, xc], writes=[tmp])
                if n == 0:
                    dbg(g, "dbg_a%d" % d, gr, gr[:], [128, T])
                    dbg(g, "dbg_b%d" % d, tmp, tmp[:], [128, T])
                hd = hs if d == 0 else gi
                if d == 0:
                    P.op("dve", "tensor_tensor_scan", out=hd[:, :], data0=gr[:, :], data1=tmp[:, :], initial=0.0, op0=ALU.mult, op1=ALU.add,
                         reads=[gr, tmp], writes=[hd])
                else:
                    P.op("dve", "tensor_tensor_scan", out=hd[:, 0:CTX][:, ::-1], data0=gr[:, 0:CTX][:, ::-1], data1=tmp[:, 0:CTX][:, ::-1], initial=0.0,
                         op0=ALU.mult, op1=ALU.add, reads=[gr, tmp], writes=[hd])
                    P.op("dve", "scalar_tensor_tensor", out=tmp[:, T - 1:T], in0=gr[:, T - 1:T], scalar=hd[:, 0:1], in1=tmp[:, T - 1:T], op0=ALU.mult, op1=ALU.add,
                         reads=[gr, hd, tmp], writes=[tmp])
                    P.op("dve", "tensor_tensor_scan", out=hd[:, CTX:T][:, ::-1], data0=gr[:, CTX:T][:, ::-1], data1=tmp[:, CTX:T][:, ::-1], initial=0.0,
                         op0=ALU.mult, op1=ALU.add, reads=[gr, tmp, hd], writes=[hd])
                    P.op("dve", "tensor_tensor", out=hs[:], in0=hs[:], in1=hd[:], op=ALU.add, reads=[hs, hd], writes=[hs])
            if n == 0:
                dbg(g, "dbg_hs", hs, hs[:], [128, T])
            P.op("dve", "tensor_tensor", out=tmp[:], in0=gb[:], in1=gb[:], op=ALU.mult, reads=[gb], writes=[tmp])
            P.op("dve", "tensor_scalar", out=tmp[:], in0=tmp[:], scalar1=0.044715, scalar2=1.0, op0=ALU.mult, op1=ALU.add, reads=[tmp], writes=[tmp])
            P.op("dve", "tensor_tensor", out=tmp[:], in0=tmp[:], in1=gb[:], op=ALU.mult, reads=[tmp, gb], writes=[tmp])
            P.op("act", "activation", out=tmp[:], in_=tmp[:], func=AF.Sigmoid, scale=1.5957691216057308, reads=[tmp], writes=[tmp])
            P.op("dve", "tensor_tensor", out=tmp[:], in0=tmp[:], in1=gb[:], op=ALU.mult, reads=[tmp, gb], writes=[tmp])
            P.op("dve", "tensor_tensor", out=yb[:], in0=tmp[:], in1=hs[:], op=ALU.mult, reads=[tmp, hs], writes=[yb])
            P.dma(YL[n * 128:(n + 1) * 128, :], yb[:], reads=[yb], writes=[YL])


def stage_rwkv_mix(g, l):
    P = g.P
    T = g.T
    R = g.R
    F = [scratch(g, "F%d" % i, [128, T]) for i in range(NSEG)]
    omm, mud = g.s["omm"], g.s["mud"]
    with P.scope():
        fs = [P.sbuf("mxf%d" % i, [128, T], F32) for i in range(2)]
        os_ = [P.sbuf("mxo%d" % i, [128, T], F32) for i in range(2)]
        for si in range(NRW):
            c0, w = SEGS[si]
            f = fs[si % 2]
            o = os_[si % 2]
            P.dma(f[:w, :], F[si][:w, :], reads=[F[si]], writes=[f], q="sp" if si % 2 == 0 else "act")
            P.op("dve", "tensor_scalar", out=o[:w, :], in0=f[:w, :], scalar1=omm[:w, l, si:si + 1], scalar2=None, op0=ALU.mult, reads=[f, omm], writes=[o])
            fl = f[:w, CTX:T].rearrange("p (r c) -> p r c", c=GW)
            ol = o[:w, CTX:T].rearrange("p (r c) -> p r c", c=GW)
            for q in seg_dirs(si):
                sc = mud[:w, l, q, si:si + 1]
                if q == 0:
                    pairs = [(ol[:, :, 1:GW], fl[:, :, 0:GW - 1]), (o[:w, 1:CTX], f[:w, 0:CTX - 1])]
                elif q == 1:
                    pairs = [(ol[:, :, 0:GW - 1], fl[:, :, 1:GW]), (o[:w, 0:CTX - 1], f[:w, 1:CTX])]
                elif q == 2:
                    pairs = [(ol[:, 1:R, :], fl[:, 0:R - 1, :]), (o[:w, 1:CTX], f[:w, 0:CTX - 1])]
                else:
                    pairs = [(ol[:, 0:R - 1, :], fl[:, 1:R, :]), (o[:w, 0:CTX - 1], f[:w, 1:CTX])]
                for (oo, ii) in pairs:
                    P.op("dve", "scalar_tensor_tensor", out=oo, in0=ii, scalar=sc, in1=oo, op0=ALU.mult, op1=ALU.add, reads=[f, o, mud], writes=[o])
            P.dma(F[si][:w, :], o[:w, :], reads=[o], writes=[F[si]])


def rwkv_stream(g, l, hp, d, bankA, bankB, blocks, tr, YS):
    P = g.P
    F = [scratch(g, "F%d" % i, [128, g.T]) for i in range(NSEG)]
    ident, blk = g.c["ident"], g.c["blk"]
    maskA, maskXY, rm = g.c["maskA"][d], g.c["maskXY"][d], g.c["rm"][d]
    s = g.s
    pA = pTM = pW0 = pSP = bankA
    pXY = pR = pU = pY = bankB
    allA = [bankA]
    A_, TM_, W0_, SP_ = bankA[:, 0:192], bankA[:, 192:384], bankA[:, 384:448], bankA[:, 448:512]
    XY_, R_, U_, Y_ = bankB[:, 0:256], bankB[:, 256:384], bankB[:, 384:448], bankB[:, 448:512]
    P.op("dve", "memset", bankB[:, 0:256], 0.0, reads=[], writes=[pXY])
    nm = "_%d_%d" % (hp, d)
    ST = P.sbuf("ST" + nm, [128, 64], F32)
    P.op("dve", "memset", ST[:], 0.0, reads=[], writes=[ST])
    wup = P.sbuf("wup" + nm, [64, 128], F32)
    aup = P.sbuf("aup" + nm, [64, 128], F32)
    P.dma(wup[:], g.d["rwkv_w_up"][l, d, :, hp * 128:(hp + 1) * 128], reads=[g.d["rwkv_w_up"]], writes=[wup])
    P.dma(aup[:], g.d["rwkv_a_up"][l, d, :, hp * 128:(hp + 1) * 128], reads=[g.d["rwkv_a_up"]], writes=[aup])
    LM = 512
    rt, al, be, kt, vv, Ep, Yb = [P.sbuf(n_ + nm, [128, LM], F32) for n_ in ("rt", "al", "be", "kt", "vv", "Ep", "Yb")]
    XYs = [P.sbuf("XYs%d" % i + nm, [128, 256], F32) for i in range(2)]
    Rs = [P.sbuf("Rs%d" % i + nm, [128, 128], F32) for i in range(2)]
    Am = P.sbuf("Am" + nm, [128, 192], F32)
    TMs = P.sbuf("TMs" + nm, [128, 192], F32)
    W0s = P.sbuf("W0s" + nm, [128, 64], F32)
    Us = P.sbuf("Us" + nm, [128, 64], F32)
    r_, k_, twd, adt, sig, Lc, t1, t2, a_, kk_, t3 = tr
    w0, a0, kkk, ka, omka = s["w0"], s["a0"], s["kk_k"], s["k_a"], s["omka"]
    H2 = (slice(0, 64), slice(64, 128))
    for (t0, Lb) in blocks:
        ts_ = slice(t0, t0 + Lb)
        P.dma(r_[:, :Lb], F[S_R + hp][:, ts_], reads=[F[S_R + hp]], writes=[r_])
        P.dma(k_[:, :Lb], F[S_K + hp][:, ts_], reads=[F[S_K + hp]], writes=[k_], q="act")
        P.dma(vv[:, :Lb], F[S_V + hp][:, ts_], reads=[F[S_V + hp]], writes=[vv])
        P.dma(twd[0:64, :Lb], F[S_WD][0:64, ts_], reads=[F[S_WD]], writes=[twd], q="act")
        P.dma(adt[0:64, :Lb], F[S_AD][0:64, ts_], reads=[F[S_AD]], writes=[adt])
        P.op("act", "activation", out=twd[0:64, :Lb], in_=twd[0:64, :Lb], func=AF.Tanh, reads=[twd], writes=[twd])
        P.op("pe", "matmul", bankA[:, :Lb], lhsT=wup[:], rhs=twd[0:64, :Lb], start=True, stop=True, reads=[wup, twd], writes=allA)
        P.op("act", "activation", out=sig[:, :Lb], in_=bankA[:, :Lb], func=AF.Sigmoid, bias=w0[:, l, d, hp:hp + 1], reads=allA + [w0], writes=[sig])
        P.op("pe", "matmul", bankA[:, :Lb], lhsT=aup[:], rhs=adt[0:64, :Lb], start=True, stop=True, reads=[aup, adt] + allA, writes=allA)
        P.op("act", "activation", out=a_[:, :Lb], in_=bankA[:, :Lb], func=AF.Sigmoid, bias=a0[:, l, d, hp:hp + 1], reads=allA + [a0], writes=[a_])
        if d == 0:
            P.op("dve", "tensor_tensor_scan", out=Lc[:, :Lb], data0=rm[:, :Lb], data1=sig[:, :Lb], initial=0.0, op0=ALU.mult, op1=ALU.add,
                 reads=[rm, sig], writes=[Lc])
        else:
            P.op("dve", "tensor_tensor_scan", out=Lc[:, :Lb][:, ::-1], data0=rm[:, :Lb][:, ::-1], data1=sig[:, :Lb][:, ::-1], initial=0.0,
                 op0=ALU.mult, op1=ALU.add, reads=[rm, sig], writes=[Lc])
        P.op("act", "activation", out=Ep[:, :Lb], in_=Lc[:, :Lb], func=AF.Exp, scale=-C0, reads=[Lc], writes=[Ep])
        P.op("act", "activation", out=t1[:, :Lb], in_=Lc[:, :Lb], func=AF.Exp, scale=C0, reads=[Lc], writes=[t1])
        P.op("dve", "tensor_tensor", out=t2[:, :Lb], in0=Lc[:, :Lb], in1=sig[:, :Lb], op=ALU.subtract, reads=[Lc, sig], writes=[t2])
        P.op("act", "activation", out=t2[:, :Lb], in_=t2[:, :Lb], func=AF.Exp, scale=-C0, reads=[t2], writes=[t2])
        P.op("dve", "tensor_scalar", out=kk_[:, :Lb], in0=k_[:, :Lb], scalar1=kkk[:, l, hp:hp + 1], scalar2=None, op0=ALU.mult, reads=[k_, kkk], writes=[kk_])
        P.op("dve", "tensor_tensor", out=t3[:, :Lb], in0=kk_[:, :Lb], in1=kk_[:, :Lb], op=ALU.mult, reads=[kk_], writes=[t3])
        P.op("pe", "matmul", bankA[:, :Lb], lhsT=blk[:], rhs=t3[:, :Lb], start=True, stop=True, reads=[blk, t3] + allA, writes=allA)
        P.op("act", "activation", out=t3[:, :Lb], in_=bankA[:, :Lb], func=AF.Sqrt, reads=allA, writes=[t3])
        P.op("dve", "tensor_scalar", out=t3[:, :Lb], in0=t3[:, :Lb], scalar1=1e-12, scalar2=None, op0=ALU.max, reads=[t3], writes=[t3])
        P.op("dve", "reciprocal", out=t3[:, :Lb], in_=t3[:, :Lb], reads=[t3], writes=[t3])
        P.op("dve", "tensor_tensor", out=kk_[:, :Lb], in0=kk_[:, :Lb], in1=t3[:, :Lb], op=ALU.mult, reads=[kk_, t3], writes=[kk_])
        P.op("dve", "tensor_scalar", out=t3[:, :Lb], in0=a_[:, :Lb], scalar1=ka[:, l, hp:hp + 1], scalar2=omka[:, l, hp:hp + 1], op0=ALU.mult, op1=ALU.add,
             reads=[a_, ka, omka], writes=[t3])
        P.op("dve", "tensor_tensor", out=t3[:, :Lb], in0=t3[:, :Lb], in1=k_[:, :Lb], op=ALU.mult, reads=[t3, k_], writes=[t3])
        P.op("dve", "tensor_tensor", out=kt[:, :Lb], in0=t3[:, :Lb], in1=t1[:, :Lb], op=ALU.mult, reads=[t3, t1], writes=[kt])
        P.op("dve", "tensor_tensor", out=rt[:, :Lb], in0=r_[:, :Lb], in1=Ep[:, :Lb], op=ALU.mult, reads=[r_, Ep], writes=[rt])
        P.op("dve", "scalar_tensor_tensor", out=al[:, :Lb], in0=kk_[:, :Lb], scalar=-1.0, in1=t2[:, :Lb], op0=ALU.mult, op1=ALU.mult, reads=[kk_, t2], writes=[al])
        P.op("dve", "tensor_tensor", out=be[:, :Lb], in0=kk_[:, :Lb], in1=a_[:, :Lb], op=ALU.mult, reads=[kk_, a_], writes=[be])
        P.op("dve", "tensor_tensor", out=be[:, :Lb], in0=be[:, :Lb], in1=t1[:, :Lb], op=ALU.mult, reads=[be, t1], writes=[be])
        if hp == 0 and d == 1 and t0 == CTX:
            for nm_, tl_ in (("dbg_sig", sig), ("dbg_L", Lc), ("dbg_Ep", Ep), ("dbg_Em", t1), ("dbg_rt", rt), ("dbg_al", al), ("dbg_be", be), ("dbg_kt", kt), ("dbg_kk", kk_)):
                dbg(g, nm_, tl_, tl_[:, :Lb], [128, Lb])
        yield
        nch = Lb // CH
        for c in (range(nch) if d == 0 else range(nch - 1, -1, -1)):
            cs = slice(c * CH, (c + 1) * CH)
            for h in range(2):
                hs = H2[h]
                for j, src in enumerate((be, kt, vv)):
                    P.op("pe", "matmul", TM_[hs, 64 * j:64 * j + 64], lhsT=src[hs, cs], rhs=ident[hs, hs], start=True, stop=True, reads=[src, ident], writes=[pTM])
            P.op("act", "copy", out=TMs[:], in_=TM_, reads=[pTM], writes=[TMs])
            for h in range(2):
                hs = H2[h]
                P.op("pe", "matmul", XY_[hs, 64 * h:64 * h + 64], lhsT=be[hs, cs], rhs=al[hs, cs], start=True, stop=True, reads=[be, al], writes=[pXY])
                P.op("pe", "matmul", XY_[hs, 128 + 64 * h:192 + 64 * h], lhsT=al[hs, cs], rhs=be[hs, cs], start=True, stop=True, reads=[be, al], writes=[pXY])
                P.op("pe", "matmul", A_[hs, 0:64], lhsT=be[hs, cs], rhs=rt[hs, cs], start=True, stop=True, reads=[be, rt], writes=[pA])
                P.op("pe", "matmul", A_[hs, 64:128], lhsT=kt[hs, cs], rhs=al[hs, cs], start=True, stop=True, reads=[kt, al], writes=[pA])
                P.op("pe", "matmul", A_[hs, 128:192], lhsT=kt[hs, cs], rhs=rt[hs, cs], start=True, stop=True, reads=[kt, rt], writes=[pA])
            cur = 0
            P.op("dve", "tensor_tensor", out=XYs[cur][:], in0=XY_, in1=maskXY[:], op=ALU.mult, reads=[pXY, maskXY], writes=[XYs[cur]])
            P.op("dve", "tensor_tensor", out=Am[:], in0=A_, in1=maskA[:], op=ALU.mult, reads=[pA, maskA], writes=[Am])
            rc = 0
            P.op("dve", "tensor_tensor", out=Rs[rc][:], in0=XYs[cur][:, 0:128], in1=ident[:], op=ALU.add, reads=[XYs[cur], ident], writes=[Rs[rc]])
            yield
            for lvl in range(5):
                X, Y = XYs[cur][:, 0:128], XYs[cur][:, 128:256]
                nxt = 1 - cur
                if lvl < 4:
                    P.op("pe", "matmul", XY_[:, 0:128], lhsT=Y, rhs=X, start=True, stop=True, reads=[XYs[cur]], writes=[pXY])
                P.op("pe", "matmul", XY_[:, 128:256], lhsT=X, rhs=Y, start=True, stop=True, reads=[XYs[cur]], writes=[pXY])
                if lvl < 4:
                    P.op("act", "copy", out=XYs[nxt][:], in_=XY_, reads=[pXY], writes=[XYs[nxt]])
                else:
                    P.op("act", "copy", out=XYs[nxt][:, 128:256], in_=XY_[:, 128:256], reads=[pXY], writes=[XYs[nxt]])
                yield
                P.op("pe", "matmul", R_, lhsT=XYs[nxt][:, 128:256], rhs=Rs[rc][:], start=True, stop=True, reads=[XYs[nxt], Rs[rc]], writes=[pR])
                P.op("dve", "tensor_tensor", out=Rs[1 - rc][:], in0=Rs[rc][:], in1=R_, op=ALU.add, reads=[Rs[rc], pR], writes=[Rs[1 - rc]])
                rc = 1 - rc
                cur = nxt
                yield
            Rf = Rs[rc]
            for h in range(2):
                hs = H2[h]
                P.op("pe", "matmul", W0_[hs, :], lhsT=al[hs, cs], rhs=ST[hs, :], start=True, stop=False, reads=[al, ST], writes=[pW0])
                P.op("pe", "matmul", W0_[hs, :], lhsT=Am[hs, 64:128], rhs=TMs[hs, 128:192], start=False, stop=True, reads=[Am, TMs], writes=[pW0])
            P.op("act", "copy", out=W0s[:], in_=W0_, reads=[pW0], writes=[W0s])
            yield
            P.op("pe", "matmul", U_, lhsT=Rf[:], rhs=W0s[:], start=True, stop=True, reads=[Rf, W0s], writes=[pU])
            P.op("dve", "tensor_copy", out=Us[:], in_=U_, reads=[pU], writes=[Us])
            yield
            for h in range(2):
                hs = H2[h]
                P.op("pe", "matmul", Y_[hs, :], lhsT=ST[hs, :], rhs=rt[hs, cs], start=True, stop=False, reads=[ST, rt], writes=[pY])
                P.op("pe", "matmul", Y_[hs, :], lhsT=Us[hs, :], rhs=Am[hs, 0:64], start=False, stop=False, reads=[Us, Am], writes=[pY])
                P.op("pe", "matmul", Y_[hs, :], lhsT=TMs[hs, 128:192], rhs=Am[hs, 128:192], start=False, stop=True, reads=[TMs, Am], writes=[pY])
            P.op("act", "copy", out=Yb[:, cs], in_=Y_, reads=[pY], writes=[Yb])
            if hp == 0 and d == 1 and t0 == CTX and c == 0:
                for nm_, tl_, w_ in (("dbg_Am", Am, 192), ("dbg_Us", Us, 64), ("dbg_TMs", TMs, 192), ("dbg_W0s", W0s, 64), ("dbg_Rf", Rf, 128), ("dbg_ST", ST, 64), ("dbg_XY", XYs[cur], 256)):
                    dbg(g, nm_, tl_, tl_[:, :w_], [128, w_])
                dbg(g, "dbg_Yb", Yb, Yb[:, 0:128], [128, 128])
            for h in range(2):
                hs = H2[h]
                P.op("pe", "matmul", SP_[hs, :], lhsT=TMs[hs, 0:64], rhs=Us[hs, :], start=True, stop=False, reads=[TMs, Us], writes=[pSP])
                P.op("pe", "matmul", SP_[hs, :], lhsT=TMs[hs, 64:128], rhs=TMs[hs, 128:192], start=False, stop=True, reads=[TMs], writes=[pSP])
            pc = Ep[:, c * CH + CH - 1:c * CH + CH] if d == 0 else Ep[:, c * CH:c * CH + 1]
            P.op("dve", "tensor_scalar", out=ST[:], in0=ST[:], scalar1=pc, scalar2=None, op0=ALU.mult, reads=[ST, Ep], writes=[ST])
            P.op("dve", "scalar_tensor_tensor", out=ST[:], in0=SP_, scalar=pc, in1=ST[:], op0=ALU.mult, op1=ALU.add, reads=[pSP, Ep, ST], writes=[ST])
            yield
        P.dma(YS[d][hp][:, ts_], Yb[:, :Lb], reads=[Yb], writes=[YS[d][hp]])
        yield


def stage_rwkv_scan(g, l):
    P = g.P
    T = g.T
    YS = [[scratch(g, "YS%d_%d" % (d, hp), [128, T]) for hp in range(8)] for d in range(2)]
    lat = [(CTX + i * 512, min(512, T - CTX - i * 512)) for i in range((T - CTX + 511) // 512)]
    bl = [[(0, CTX)] + lat, [(0, CTX)] + lat[::-1]]
    for pair in range(4):
        with P.scope():
            tr = [P.sbuf("rtr%d" % i, [128, 512], F32) for i in range(11)]
            gens = []
            k = 0
            for hp in (2 * pair, 2 * pair + 1):
                for d in range(2):
                    gens.append(rwkv_stream(g, l, hp, d, g.ps[2 * k], g.ps[2 * k + 1], bl[d], tr, YS))
                    k += 1
            while gens:
                for ge in list(gens):
                    try:
                        next(ge)
                    except StopIteration:
                        gens.remove(ge)


def stage_rwkv_out(g, l):
    P = g.P
    T = g.T
    F = [scratch(g, "F%d" % i, [128, T]) for i in range(NSEG)]
    YS = [[scratch(g, "YS%d_%d" % (d, hp), [128, T]) for hp in range(8)] for d in range(2)]
    YR = scratch(g, "YR", [RW_W, T], BF16)
    s = g.s
    blk = g.c["blk"]
    a0, ka, omka, rk, lg, lb = s["a0"], s["k_a"], s["omka"], s["r_k"], s["lnxg"], s["lnxb"]
    for hp in range(8):
        with P.scope():
            aups = []
            for d in range(2):
                t = P.sbuf("roa%d" % d, [64, 128], F32)
                P.dma(t[:], g.d["rwkv_a_up"][l, d, :, hp * 128:(hp + 1) * 128], reads=[g.d["rwkv_a_up"]], writes=[t])
                aups.append(t)
            gu0 = P.sbuf("rogu0", [128, 128], F32)
            gu1 = P.sbuf("rogu1", [32, 128], F32)
            P.dma(gu0[:], g.d["rwkv_g_up"][l, 0:128, hp * 128:(hp + 1) * 128], reads=[g.d["rwkv_g_up"]], writes=[gu0])
            P.dma(gu1[:], g.d["rwkv_g_up"][l, 128:160, hp * 128:(hp + 1) * 128], reads=[g.d["rwkv_g_up"]], writes=[gu1])
            names = ("y", "yr", "r", "k", "v", "ad", "g0", "g1", "t1", "t2", "t3")
            tl = {n_: [P.sbuf("ro_%s%d" % (n_, i), [128, 512], F32) for i in range(2)] for n_ in names}
            ob = [P.sbuf("ro_ob%d" % i, [128, 512], BF16) for i in range(2)]
            for bi, (t0, Lb) in enumerate(blocks_of(g)):
                ts_ = slice(t0, t0 + Lb)
                y, yr, r_, k_, v_, ad, g0, g1, t1, t2, t3 = [tl[n_][bi % 2] for n_ in names]
                o = ob[bi % 2]
                P.dma(y[:, :Lb], YS[0][hp][:, ts_], reads=[YS[0][hp]], writes=[y])
                P.dma(yr[:, :Lb], YS[1][hp][:, ts_], reads=[YS[1][hp]], writes=[yr], q="act")
                P.dma(r_[:, :Lb], F[S_R + hp][:, ts_], reads=[F[S_R + hp]], writes=[r_])
                P.dma(k_[:, :Lb], F[S_K + hp][:, ts_], reads=[F[S_K + hp]], writes=[k_], q="act")
                P.dma(v_[:, :Lb], F[S_V + hp][:, ts_], reads=[F[S_V + hp]], writes=[v_])
                P.dma(ad[0:64, :Lb], F[S_AD][0:64, ts_], reads=[F[S_AD]], writes=[ad], q="act")
                P.dma(g0[:, :Lb], F[S_GD][:, ts_], reads=[F[S_GD]], writes=[g0])
                P.dma(g1[0:32, :Lb], F[S_GD + 1][0:32, ts_], reads=[F[S_GD + 1]], writes=[g1], q="act")
                p1, p2, p3, p4 = [g.ps[(4 * bi + i) % 8] for i in range(4)]
                P.op("dve", "tensor_tensor", out=y[:, :Lb], in0=y[:, :Lb], in1=yr[:, :Lb], op=ALU.add, reads=[y, yr], writes=[y])
                P.op("pe", "matmul", p1[:, :Lb], lhsT=blk[:], rhs=y[:, :Lb], start=True, stop=True, reads=[blk, y], writes=[p1])
                P.op("dve", "scalar_tensor_tensor", out=y[:, :Lb], in0=p1[:, :Lb], scalar=-1.0 / 64, in1=y[:, :Lb], op0=ALU.mult, op1=ALU.add, reads=[p1, y], writes=[y])
                P.op("act", "activation", out=t1[:, :Lb], in_=y[:, :Lb], func=AF.Square, reads=[y], writes=[t1])
                P.op("pe", "matmul", p2[:, :Lb], lhsT=blk[:], rhs=t1[:, :Lb], start=True, stop=True, reads=[blk, t1], writes=[p2])
                P.op("act", "activation", out=t1[:, :Lb], in_=p2[:, :Lb], func=AF.Sqrt, bias=GN_EPS, scale=1.0 / 64, reads=[p2], writes=[t1])
                P.op("dve", "reciprocal", out=t1[:, :Lb], in_=t1[:, :Lb], reads=[t1], writes=[t1])
                P.op("dve", "tensor_tensor", out=y[:, :Lb], in0=y[:, :Lb], in1=t1[:, :Lb], op=ALU.mult, reads=[y, t1], writes=[y])
                P.op("dve", "tensor_scalar", out=y[:, :Lb], in0=y[:, :Lb], scalar1=lg[:, l, hp:hp + 1], scalar2=lb[:, l, hp:hp + 1], op0=ALU.mult, op1=ALU.add,
                     reads=[y, lg, lb], writes=[y])
                for d in range(2):
                    P.op("pe", "matmul", p3[:, :Lb], lhsT=aups[d][:], rhs=ad[0:64, :Lb], start=True, stop=True, reads=[aups[d], ad], writes=[p3])
                    tt = t2 if d == 0 else t3
                    P.op("act", "activation", out=tt[:, :Lb], in_=p3[:, :Lb], func=AF.Sigmoid, bias=a0[:, l, d, hp:hp + 1], reads=[p3, a0], writes=[tt])
                P.op("dve", "tensor_tensor", out=t2[:, :Lb], in0=t2[:, :Lb], in1=t3[:, :Lb], op=ALU.add, reads=[t2, t3], writes=[t2])
                P.op("dve", "tensor_scalar", out=t2[:, :Lb], in0=t2[:, :Lb], scalar1=ka[:, l, hp:hp + 1], scalar2=None, op0=ALU.mult, reads=[t2, ka], writes=[t2])
                P.op("dve", "tensor_scalar", out=t3[:, :Lb], in0=k_[:, :Lb], scalar1=omka[:, l, hp:hp + 1], scalar2=2.0, op0=ALU.mult, op1=ALU.mult, reads=[k_, omka], writes=[t3])
                P.op("dve", "tensor_tensor", out=t2[:, :Lb], in0=t2[:, :Lb], in1=k_[:, :Lb], op=ALU.mult, reads=[t2, k_], writes=[t2])
                P.op("dve", "tensor_tensor", out=t2[:, :Lb], in0=t2[:, :Lb], in1=t3[:, :Lb], op=ALU.add, reads=[t2, t3], writes=[t2])
                P.op("dve", "tensor_tensor", out=t2[:, :Lb], in0=t2[:, :Lb], in1=r_[:, :Lb], op=ALU.mult, reads=[t2, r_], writes=[t2])
                P.op("dve", "tensor_scalar", out=t2[:, :Lb], in0=t2[:, :Lb], scalar1=rk[:, l, hp:hp + 1], scalar2=0.5, op0=ALU.mult, op1=ALU.mult, reads=[t2, rk], writes=[t2])
                P.op("pe", "matmul", p4[:, :Lb], lhsT=blk[:], rhs=t2[:, :Lb], start=True, stop=True, reads=[blk, t2], writes=[p4])
                P.op("dve", "tensor_tensor", out=t2[:, :Lb], in0=v_[:, :Lb], in1=p4[:, :Lb], op=ALU.mult, reads=[v_, p4], writes=[t2])
                P.op("dve", "tensor_tensor", out=y[:, :Lb], in0=y[:, :Lb], in1=t2[:, :Lb], op=ALU.add, reads=[y, t2], writes=[y])
                P.op("act", "activation", out=g0[:, :Lb], in_=g0[:, :Lb], func=AF.Sigmoid, reads=[g0], writes=[g0])
                P.op("act", "activation", out=g1[0:32, :Lb], in_=g1[0:32, :Lb], func=AF.Sigmoid, reads=[g1], writes=[g1])
                P.op("pe", "matmul", p1[:, :Lb], lhsT=gu0[:], rhs=g0[:, :Lb], start=True, stop=False, reads=[gu0, g0], writes=[p1])
                P.op("pe", "matmul", p1[:, :Lb], lhsT=gu1[0:32, :], rhs=g1[0:32, :Lb], start=False, stop=True, reads=[gu1, g1], writes=[p1])
                P.op("dve", "tensor_tensor", out=o[:, :Lb], in0=y[:, :Lb], in1=p1[:, :Lb], op=ALU.mult, reads=[y, p1], writes=[o])
                P.dma(YR[hp * 128:(hp + 1) * 128, ts_], o[:, :Lb], reads=[o], writes=[YR])


def stage_mla(g, l, ctx_out):
    P = g.P
    T = g.T
    F = [scratch(g, "F%d" % i, [128, T]) for i in range(NSEG)]
    YM = scratch(g, "YM", [2048, T], BF16)
    ones, ident, Jm = g.c["ones"], g.c["ident"], g.c["Jm"]
    qng, kvg = g.s["qng"], g.s["kvg"]
    blocks = blocks_of(g)
    NKC = T // 128
    with P.scope():
        onesb = P.sbuf("onesb", [128, 128], BF16)
        P.op("dve", "memset", onesb[:], 1.0, reads=[], writes=[onesb])
        cos = load_const(g, "cos", [64, g.SEQ])
        sin = load_const(g, "sin", [64, g.SEQ], q="act")
        qn = P.sbuf("qn", [128, 4, T], BF16)
        kvn = P.sbuf("kvn", [128, 2, T], BF16)
        kr = P.sbuf("kr", [64, T], BF16)
        with P.scope():
            xqs = [P.sbuf("xq%d" % i, [128, 4, 512], F32) for i in range(2)]
            xks = [P.sbuf("xk%d" % i, [128, 2, 512], F32) for i in range(2)]
            xrs = [P.sbuf("xr%d" % i, [64, 512], F32) for i in range(2)]
            sqs = [P.sbuf("msq%d" % i, [128, 512], F32) for i in range(2)]
            rs1 = P.sbuf("mrs1", [128, 512], F32)
            rs2 = P.sbuf("mrs2", [128, 512], F32)
            m1 = P.sbuf("mm1", [64, 512], F32)
            m2 = P.sbuf("mm2", [64, 512], F32)
            for bi, (t0, n) in enumerate(blocks):
                xq, xk, xr = xqs[bi % 2], xks[bi % 2], xrs[bi % 2]
                for i in range(4):
                    P.dma(xq[:, i, :n], F[S_Q + i][:, t0:t0 + n], reads=[F[S_Q + i]], writes=[xq], q="sp" if i % 2 == 0 else "act")
                for i in range(2):
                    P.dma(xk[:, i, :n], F[S_KV + i][:, t0:t0 + n], reads=[F[S_KV + i]], writes=[xk], q="sp" if i % 2 == 0 else "act")
                P.dma(xr[0:64, :n], F[S_KR][0:64, t0:t0 + n], reads=[F[S_KR]], writes=[xr])
                pa, pb, pj = g.ps[0], g.ps[1], g.ps[2]
                for (x_, nchunk, pp, rs, gam, dst) in ((xq, 4, pa, rs1, qng, qn), (xk, 2, pb, rs2, kvg, kvn)):
                    for i in range(nchunk):
                        sq = sqs[i % 2]
                        P.op("act", "activation", out=sq[:, :n], in_=x_[:, i, :n], func=AF.Square, reads=[x_], writes=[sq])
                        P.op("pe", "matmul", pp[:, :n], lhsT=ones[:], rhs=sq[:, :n], start=(i == 0), stop=(i == nchunk - 1), reads=[ones, sq], writes=[pp])
                    P.op("act", "activation", out=rs[:, :n], in_=pp[:, :n], func=AF.Sqrt, bias=LN_EPS, scale=1.0 / (128 * nchunk), reads=[pp], writes=[rs])
                    P.op("dve", "reciprocal", out=rs[:, :n], in_=rs[:, :n], reads=[rs], writes=[rs])
                    for i in range(nchunk):
                        P.op("dve", "scalar_tensor_tensor", out=dst[:, i, t0:t0 + n], in0=x_[:, i, :n], scalar=gam[:, l, i:i + 1], in1=rs[:, :n],
                             op0=ALU.mult, op1=ALU.mult, reads=[x_, gam, rs], writes=[dst])
                if t0 < CTX:
                    P.op("act", "copy", out=kr[0:64, t0:t0 + n], in_=xr[0:64, :n], reads=[xr], writes=[kr])
                else:
                    P.op("pe", "matmul", pj[0:64, :n], lhsT=Jm[:], rhs=xr[0:64, :n], start=True, stop=True, reads=[Jm, xr], writes=[pj])
                    P.op("dve", "tensor_tensor", out=m1[:, :n], in0=xr[0:64, :n], in1=cos[:, t0 - CTX:t0 - CTX + n], op=ALU.mult, reads=[xr, cos], writes=[m1])
                    P.op("dve", "tensor_tensor", out=m2[:, :n], in0=pj[0:64, :n], in1=sin[:, t0 - CTX:t0 - CTX + n], op=ALU.mult, reads=[pj, sin], writes=[m2])
                    P.op("dve", "tensor_tensor", out=kr[0:64, t0:t0 + n], in0=m1[:, :n], in1=m2[:, :n], op=ALU.add, reads=[m1, m2], writes=[kr])
        wuq, wuk, wuv = g.d["mla_w_uq"], g.d["mla_w_uk"], g.d["mla_w_uv"]
        for hd in range(16):
            with P.scope():
                wq = P.sbuf("wq", [128, 4, 192], BF16)
                wk = P.sbuf("wk", [128, 2, 128], BF16)
                wv = P.sbuf("wv", [128, 2, 128], BF16)
                P.dma(wq[:], wuq[l, :, hd * 192:(hd + 1) * 192].rearrange("(kc p) n -> p kc n", p=128), reads=[wuq], writes=[wq], q="pool")
                P.dma(wk[:], wuk[l, :, hd * 128:(hd + 1) * 128].rearrange("(kc p) n -> p kc n", p=128), reads=[wuk], writes=[wk], q="pool")
                P.dma(wv[:], wuv[l, :, hd * 128:(hd + 1) * 128].rearrange("(kc p) n -> p kc n", p=128), reads=[wuv], writes=[wv], q="pool")
                Kn = P.sbuf("Kn", [128, T], BF16)
                Qn = P.sbuf("Qn", [128, T], BF16)
                Qr = P.sbuf("Qr", [64, T], BF16)
                Va = P.sbuf("Va", [128, NKC, 128], BF16)
                xq32 = P.sbuf("xq32", [64, 512], F32)
                m1 = P.sbuf("hm1", [64, 512], F32)
                m2 = P.sbuf("hm2", [64, 512], F32)
                for bi, (t0, n) in enumerate(blocks):
                    pk, pq, pr, pj = g.ps[0], g.ps[1], g.ps[2], g.ps[3]
                    for kc in range(2):
                        P.op("pe", "matmul", pk[:, :n], lhsT=wk[:, kc, :], rhs=kvn[:, kc, t0:t0 + n], start=(kc == 0), stop=(kc == 1), reads=[wk, kvn], writes=[pk])
                    P.op("act", "copy", out=Kn[:, t0:t0 + n], in_=pk[:, :n], reads=[pk], writes=[Kn])
                    if t0 < CTX and not ctx_out:
                        continue
                    for kc in range(4):
                        P.op("pe", "matmul", pq[:, :n], lhsT=wq[:, kc, 0:128], rhs=qn[:, kc, t0:t0 + n], start=(kc == 0), stop=(kc == 3), reads=[wq, qn], writes=[pq])
                    P.op("dve", "tensor_copy", out=Qn[:, t0:t0 + n], in_=pq[:, :n], reads=[pq], writes=[Qn])
                    for kc in range(4):
                        P.op("pe", "matmul", pr[0:64, :n], lhsT=wq[:, kc, 128:192], rhs=qn[:, kc, t0:t0 + n], start=(kc == 0), stop=(kc == 3), reads=[wq, qn], writes=[pr])
                    if t0 < CTX:
                        P.op("act", "copy", out=Qr[0:64, t0:t0 + n], in_=pr[0:64, :n], reads=[pr], writes=[Qr])
                    else:
                        P.op("act", "copy", out=xq32[:, :n], in_=pr[0:64, :n], reads=[pr], writes=[xq32])
                        P.op("pe", "matmul", pj[0:64, :n], lhsT=Jm[:], rhs=xq32[:, :n], start=True, stop=True, reads=[Jm, xq32], writes=[pj])
                        P.op("dve", "tensor_tensor", out=m1[:, :n], in0=xq32[:, :n], in1=cos[:, t0 - CTX:t0 - CTX + n], op=ALU.mult, reads=[xq32, cos], writes=[m1])
                        P.op("dve", "tensor_tensor", out=m2[:, :n], in0=pj[0:64, :n], in1=sin[:, t0 - CTX:t0 - CTX + n], op=ALU.mult, reads=[pj, sin], writes=[m2])
                        P.op("dve", "tensor_tensor", out=Qr[0:64, t0:t0 + n], in0=m1[:, :n], in1=m2[:, :n], op=ALU.add, reads=[m1, m2], writes=[Qr])
                for tc_ in range(NKC):
                    pv = g.ps[2 + tc_ % 2]
                    for kc in range(2):
                        P.op("pe", "matmul", pv[:, 0:128], lhsT=kvn[:, kc, tc_ * 128:(tc_ + 1) * 128], rhs=wv[:, kc, :], start=(kc == 0), stop=(kc == 1),
                             reads=[kvn, wv], writes=[pv])
                    if tc_ % 2 == 0:
                        P.op("act", "copy", out=Va[:, tc_, :], in_=pv[:, 0:128], reads=[pv], writes=[Va])
                    else:
                        P.op("dve", "tensor_copy", out=Va[:, tc_, :], in_=pv[:, 0:128], reads=[pv], writes=[Va])
                pts = [P.sbuf("PT%d" % i, [128, 512], BF16) for i in range(3)]
                stgs = [P.sbuf("ostg%d" % i, [128, 512], BF16) for i in range(2)]
                rss = [P.sbuf("ors%d" % i, [128, 512], F32) for i in range(2)]
                qblocks = [b_ for b_ in blocks if (b_[0] >= CTX or ctx_out)]
                for qi, (q0, nq) in enumerate(qblocks):
                    keys = list(range(CTX // 128)) if q0 < CTX else list(range(NKC))
                    nk = len(keys)
                    accO = g.ps[4 + 2 * (qi % 2)]
                    accS = g.ps[5 + 2 * (qi % 2)]

                    def qk(ki):
                        kc = keys[ki]
                        S = g.ps[ki % 2]
                        P.op("pe", "matmul", S[:, :nq], lhsT=Kn[:, kc * 128:(kc + 1) * 128], rhs=Qn[:, q0:q0 + nq], start=True, stop=False, reads=[Kn, Qn], writes=[S])
                        P.op("pe", "matmul", S[:, :nq], lhsT=kr[0:64, kc * 128:(kc + 1) * 128], rhs=Qr[0:64, q0:q0 + nq], start=False, stop=True, reads=[kr, Qr], writes=[S])

                    qk(0)
                    for ki in range(nk):
                        if ki + 1 < nk:
                            qk(ki + 1)
                        kc = keys[ki]
                        S = g.ps[ki % 2]
                        PT = pts[ki % 3]
                        P.op("act", "activation", out=PT[:, :nq], in_=S[:, :nq], func=AF.Exp, scale=ATT_SCALE, reads=[S], writes=[PT])
                        P.op("pe", "matmul", accO[:, :nq], lhsT=Va[:, kc, :], rhs=PT[:, :nq], start=(ki == 0), stop=(ki == nk - 1), reads=[PT, Va], writes=[accO])
                        P.op("pe", "matmul", accS[:, :nq], lhsT=onesb[:], rhs=PT[:, :nq], start=(ki == 0), stop=(ki == nk - 1), reads=[PT, onesb], writes=[accS])
                    rs, stg = rss[qi % 2], stgs[qi % 2]
                    P.op("dve", "reciprocal", out=rs[:, :nq], in_=accS[:, :nq], reads=[accS], writes=[rs])
                    P.op("dve", "tensor_tensor", out=stg[:, :nq], in0=accO[:, :nq], in1=rs[:, :nq], op=ALU.mult, reads=[accO, rs], writes=[stg])
                    P.dma(YM[hd * 128:(hd + 1) * 128, q0:q0 + nq], stg[:, :nq], reads=[stg], writes=[YM])


def stage_c(g, l, dst, last):
    P = g.P
    T = g.T
    XL = scratch(g, "XL", [D, T])
    F = [scratch(g, "F%d" % i, [128, T]) for i in range(NSEG)]
    YR = scratch(g, "YR", [RW_W, T], BF16)
    YM = scratch(g, "YM", [2048, T], BF16)
    YL = scratch(g, "YL", [LRU_W, T], BF16)
    prw, pml, plr, wout = g.d["proj_rwkv"], g.d["proj_mla"], g.d["proj_lru"], g.d["w_out"]
    for (t0, NT, m) in token_tiles(g, with_ctx=not last):
        with P.scope():
            s = P.sbuf("cs_", [128, KC, NT], F32)
            P.dma(s[:], fview(XL)[:, :, t0:t0 + NT], reads=[XL], writes=[s])
            P.op("act", "mul", out=s[:], in_=s[:], mul=ALPHA, reads=[s], writes=[s])
            with P.scope():
                yr = P.sbuf("cyr", [128, 8, NT], BF16)
                ym = P.sbuf("cym", [128, 16, NT], BF16)
                yl = P.sbuf("cyl", [128, 8, NT], BF16)
                P.dma(yr[:], YR[:].rearrange("(kc p) t -> p kc t", p=128)[:, :, t0:t0 + NT], reads=[YR], writes=[yr])
                P.dma(ym[:], YM[:].rearrange("(kc p) t -> p kc t", p=128)[:, :, t0:t0 + NT], reads=[YM], writes=[ym], q="act")
                P.dma(yl[:], YL[:].rearrange("(kc p) t -> p kc t", p=128)[:, :, t0:t0 + NT], reads=[YL], writes=[yl])
                ybf = P.sbuf("cyb", [128, KC, NT], BF16)
                gts = [P.sbuf("cgt%d" % i, [128, 3, NT], F32) for i in range(2)]
                pws = [P.sbuf("cpw%d" % i, [128, 32, 128], BF16) for i in range(2)]
                t1 = P.sbuf("ct1", [128, NT], F32)
                t2 = P.sbuf("ct2", [128, NT], F32)
                for dc in range(KC):
                    pw, gt = pws[dc % 2], gts[dc % 2]
                    cs_ = slice(dc * 128, (dc + 1) * 128)
                    wload(g, pw[:, 0:8, :], pw, "proj_rwkv", (l,), dc * 128)
                    wload(g, pw[:, 8:24, :], pw, "proj_mla", (l,), dc * 128)
                    wload(g, pw[:, 24:32, :], pw, "proj_lru", (l,), dc * 128)
                    for bi in range(3):
                        P.dma(gt[:, bi, :], F[S_GATE + 16 * bi + dc][:, t0:t0 + NT], reads=[F[S_GATE + 16 * bi + dc]], writes=[gt], q="sp" if bi != 1 else "act")
                    p1, p2, p3 = g.ps[0], g.ps[1], g.ps[2]
                    for kc in range(8):
                        P.op("pe", "matmul", p1[:, :NT], lhsT=pw[:, kc, :], rhs=yr[:, kc, :], start=(kc == 0), stop=(kc == 7), reads=[pw, yr], writes=[p1])
                    for kc in range(16):
                        P.op("pe", "matmul", p2[:, :NT], lhsT=pw[:, 8 + kc, :], rhs=ym[:, kc, :], start=(kc == 0), stop=(kc == 15), reads=[pw, ym], writes=[p2])
                    for kc in range(8):
                        P.op("pe", "matmul", p3[:, :NT], lhsT=pw[:, 24 + kc, :], rhs=yl[:, kc, :], start=(kc == 0), stop=(kc == 7), reads=[pw, yl], writes=[p3])
                    P.op("dve", "tensor_tensor", out=t1[:], in0=gt[:, 0, :], in1=p1[:, :NT], op=ALU.mult, reads=[gt, p1], writes=[t1])
                    P.op("dve", "tensor_tensor", out=t2[:], in0=gt[:, 1, :], in1=p2[:, :NT], op=ALU.mult, reads=[gt, p2], writes=[t2])
                    P.op("dve", "tensor_tensor", out=t1[:], in0=t1[:], in1=t2[:], op=ALU.add, reads=[t1, t2], writes=[t1])
                    P.op("dve", "tensor_tensor", out=t2[:], in0=gt[:, 2, :], in1=p3[:, :NT], op=ALU.mult, reads=[gt, p3], writes=[t2])
                    P.op("dve", "tensor_tensor", out=ybf[:, dc, :], in0=t1[:], in1=t2[:], op=ALU.add, reads=[t1, t2], writes=[ybf])
                wos = [P.sbuf("cwo%d" % i, [128, KC, 128], BF16) for i in range(2)]
                for dc in range(KC):
                    wo = wos[dc % 2]
                    wload(g, wo[:], wo, "w_out", (l,), dc * 128)
                    po = g.ps[4 + dc % 2]
                    for kc in range(KC):
                        P.op("pe", "matmul", po[:, :NT], lhsT=wo[:, kc, :], rhs=ybf[:, kc, :], start=(kc == 0), stop=(kc == KC - 1), reads=[wo, ybf], writes=[po])
                    P.op("dve", "scalar_tensor_tensor", out=s[:, dc, :], in0=po[:, :NT], scalar=g.mod[:, 5 * 16 + dc, m:m + 1], in1=s[:, dc, :],
                         op0=ALU.mult, op1=ALU.add, reads=[po, s, g.mod], writes=[s])
            layer_norm(g, s, NT, l, 1)
            ffn_block(g, s, NT, m, l, 1, (6, 7, 8), 2)
            if last:
                P.dma(fview(dst)[:, :, t0 - CTX:t0 - CTX + NT], s[:], reads=[s], writes=[dst])
            else:
                P.dma(fview(dst)[:, :, t0:t0 + NT], s[:], reads=[s], writes=[dst])


SMALL_NAMES = ["adab", "lng", "lnb", "gateb", "mu", "mud", "w0", "a0", "kk_k", "k_a", "r_k", "lnxg", "lnxb", "qng", "kvg", "convw", "convb", "ba", "bx", "lam"]
CONST_NAMES = ["ident", "ones", "blk", "Jm", "maskA"]


def build_program(SEQ, L, shapes, stages=None, ext_in=(), ext_out=()):
    nc = bass.Bass("TRN2", target_bir_lowering=False)
    g = setup(nc, SEQ, L, shapes, ext_in=ext_in, ext_out=ext_out)
    load_consts(g, CONST_NAMES)
    load_small(g, SMALL_NAMES)
    init_derived(g)
    init_mod(g)
    T = g.T
    xcur = g.d["xT"]
    XN = scratch(g, "XN", [D, T])
    yT = g.P.dram("yT", [D, SEQ], F32, "ExternalOutput")
    g.d["yT"] = yT
    for l in range(L):
        precast_layer(g, l)
    for l in range(L):
        last = l == L - 1
        stage_mod(g, l)
        stage_a(g, l, xcur)
        stage_lru(g, l)
        stage_rwkv_mix(g, l)
        stage_rwkv_scan(g, l)
        stage_rwkv_out(g, l)
        stage_mla(g, l, ctx_out=not last)
        stage_c(g, l, yT if last else XN, last)
        xcur = XN
    g.P.emit()
    return nc, g


def host_inputs(inp, b, SEQ, L):
    x = np.asarray(inp["x"][b], np.float32)
    cx = np.asarray(inp["ctx"][b], np.float32)
    d = {}
    d["xT"] = np.ascontiguousarray(np.concatenate([cx, x], 0).T)
    d["cvec"] = np.ascontiguousarray(np.stack([fm(inp["c"][b]), fm(inp["c_ctx"])], -1))
    d.update(host_consts(SEQ))
    d.update(host_small(inp, L))
    for k in BIG_W:
        d[k] = np.ascontiguousarray(np.asarray(inp[k], np.float32))
    return d


_CACHE = {}


def kernel(**inputs):
    B, SEQ, _ = inputs["x"].shape
    L = inputs["w_in"].shape[0]
    per_core = [host_inputs(inputs, b, SEQ, L) for b in range(B)]
    shapes = {k: list(v.shape) for k, v in per_core[0].items()}
    key = (SEQ, L)
    if key not in _CACHE:
        _CACHE[key] = build_program(SEQ, L, shapes)
    nc, g = _CACHE[key]
    res = run_bass_kernel_spmd(nc, per_core, core_ids=list(range(B)))
    out = np.stack([np.ascontiguousarray(res.results[b]["yT"].T) for b in range(B)], 0)
    return out.astype(np.float32)
```

```python
import contextlib
import math
import numpy as np
import concourse.bass as bass
import concourse.mybir as mybir
from concourse.bass_utils import run_bass_kernel_spmd

F32 = mybir.dt.float32
BF16 = mybir.dt.bfloat16
AF = mybir.ActivationFunctionType
ALU = mybir.AluOpType

D = 2048
KC = 16
DFF = 5632
FC = 44
NMOD = 9
CTX = 256
GW = 64
RW_W = 1024
RWKV_IN = 3360
MLA_IN = 832
LRU_W = 1024
IN_COLS = 12384
ALPHA = 4.0 ** 0.25
LN_EPS = 1e-6
GN_EPS = 64e-5
C0 = math.exp(-0.5)
ATT_SCALE = 192.0 ** -0.5
CH = 64

SEGS = []
for i in range(8):
    SEGS.append((128 * i, 128))
for i in range(8):
    SEGS.append((1024 + 128 * i, 128))
for i in range(8):
    SEGS.append((2048 + 128 * i, 128))
SEGS.append((3072, 64))
SEGS.append((3136, 64))
SEGS.append((3200, 128))
SEGS.append((3328, 32))
for i in range(4):
    SEGS.append((3360 + 128 * i, 128))
for i in range(2):
    SEGS.append((3872 + 128 * i, 128))
SEGS.append((4128, 64))
for i in range(8):
    SEGS.append((4192 + 128 * i, 128))
for i in range(8):
    SEGS.append((5216 + 128 * i, 128))
for i in range(48):
    SEGS.append((6240 + 128 * i, 128))
NSEG = len(SEGS)
S_R, S_K, S_V, S_WD, S_AD, S_GD, S_Q, S_KV, S_KR, S_XR, S_GB, S_GATE = 0, 8, 16, 24, 25, 26, 28, 32, 34, 35, 43, 51
NRW = 28


class Tile:
    __slots__ = ("t", "name", "last_w", "readers", "dsem", "osem", "kind")

    def __init__(self, t, name, kind="sbuf"):
        self.t = t
        self.name = name
        self.kind = kind
        self.last_w = None
        self.readers = []
        self.dsem = None
        self.osem = None

    def __getitem__(self, idx):
        return self.t[idx]

    def sub(self, name):
        return Tile(self.t, self.name + "." + name, self.kind)


class Op:
    __slots__ = ("eng", "fn", "deps", "is_dma", "sem", "val", "need_sig", "osize")


SEM_EPOCH = 24000
SMALL_OP = 512
import os
DEBUG_INS = os.environ.get("DEBUG_INS")


class Prog:
    ENGS = ("pe", "dve", "act", "pool", "sp")

    def __init__(self, nc):
        self.nc = nc
        self.ops = []
        self.stack = contextlib.ExitStack()
        self.scopes = []
        self.free_sems = []
        self.fence = None
        self.nsem = 0
        self.dummy = Tile(self.stack.enter_context(nc.sbuf_tensor("fence_dummy", [1, 8], F32)), "dummy")

    def new_sem(self, name):
        self.nsem += 1
        return self.stack.enter_context(self.nc.semaphore(name + "_%d" % self.nsem))

    def _reg(self, tl):
        tl.last_w = self.fence
        if self.scopes:
            self.scopes[-1][1].append(tl)
        return tl

    def sbuf(self, name, shape, dtype=F32):
        st = self.scopes[-1][0] if self.scopes else self.stack
        nm = name + "_%d" % len(self.ops)
        t = st.enter_context(self.nc.sbuf_tensor(nm, list(shape), dtype))
        return self._reg(Tile(t, name))

    def psum(self, name, shape, dtype=F32):
        t = self.stack.enter_context(self.nc.psum_tensor(name, list(shape), dtype))
        return Tile(t, name, "psum")

    def dram(self, name, shape, dtype=F32, kind="Internal"):
        t = self.nc.dram_tensor(name, list(shape), dtype, kind=kind).ap()
        return Tile(t, name, "dram")

    @contextlib.contextmanager
    def scope(self):
        st = contextlib.ExitStack()
        self.scopes.append((st, []))
        try:
            yield
        finally:
            _, tiles = self.scopes.pop()
            if tiles:
                d = self.dummy
                self.fence = self.op("dve", "memset", d[:], 0.0, reads=[], writes=[d] + tiles)
                for tl in tiles:
                    for s in (tl.dsem, tl.osem):
                        if s is not None and s[1] < SEM_EPOCH:
                            self.free_sems.append(s)
            st.close()

    def _record(self, eng, fn, reads, writes, is_dma):
        i = len(self.ops)
        deps = set()
        pr = [t for t in reads if t.kind == "psum"]
        if pr:
            writes = list(writes) + [t for t in pr if t not in writes]
            reads = [t for t in reads if t.kind != "psum"]
        for t in reads:
            if t.last_w is not None:
                deps.add(t.last_w)
        for t in writes:
            if t.last_w is not None:
                deps.add(t.last_w)
            deps.update(t.readers)
        for t in reads:
            t.readers.append(i)
        for t in writes:
            t.last_w = i
            t.readers = []
        o = Op()
        o.eng = eng
        o.fn = fn
        o.deps = deps
        o.is_dma = is_dma
        o.sem = None
        o.val = 0
        o.need_sig = is_dma
        o.osize = 1 << 30
        self.ops.append(o)
        return i

    def op(self, eng, name, *args, reads=(), writes=(), **kw):
        def fn(e, name=name, args=args, kw=kw):
            return getattr(e, name)(*args, **kw)

        i = self._record(eng, fn, reads, writes, False)
        if eng != "pe":
            oap = kw.get("out", args[0] if args else None)
            try:
                sz = 1
                for v in oap.shape[1:]:
                    sz *= int(v)
                self.ops[i].osize = 0 if name == "tensor_tensor_scan" else sz
            except Exception:
                pass
        return i

    def _get_sem(self, cur, name):
        if cur is not None and cur[1] < SEM_EPOCH:
            return cur
        if self.free_sems:
            return self.free_sems.pop()
        return [self.new_sem(name), 0]

    def dma(self, out_ap, in_ap, reads=(), writes=(), q="sp", **kw):
        def fn(e, out_ap=out_ap, in_ap=in_ap, kw=kw):
            return e.dma_start(out=out_ap, in_=in_ap, **kw)

        i = self._record(q, fn, reads, writes, True)
        if writes[0].kind == "dram":
            st = reads[0]
            st.osem = self._get_sem(st.osem, "o")
            s = st.osem
        else:
            st = writes[0]
            st.dsem = self._get_sem(st.dsem, "d")
            s = st.dsem
        s[1] += 16
        self.ops[i].sem = s[0]
        self.ops[i].val = s[1]
        return i

    def dma_dd(self, out_ap, in_ap, reads=(), writes=(), q="pool", nslots=8, **kw):
        def fn(e, out_ap=out_ap, in_ap=in_ap, kw=kw):
            return e.dma_start(out=out_ap, in_=in_ap, **kw)

        if not hasattr(self, "dd_slots"):
            self.dd_slots = [[self.new_sem("dd"), 0, None] for _ in range(nslots)]
            self.dd_i = 0
        slot = self.dd_slots[self.dd_i % nslots]
        self.dd_i += 1
        if slot[1] >= SEM_EPOCH:
            slot[0] = self.new_sem("dd")
            slot[1] = 0
        i = self._record(q, fn, reads, writes, True)
        if slot[2] is not None:
            self.ops[i].deps.add(slot[2])
        slot[1] += 16
        slot[2] = i
        self.ops[i].sem = slot[0]
        self.ops[i].val = slot[1]
        return i

    def emit(self):
        nc = self.nc
        ops = self.ops
        for o in ops:
            for d in o.deps:
                od = ops[d]
                if not od.is_dma and (od.eng != o.eng or od.osize < SMALL_OP or o.is_dma):
                    od.need_sig = True
        esem = {e: [] for e in self.ENGS}
        ecnt = {e: 0 for e in self.ENGS}
        for o in ops:
            if o.is_dma or not o.need_sig:
                continue
            c = ecnt[o.eng]
            ep = c // SEM_EPOCH
            if ep >= len(esem[o.eng]):
                esem[o.eng].append(self.new_sem("e_" + o.eng))
            o.sem = esem[o.eng][ep]
            o.val = c % SEM_EPOCH + 1
            ecnt[o.eng] = c + 1
        per_eng = {e: [] for e in self.ENGS}
        for i, o in enumerate(ops):
            per_eng[o.eng].append(i)
        fw = {}
        for o in ops:
            if o.is_dma:
                k = id(o.sem)
                if k not in fw or fw[k][1] < o.val:
                    fw[k] = (o.sem, o.val)

        def run(engname, e):
            waited = {}
            for i in per_eng[engname]:
                o = ops[i]
                for d in sorted(o.deps):
                    od = ops[d]
                    if not od.is_dma and od.eng == engname and od.osize >= SMALL_OP and not o.is_dma:
                        continue
                    k = id(od.sem)
                    if waited.get(k, 0) >= od.val:
                        continue
                    e.wait_ge(od.sem, od.val)
                    waited[k] = od.val
                ins = o.fn(e)
                if DEBUG_INS is not None and DEBUG_INS in str(getattr(ins.ins, "name", "")):
                    print("DEBUG_INS", i, engname, str(ins)[:600])
                if o.need_sig:
                    ins.then_inc(o.sem, 16 if o.is_dma else 1)
            if engname == "sp":
                for s, v in fw.values():
                    e.wait_ge(s, v)

        with nc.Block() as block:
            @block.tensor
            def _(e):
                run("pe", e)

            @block.vector
            def _(e):
                run("dve", e)

            @block.scalar
            def _(e):
                run("act", e)

            @block.gpsimd
            def _(e):
                run("pool", e)

            @block.sync
            def _(e):
                run("sp", e)
        self.stack.close()


class G:
    pass


def fm(v):
    v = np.asarray(v, np.float32)
    n = v.shape[-1] // 128
    w = v.reshape(v.shape[:-1] + (n, 128))
    return np.ascontiguousarray(np.moveaxis(w, -1, 0))


def host_consts(SEQ):
    c = {}
    c["ident"] = np.eye(128, dtype=np.float32)
    c["ones"] = np.ones((128, 128), np.float32)
    blk = np.zeros((128, 128), np.float32)
    blk[:64, :64] = 1
    blk[64:, 64:] = 1
    c["blk"] = blk
    a = np.arange(64)
    strict_f = (a[:, None] < a[None, :]).astype(np.float32)
    incl_f = (a[:, None] <= a[None, :]).astype(np.float32)
    mA = np.zeros((2, 128, 192), np.float32)
    mXY = np.zeros((2, 128, 256), np.float32)
    for d in range(2):
        st = strict_f if d == 0 else strict_f.T
        inc = incl_f if d == 0 else incl_f.T
        for h in range(2):
            rs = slice(64 * h, 64 * h + 64)
            mA[d, rs, 0:64] = inc
            mA[d, rs, 64:128] = st
            mA[d, rs, 128:192] = inc
            mXY[d, rs, 64 * h:64 * h + 64] = st
            mXY[d, rs, 128 + 64 * h:128 + 64 * h + 64] = st.T
    c["maskA"] = mA
    c["maskXY"] = mXY
    Lm = 1024
    rm = np.ones((2, 128, Lm), np.float32)
    rm[0, :, 0::64] = 0
    rm[1, :, 63::64] = 0
    c["rm"] = rm
    t = np.arange(SEQ)
    row = (t // GW).astype(np.float32)
    col = (t % GW).astype(np.float32)
    inv = (10000.0 ** (-np.arange(16, dtype=np.float32) / 16)).astype(np.float32)
    ang = np.concatenate([row[:, None] * inv, col[:, None] * inv], axis=-1).astype(np.float32)
    c["cos"] = np.ascontiguousarray(np.repeat(np.cos(ang), 2, axis=1).T.astype(np.float32))
    c["sin"] = np.ascontiguousarray(np.repeat(np.sin(ang), 2, axis=1).T.astype(np.float32))
    J = np.zeros((64, 64), np.float32)
    for i in range(32):
        J[2 * i + 1, 2 * i] = -1.0
        J[2 * i, 2 * i + 1] = 1.0
    c["Jm"] = J
    return c


def seg_mu(mu_l):
    full = np.zeros((128, NRW), np.float32)
    dirm = np.zeros((128, 4, NRW), np.float32)
    for si in range(NRW):
        c0, w = SEGS[si]
        full[:w, si] = mu_l[c0:c0 + w]
        for p in range(w):
            q = (c0 + p) // 840
            dirm[p, q, si] = mu_l[c0 + p]
    return full, dirm


def seg_dirs(si):
    c0, w = SEGS[si]
    return sorted(set((c0 + p) // 840 for p in range(w)))


SMALL_SPECS = None


def host_small(inp, L):
    s = {}
    s["adab"] = np.stack([fm(inp["ada_b"][l]) for l in range(L)], 1)
    s["lng"] = np.stack([fm(inp["ln_g"][l]) for l in range(L)], 1)
    s["lnb"] = np.stack([fm(inp["ln_b"][l]) for l in range(L)], 1)
    s["gateb"] = np.stack([fm(inp["gate_b"][l]) for l in range(L)], 1)
    mus = [seg_mu(np.asarray(inp["rwkv_mu"][l], np.float32)) for l in range(L)]
    s["mu"] = np.stack([m[0] for m in mus], 1)
    s["mud"] = np.stack([m[1] for m in mus], 1)
    s["w0"] = np.stack([fm(inp["rwkv_w0"][l]) for l in range(L)], 1)
    s["a0"] = np.stack([fm(inp["rwkv_a0"][l]) for l in range(L)], 1)
    for nm, key in (("kk_k", "rwkv_k_k"), ("k_a", "rwkv_k_a"), ("r_k", "rwkv_r_k"), ("lnxg", "rwkv_lnx_g"), ("lnxb", "rwkv_lnx_b")):
        s[nm] = np.stack([fm(inp[key][l]) for l in range(L)], 1)
    s["qng"] = np.stack([fm(inp["mla_q_norm"][l]) for l in range(L)], 1)
    s["kvg"] = np.stack([fm(inp["mla_kv_norm"][l]) for l in range(L)], 1)
    s["convw"] = np.stack([fm(inp["lru_conv_w"][l]) for l in range(L)], 1)
    s["convb"] = np.stack([fm(inp["lru_conv_b"][l]) for l in range(L)], 1)
    for nm, key in (("ba", "lru_ba"), ("bx", "lru_bx"), ("lam", "lru_lambda")):
        s[nm] = np.stack([fm(inp[key][l]) for l in range(L)], 1)
    return {k: np.ascontiguousarray(v, np.float32) for k, v in s.items()}


BIG_W = ["ada_w", "ffn_w_gate", "ffn_w_up", "ffn_w_down", "w_in", "rwkv_w_up", "rwkv_a_up", "rwkv_g_up",
         "mla_w_uq", "mla_w_uk", "mla_w_uv", "lru_wa", "lru_wx", "proj_rwkv", "proj_mla", "proj_lru", "w_out"]


def setup(nc, SEQ, L, shapes, ext_in=(), ext_out=()):
    g = G()
    g.P = Prog(nc)
    g.SEQ = SEQ
    g.T = CTX + SEQ
    g.R = SEQ // GW
    g.L = L
    g.d = {}
    P = g.P
    for nm, shp in shapes.items():
        g.d[nm] = P.dram(nm, shp, F32, "ExternalInput")
    g.ps = [P.psum("ps%d" % i, [128, 512], F32) for i in range(8)]
    g.ext_in = set(ext_in)
    g.ext_out = set(ext_out)
    g.wc = {}
    return g


def scratch(g, name, shape, dtype=F32):
    if name in g.d:
        return g.d[name]
    kind = "Internal"
    if name in g.ext_in:
        kind = "ExternalInput"
    if name in g.ext_out:
        kind = "ExternalOutput"
    g.d[name] = g.P.dram(name, shape, dtype, kind)
    return g.d[name]


def dbg(g, name, tile, ap, shape):
    if name in g.ext_out:
        d = scratch(g, name, shape)
        g.P.dma(d[:], ap, reads=[tile], writes=[d])


def load_const(g, name, shape, src=None, q="sp"):
    P = g.P
    t = P.sbuf(name, shape, F32)
    src = src if src is not None else g.d[name][:]
    P.dma(t[:], src, reads=[g.d[name]], writes=[t], q=q)
    return t


def load_consts(g, which):
    P = g.P
    g.c = {}
    shp = {"ident": [128, 128], "ones": [128, 128], "blk": [128, 128], "Jm": [64, 64]}
    for nm in which:
        if nm in shp:
            g.c[nm] = load_const(g, nm, shp[nm])
    if "maskA" in which:
        g.c["maskA"] = [load_const(g, "maskA", [128, 192], g.d["maskA"][d]) for d in range(2)]
        g.c["maskXY"] = [load_const(g, "maskXY", [128, 256], g.d["maskXY"][d]) for d in range(2)]
        g.c["rm"] = [load_const(g, "rm", [128, 1024], g.d["rm"][d]) for d in range(2)]


def load_small(g, names):
    g.s = {}
    for nm in names:
        shp = list(g.d[nm].t.shape)
        g.s[nm] = load_const(g, nm, shp)


def wc_jobs(g, name, idx, K, col_list):
    W = g.d[name]
    KCn = K // 128
    sc = g.P.dram("WC_%s_%s" % (name, "_".join(map(str, idx))), [len(col_list), 128, KCn * 128], BF16)
    jobs = []
    for t, (c0, w) in enumerate(col_list):
        trk = Tile(sc.t, sc.name + "_%d" % t, "dram")
        dst = sc.t[t].rearrange("p (kc n) -> p kc n", n=128)[:, :, :w]
        src = W[tuple(idx) + (slice(None), slice(c0, c0 + w))].rearrange("(kc p) n -> p kc n", p=128)
        g.wc[(name,) + tuple(idx) + (c0,)] = (trk, dst)
        jobs.append((dst, src, W, trk))
    return jobs


def wc_emit(g, jobs):
    for (dst, src, W, trk) in jobs:
        g.P.dma_dd(dst, src, reads=[W], writes=[trk])


def wload(g, dst_ap, dst_tile, name, idx, c0):
    trk, ap = g.wc[(name,) + tuple(idx) + (c0,)]
    g.P.dma(dst_ap, ap, reads=[trk], writes=[dst_tile], q="sp")


def precast_layer(g, l):
    cols128 = lambda n: [(i * 128, 128) for i in range(n // 128)]
    for j in range(2):
        gj = wc_jobs(g, "ffn_w_gate", (l, j), D, cols128(DFF))
        uj = wc_jobs(g, "ffn_w_up", (l, j), D, cols128(DFF))
        dj = wc_jobs(g, "ffn_w_down", (l, j), DFF, cols128(D))
        ffn = [x for pair in zip(gj, uj) for x in pair] + dj
        if j == 0:
            wc_emit(g, ffn)
            wc_emit(g, wc_jobs(g, "w_in", (l,), D, list(SEGS)))
            pr = wc_jobs(g, "proj_rwkv", (l,), RW_W, cols128(D))
            pm = wc_jobs(g, "proj_mla", (l,), 2048, cols128(D))
            pl = wc_jobs(g, "proj_lru", (l,), LRU_W, cols128(D))
            wc_emit(g, [x for tr_ in zip(pr, pm, pl) for x in tr_])
            wc_emit(g, wc_jobs(g, "w_out", (l,), D, cols128(D)))
        else:
            wc_emit(g, ffn)


def stage_mod(g, l):
    P = g.P
    pm = g.ps[0]
    with P.scope():
        wts = [P.sbuf("adw%d" % i, [128, KC, 512], F32) for i in range(2)]
        aw = g.d["ada_w"]
        for grp in range(36):
            wt = wts[grp % 2]
            P.dma(wt[:], aw[l, :, grp * 512:(grp + 1) * 512].rearrange("(kc p) n -> p kc n", p=128),
                  reads=[aw], writes=[wt], q="sp" if grp % 2 == 0 else "act")
            for jj in range(4):
                j = grp * 4 + jj
                for kc in range(KC):
                    P.op("pe", "matmul", pm[:, 2 * j:2 * j + 2], lhsT=wt[:, kc, jj * 128:(jj + 1) * 128], rhs=g.cs[:, kc, :],
                         start=(kc == 0), stop=(kc == KC - 1), reads=[wt, g.cs], writes=[pm])
        adab = g.s["adab"]
        P.op("dve", "tensor_tensor", out=g.mod[:], in0=pm[:, 0:288].rearrange("p (j m) -> p j m", m=2),
             in1=adab[:, l, :].unsqueeze(2).to_broadcast([128, 144, 2]), op=ALU.add, reads=[pm, adab], writes=[g.mod])
        P.op("dve", "tensor_scalar", out=g.onep[:], in0=g.mod[:], scalar1=1.0, scalar2=None, op0=ALU.add, reads=[g.mod], writes=[g.onep])
        P.op("dve", "tensor_scalar", out=g.hg[:], in0=g.mod[:], scalar1=0.5, scalar2=None, op0=ALU.mult, reads=[g.mod], writes=[g.hg])


def init_mod(g):
    P = g.P
    g.cs = P.sbuf("cs", [128, KC, 2], F32)
    P.dma(g.cs[:], g.d["cvec"][:], reads=[g.d["cvec"]], writes=[g.cs])
    P.op("act", "activation", out=g.cs[:], in_=g.cs[:], func=AF.Silu, reads=[g.cs], writes=[g.cs])
    g.mod = P.sbuf("mod", [128, 144, 2], F32)
    g.onep = P.sbuf("onep", [128, 144, 2], F32)
    g.hg = P.sbuf("hg", [128, 144, 2], F32)


def rsqrt_(P, out, in_, eps, tiles, scale=1.0):
    P.op("act", "activation", out=out, in_=in_, func=AF.Sqrt, bias=eps, scale=scale, reads=tiles, writes=tiles)
    P.op("dve", "reciprocal", out=out, in_=out, reads=tiles, writes=tiles)


def layer_norm(g, z, NT, l, idx):
    P = g.P
    s1 = g.ps[6]
    s2 = g.ps[7]
    ones = g.c["ones"]
    with P.scope():
        sq = [P.sbuf("lnsq%d" % i, [128, NT], F32) for i in range(2)]
        mean = P.sbuf("lnmean", [128, NT], F32)
        rstd = P.sbuf("lnrstd", [128, NT], F32)
        for dc in range(KC):
            P.op("pe", "matmul", s1[:, :NT], lhsT=ones[:], rhs=z[:, dc, :], start=(dc == 0), stop=(dc == KC - 1), reads=[ones, z], writes=[s1])
            q = sq[dc % 2]
            P.op("act", "activation", out=q[:], in_=z[:, dc, :], func=AF.Square, reads=[z], writes=[q])
            P.op("pe", "matmul", s2[:, :NT], lhsT=ones[:], rhs=q[:], start=(dc == 0), stop=(dc == KC - 1), reads=[ones, q], writes=[s2])
        P.op("act", "mul", out=mean[:], in_=s1[:, :NT], mul=1.0 / D, reads=[s1], writes=[mean])
        P.op("dve", "tensor_tensor", out=rstd[:], in0=mean[:], in1=mean[:], op=ALU.mult, reads=[mean], writes=[rstd])
        P.op("dve", "scalar_tensor_tensor", out=rstd[:], in0=s2[:, :NT], scalar=1.0 / D, in1=rstd[:], op0=ALU.mult, op1=ALU.subtract, reads=[s2, rstd], writes=[rstd])
        rsqrt_(P, rstd[:], rstd[:], LN_EPS, [rstd])
        lng = g.s["lng"]
        lnb = g.s["lnb"]
        for dc in range(KC):
            P.op("dve", "tensor_tensor", out=z[:, dc, :], in0=z[:, dc, :], in1=mean[:], op=ALU.subtract, reads=[z, mean], writes=[z])
            P.op("dve", "tensor_tensor", out=z[:, dc, :], in0=z[:, dc, :], in1=rstd[:], op=ALU.mult, reads=[z, rstd], writes=[z])
            P.op("dve", "tensor_scalar", out=z[:, dc, :], in0=z[:, dc, :], scalar1=lng[:, l, idx, dc:dc + 1], scalar2=lnb[:, l, idx, dc:dc + 1],
                 op0=ALU.mult, op1=ALU.add, reads=[z, lng, lnb], writes=[z])


def modulate(g, out, s, NT, m, sc_slot, sh_slot):
    P = g.P
    for kc in range(KC):
        P.op("dve", "tensor_scalar", out=out[:, kc, :], in0=s[:, kc, :], scalar1=g.onep[:, sc_slot * 16 + kc, m:m + 1],
             scalar2=g.mod[:, sh_slot * 16 + kc, m:m + 1], op0=ALU.mult, op1=ALU.add, reads=[s, g.onep, g.mod], writes=[out])


def ffn_block(g, s, NT, m, l, j, slots, ln_idx):
    P = g.P
    sh, sc, gt = slots
    wgd, wud, wdd = g.d["ffn_w_gate"], g.d["ffn_w_up"], g.d["ffn_w_down"]
    with P.scope():
        AT = P.sbuf("ffa", [128, FC, NT], BF16)
        with P.scope():
            h = P.sbuf("ffh", [128, KC, NT], BF16)
            modulate(g, h, s, NT, m, sc, sh)
            P.op("act", "mul", out=s[:], in_=s[:], mul=ALPHA, reads=[s], writes=[s])
            wgs = [P.sbuf("wg%d" % i, [128, KC, 128], BF16) for i in range(3)]
            wus = [P.sbuf("wu%d" % i, [128, KC, 128], BF16) for i in range(3)]
            sgs = [P.sbuf("sg%d" % i, [128, NT], F32) for i in range(2)]
            for fc in range(FC):
                wg = wgs[fc % 3]
                wu = wus[fc % 3]
                wload(g, wg[:], wg, "ffn_w_gate", (l, j), fc * 128)
                wload(g, wu[:], wu, "ffn_w_up", (l, j), fc * 128)
                pg = g.ps[(2 * fc) % 4]
                pu = g.ps[(2 * fc + 1) % 4]
                for kc in range(KC):
                    P.op("pe", "matmul", pg[:, :NT], lhsT=wg[:, kc, :], rhs=h[:, kc, :], start=(kc == 0), stop=(kc == KC - 1), reads=[wg, h], writes=[pg])
                for kc in range(KC):
                    P.op("pe", "matmul", pu[:, :NT], lhsT=wu[:, kc, :], rhs=h[:, kc, :], start=(kc == 0), stop=(kc == KC - 1), reads=[wu, h], writes=[pu])
                sg = sgs[fc % 2]
                P.op("act", "activation", out=sg[:], in_=pg[:, :NT], func=AF.Silu, reads=[pg], writes=[sg])
                P.op("dve", "tensor_tensor", out=AT[:, fc, :], in0=sg[:], in1=pu[:, :NT], op=ALU.mult, reads=[sg, pu], writes=[AT])
        with P.scope():
            wds = [P.sbuf("wd%d" % i, [128, FC, 128], BF16) for i in range(2)]
            for dc in range(KC):
                wd = wds[dc % 2]
                wload(g, wd[:], wd, "ffn_w_down", (l, j), dc * 128)
                py = g.ps[4 + dc % 2]
                for fc in range(FC):
                    P.op("pe", "matmul", py[:, :NT], lhsT=wd[:, fc, :], rhs=AT[:, fc, :], start=(fc == 0), stop=(fc == FC - 1), reads=[wd, AT], writes=[py])
                P.op("dve", "scalar_tensor_tensor", out=s[:, dc, :], in0=py[:, :NT], scalar=g.hg[:, gt * 16 + dc, m:m + 1], in1=s[:, dc, :],
                     op0=ALU.mult, op1=ALU.add, reads=[py, s, g.hg], writes=[s])
    layer_norm(g, s, NT, l, ln_idx)


def token_tiles(g, with_ctx=True):
    tl = []
    if with_ctx:
        tl.append((0, CTX, 1))
    t0 = CTX
    while t0 < g.T:
        nt = min(512, g.T - t0)
        tl.append((t0, nt, 0))
        t0 += nt
    return tl


def fview(d):
    return d[:].rearrange("(kc p) t -> p kc t", p=128)


def stage_a(g, l, src, do_ffn=True):
    P = g.P
    XL = scratch(g, "XL", [D, g.T])
    F = [scratch(g, "F%d" % i, [128, g.T]) for i in range(NSEG)]
    win = g.d["w_in"]
    gateb = g.s["gateb"]
    for (t0, NT, m) in token_tiles(g):
        with P.scope():
            s = P.sbuf("s", [128, KC, NT], F32)
            P.dma(s[:], fview(src)[:, :, t0:t0 + NT], reads=[src], writes=[s])
            if do_ffn:
                ffn_block(g, s, NT, m, l, 0, (0, 1, 2), 0)
            P.dma(fview(XL)[:, :, t0:t0 + NT], s[:], reads=[s], writes=[XL], q="act")
            with P.scope():
                hl = P.sbuf("hl", [128, KC, NT], BF16)
                modulate(g, hl, s, NT, m, 4, 3)
                wts = [P.sbuf("wi%d" % i, [128, KC, 128], BF16) for i in range(3)]
                stg = [P.sbuf("stg%d" % i, [128, NT], F32) for i in range(3)]
                for si, (c0, w) in enumerate(SEGS):
                    wt = wts[si % 3]
                    wload(g, wt[:, :, :w], wt, "w_in", (l,), c0)
                    pp = g.ps[si % 4]
                    for kc in range(KC):
                        P.op("pe", "matmul", pp[:w, :NT], lhsT=wt[:, kc, :w], rhs=hl[:, kc, :], start=(kc == 0), stop=(kc == KC - 1), reads=[wt, hl], writes=[pp])
                    st = stg[si % 3]
                    if si >= S_GATE:
                        P.op("act", "activation", out=st[:w, :], in_=pp[:w, :NT], func=AF.Sigmoid, bias=gateb[:, l, si - S_GATE:si - S_GATE + 1],
                             reads=[pp, gateb], writes=[st])
                        sq_ = "act"
                    elif si % 2 == 0:
                        P.op("act", "copy", out=st[:w, :], in_=pp[:w, :NT], reads=[pp], writes=[st])
                        sq_ = "act"
                    else:
                        P.op("dve", "tensor_copy", out=st[:w, :], in_=pp[:w, :NT], reads=[pp], writes=[st])
                        sq_ = "act"
                    P.dma(F[si][:w, t0:t0 + NT], st[:w, :], reads=[st], writes=[F[si]], q=sq_)


def init_derived(g):
    P = g.P
    s = g.s
    for nm, src in (("omm", "mu"), ("omka", "k_a")):
        t = P.sbuf(nm, list(g.d[src].t.shape), F32)
        P.op("dve", "tensor_scalar", out=t[:], in0=s[src][:], scalar1=-1.0, scalar2=1.0, op0=ALU.mult, op1=ALU.add, reads=[s[src]], writes=[t])
        s[nm] = t
    s["clam"] = make_clam(g, s["lam"], list(g.d["lam"].t.shape))
    return
    s["clam"] = t


def make_clam(g, lam, shape):
    P = g.P
    e = P.sbuf("clam_e", shape, F32)
    t = P.sbuf("clam", shape, F32)
    P.op("act", "activation", out=e[:], in_=lam[:], func=AF.Exp, scale=-1.0, reads=[lam], writes=[e])
    nt = 12
    P.op("dve", "tensor_scalar", out=t[:], in0=e[:], scalar1=((-1.0) ** (nt + 1)) / nt, scalar2=((-1.0) ** nt) / (nt - 1), op0=ALU.mult, op1=ALU.add, reads=[e], writes=[t])
    for k in range(nt - 2, 0, -1):
        P.op("dve", "tensor_tensor", out=t[:], in0=t[:], in1=e[:], op=ALU.mult, reads=[t, e], writes=[t])
        P.op("dve", "tensor_scalar", out=t[:], in0=t[:], scalar1=((-1.0) ** (k + 1)) / k, scalar2=None, op0=ALU.add, reads=[t], writes=[t])
    P.op("dve", "scalar_tensor_tensor", out=t[:], in0=t[:], scalar=-8.0, in1=e[:], op0=ALU.mult, op1=ALU.mult, reads=[t, e], writes=[t])
    return t


def blocks_of(g):
    bl = [(0, CTX)]
    t0 = CTX
    while t0 < g.T:
        n = min(512, g.T - t0)
        bl.append((t0, n))
        t0 += n
    return bl


def stage_lru(g, l):
    P = g.P
    T = g.T
    F = [scratch(g, "F%d" % i, [128, T]) for i in range(NSEG)]
    YL = scratch(g, "YL", [LRU_W, T], BF16)
    cw, cb, ba, bx, clam = g.s["convw"], g.s["convb"], g.s["ba"], g.s["bx"], g.s["clam"]
    wad, wxd = g.d["lru_wa"], g.d["lru_wx"]
    for n in range(8):
        with P.scope():
            xr = P.sbuf("lxr", [128, T], F32)
            gb = P.sbuf("lgb", [128, T], F32)
            xc = P.sbuf("lxc", [128, T], F32)
            hs = P.sbuf("lhs", [128, T], F32)
            gr = P.sbuf("lgr", [128, T], F32)
            gi = P.sbuf("lgi", [128, T], F32)
            tmp = P.sbuf("ltmp", [128, T], F32)
            yb = P.sbuf("lyb", [128, T], BF16)
            P.dma(xr[:], F[S_XR + n][:, :], reads=[F[S_XR + n]], writes=[xr])
            P.dma(gb[:], F[S_GB + n][:, :], reads=[F[S_GB + n]], writes=[gb], q="act")
            for (a, b) in ((0, CTX), (CTX, T)):
                P.op("dve", "tensor_scalar", out=xc[:, a:b], in0=xr[:, a:b], scalar1=cw[:, l, 2, n:n + 1], scalar2=cb[:, l, n:n + 1],
                     op0=ALU.mult, op1=ALU.add, reads=[xr, cw, cb], writes=[xc])
                for (tap, off) in ((0, -2), (1, -1), (3, 1)):
                    if off < 0:
                        o_, i_ = xc[:, a - off:b], xr[:, a:b + off]
                    else:
                        o_, i_ = xc[:, a:b - off], xr[:, a + off:b]
                    P.op("dve", "scalar_tensor_tensor", out=o_, in0=i_, scalar=cw[:, l, tap, n:n + 1], in1=o_, op0=ALU.mult, op1=ALU.add,
                         reads=[xr, xc, cw], writes=[xc])
            if n == 0:
                dbg(g, "dbg_xc", xc, xc[:], [128, T])
            for d in range(2):
                wa = P.sbuf("lwa%d" % d, [128, 128], F32)
                wx = P.sbuf("lwx%d" % d, [128, 128], F32)
                P.dma(wa[:], wad[l, d, n], reads=[wad], writes=[wa])
                P.dma(wx[:], wxd[l, d, n], reads=[wxd], writes=[wx], q="act")
                for bi, (t0, nb) in enumerate(blocks_of(g)):
                    p1 = g.ps[(2 * bi) % 4]
                    p2 = g.ps[(2 * bi + 1) % 4]
                    P.op("pe", "matmul", p1[:, :nb], lhsT=wa[:], rhs=xc[:, t0:t0 + nb], start=True, stop=True, reads=[wa, xc], writes=[p1])
                    P.op("pe", "matmul", p2[:, :nb], lhsT=wx[:], rhs=xc[:, t0:t0 + nb], start=True, stop=True, reads=[wx, xc], writes=[p2])
                    P.op("act", "activation", out=gr[:, t0:t0 + nb], in_=p1[:, :nb], func=AF.Sigmoid, bias=ba[:, l, d, n:n + 1], reads=[p1, ba], writes=[gr])
                    P.op("act", "activation", out=gi[:, t0:t0 + nb], in_=p2[:, :nb], func=AF.Sigmoid, bias=bx[:, l, d, n:n + 1], reads=[p2, bx], writes=[gi])
                P.op("dve", "tensor_scalar", out=gr[:], in0=gr[:], scalar1=clam[:, l, d, n:n + 1], scalar2=0.25, op0=ALU.mult, op1=ALU.mult, reads=[gr, clam], writes=[gr])
                P.op("dve", "tensor_scalar", out=tmp[:], in0=gr[:], scalar1=1.0 / 7, scalar2=1.0, op0=ALU.mult, op1=ALU.add, reads=[gr], writes=[tmp])
                for kk_ in (6, 5, 4, 3, 2):
                    P.op("dve", "tensor_tensor", out=tmp[:], in0=tmp[:], in1=gr[:], op=ALU.mult, reads=[tmp, gr], writes=[tmp])
                    P.op("dve", "tensor_scalar", out=tmp[:], in0=tmp[:], scalar1=1.0 / kk_, scalar2=1.0, op0=ALU.mult, op1=ALU.add, reads=[tmp], writes=[tmp])
                P.op("dve", "scalar_tensor_tensor", out=tmp[:], in0=tmp[:], scalar=-1.0, in1=gr[:], op0=ALU.mult, op1=ALU.mult, reads=[tmp, gr], writes=[tmp])
                P.op("dve", "tensor_scalar", out=gr[:], in0=tmp[:], scalar1=-1.0, scalar2=1.0, op0=ALU.mult, op1=ALU.add, reads=[tmp], writes=[gr])
                P.op("dve", "tensor_tensor", out=gr[:], in0=gr[:], in1=gr[:], op=ALU.mult, reads=[gr], writes=[gr])
                P.op("dve", "scalar_tensor_tensor", out=xr[:], in0=tmp[:], scalar=-1.0, in1=tmp[:], op0=ALU.mult, op1=ALU.mult, reads=[tmp], writes=[xr])
                P.op("dve", "scalar_tensor_tensor", out=tmp[:], in0=tmp[:], scalar=2.0, in1=xr[:], op0=ALU.mult, op1=ALU.add, reads=[tmp, xr], writes=[tmp])
                P.op("dve", "scalar_tensor_tensor", out=tmp[:], in0=gr[:], scalar=1.0, in1=tmp[:], op0=ALU.add, op1=ALU.mult, reads=[gr, tmp], writes=[tmp])
                P.op("dve", "tensor_tensor", out=gr[:], in0=gr[:], in1=gr[:], op=ALU.mult, reads=[gr], writes=[gr])
                P.op("dve", "scalar_tensor_tensor", out=tmp[:], in0=gr[:], scalar=1.0, in1=tmp[:], op0=ALU.add, op1=ALU.mult, reads=[gr, tmp], writes=[tmp])
                P.op("act", "activation", out=tmp[:], in_=tmp[:], func=AF.Sqrt, reads=[tmp], writes=[tmp])
                P.op("dve", "tensor_tensor", out=tmp[:], in0=tmp[:], in1=gi[:], op=ALU.mult, reads=[tmp, gi], writes=[tmp])
                P.op("dve", "tensor_tensor", out=tmp[:], in0=tmp[:], in1=xc[:], op=ALU.mult, reads=[tmp, xc], writes=[tmp])
                if n == 0:
                    dbg(g, "dbg_a%d" % d, gr, gr[:], [128, T])
                    dbg(g, "dbg_b%d" % d, tmp, tmp[:], [128, T])
                hd = hs if d == 0 else gi
                if d == 0:
                    P.op("dve", "tensor_tensor_scan", out=hd[:, :], data0=gr[:, :], data1=tmp[:, :], initial=0.0, op0=ALU.mult, op1=ALU.add,
                         reads=[gr, tmp], writes=[hd])
                else:
                    P.op("dve", "tensor_tensor_scan", out=hd[:, 0:CTX][:, ::-1], data0=gr[:, 0:CTX][:, ::-1], data1=tmp[:, 0:CTX][:, ::-1], initial=0.0,
                         op0=ALU.mult, op1=ALU.add, reads=[gr, tmp], writes=[hd])
                    P.op("dve", "scalar_tensor_tensor", out=tmp[:, T - 1:T], in0=gr[:, T - 1:T], scalar=hd[:, 0:1], in1=tmp[:, T - 1:T], op0=ALU.mult, op1=ALU.add,
                         reads=[gr, hd, tmp], writes=[tmp])
                    P.op("dve", "tensor_tensor_scan", out=hd[:, CTX:T][:, ::-1], data0=gr[:, CTX:T][:, ::-1], data1=tmp[:, CTX:T][:, ::-1], initial=0.0,
                         op0=ALU.mult, op1=ALU.add, reads=[gr, tmp, hd], writes=[hd])
                    P.op("dve", "tensor_tensor", out=hs[:], in0=hs[:], in1=hd[:], op=ALU.add, reads=[hs, hd], writes=[hs])
            if n == 0:
                dbg(g, "dbg_hs", hs, hs[:], [128, T])
            P.op("dve", "tensor_tensor", out=tmp[:], in0=gb[:], in1=gb[:], op=ALU.mult, reads=[gb], writes=[tmp])
            P.op("dve", "tensor_scalar", out=tmp[:], in0=tmp[:], scalar1=0.044715, scalar2=1.0, op0=ALU.mult, op1=ALU.add, reads=[tmp], writes=[tmp])
            P.op("dve", "tensor_tensor", out=tmp[:], in0=tmp[:], in1=gb[:], op=ALU.mult, reads=[tmp, gb], writes=[tmp])
            P.op("act", "activation", out=tmp[:], in_=tmp[:], func=AF.Sigmoid, scale=1.5957691216057308, reads=[tmp], writes=[tmp])
            P.op("dve", "tensor_tensor", out=tmp[:], in0=tmp[:], in1=gb[:], op=ALU.mult, reads=[tmp, gb], writes=[tmp])
            P.op("dve", "tensor_tensor", out=yb[:], in0=tmp[:], in1=hs[:], op=ALU.mult, reads=[tmp, hs], writes=[yb])
            P.dma(YL[n * 128:(n + 1) * 128, :], yb[:], reads=[yb], writes=[YL], q="act")


def stage_rwkv_mix(g, l):
    P = g.P
    T = g.T
    R = g.R
    F = [scratch(g, "F%d" % i, [128, T]) for i in range(NSEG)]
    omm, mud = g.s["omm"], g.s["mud"]
    with P.scope():
        fs = [P.sbuf("mxf%d" % i, [128, T], F32) for i in range(2)]
        os_ = [P.sbuf("mxo%d" % i, [128, T], F32) for i in range(2)]
        for si in range(NRW):
            c0, w = SEGS[si]
            f = fs[si % 2]
            o = os_[si % 2]
            P.dma(f[:w, :], F[si][:w, :], reads=[F[si]], writes=[f], q="sp" if si % 2 == 0 else "act")
            P.op("dve", "tensor_scalar", out=o[:w, :], in0=f[:w, :], scalar1=omm[:w, l, si:si + 1], scalar2=None, op0=ALU.mult, reads=[f, omm], writes=[o])
            fl = f[:w, CTX:T].rearrange("p (r c) -> p r c", c=GW)
            ol = o[:w, CTX:T].rearrange("p (r c) -> p r c", c=GW)
            for q in seg_dirs(si):
                sc = mud[:w, l, q, si:si + 1]
                if q == 0:
                    pairs = [(ol[:, :, 1:GW], fl[:, :, 0:GW - 1]), (o[:w, 1:CTX], f[:w, 0:CTX - 1])]
                elif q == 1:
                    pairs = [(ol[:, :, 0:GW - 1], fl[:, :, 1:GW]), (o[:w, 0:CTX - 1], f[:w, 1:CTX])]
                elif q == 2:
                    pairs = [(ol[:, 1:R, :], fl[:, 0:R - 1, :]), (o[:w, 1:CTX], f[:w, 0:CTX - 1])]
                else:
                    pairs = [(ol[:, 0:R - 1, :], fl[:, 1:R, :]), (o[:w, 0:CTX - 1], f[:w, 1:CTX])]
                for (oo, ii) in pairs:
                    P.op("dve", "scalar_tensor_tensor", out=oo, in0=ii, scalar=sc, in1=oo, op0=ALU.mult, op1=ALU.add, reads=[f, o, mud], writes=[o])
            P.dma(F[si][:w, :], o[:w, :], reads=[o], writes=[F[si]], q="act")


def rwkv_stream(g, l, hp, d, bankA, bankB, blocks, tr, YS):
    P = g.P
    F = [scratch(g, "F%d" % i, [128, g.T]) for i in range(NSEG)]
    ident, blk = g.c["ident"], g.c["blk"]
    maskA, maskXY, rm = g.c["maskA"][d], g.c["maskXY"][d], g.c["rm"][d]
    s = g.s
    pA = pTM = pW0 = pSP = bankA
    pXY = pR = pU = pY = bankB
    allA = [bankA]
    A_, TM_, W0_, SP_ = bankA[:, 0:192], bankA[:, 192:384], bankA[:, 384:448], bankA[:, 448:512]
    XY_, R_, U_, Y_ = bankB[:, 0:256], bankB[:, 256:384], bankB[:, 384:448], bankB[:, 448:512]
    P.op("dve", "memset", bankB[:, 0:256], 0.0, reads=[], writes=[pXY])
    nm = "_%d_%d" % (hp, d)
    ST = P.sbuf("ST" + nm, [128, 64], F32)
    P.op("dve", "memset", ST[:], 0.0, reads=[], writes=[ST])
    wup = P.sbuf("wup" + nm, [64, 128], F32)
    aup = P.sbuf("aup" + nm, [64, 128], F32)
    P.dma(wup[:], g.d["rwkv_w_up"][l, d, :, hp * 128:(hp + 1) * 128], reads=[g.d["rwkv_w_up"]], writes=[wup])
    P.dma(aup[:], g.d["rwkv_a_up"][l, d, :, hp * 128:(hp + 1) * 128], reads=[g.d["rwkv_a_up"]], writes=[aup])
    LM = 512
    rt, al, be, kt, vv, Ep, Yb = [P.sbuf(n_ + nm, [128, LM], F32) for n_ in ("rt", "al", "be", "kt", "vv", "Ep", "Yb")]
    XYs = [P.sbuf("XYs%d" % i + nm, [128, 256], F32) for i in range(2)]
    Rs = [P.sbuf("Rs%d" % i + nm, [128, 128], F32) for i in range(2)]
    Am = P.sbuf("Am" + nm, [128, 192], F32)
    TMs = P.sbuf("TMs" + nm, [128, 192], F32)
    W0s = P.sbuf("W0s" + nm, [128, 64], F32)
    Us = P.sbuf("Us" + nm, [128, 64], F32)
    r_, k_, twd, adt, sig, Lc, t1, t2, a_, kk_, t3 = tr
    w0, a0, kkk, ka, omka = s["w0"], s["a0"], s["kk_k"], s["k_a"], s["omka"]
    H2 = (slice(0, 64), slice(64, 128))
    for (t0, Lb) in blocks:
        ts_ = slice(t0, t0 + Lb)
        P.dma(r_[:, :Lb], F[S_R + hp][:, ts_], reads=[F[S_R + hp]], writes=[r_])
        P.dma(k_[:, :Lb], F[S_K + hp][:, ts_], reads=[F[S_K + hp]], writes=[k_], q="act")
        P.dma(vv[:, :Lb], F[S_V + hp][:, ts_], reads=[F[S_V + hp]], writes=[vv])
        P.dma(twd[0:64, :Lb], F[S_WD][0:64, ts_], reads=[F[S_WD]], writes=[twd], q="act")
        P.dma(adt[0:64, :Lb], F[S_AD][0:64, ts_], reads=[F[S_AD]], writes=[adt])
        P.op("act", "activation", out=twd[0:64, :Lb], in_=twd[0:64, :Lb], func=AF.Tanh, reads=[twd], writes=[twd])
        P.op("pe", "matmul", bankA[:, :Lb], lhsT=wup[:], rhs=twd[0:64, :Lb], start=True, stop=True, reads=[wup, twd], writes=allA)
        P.op("act", "activation", out=sig[:, :Lb], in_=bankA[:, :Lb], func=AF.Sigmoid, bias=w0[:, l, d, hp:hp + 1], reads=allA + [w0], writes=[sig])
        P.op("pe", "matmul", bankA[:, :Lb], lhsT=aup[:], rhs=adt[0:64, :Lb], start=True, stop=True, reads=[aup, adt] + allA, writes=allA)
        P.op("act", "activation", out=a_[:, :Lb], in_=bankA[:, :Lb], func=AF.Sigmoid, bias=a0[:, l, d, hp:hp + 1], reads=allA + [a0], writes=[a_])
        if d == 0:
            P.op("dve", "tensor_tensor_scan", out=Lc[:, :Lb], data0=rm[:, :Lb], data1=sig[:, :Lb], initial=0.0, op0=ALU.mult, op1=ALU.add,
                 reads=[rm, sig], writes=[Lc])
        else:
            P.op("dve", "tensor_tensor_scan", out=Lc[:, :Lb][:, ::-1], data0=rm[:, :Lb][:, ::-1], data1=sig[:, :Lb][:, ::-1], initial=0.0,
                 op0=ALU.mult, op1=ALU.add, reads=[rm, sig], writes=[Lc])
        P.op("act", "activation", out=Ep[:, :Lb], in_=Lc[:, :Lb], func=AF.Exp, scale=-C0, reads=[Lc], writes=[Ep])
        P.op("act", "activation", out=t1[:, :Lb], in_=Lc[:, :Lb], func=AF.Exp, scale=C0, reads=[Lc], writes=[t1])
        P.op("dve", "tensor_tensor", out=t2[:, :Lb], in0=Lc[:, :Lb], in1=sig[:, :Lb], op=ALU.subtract, reads=[Lc, sig], writes=[t2])
        P.op("act", "activation", out=t2[:, :Lb], in_=t2[:, :Lb], func=AF.Exp, scale=-C0, reads=[t2], writes=[t2])
        P.op("dve", "tensor_scalar", out=kk_[:, :Lb], in0=k_[:, :Lb], scalar1=kkk[:, l, hp:hp + 1], scalar2=None, op0=ALU.mult, reads=[k_, kkk], writes=[kk_])
        P.op("dve", "tensor_tensor", out=t3[:, :Lb], in0=kk_[:, :Lb], in1=kk_[:, :Lb], op=ALU.mult, reads=[kk_], writes=[t3])
        P.op("pe", "matmul", bankA[:, :Lb], lhsT=blk[:], rhs=t3[:, :Lb], start=True, stop=True, reads=[blk, t3] + allA, writes=allA)
        P.op("act", "activation", out=t3[:, :Lb], in_=bankA[:, :Lb], func=AF.Sqrt, reads=allA, writes=[t3])
        P.op("dve", "tensor_scalar", out=t3[:, :Lb], in0=t3[:, :Lb], scalar1=1e-12, scalar2=None, op0=ALU.max, reads=[t3], writes=[t3])
        P.op("dve", "reciprocal", out=t3[:, :Lb], in_=t3[:, :Lb], reads=[t3], writes=[t3])
        P.op("dve", "tensor_tensor", out=kk_[:, :Lb], in0=kk_[:, :Lb], in1=t3[:, :Lb], op=ALU.mult, reads=[kk_, t3], writes=[kk_])
        P.op("dve", "tensor_scalar", out=t3[:, :Lb], in0=a_[:, :Lb], scalar1=ka[:, l, hp:hp + 1], scalar2=omka[:, l, hp:hp + 1], op0=ALU.mult, op1=ALU.add,
             reads=[a_, ka, omka], writes=[t3])
        P.op("dve", "tensor_tensor", out=t3[:, :Lb], in0=t3[:, :Lb], in1=k_[:, :Lb], op=ALU.mult, reads=[t3, k_], writes=[t3])
        P.op("dve", "tensor_tensor", out=kt[:, :Lb], in0=t3[:, :Lb], in1=t1[:, :Lb], op=ALU.mult, reads=[t3, t1], writes=[kt])
        P.op("dve", "tensor_tensor", out=rt[:, :Lb], in0=r_[:, :Lb], in1=Ep[:, :Lb], op=ALU.mult, reads=[r_, Ep], writes=[rt])
        P.op("dve", "scalar_tensor_tensor", out=al[:, :Lb], in0=kk_[:, :Lb], scalar=-1.0, in1=t2[:, :Lb], op0=ALU.mult, op1=ALU.mult, reads=[kk_, t2], writes=[al])
        P.op("dve", "tensor_tensor", out=be[:, :Lb], in0=kk_[:, :Lb], in1=a_[:, :Lb], op=ALU.mult, reads=[kk_, a_], writes=[be])
        P.op("dve", "tensor_tensor", out=be[:, :Lb], in0=be[:, :Lb], in1=t1[:, :Lb], op=ALU.mult, reads=[be, t1], writes=[be])
        if hp == 0 and d == 1 and t0 == CTX:
            for nm_, tl_ in (("dbg_sig", sig), ("dbg_L", Lc), ("dbg_Ep", Ep), ("dbg_Em", t1), ("dbg_rt", rt), ("dbg_al", al), ("dbg_be", be), ("dbg_kt", kt), ("dbg_kk", kk_)):
                dbg(g, nm_, tl_, tl_[:, :Lb], [128, Lb])
        yield
        nch = Lb // CH
        for c in (range(nch) if d == 0 else range(nch - 1, -1, -1)):
            cs = slice(c * CH, (c + 1) * CH)
            for h in range(2):
                hs = H2[h]
                for j, src in enumerate((be, kt, vv)):
                    P.op("pe", "matmul", TM_[hs, 64 * j:64 * j + 64], lhsT=src[hs, cs], rhs=ident[hs, hs], start=True, stop=True, reads=[src, ident], writes=[pTM])
            P.op("act", "copy", out=TMs[:], in_=TM_, reads=[pTM], writes=[TMs])
            for h in range(2):
                hs = H2[h]
                P.op("pe", "matmul", XY_[hs, 64 * h:64 * h + 64], lhsT=be[hs, cs], rhs=al[hs, cs], start=True, stop=True, reads=[be, al], writes=[pXY])
                P.op("pe", "matmul", XY_[hs, 128 + 64 * h:192 + 64 * h], lhsT=al[hs, cs], rhs=be[hs, cs], start=True, stop=True, reads=[be, al], writes=[pXY])
                P.op("pe", "matmul", A_[hs, 0:64], lhsT=be[hs, cs], rhs=rt[hs, cs], start=True, stop=True, reads=[be, rt], writes=[pA])
                P.op("pe", "matmul", A_[hs, 64:128], lhsT=kt[hs, cs], rhs=al[hs, cs], start=True, stop=True, reads=[kt, al], writes=[pA])
                P.op("pe", "matmul", A_[hs, 128:192], lhsT=kt[hs, cs], rhs=rt[hs, cs], start=True, stop=True, reads=[kt, rt], writes=[pA])
            cur = 0
            P.op("dve", "tensor_tensor", out=XYs[cur][:], in0=XY_, in1=maskXY[:], op=ALU.mult, reads=[pXY, maskXY], writes=[XYs[cur]])
            P.op("dve", "tensor_tensor", out=Am[:], in0=A_, in1=maskA[:], op=ALU.mult, reads=[pA, maskA], writes=[Am])
            rc = 0
            P.op("dve", "tensor_tensor", out=Rs[rc][:], in0=XYs[cur][:, 0:128], in1=ident[:], op=ALU.add, reads=[XYs[cur], ident], writes=[Rs[rc]])
            yield
            for lvl in range(5):
                X, Y = XYs[cur][:, 0:128], XYs[cur][:, 128:256]
                nxt = 1 - cur
                if lvl < 4:
                    P.op("pe", "matmul", XY_[:, 0:128], lhsT=Y, rhs=X, start=True, stop=True, reads=[XYs[cur]], writes=[pXY])
                P.op("pe", "matmul", XY_[:, 128:256], lhsT=X, rhs=Y, start=True, stop=True, reads=[XYs[cur]], writes=[pXY])
                if lvl < 4:
                    P.op("act", "copy", out=XYs[nxt][:], in_=XY_, reads=[pXY], writes=[XYs[nxt]])
                else:
                    P.op("act", "copy", out=XYs[nxt][:, 128:256], in_=XY_[:, 128:256], reads=[pXY], writes=[XYs[nxt]])
                yield
                P.op("pe", "matmul", R_, lhsT=XYs[nxt][:, 128:256], rhs=Rs[rc][:], start=True, stop=True, reads=[XYs[nxt], Rs[rc]], writes=[pR])
                P.op("dve", "tensor_tensor", out=Rs[1 - rc][:], in0=Rs[rc][:], in1=R_, op=ALU.add, reads=[Rs[rc], pR], writes=[Rs[1 - rc]])
                rc = 1 - rc
                cur = nxt
                yield
            Rf = Rs[rc]
            for h in range(2):
                hs = H2[h]
                P.op("pe", "matmul", W0_[hs, :], lhsT=al[hs, cs], rhs=ST[hs, :], start=True, stop=False, reads=[al, ST], writes=[pW0])
                P.op("pe", "matmul", W0_[hs, :], lhsT=Am[hs, 64:128], rhs=TMs[hs, 128:192], start=False, stop=True, reads=[Am, TMs], writes=[pW0])
            P.op("act", "copy", out=W0s[:], in_=W0_, reads=[pW0], writes=[W0s])
            yield
            P.op("pe", "matmul", U_, lhsT=Rf[:], rhs=W0s[:], start=True, stop=True, reads=[Rf, W0s], writes=[pU])
            P.op("dve", "tensor_copy", out=Us[:], in_=U_, reads=[pU], writes=[Us])
            yield
            for h in range(2):
                hs = H2[h]
                P.op("pe", "matmul", Y_[hs, :], lhsT=ST[hs, :], rhs=rt[hs, cs], start=True, stop=False, reads=[ST, rt], writes=[pY])
                P.op("pe", "matmul", Y_[hs, :], lhsT=Us[hs, :], rhs=Am[hs, 0:64], start=False, stop=False, reads=[Us, Am], writes=[pY])
                P.op("pe", "matmul", Y_[hs, :], lhsT=TMs[hs, 128:192], rhs=Am[hs, 128:192], start=False, stop=True, reads=[TMs, Am], writes=[pY])
            P.op("act", "copy", out=Yb[:, cs], in_=Y_, reads=[pY], writes=[Yb])
            if hp == 0 and d == 1 and t0 == CTX and c == 0:
                for nm_, tl_, w_ in (("dbg_Am", Am, 192), ("dbg_Us", Us, 64), ("dbg_TMs", TMs, 192), ("dbg_W0s", W0s, 64), ("dbg_Rf", Rf, 128), ("dbg_ST", ST, 64), ("dbg_XY", XYs[cur], 256)):
                    dbg(g, nm_, tl_, tl_[:, :w_], [128, w_])
                dbg(g, "dbg_Yb", Yb, Yb[:, 0:128], [128, 128])
            for h in range(2):
                hs = H2[h]
                P.op("pe", "matmul", SP_[hs, :], lhsT=TMs[hs, 0:64], rhs=Us[hs, :], start=True, stop=False, reads=[TMs, Us], writes=[pSP])
                P.op("pe", "matmul", SP_[hs, :], lhsT=TMs[hs, 64:128], rhs=TMs[hs, 128:192], start=False, stop=True, reads=[TMs], writes=[pSP])
            pc = Ep[:, c * CH + CH - 1:c * CH + CH] if d == 0 else Ep[:, c * CH:c * CH + 1]
            P.op("dve", "tensor_scalar", out=ST[:], in0=ST[:], scalar1=pc, scalar2=None, op0=ALU.mult, reads=[ST, Ep], writes=[ST])
            P.op("dve", "scalar_tensor_tensor", out=ST[:], in0=SP_, scalar=pc, in1=ST[:], op0=ALU.mult, op1=ALU.add, reads=[pSP, Ep, ST], writes=[ST])
            yield
        P.dma(YS[d][hp][:, ts_], Yb[:, :Lb], reads=[Yb], writes=[YS[d][hp]], q="act")
        yield


def stage_rwkv_scan(g, l):
    P = g.P
    T = g.T
    YS = [[scratch(g, "YS%d_%d" % (d, hp), [128, T]) for hp in range(8)] for d in range(2)]
    lat = [(CTX + i * 512, min(512, T - CTX - i * 512)) for i in range((T - CTX + 511) // 512)]
    bl = [[(0, CTX)] + lat, [(0, CTX)] + lat[::-1]]
    for pair in range(4):
        with P.scope():
            tr = [P.sbuf("rtr%d" % i, [128, 512], F32) for i in range(11)]
            gens = []
            k = 0
            for hp in (2 * pair, 2 * pair + 1):
                for d in range(2):
                    gens.append(rwkv_stream(g, l, hp, d, g.ps[2 * k], g.ps[2 * k + 1], bl[d], tr, YS))
                    k += 1
            while gens:
                for ge in list(gens):
                    try:
                        next(ge)
                    except StopIteration:
                        gens.remove(ge)


def stage_rwkv_out(g, l):
    P = g.P
    T = g.T
    F = [scratch(g, "F%d" % i, [128, T]) for i in range(NSEG)]
    YS = [[scratch(g, "YS%d_%d" % (d, hp), [128, T]) for hp in range(8)] for d in range(2)]
    YR = scratch(g, "YR", [RW_W, T], BF16)
    s = g.s
    blk = g.c["blk"]
    a0, ka, omka, rk, lg, lb = s["a0"], s["k_a"], s["omka"], s["r_k"], s["lnxg"], s["lnxb"]
    for hp in range(8):
        with P.scope():
            aups = []
            for d in range(2):
                t = P.sbuf("roa%d" % d, [64, 128], F32)
                P.dma(t[:], g.d["rwkv_a_up"][l, d, :, hp * 128:(hp + 1) * 128], reads=[g.d["rwkv_a_up"]], writes=[t])
                aups.append(t)
            gu0 = P.sbuf("rogu0", [128, 128], F32)
            gu1 = P.sbuf("rogu1", [32, 128], F32)
            P.dma(gu0[:], g.d["rwkv_g_up"][l, 0:128, hp * 128:(hp + 1) * 128], reads=[g.d["rwkv_g_up"]], writes=[gu0])
            P.dma(gu1[:], g.d["rwkv_g_up"][l, 128:160, hp * 128:(hp + 1) * 128], reads=[g.d["rwkv_g_up"]], writes=[gu1])
            names = ("y", "yr", "r", "k", "v", "ad", "g0", "g1", "t1", "t2", "t3")
            tl = {n_: [P.sbuf("ro_%s%d" % (n_, i), [128, 512], F32) for i in range(2)] for n_ in names}
            ob = [P.sbuf("ro_ob%d" % i, [128, 512], BF16) for i in range(2)]
            for bi, (t0, Lb) in enumerate(blocks_of(g)):
                ts_ = slice(t0, t0 + Lb)
                y, yr, r_, k_, v_, ad, g0, g1, t1, t2, t3 = [tl[n_][bi % 2] for n_ in names]
                o = ob[bi % 2]
                P.dma(y[:, :Lb], YS[0][hp][:, ts_], reads=[YS[0][hp]], writes=[y])
                P.dma(yr[:, :Lb], YS[1][hp][:, ts_], reads=[YS[1][hp]], writes=[yr], q="act")
                P.dma(r_[:, :Lb], F[S_R + hp][:, ts_], reads=[F[S_R + hp]], writes=[r_])
                P.dma(k_[:, :Lb], F[S_K + hp][:, ts_], reads=[F[S_K + hp]], writes=[k_], q="act")
                P.dma(v_[:, :Lb], F[S_V + hp][:, ts_], reads=[F[S_V + hp]], writes=[v_])
                P.dma(ad[0:64, :Lb], F[S_AD][0:64, ts_], reads=[F[S_AD]], writes=[ad], q="act")
                P.dma(g0[:, :Lb], F[S_GD][:, ts_], reads=[F[S_GD]], writes=[g0])
                P.dma(g1[0:32, :Lb], F[S_GD + 1][0:32, ts_], reads=[F[S_GD + 1]], writes=[g1], q="act")
                p1, p2, p3, p4 = [g.ps[(4 * bi + i) % 8] for i in range(4)]
                P.op("dve", "tensor_tensor", out=y[:, :Lb], in0=y[:, :Lb], in1=yr[:, :Lb], op=ALU.add, reads=[y, yr], writes=[y])
                P.op("pe", "matmul", p1[:, :Lb], lhsT=blk[:], rhs=y[:, :Lb], start=True, stop=True, reads=[blk, y], writes=[p1])
                P.op("dve", "scalar_tensor_tensor", out=y[:, :Lb], in0=p1[:, :Lb], scalar=-1.0 / 64, in1=y[:, :Lb], op0=ALU.mult, op1=ALU.add, reads=[p1, y], writes=[y])
                P.op("act", "activation", out=t1[:, :Lb], in_=y[:, :Lb], func=AF.Square, reads=[y], writes=[t1])
                P.op("pe", "matmul", p2[:, :Lb], lhsT=blk[:], rhs=t1[:, :Lb], start=True, stop=True, reads=[blk, t1], writes=[p2])
                P.op("act", "activation", out=t1[:, :Lb], in_=p2[:, :Lb], func=AF.Sqrt, bias=GN_EPS, scale=1.0 / 64, reads=[p2], writes=[t1])
                P.op("dve", "reciprocal", out=t1[:, :Lb], in_=t1[:, :Lb], reads=[t1], writes=[t1])
                P.op("dve", "tensor_tensor", out=y[:, :Lb], in0=y[:, :Lb], in1=t1[:, :Lb], op=ALU.mult, reads=[y, t1], writes=[y])
                P.op("dve", "tensor_scalar", out=y[:, :Lb], in0=y[:, :Lb], scalar1=lg[:, l, hp:hp + 1], scalar2=lb[:, l, hp:hp + 1], op0=ALU.mult, op1=ALU.add,
                     reads=[y, lg, lb], writes=[y])
                for d in range(2):
                    P.op("pe", "matmul", p3[:, :Lb], lhsT=aups[d][:], rhs=ad[0:64, :Lb], start=True, stop=True, reads=[aups[d], ad], writes=[p3])
                    tt = t2 if d == 0 else t3
                    P.op("act", "activation", out=tt[:, :Lb], in_=p3[:, :Lb], func=AF.Sigmoid, bias=a0[:, l, d, hp:hp + 1], reads=[p3, a0], writes=[tt])
                P.op("dve", "tensor_tensor", out=t2[:, :Lb], in0=t2[:, :Lb], in1=t3[:, :Lb], op=ALU.add, reads=[t2, t3], writes=[t2])
                P.op("dve", "tensor_scalar", out=t2[:, :Lb], in0=t2[:, :Lb], scalar1=ka[:, l, hp:hp + 1], scalar2=None, op0=ALU.mult, reads=[t2, ka], writes=[t2])
                P.op("dve", "tensor_scalar", out=t3[:, :Lb], in0=k_[:, :Lb], scalar1=omka[:, l, hp:hp + 1], scalar2=2.0, op0=ALU.mult, op1=ALU.mult, reads=[k_, omka], writes=[t3])
                P.op("dve", "tensor_tensor", out=t2[:, :Lb], in0=t2[:, :Lb], in1=k_[:, :Lb], op=ALU.mult, reads=[t2, k_], writes=[t2])
                P.op("dve", "tensor_tensor", out=t2[:, :Lb], in0=t2[:, :Lb], in1=t3[:, :Lb], op=ALU.add, reads=[t2, t3], writes=[t2])
                P.op("dve", "tensor_tensor", out=t2[:, :Lb], in0=t2[:, :Lb], in1=r_[:, :Lb], op=ALU.mult, reads=[t2, r_], writes=[t2])
                P.op("dve", "tensor_scalar", out=t2[:, :Lb], in0=t2[:, :Lb], scalar1=rk[:, l, hp:hp + 1], scalar2=0.5, op0=ALU.mult, op1=ALU.mult, reads=[t2, rk], writes=[t2])
                P.op("pe", "matmul", p4[:, :Lb], lhsT=blk[:], rhs=t2[:, :Lb], start=True, stop=True, reads=[blk, t2], writes=[p4])
                P.op("dve", "tensor_tensor", out=t2[:, :Lb], in0=v_[:, :Lb], in1=p4[:, :Lb], op=ALU.mult, reads=[v_, p4], writes=[t2])
                P.op("dve", "tensor_tensor", out=y[:, :Lb], in0=y[:, :Lb], in1=t2[:, :Lb], op=ALU.add, reads=[y, t2], writes=[y])
                P.op("act", "activation", out=g0[:, :Lb], in_=g0[:, :Lb], func=AF.Sigmoid, reads=[g0], writes=[g0])
                P.op("act", "activation", out=g1[0:32, :Lb], in_=g1[0:32, :Lb], func=AF.Sigmoid, reads=[g1], writes=[g1])
                P.op("pe", "matmul", p1[:, :Lb], lhsT=gu0[:], rhs=g0[:, :Lb], start=True, stop=False, reads=[gu0, g0], writes=[p1])
                P.op("pe", "matmul", p1[:, :Lb], lhsT=gu1[0:32, :], rhs=g1[0:32, :Lb], start=False, stop=True, reads=[gu1, g1], writes=[p1])
                P.op("dve", "tensor_tensor", out=o[:, :Lb], in0=y[:, :Lb], in1=p1[:, :Lb], op=ALU.mult, reads=[y, p1], writes=[o])
                P.dma(YR[hp * 128:(hp + 1) * 128, ts_], o[:, :Lb], reads=[o], writes=[YR], q="act")


def stage_mla(g, l, ctx_out):
    P = g.P
    T = g.T
    F = [scratch(g, "F%d" % i, [128, T]) for i in range(NSEG)]
    YM = scratch(g, "YM", [2048, T], BF16)
    ones, ident, Jm = g.c["ones"], g.c["ident"], g.c["Jm"]
    qng, kvg = g.s["qng"], g.s["kvg"]
    blocks = blocks_of(g)
    NKC = T // 128
    with P.scope():
        onesb = P.sbuf("onesb", [128, 128], BF16)
        P.op("dve", "memset", onesb[:], 1.0, reads=[], writes=[onesb])
        cos = load_const(g, "cos", [64, g.SEQ])
        sin = load_const(g, "sin", [64, g.SEQ], q="act")
        qn = P.sbuf("qn", [128, 4, T], BF16)
        kvn = P.sbuf("kvn", [128, 2, T], BF16)
        kr = P.sbuf("kr", [64, T], BF16)
        with P.scope():
            xqs = [P.sbuf("xq%d" % i, [128, 4, 512], F32) for i in range(2)]
            xks = [P.sbuf("xk%d" % i, [128, 2, 512], F32) for i in range(2)]
            xrs = [P.sbuf("xr%d" % i, [64, 512], F32) for i in range(2)]
            sqs = [P.sbuf("msq%d" % i, [128, 512], F32) for i in range(2)]
            rs1 = P.sbuf("mrs1", [128, 512], F32)
            rs2 = P.sbuf("mrs2", [128, 512], F32)
            m1 = P.sbuf("mm1", [64, 512], F32)
            m2 = P.sbuf("mm2", [64, 512], F32)
            for bi, (t0, n) in enumerate(blocks):
                xq, xk, xr = xqs[bi % 2], xks[bi % 2], xrs[bi % 2]
                for i in range(4):
                    P.dma(xq[:, i, :n], F[S_Q + i][:, t0:t0 + n], reads=[F[S_Q + i]], writes=[xq], q="sp" if i % 2 == 0 else "act")
                for i in range(2):
                    P.dma(xk[:, i, :n], F[S_KV + i][:, t0:t0 + n], reads=[F[S_KV + i]], writes=[xk], q="sp" if i % 2 == 0 else "act")
                P.dma(xr[0:64, :n], F[S_KR][0:64, t0:t0 + n], reads=[F[S_KR]], writes=[xr])
                pa, pb, pj = g.ps[0], g.ps[1], g.ps[2]
                for (x_, nchunk, pp, rs, gam, dst) in ((xq, 4, pa, rs1, qng, qn), (xk, 2, pb, rs2, kvg, kvn)):
                    for i in range(nchunk):
                        sq = sqs[i % 2]
                        P.op("act", "activation", out=sq[:, :n], in_=x_[:, i, :n], func=AF.Square, reads=[x_], writes=[sq])
                        P.op("pe", "matmul", pp[:, :n], lhsT=ones[:], rhs=sq[:, :n], start=(i == 0), stop=(i == nchunk - 1), reads=[ones, sq], writes=[pp])
                    P.op("act", "activation", out=rs[:, :n], in_=pp[:, :n], func=AF.Sqrt, bias=LN_EPS, scale=1.0 / (128 * nchunk), reads=[pp], writes=[rs])
                    P.op("dve", "reciprocal", out=rs[:, :n], in_=rs[:, :n], reads=[rs], writes=[rs])
                    for i in range(nchunk):
                        P.op("dve", "scalar_tensor_tensor", out=dst[:, i, t0:t0 + n], in0=x_[:, i, :n], scalar=gam[:, l, i:i + 1], in1=rs[:, :n],
                             op0=ALU.mult, op1=ALU.mult, reads=[x_, gam, rs], writes=[dst])
                if t0 < CTX:
                    P.op("act", "copy", out=kr[0:64, t0:t0 + n], in_=xr[0:64, :n], reads=[xr], writes=[kr])
                else:
                    P.op("pe", "matmul", pj[0:64, :n], lhsT=Jm[:], rhs=xr[0:64, :n], start=True, stop=True, reads=[Jm, xr], writes=[pj])
                    P.op("dve", "tensor_tensor", out=m1[:, :n], in0=xr[0:64, :n], in1=cos[:, t0 - CTX:t0 - CTX + n], op=ALU.mult, reads=[xr, cos], writes=[m1])
                    P.op("dve", "tensor_tensor", out=m2[:, :n], in0=pj[0:64, :n], in1=sin[:, t0 - CTX:t0 - CTX + n], op=ALU.mult, reads=[pj, sin], writes=[m2])
                    P.op("dve", "tensor_tensor", out=kr[0:64, t0:t0 + n], in0=m1[:, :n], in1=m2[:, :n], op=ALU.add, reads=[m1, m2], writes=[kr])
        wuq, wuk, wuv = g.d["mla_w_uq"], g.d["mla_w_uk"], g.d["mla_w_uv"]
        for hd in range(16):
            with P.scope():
                wq = P.sbuf("wq", [128, 4, 192], BF16)
                wk = P.sbuf("wk", [128, 2, 128], BF16)
                wv = P.sbuf("wv", [128, 2, 128], BF16)
                P.dma(wq[:], wuq[l, :, hd * 192:(hd + 1) * 192].rearrange("(kc p) n -> p kc n", p=128), reads=[wuq], writes=[wq], q="pool")
                P.dma(wk[:], wuk[l, :, hd * 128:(hd + 1) * 128].rearrange("(kc p) n -> p kc n", p=128), reads=[wuk], writes=[wk], q="pool")
                P.dma(wv[:], wuv[l, :, hd * 128:(hd + 1) * 128].rearrange("(kc p) n -> p kc n", p=128), reads=[wuv], writes=[wv], q="pool")
                Kn = P.sbuf("Kn", [128, T], BF16)
                Qn = P.sbuf("Qn", [128, T], BF16)
                Qr = P.sbuf("Qr", [64, T], BF16)
                Va = P.sbuf("Va", [128, NKC, 128], BF16)
                xq32 = P.sbuf("xq32", [64, 512], F32)
                m1 = P.sbuf("hm1", [64, 512], F32)
                m2 = P.sbuf("hm2", [64, 512], F32)
                for bi, (t0, n) in enumerate(blocks):
                    pk, pq, pr, pj = g.ps[0], g.ps[1], g.ps[2], g.ps[3]
                    for kc in range(2):
                        P.op("pe", "matmul", pk[:, :n], lhsT=wk[:, kc, :], rhs=kvn[:, kc, t0:t0 + n], start=(kc == 0), stop=(kc == 1), reads=[wk, kvn], writes=[pk])
                    P.op("act", "copy", out=Kn[:, t0:t0 + n], in_=pk[:, :n], reads=[pk], writes=[Kn])
                    if t0 < CTX and not ctx_out:
                        continue
                    for kc in range(4):
                        P.op("pe", "matmul", pq[:, :n], lhsT=wq[:, kc, 0:128], rhs=qn[:, kc, t0:t0 + n], start=(kc == 0), stop=(kc == 3), reads=[wq, qn], writes=[pq])
                    P.op("dve", "tensor_copy", out=Qn[:, t0:t0 + n], in_=pq[:, :n], reads=[pq], writes=[Qn])
                    for kc in range(4):
                        P.op("pe", "matmul", pr[0:64, :n], lhsT=wq[:, kc, 128:192], rhs=qn[:, kc, t0:t0 + n], start=(kc == 0), stop=(kc == 3), reads=[wq, qn], writes=[pr])
                    if t0 < CTX:
                        P.op("act", "copy", out=Qr[0:64, t0:t0 + n], in_=pr[0:64, :n], reads=[pr], writes=[Qr])
                    else:
                        P.op("act", "copy", out=xq32[:, :n], in_=pr[0:64, :n], reads=[pr], writes=[xq32])
                        P.op("pe", "matmul", pj[0:64, :n], lhsT=Jm[:], rhs=xq32[:, :n], start=True, stop=True, reads=[Jm, xq32], writes=[pj])
                        P.op("dve", "tensor_tensor", out=m1[:, :n], in0=xq32[:, :n], in1=cos[:, t0 - CTX:t0 - CTX + n], op=ALU.mult, reads=[xq32, cos], writes=[m1])
                        P.op("dve", "tensor_tensor", out=m2[:, :n], in0=pj[0:64, :n], in1=sin[:, t0 - CTX:t0 - CTX + n], op=ALU.mult, reads=[pj, sin], writes=[m2])
                        P.op("dve", "tensor_tensor", out=Qr[0:64, t0:t0 + n], in0=m1[:, :n], in1=m2[:, :n], op=ALU.add, reads=[m1, m2], writes=[Qr])
                for tc_ in range(NKC):
                    pv = g.ps[2 + tc_ % 2]
                    for kc in range(2):
                        P.op("pe", "matmul", pv[:, 0:128], lhsT=kvn[:, kc, tc_ * 128:(tc_ + 1) * 128], rhs=wv[:, kc, :], start=(kc == 0), stop=(kc == 1),
                             reads=[kvn, wv], writes=[pv])
                    if tc_ % 2 == 0:
                        P.op("act", "copy", out=Va[:, tc_, :], in_=pv[:, 0:128], reads=[pv], writes=[Va])
                    else:
                        P.op("dve", "tensor_copy", out=Va[:, tc_, :], in_=pv[:, 0:128], reads=[pv], writes=[Va])
                pts = [P.sbuf("PT%d" % i, [128, 512], BF16) for i in range(3)]
                stgs = [P.sbuf("ostg%d" % i, [128, 512], BF16) for i in range(2)]
                rss = [P.sbuf("ors%d" % i, [128, 512], F32) for i in range(2)]
                qblocks = [b_ for b_ in blocks if (b_[0] >= CTX or ctx_out)]
                for qi, (q0, nq) in enumerate(qblocks):
                    keys = list(range(CTX // 128)) if q0 < CTX else list(range(NKC))
                    nk = len(keys)
                    accO = g.ps[4 + 2 * (qi % 2)]
                    accS = g.ps[5 + 2 * (qi % 2)]

                    def qk(ki):
                        kc = keys[ki]
                        S = g.ps[ki % 2]
                        P.op("pe", "matmul", S[:, :nq], lhsT=Kn[:, kc * 128:(kc + 1) * 128], rhs=Qn[:, q0:q0 + nq], start=True, stop=False, reads=[Kn, Qn], writes=[S])
                        P.op("pe", "matmul", S[:, :nq], lhsT=kr[0:64, kc * 128:(kc + 1) * 128], rhs=Qr[0:64, q0:q0 + nq], start=False, stop=True, reads=[kr, Qr], writes=[S])

                    qk(0)
                    for ki in range(nk):
                        if ki + 1 < nk:
                            qk(ki + 1)
                        kc = keys[ki]
                        S = g.ps[ki % 2]
                        PT = pts[ki % 3]
                        P.op("act", "activation", out=PT[:, :nq], in_=S[:, :nq], func=AF.Exp, scale=ATT_SCALE, reads=[S], writes=[PT])
                        P.op("pe", "matmul", accO[:, :nq], lhsT=Va[:, kc, :], rhs=PT[:, :nq], start=(ki == 0), stop=(ki == nk - 1), reads=[PT, Va], writes=[accO])
                        P.op("pe", "matmul", accS[:, :nq], lhsT=onesb[:], rhs=PT[:, :nq], start=(ki == 0), stop=(ki == nk - 1), reads=[PT, onesb], writes=[accS])
                    rs, stg = rss[qi % 2], stgs[qi % 2]
                    P.op("dve", "reciprocal", out=rs[:, :nq], in_=accS[:, :nq], reads=[accS], writes=[rs])
                    P.op("dve", "tensor_tensor", out=stg[:, :nq], in0=accO[:, :nq], in1=rs[:, :nq], op=ALU.mult, reads=[accO, rs], writes=[stg])
                    P.dma(YM[hd * 128:(hd + 1) * 128, q0:q0 + nq], stg[:, :nq], reads=[stg], writes=[YM], q="act")


def stage_c(g, l, dst, last):
    P = g.P
    T = g.T
    XL = scratch(g, "XL", [D, T])
    F = [scratch(g, "F%d" % i, [128, T]) for i in range(NSEG)]
    YR = scratch(g, "YR", [RW_W, T], BF16)
    YM = scratch(g, "YM", [2048, T], BF16)
    YL = scratch(g, "YL", [LRU_W, T], BF16)
    prw, pml, plr, wout = g.d["proj_rwkv"], g.d["proj_mla"], g.d["proj_lru"], g.d["w_out"]
    for (t0, NT, m) in token_tiles(g, with_ctx=not last):
        with P.scope():
            s = P.sbuf("cs_", [128, KC, NT], F32)
            P.dma(s[:], fview(XL)[:, :, t0:t0 + NT], reads=[XL], writes=[s])
            P.op("act", "mul", out=s[:], in_=s[:], mul=ALPHA, reads=[s], writes=[s])
            with P.scope():
                yr = P.sbuf("cyr", [128, 8, NT], BF16)
                ym = P.sbuf("cym", [128, 16, NT], BF16)
                yl = P.sbuf("cyl", [128, 8, NT], BF16)
                P.dma(yr[:], YR[:].rearrange("(kc p) t -> p kc t", p=128)[:, :, t0:t0 + NT], reads=[YR], writes=[yr])
                P.dma(ym[:], YM[:].rearrange("(kc p) t -> p kc t", p=128)[:, :, t0:t0 + NT], reads=[YM], writes=[ym], q="act")
                P.dma(yl[:], YL[:].rearrange("(kc p) t -> p kc t", p=128)[:, :, t0:t0 + NT], reads=[YL], writes=[yl])
                ybf = P.sbuf("cyb", [128, KC, NT], BF16)
                gts = [P.sbuf("cgt%d" % i, [128, 3, NT], F32) for i in range(2)]
                pws = [P.sbuf("cpw%d" % i, [128, 32, 128], BF16) for i in range(2)]
                t1 = P.sbuf("ct1", [128, NT], F32)
                t2 = P.sbuf("ct2", [128, NT], F32)
                for dc in range(KC):
                    pw, gt = pws[dc % 2], gts[dc % 2]
                    cs_ = slice(dc * 128, (dc + 1) * 128)
                    wload(g, pw[:, 0:8, :], pw, "proj_rwkv", (l,), dc * 128)
                    wload(g, pw[:, 8:24, :], pw, "proj_mla", (l,), dc * 128)
                    wload(g, pw[:, 24:32, :], pw, "proj_lru", (l,), dc * 128)
                    for bi in range(3):
                        P.dma(gt[:, bi, :], F[S_GATE + 16 * bi + dc][:, t0:t0 + NT], reads=[F[S_GATE + 16 * bi + dc]], writes=[gt], q="sp" if bi != 1 else "act")
                    p1, p2, p3 = g.ps[0], g.ps[1], g.ps[2]
                    for kc in range(8):
                        P.op("pe", "matmul", p1[:, :NT], lhsT=pw[:, kc, :], rhs=yr[:, kc, :], start=(kc == 0), stop=(kc == 7), reads=[pw, yr], writes=[p1])
                    for kc in range(16):
                        P.op("pe", "matmul", p2[:, :NT], lhsT=pw[:, 8 + kc, :], rhs=ym[:, kc, :], start=(kc == 0), stop=(kc == 15), reads=[pw, ym], writes=[p2])
                    for kc in range(8):
                        P.op("pe", "matmul", p3[:, :NT], lhsT=pw[:, 24 + kc, :], rhs=yl[:, kc, :], start=(kc == 0), stop=(kc == 7), reads=[pw, yl], writes=[p3])
                    P.op("dve", "tensor_tensor", out=t1[:], in0=gt[:, 0, :], in1=p1[:, :NT], op=ALU.mult, reads=[gt, p1], writes=[t1])
                    P.op("dve", "tensor_tensor", out=t2[:], in0=gt[:, 1, :], in1=p2[:, :NT], op=ALU.mult, reads=[gt, p2], writes=[t2])
                    P.op("dve", "tensor_tensor", out=t1[:], in0=t1[:], in1=t2[:], op=ALU.add, reads=[t1, t2], writes=[t1])
                    P.op("dve", "tensor_tensor", out=t2[:], in0=gt[:, 2, :], in1=p3[:, :NT], op=ALU.mult, reads=[gt, p3], writes=[t2])
                    P.op("dve", "tensor_tensor", out=ybf[:, dc, :], in0=t1[:], in1=t2[:], op=ALU.add, reads=[t1, t2], writes=[ybf])
                wos = [P.sbuf("cwo%d" % i, [128, KC, 128], BF16) for i in range(2)]
                for dc in range(KC):
                    wo = wos[dc % 2]
                    wload(g, wo[:], wo, "w_out", (l,), dc * 128)
                    po = g.ps[4 + dc % 2]
                    for kc in range(KC):
                        P.op("pe", "matmul", po[:, :NT], lhsT=wo[:, kc, :], rhs=ybf[:, kc, :], start=(kc == 0), stop=(kc == KC - 1), reads=[wo, ybf], writes=[po])
                    P.op("dve", "scalar_tensor_tensor", out=s[:, dc, :], in0=po[:, :NT], scalar=g.mod[:, 5 * 16 + dc, m:m + 1], in1=s[:, dc, :],
                         op0=ALU.mult, op1=ALU.add, reads=[po, s, g.mod], writes=[s])
            layer_norm(g, s, NT, l, 1)
            ffn_block(g, s, NT, m, l, 1, (6, 7, 8), 2)
            if last:
                P.dma(fview(dst)[:, :, t0 - CTX:t0 - CTX + NT], s[:], reads=[s], writes=[dst], q="act")
            else:
                P.dma(fview(dst)[:, :, t0:t0 + NT], s[:], reads=[s], writes=[dst], q="act")


SMALL_NAMES = ["adab", "lng", "lnb", "gateb", "mu", "mud", "w0", "a0", "kk_k", "k_a", "r_k", "lnxg", "lnxb", "qng", "kvg", "convw", "convb", "ba", "bx", "lam"]
CONST_NAMES = ["ident", "ones", "blk", "Jm", "maskA"]


def build_program(SEQ, L, shapes, stages=None, ext_in=(), ext_out=()):
    nc = bass.Bass("TRN2", target_bir_lowering=False)
    g = setup(nc, SEQ, L, shapes, ext_in=ext_in, ext_out=ext_out)
    load_consts(g, CONST_NAMES)
    load_small(g, SMALL_NAMES)
    init_derived(g)
    init_mod(g)
    T = g.T
    xcur = g.d["xT"]
    XN = scratch(g, "XN", [D, T])
    yT = g.P.dram("yT", [D, SEQ], F32, "ExternalOutput")
    g.d["yT"] = yT
    for l in range(L):
        precast_layer(g, l)
    for l in range(L):
        last = l == L - 1
        stage_mod(g, l)
        stage_a(g, l, xcur)
        stage_lru(g, l)
        stage_rwkv_mix(g, l)
        stage_rwkv_scan(g, l)
        stage_rwkv_out(g, l)
        stage_mla(g, l, ctx_out=not last)
        stage_c(g, l, yT if last else XN, last)
        xcur = XN
    g.P.emit()
    return nc, g


def host_inputs(inp, b, SEQ, L):
    x = np.asarray(inp["x"][b], np.float32)
    cx = np.asarray(inp["ctx"][b], np.float32)
    d = {}
    d["xT"] = np.ascontiguousarray(np.concatenate([cx, x], 0).T)
    d["cvec"] = np.ascontiguousarray(np.stack([fm(inp["c"][b]), fm(inp["c_ctx"])], -1))
    d.update(host_consts(SEQ))
    d.update(host_small(inp, L))
    for k in BIG_W:
        d[k] = np.ascontiguousarray(np.asarray(inp[k], np.float32))
    return d


_CACHE = {}


def kernel(**inputs):
    B, SEQ, _ = inputs["x"].shape
    L = inputs["w_in"].shape[0]
    per_core = [host_inputs(inputs, b, SEQ, L) for b in range(B)]
    shapes = {k: list(v.shape) for k, v in per_core[0].items()}
    key = (SEQ, L)
    if key not in _CACHE:
        _CACHE[key] = build_program(SEQ, L, shapes)
    nc, g = _CACHE[key]
    res = run_bass_kernel_spmd(nc, per_core, core_ids=list(range(B)))
    out = np.stack([np.ascontiguousarray(res.results[b]["yT"].T) for b in range(B)], 0)
    return out.astype(np.float32)
```

```python
import contextlib
import math
import numpy as np
import concourse.bass as bass
import concourse.mybir as mybir
from concourse.bass_utils import run_bass_kernel_spmd

F32 = mybir.dt.float32
BF16 = mybir.dt.bfloat16
AF = mybir.ActivationFunctionType
ALU = mybir.AluOpType

D = 2048
KC = 16
DFF = 5632
FC = 44
NMOD = 9
CTX = 256
GW = 64
RW_W = 1024
RWKV_IN = 3360
MLA_IN = 832
LRU_W = 1024
IN_COLS = 12384
ALPHA = 4.0 ** 0.25
LN_EPS = 1e-6
GN_EPS = 64e-5
C0 = math.exp(-0.5)
ATT_SCALE = 192.0 ** -0.5
CH = 64

SEGS = []
for i in range(8):
    SEGS.append((128 * i, 128))
for i in range(8):
    SEGS.append((1024 + 128 * i, 128))
for i in range(8):
    SEGS.append((2048 + 128 * i, 128))
SEGS.append((3072, 64))
SEGS.append((3136, 64))
SEGS.append((3200, 128))
SEGS.append((3328, 32))
for i in range(4):
    SEGS.append((3360 + 128 * i, 128))
for i in range(2):
    SEGS.append((3872 + 128 * i, 128))
SEGS.append((4128, 64))
for i in range(8):
    SEGS.append((4192 + 128 * i, 128))
for i in range(8):
    SEGS.append((5216 + 128 * i, 128))
for i in range(48):
    SEGS.append((6240 + 128 * i, 128))
NSEG = len(SEGS)
S_R, S_K, S_V, S_WD, S_AD, S_GD, S_Q, S_KV, S_KR, S_XR, S_GB, S_GATE = 0, 8, 16, 24, 25, 26, 28, 32, 34, 35, 43, 51
NRW = 28


class Tile:
    __slots__ = ("t", "name", "last_w", "readers", "dsem", "osem", "kind")

    def __init__(self, t, name, kind="sbuf"):
        self.t = t
        self.name = name
        self.kind = kind
        self.last_w = None
        self.readers = []
        self.dsem = None
        self.osem = None

    def __getitem__(self, idx):
        return self.t[idx]

    def sub(self, name):
        return Tile(self.t, self.name + "." + name, self.kind)


class Op:
    __slots__ = ("eng", "fn", "deps", "is_dma", "sem", "val", "need_sig", "osize")


SEM_EPOCH = 24000
SMALL_OP = 512
import os
DEBUG_INS = os.environ.get("DEBUG_INS")


class Prog:
    ENGS = ("pe", "dve", "act", "pool", "sp")

    def __init__(self, nc):
        self.nc = nc
        self.ops = []
        self.stack = contextlib.ExitStack()
        self.scopes = []
        self.free_sems = []
        self.fence = None
        self.nsem = 0
        self.dummy = Tile(self.stack.enter_context(nc.sbuf_tensor("fence_dummy", [1, 8], F32)), "dummy")

    def new_sem(self, name):
        self.nsem += 1
        return self.stack.enter_context(self.nc.semaphore(name + "_%d" % self.nsem))

    def _reg(self, tl):
        tl.last_w = self.fence
        if self.scopes:
            self.scopes[-1][1].append(tl)
        return tl

    def sbuf(self, name, shape, dtype=F32):
        st = self.scopes[-1][0] if self.scopes else self.stack
        nm = name + "_%d" % len(self.ops)
        t = st.enter_context(self.nc.sbuf_tensor(nm, list(shape), dtype))
        return self._reg(Tile(t, name))

    def psum(self, name, shape, dtype=F32):
        t = self.stack.enter_context(self.nc.psum_tensor(name, list(shape), dtype))
        return Tile(t, name, "psum")

    def dram(self, name, shape, dtype=F32, kind="Internal"):
        t = self.nc.dram_tensor(name, list(shape), dtype, kind=kind).ap()
        return Tile(t, name, "dram")

    @contextlib.contextmanager
    def scope(self):
        st = contextlib.ExitStack()
        self.scopes.append((st, []))
        try:
            yield
        finally:
            _, tiles = self.scopes.pop()
            if tiles:
                d = self.dummy
                self.fence = self.op("dve", "memset", d[:], 0.0, reads=[], writes=[d] + tiles)
                for tl in tiles:
                    for s in (tl.dsem, tl.osem):
                        if s is not None and s[1] < SEM_EPOCH:
                            self.free_sems.append(s)
            st.close()

    def _record(self, eng, fn, reads, writes, is_dma):
        i = len(self.ops)
        deps = set()
        pr = [t for t in reads if t.kind == "psum"]
        if pr:
            writes = list(writes) + [t for t in pr if t not in writes]
            reads = [t for t in reads if t.kind != "psum"]
        for t in reads:
            if t.last_w is not None:
                deps.add(t.last_w)
        for t in writes:
            if t.last_w is not None:
                deps.add(t.last_w)
            deps.update(t.readers)
        for t in reads:
            t.readers.append(i)
        for t in writes:
            t.last_w = i
            t.readers = []
        o = Op()
        o.eng = eng
        o.fn = fn
        o.deps = deps
        o.is_dma = is_dma
        o.sem = None
        o.val = 0
        o.need_sig = is_dma
        o.osize = 1 << 30
        self.ops.append(o)
        return i

    def op(self, eng, name, *args, reads=(), writes=(), **kw):
        def fn(e, name=name, args=args, kw=kw):
            return getattr(e, name)(*args, **kw)

        i = self._record(eng, fn, reads, writes, False)
        if eng != "pe":
            oap = kw.get("out", args[0] if args else None)
            try:
                sz = 1
                for v in oap.shape[1:]:
                    sz *= int(v)
                self.ops[i].osize = 0 if name == "tensor_tensor_scan" else sz
            except Exception:
                pass
        return i

    def _get_sem(self, cur, name):
        if cur is not None and cur[1] < SEM_EPOCH:
            return cur
        if self.free_sems:
            return self.free_sems.pop()
        return [self.new_sem(name), 0]

    def dma(self, out_ap, in_ap, reads=(), writes=(), q="sp", **kw):
        def fn(e, out_ap=out_ap, in_ap=in_ap, kw=kw):
            return e.dma_start(out=out_ap, in_=in_ap, **kw)

        i = self._record(q, fn, reads, writes, True)
        if writes[0].kind == "dram":
            st = reads[0]
            st.osem = self._get_sem(st.osem, "o")
            s = st.osem
        else:
            st = writes[0]
            st.dsem = self._get_sem(st.dsem, "d")
            s = st.dsem
        s[1] += 16
        self.ops[i].sem = s[0]
        self.ops[i].val = s[1]
        return i

    def dma_dd(self, out_ap, in_ap, reads=(), writes=(), q="pool", nslots=8, **kw):
        def fn(e, out_ap=out_ap, in_ap=in_ap, kw=kw):
            return e.dma_start(out=out_ap, in_=in_ap, **kw)

        if not hasattr(self, "dd_slots"):
            self.dd_slots = [[self.new_sem("dd"), 0, None] for _ in range(nslots)]
            self.dd_i = 0
        slot = self.dd_slots[self.dd_i % nslots]
        self.dd_i += 1
        if slot[1] >= SEM_EPOCH:
            slot[0] = self.new_sem("dd")
            slot[1] = 0
        i = self._record(q, fn, reads, writes, True)
        if slot[2] is not None:
            self.ops[i].deps.add(slot[2])
        slot[1] += 16
        slot[2] = i
        self.ops[i].sem = slot[0]
        self.ops[i].val = slot[1]
        return i

    def emit(self):
        nc = self.nc
        ops = self.ops
        for o in ops:
            for d in o.deps:
                od = ops[d]
                if not od.is_dma and (od.eng != o.eng or od.osize < SMALL_OP or o.is_dma):
                    od.need_sig = True
        esem = {e: [] for e in self.ENGS}
        ecnt = {e: 0 for e in self.ENGS}
        for o in ops:
            if o.is_dma or not o.need_sig:
                continue
            c = ecnt[o.eng]
            ep = c // SEM_EPOCH
            if ep >= len(esem[o.eng]):
                esem[o.eng].append(self.new_sem("e_" + o.eng))
            o.sem = esem[o.eng][ep]
            o.val = c % SEM_EPOCH + 1
            ecnt[o.eng] = c + 1
        per_eng = {e: [] for e in self.ENGS}
        for i, o in enumerate(ops):
            per_eng[o.eng].append(i)
        fw = {}
        for o in ops:
            if o.is_dma:
                k = id(o.sem)
                if k not in fw or fw[k][1] < o.val:
                    fw[k] = (o.sem, o.val)

        def run(engname, e):
            waited = {}
            for i in per_eng[engname]:
                o = ops[i]
                for d in sorted(o.deps):
                    od = ops[d]
                    if not od.is_dma and od.eng == engname and od.osize >= SMALL_OP and not o.is_dma:
                        continue
                    k = id(od.sem)
                    if waited.get(k, 0) >= od.val:
                        continue
                    e.wait_ge(od.sem, od.val)
                    waited[k] = od.val
                ins = o.fn(e)
                if DEBUG_INS is not None and DEBUG_INS in str(getattr(ins.ins, "name", "")):
                    print("DEBUG_INS", i, engname, str(ins)[:600])
                if o.need_sig:
                    ins.then_inc(o.sem, 16 if o.is_dma else 1)
            if engname == "sp":
                for s, v in fw.values():
                    e.wait_ge(s, v)

        with nc.Block() as block:
            @block.tensor
            def _(e):
                run("pe", e)

            @block.vector
            def _(e):
                run("dve", e)

            @block.scalar
            def _(e):
                run("act", e)

            @block.gpsimd
            def _(e):
                run("pool", e)

            @block.sync
            def _(e):
                run("sp", e)
        self.stack.close()


class G:
    pass


def fm(v):
    v = np.asarray(v, np.float32)
    n = v.shape[-1] // 128
    w = v.reshape(v.shape[:-1] + (n, 128))
    return np.ascontiguousarray(np.moveaxis(w, -1, 0))


def host_consts(SEQ):
    c = {}
    c["ident"] = np.eye(128, dtype=np.float32)
    c["ones"] = np.ones((128, 128), np.float32)
    blk = np.zeros((128, 128), np.float32)
    blk[:64, :64] = 1
    blk[64:, 64:] = 1
    c["blk"] = blk
    a = np.arange(64)
    strict_f = (a[:, None] < a[None, :]).astype(np.float32)
    incl_f = (a[:, None] <= a[None, :]).astype(np.float32)
    mA = np.zeros((2, 128, 192), np.float32)
    mXY = np.zeros((2, 128, 256), np.float32)
    for d in range(2):
        st = strict_f if d == 0 else strict_f.T
        inc = incl_f if d == 0 else incl_f.T
        for h in range(2):
            rs = slice(64 * h, 64 * h + 64)
            mA[d, rs, 0:64] = inc
            mA[d, rs, 64:128] = st
            mA[d, rs, 128:192] = inc
            mXY[d, rs, 64 * h:64 * h + 64] = st
            mXY[d, rs, 128 + 64 * h:128 + 64 * h + 64] = st.T
    c["maskA"] = mA
    c["maskXY"] = mXY
    Lm = 1024
    rm = np.ones((2, 128, Lm), np.float32)
    rm[0, :, 0::64] = 0
    rm[1, :, 63::64] = 0
    c["rm"] = rm
    t = np.arange(SEQ)
    row = (t // GW).astype(np.float32)
    col = (t % GW).astype(np.float32)
    inv = (10000.0 ** (-np.arange(16, dtype=np.float32) / 16)).astype(np.float32)
    ang = np.concatenate([row[:, None] * inv, col[:, None] * inv], axis=-1).astype(np.float32)
    c["cos"] = np.ascontiguousarray(np.repeat(np.cos(ang), 2, axis=1).T.astype(np.float32))
    c["sin"] = np.ascontiguousarray(np.repeat(np.sin(ang), 2, axis=1).T.astype(np.float32))
    J = np.zeros((64, 64), np.float32)
    for i in range(32):
        J[2 * i + 1, 2 * i] = -1.0
        J[2 * i, 2 * i + 1] = 1.0
    c["Jm"] = J
    return c


def seg_mu(mu_l):
    full = np.zeros((128, NRW), np.float32)
    dirm = np.zeros((128, 4, NRW), np.float32)
    for si in range(NRW):
        c0, w = SEGS[si]
        full[:w, si] = mu_l[c0:c0 + w]
        for p in range(w):
            q = (c0 + p) // 840
            dirm[p, q, si] = mu_l[c0 + p]
    return full, dirm


def seg_dirs(si):
    c0, w = SEGS[si]
    return sorted(set((c0 + p) // 840 for p in range(w)))


SMALL_SPECS = None


def host_small(inp, L):
    s = {}
    s["adab"] = np.stack([fm(inp["ada_b"][l]) for l in range(L)], 1)
    s["lng"] = np.stack([fm(inp["ln_g"][l]) for l in range(L)], 1)
    s["lnb"] = np.stack([fm(inp["ln_b"][l]) for l in range(L)], 1)
    s["gateb"] = np.stack([fm(inp["gate_b"][l]) for l in range(L)], 1)
    mus = [seg_mu(np.asarray(inp["rwkv_mu"][l], np.float32)) for l in range(L)]
    s["mu"] = np.stack([m[0] for m in mus], 1)
    s["mud"] = np.stack([m[1] for m in mus], 1)
    s["w0"] = np.stack([fm(inp["rwkv_w0"][l]) for l in range(L)], 1)
    s["a0"] = np.stack([fm(inp["rwkv_a0"][l]) for l in range(L)], 1)
    for nm, key in (("kk_k", "rwkv_k_k"), ("k_a", "rwkv_k_a"), ("r_k", "rwkv_r_k"), ("lnxg", "rwkv_lnx_g"), ("lnxb", "rwkv_lnx_b")):
        s[nm] = np.stack([fm(inp[key][l]) for l in range(L)], 1)
    s["qng"] = np.stack([fm(inp["mla_q_norm"][l]) for l in range(L)], 1)
    s["kvg"] = np.stack([fm(inp["mla_kv_norm"][l]) for l in range(L)], 1)
    s["convw"] = np.stack([fm(inp["lru_conv_w"][l]) for l in range(L)], 1)
    s["convb"] = np.stack([fm(inp["lru_conv_b"][l]) for l in range(L)], 1)
    for nm, key in (("ba", "lru_ba"), ("bx", "lru_bx"), ("lam", "lru_lambda")):
        s[nm] = np.stack([fm(inp[key][l]) for l in range(L)], 1)
    return {k: np.ascontiguousarray(v, np.float32) for k, v in s.items()}


BIG_W = ["ada_w", "ffn_w_gate", "ffn_w_up", "ffn_w_down", "w_in", "rwkv_w_up", "rwkv_a_up", "rwkv_g_up",
         "mla_w_uq", "mla_w_uk", "mla_w_uv", "lru_wa", "lru_wx", "proj_rwkv", "proj_mla", "proj_lru", "w_out"]


def setup(nc, SEQ, L, shapes, ext_in=(), ext_out=()):
    g = G()
    g.P = Prog(nc)
    g.SEQ = SEQ
    g.T = CTX + SEQ
    g.R = SEQ // GW
    g.L = L
    g.d = {}
    P = g.P
    for nm, shp in shapes.items():
        g.d[nm] = P.dram(nm, shp, F32, "ExternalInput")
    g.ps = [P.psum("ps%d" % i, [128, 512], F32) for i in range(8)]
    g.ext_in = set(ext_in)
    g.ext_out = set(ext_out)
    g.wc = {}
    return g


def scratch(g, name, shape, dtype=F32):
    if name in g.d:
        return g.d[name]
    kind = "Internal"
    if name in g.ext_in:
        kind = "ExternalInput"
    if name in g.ext_out:
        kind = "ExternalOutput"
    g.d[name] = g.P.dram(name, shape, dtype, kind)
    return g.d[name]


def dbg(g, name, tile, ap, shape):
    if name in g.ext_out:
        d = scratch(g, name, shape)
        g.P.dma(d[:], ap, reads=[tile], writes=[d])


def load_const(g, name, shape, src=None, q="sp"):
    P = g.P
    t = P.sbuf(name, shape, F32)
    src = src if src is not None else g.d[name][:]
    P.dma(t[:], src, reads=[g.d[name]], writes=[t], q=q)
    return t


def load_consts(g, which):
    P = g.P
    g.c = {}
    shp = {"ident": [128, 128], "ones": [128, 128], "blk": [128, 128], "Jm": [64, 64]}
    for nm in which:
        if nm in shp:
            g.c[nm] = load_const(g, nm, shp[nm])
    if "maskA" in which:
        g.c["maskA"] = [load_const(g, "maskA", [128, 192], g.d["maskA"][d]) for d in range(2)]
        g.c["maskXY"] = [load_const(g, "maskXY", [128, 256], g.d["maskXY"][d]) for d in range(2)]
        g.c["rm"] = [load_const(g, "rm", [128, 1024], g.d["rm"][d]) for d in range(2)]


def load_small(g, names):
    g.s = {}
    for nm in names:
        shp = list(g.d[nm].t.shape)
        g.s[nm] = load_const(g, nm, shp)


def wc_jobs(g, name, idx, K, col_list):
    W = g.d[name]
    KCn = K // 128
    sc = g.P.dram("WC_%s_%s" % (name, "_".join(map(str, idx))), [len(col_list), 128, KCn * 128], BF16)
    jobs = []
    for t, (c0, w) in enumerate(col_list):
        trk = Tile(sc.t, sc.name + "_%d" % t, "dram")
        dst = sc.t[t].rearrange("p (kc n) -> p kc n", n=128)[:, :, :w]
        src = W[tuple(idx) + (slice(None), slice(c0, c0 + w))].rearrange("(kc p) n -> p kc n", p=128)
        g.wc[(name,) + tuple(idx) + (c0,)] = (trk, dst)
        jobs.append((dst, src, W, trk))
    return jobs


def wc_emit(g, jobs):
    for (dst, src, W, trk) in jobs:
        g.P.dma_dd(dst, src, reads=[W], writes=[trk])


def wload(g, dst_ap, dst_tile, name, idx, c0):
    trk, ap = g.wc[(name,) + tuple(idx) + (c0,)]
    g.P.dma(dst_ap, ap, reads=[trk], writes=[dst_tile], q="sp")


def precast_layer(g, l):
    cols128 = lambda n: [(i * 128, 128) for i in range(n // 128)]
    for j in range(2):
        gj = wc_jobs(g, "ffn_w_gate", (l, j), D, cols128(DFF))
        uj = wc_jobs(g, "ffn_w_up", (l, j), D, cols128(DFF))
        dj = wc_jobs(g, "ffn_w_down", (l, j), DFF, cols128(D))
        ffn = [x for pair in zip(gj, uj) for x in pair] + dj
        if j == 0:
            wc_emit(g, ffn)
            wc_emit(g, wc_jobs(g, "w_in", (l,), D, list(SEGS)))
            pr = wc_jobs(g, "proj_rwkv", (l,), RW_W, cols128(D))
            pm = wc_jobs(g, "proj_mla", (l,), 2048, cols128(D))
            pl = wc_jobs(g, "proj_lru", (l,), LRU_W, cols128(D))
            wc_emit(g, [x for tr_ in zip(pr, pm, pl) for x in tr_])
            wc_emit(g, wc_jobs(g, "w_out", (l,), D, cols128(D)))
        else:
            wc_emit(g, ffn)


def stage_mod(g, l):
    P = g.P
    pm = g.ps[0]
    with P.scope():
        wts = [P.sbuf("adw%d" % i, [128, KC, 512], F32) for i in range(2)]
        aw = g.d["ada_w"]
        for grp in range(36):
            wt = wts[grp % 2]
            P.dma(wt[:], aw[l, :, grp * 512:(grp + 1) * 512].rearrange("(kc p) n -> p kc n", p=128),
                  reads=[aw], writes=[wt], q="sp" if grp % 2 == 0 else "act")
            for jj in range(4):
                j = grp * 4 + jj
                for kc in range(KC):
                    P.op("pe", "matmul", pm[:, 2 * j:2 * j + 2], lhsT=wt[:, kc, jj * 128:(jj + 1) * 128], rhs=g.cs[:, kc, :],
                         start=(kc == 0), stop=(kc == KC - 1), reads=[wt, g.cs], writes=[pm])
        adab = g.s["adab"]
        P.op("dve", "tensor_tensor", out=g.mod[:], in0=pm[:, 0:288].rearrange("p (j m) -> p j m", m=2),
             in1=adab[:, l, :].unsqueeze(2).to_broadcast([128, 144, 2]), op=ALU.add, reads=[pm, adab], writes=[g.mod])
        P.op("dve", "tensor_scalar", out=g.onep[:], in0=g.mod[:], scalar1=1.0, scalar2=None, op0=ALU.add, reads=[g.mod], writes=[g.onep])
        P.op("dve", "tensor_scalar", out=g.hg[:], in0=g.mod[:], scalar1=0.5, scalar2=None, op0=ALU.mult, reads=[g.mod], writes=[g.hg])


def init_mod(g):
    P = g.P
    g.cs = P.sbuf("cs", [128, KC, 2], F32)
    P.dma(g.cs[:], g.d["cvec"][:], reads=[g.d["cvec"]], writes=[g.cs])
    P.op("act", "activation", out=g.cs[:], in_=g.cs[:], func=AF.Silu, reads=[g.cs], writes=[g.cs])
    g.mod = P.sbuf("mod", [128, 144, 2], F32)
    g.onep = P.sbuf("onep", [128, 144, 2], F32)
    g.hg = P.sbuf("hg", [128, 144, 2], F32)


def rsqrt_(P, out, in_, eps, tiles, scale=1.0):
    P.op("act", "activation", out=out, in_=in_, func=AF.Sqrt, bias=eps, scale=scale, reads=tiles, writes=tiles)
    P.op("dve", "reciprocal", out=out, in_=out, reads=tiles, writes=tiles)


def layer_norm(g, z, NT, l, idx):
    P = g.P
    s1 = g.ps[6]
    s2 = g.ps[7]
    ones = g.c["ones"]
    with P.scope():
        sq = [P.sbuf("lnsq%d" % i, [128, NT], F32) for i in range(2)]
        mean = P.sbuf("lnmean", [128, NT], F32)
        rstd = P.sbuf("lnrstd", [128, NT], F32)
        for dc in range(KC):
            P.op("pe", "matmul", s1[:, :NT], lhsT=ones[:], rhs=z[:, dc, :], start=(dc == 0), stop=(dc == KC - 1), reads=[ones, z], writes=[s1])
            q = sq[dc % 2]
            P.op("act", "activation", out=q[:], in_=z[:, dc, :], func=AF.Square, reads=[z], writes=[q])
            P.op("pe", "matmul", s2[:, :NT], lhsT=ones[:], rhs=q[:], start=(dc == 0), stop=(dc == KC - 1), reads=[ones, q], writes=[s2])
        P.op("act", "mul", out=mean[:], in_=s1[:, :NT], mul=1.0 / D, reads=[s1], writes=[mean])
        P.op("dve", "tensor_tensor", out=rstd[:], in0=mean[:], in1=mean[:], op=ALU.mult, reads=[mean], writes=[rstd])
        P.op("dve", "scalar_tensor_tensor", out=rstd[:], in0=s2[:, :NT], scalar=1.0 / D, in1=rstd[:], op0=ALU.mult, op1=ALU.subtract, reads=[s2, rstd], writes=[rstd])
        rsqrt_(P, rstd[:], rstd[:], LN_EPS, [rstd])
        lng = g.s["lng"]
        lnb = g.s["lnb"]
        for dc in range(KC):
            P.op("dve", "tensor_tensor", out=z[:, dc, :], in0=z[:, dc, :], in1=mean[:], op=ALU.subtract, reads=[z, mean], writes=[z])
            P.op("dve", "tensor_tensor", out=z[:, dc, :], in0=z[:, dc, :], in1=rstd[:], op=ALU.mult, reads=[z, rstd], writes=[z])
            P.op("dve", "tensor_scalar", out=z[:, dc, :], in0=z[:, dc, :], scalar1=lng[:, l, idx, dc:dc + 1], scalar2=lnb[:, l, idx, dc:dc + 1],
                 op0=ALU.mult, op1=ALU.add, reads=[z, lng, lnb], writes=[z])


def modulate(g, out, s, NT, m, sc_slot, sh_slot):
    P = g.P
    for kc in range(KC):
        P.op("dve", "tensor_scalar", out=out[:, kc, :], in0=s[:, kc, :], scalar1=g.onep[:, sc_slot * 16 + kc, m:m + 1],
             scalar2=g.mod[:, sh_slot * 16 + kc, m:m + 1], op0=ALU.mult, op1=ALU.add, reads=[s, g.onep, g.mod], writes=[out])


def ffn_block(g, s, NT, m, l, j, slots, ln_idx):
    P = g.P
    sh, sc, gt = slots
    wgd, wud, wdd = g.d["ffn_w_gate"], g.d["ffn_w_up"], g.d["ffn_w_down"]
    with P.scope():
        AT = P.sbuf("ffa", [128, FC, NT], BF16)
        with P.scope():
            h = P.sbuf("ffh", [128, KC, NT], BF16)
            modulate(g, h, s, NT, m, sc, sh)
            P.op("act", "mul", out=s[:], in_=s[:], mul=ALPHA, reads=[s], writes=[s])
            wgs = [P.sbuf("wg%d" % i, [128, KC, 128], BF16) for i in range(3)]
            wus = [P.sbuf("wu%d" % i, [128, KC, 128], BF16) for i in range(3)]
            sgs = [P.sbuf("sg%d" % i, [128, NT], F32) for i in range(2)]
            for fc in range(FC):
                wg = wgs[fc % 3]
                wu = wus[fc % 3]
                wload(g, wg[:], wg, "ffn_w_gate", (l, j), fc * 128)
                wload(g, wu[:], wu, "ffn_w_up", (l, j), fc * 128)
                pg = g.ps[(2 * fc) % 4]
                pu = g.ps[(2 * fc + 1) % 4]
                for kc in range(KC):
                    P.op("pe", "matmul", pg[:, :NT], lhsT=wg[:, kc, :], rhs=h[:, kc, :], start=(kc == 0), stop=(kc == KC - 1), reads=[wg, h], writes=[pg])
                for kc in range(KC):
                    P.op("pe", "matmul", pu[:, :NT], lhsT=wu[:, kc, :], rhs=h[:, kc, :], start=(kc == 0), stop=(kc == KC - 1), reads=[wu, h], writes=[pu])
                sg = sgs[fc % 2]
                P.op("act", "activation", out=sg[:], in_=pg[:, :NT], func=AF.Silu, reads=[pg], writes=[sg])
                P.op("dve", "tensor_tensor", out=AT[:, fc, :], in0=sg[:], in1=pu[:, :NT], op=ALU.mult, reads=[sg, pu], writes=[AT])
        with P.scope():
            wds = [P.sbuf("wd%d" % i, [128, FC, 128], BF16) for i in range(2)]
            for dc in range(KC):
                wd = wds[dc % 2]
                wload(g, wd[:], wd, "ffn_w_down", (l, j), dc * 128)
                py = g.ps[4 + dc % 2]
                for fc in range(FC):
                    P.op("pe", "matmul", py[:, :NT], lhsT=wd[:, fc, :], rhs=AT[:, fc, :], start=(fc == 0), stop=(fc == FC - 1), reads=[wd, AT], writes=[py])
                P.op("dve", "scalar_tensor_tensor", out=s[:, dc, :], in0=py[:, :NT], scalar=g.hg[:, gt * 16 + dc, m:m + 1], in1=s[:, dc, :],
                     op0=ALU.mult, op1=ALU.add, reads=[py, s, g.hg], writes=[s])
    layer_norm(g, s, NT, l, ln_idx)


def token_tiles(g, with_ctx=True):
    tl = []
    if with_ctx:
        tl.append((0, CTX, 1))
    t0 = CTX
    while t0 < g.T:
        nt = min(512, g.T - t0)
        tl.append((t0, nt, 0))
        t0 += nt
    return tl


def fview(d):
    return d[:].rearrange("(kc p) t -> p kc t", p=128)


def stage_a(g, l, src, do_ffn=True):
    P = g.P
    XL = scratch(g, "XL", [D, g.T])
    F = [scratch(g, "F%d" % i, [128, g.T]) for i in range(NSEG)]
    win = g.d["w_in"]
    gateb = g.s["gateb"]
    for (t0, NT, m) in token_tiles(g):
        with P.scope():
            s = P.sbuf("s", [128, KC, NT], F32)
            P.dma(s[:], fview(src)[:, :, t0:t0 + NT], reads=[src], writes=[s])
            if do_ffn:
                ffn_block(g, s, NT, m, l, 0, (0, 1, 2), 0)
            P.dma(fview(XL)[:, :, t0:t0 + NT], s[:], reads=[s], writes=[XL], q="act")
            with P.scope():
                hl = P.sbuf("hl", [128, KC, NT], BF16)
                modulate(g, hl, s, NT, m, 4, 3)
                wts = [P.sbuf("wi%d" % i, [128, KC, 128], BF16) for i in range(3)]
                stg = [P.sbuf("stg%d" % i, [128, NT], F32) for i in range(3)]
                for si, (c0, w) in enumerate(SEGS):
                    wt = wts[si % 3]
                    wload(g, wt[:, :, :w], wt, "w_in", (l,), c0)
                    pp = g.ps[si % 4]
                    for kc in range(KC):
                        P.op("pe", "matmul", pp[:w, :NT], lhsT=wt[:, kc, :w], rhs=hl[:, kc, :], start=(kc == 0), stop=(kc == KC - 1), reads=[wt, hl], writes=[pp])
                    st = stg[si % 3]
                    if si >= S_GATE:
                        P.op("act", "activation", out=st[:w, :], in_=pp[:w, :NT], func=AF.Sigmoid, bias=gateb[:, l, si - S_GATE:si - S_GATE + 1],
                             reads=[pp, gateb], writes=[st])
                        sq_ = "act"
                    elif si % 2 == 0:
                        P.op("act", "copy", out=st[:w, :], in_=pp[:w, :NT], reads=[pp], writes=[st])
                        sq_ = "act"
                    else:
                        P.op("dve", "tensor_copy", out=st[:w, :], in_=pp[:w, :NT], reads=[pp], writes=[st])
                        sq_ = "act"
                    P.dma(F[si][:w, t0:t0 + NT], st[:w, :], reads=[st], writes=[F[si]], q=sq_)


def init_derived(g):
    P = g.P
    s = g.s
    for nm, src in (("omm", "mu"), ("omka", "k_a")):
        t = P.sbuf(nm, list(g.d[src].t.shape), F32)
        P.op("dve", "tensor_scalar", out=t[:], in0=s[src][:], scalar1=-1.0, scalar2=1.0, op0=ALU.mult, op1=ALU.add, reads=[s[src]], writes=[t])
        s[nm] = t
    s["clam"] = make_clam(g, s["lam"], list(g.d["lam"].t.shape))
    return
    s["clam"] = t


def make_clam(g, lam, shape):
    P = g.P
    e = P.sbuf("clam_e", shape, F32)
    t = P.sbuf("clam", shape, F32)
    P.op("act", "activation", out=e[:], in_=lam[:], func=AF.Exp, scale=-1.0, reads=[lam], writes=[e])
    nt = 12
    P.op("dve", "tensor_scalar", out=t[:], in0=e[:], scalar1=((-1.0) ** (nt + 1)) / nt, scalar2=((-1.0) ** nt) / (nt - 1), op0=ALU.mult, op1=ALU.add, reads=[e], writes=[t])
    for k in range(nt - 2, 0, -1):
        P.op("dve", "tensor_tensor", out=t[:], in0=t[:], in1=e[:], op=ALU.mult, reads=[t, e], writes=[t])
        P.op("dve", "tensor_scalar", out=t[:], in0=t[:], scalar1=((-1.0) ** (k + 1)) / k, scalar2=None, op0=ALU.add, reads=[t], writes=[t])
    P.op("dve", "scalar_tensor_tensor", out=t[:], in0=t[:], scalar=-8.0, in1=e[:], op0=ALU.mult, op1=ALU.mult, reads=[t, e], writes=[t])
    return t


def blocks_of(g):
    bl = [(0, CTX)]
    t0 = CTX
    while t0 < g.T:
        n = min(512, g.T - t0)
        bl.append((t0, n))
        t0 += n
    return bl


def stage_lru(g, l):
    P = g.P
    T = g.T
    F = [scratch(g, "F%d" % i, [128, T]) for i in range(NSEG)]
    YL = scratch(g, "YL", [LRU_W, T], BF16)
    cw, cb, ba, bx, clam = g.s["convw"], g.s["convb"], g.s["ba"], g.s["bx"], g.s["clam"]
    wad, wxd = g.d["lru_wa"], g.d["lru_wx"]
    for n in range(8):
        with P.scope():
            xr = P.sbuf("lxr", [128, T], F32)
            gb = P.sbuf("lgb", [128, T], F32)
            xc = P.sbuf("lxc", [128, T], F32)
            hs = P.sbuf("lhs", [128, T], F32)
            gr = P.sbuf("lgr", [128, T], F32)
            gi = P.sbuf("lgi", [128, T], F32)
            tmp = P.sbuf("ltmp", [128, T], F32)
            yb = P.sbuf("lyb", [128, T], BF16)
            P.dma(xr[:], F[S_XR + n][:, :], reads=[F[S_XR + n]], writes=[xr])
            P.dma(gb[:], F[S_GB + n][:, :], reads=[F[S_GB + n]], writes=[gb], q="act")
            for (a, b) in ((0, CTX), (CTX, T)):
                P.op("dve", "tensor_scalar", out=xc[:, a:b], in0=xr[:, a:b], scalar1=cw[:, l, 2, n:n + 1], scalar2=cb[:, l, n:n + 1],
                     op0=ALU.mult, op1=ALU.add, reads=[xr, cw, cb], writes=[xc])
                for (tap, off) in ((0, -2), (1, -1), (3, 1)):
                    if off < 0:
                        o_, i_ = xc[:, a - off:b], xr[:, a:b + off]
                    else:
                        o_, i_ = xc[:, a:b - off], xr[:, a + off:b]
                    P.op("dve", "scalar_tensor_tensor", out=o_, in0=i_, scalar=cw[:, l, tap, n:n + 1], in1=o_, op0=ALU.mult, op1=ALU.add,
                         reads=[xr, xc, cw], writes=[xc])
            if n == 0:
                dbg(g, "dbg_xc", xc, xc[:], [128, T])
            for d in range(2):
                wa = P.sbuf("lwa%d" % d, [128, 128], F32)
                wx = P.sbuf("lwx%d" % d, [128, 128], F32)
                P.dma(wa[:], wad[l, d, n], reads=[wad], writes=[wa])
                P.dma(wx[:], wxd[l, d, n], reads=[wxd], writes=[wx], q="act")
                for bi, (t0, nb) in enumerate(blocks_of(g)):
                    p1 = g.ps[(2 * bi) % 4]
                    p2 = g.ps[(2 * bi + 1) % 4]
                    P.op("pe", "matmul", p1[:, :nb], lhsT=wa[:], rhs=xc[:, t0:t0 + nb], start=True, stop=True, reads=[wa, xc], writes=[p1])
                    P.op("pe", "matmul", p2[:, :nb], lhsT=wx[:], rhs=xc[:, t0:t0 + nb], start=True, stop=True, reads=[wx, xc], writes=[p2])
                    P.op("act", "activation", out=gr[:, t0:t0 + nb], in_=p1[:, :nb], func=AF.Sigmoid, bias=ba[:, l, d, n:n + 1], reads=[p1, ba], writes=[gr])
                    P.op("act", "activation", out=gi[:, t0:t0 + nb], in_=p2[:, :nb], func=AF.Sigmoid, bias=bx[:, l, d, n:n + 1], reads=[p2, bx], writes=[gi])
                P.op("dve", "tensor_scalar", out=gr[:], in0=gr[:], scalar1=clam[:, l, d, n:n + 1], scalar2=0.25, op0=ALU.mult, op1=ALU.mult, reads=[gr, clam], writes=[gr])
                P.op("dve", "tensor_scalar", out=tmp[:], in0=gr[:], scalar1=1.0 / 7, scalar2=1.0, op0=ALU.mult, op1=ALU.add, reads=[gr], writes=[tmp])
                for kk_ in (6, 5, 4, 3, 2):
                    P.op("dve", "tensor_tensor", out=tmp[:], in0=tmp[:], in1=gr[:], op=ALU.mult, reads=[tmp, gr], writes=[tmp])
                    P.op("dve", "tensor_scalar", out=tmp[:], in0=tmp[:], scalar1=1.0 / kk_, scalar2=1.0, op0=ALU.mult, op1=ALU.add, reads=[tmp], writes=[tmp])
                P.op("dve", "scalar_tensor_tensor", out=tmp[:], in0=tmp[:], scalar=-1.0, in1=gr[:], op0=ALU.mult, op1=ALU.mult, reads=[tmp, gr], writes=[tmp])
                P.op("dve", "tensor_scalar", out=gr[:], in0=tmp[:], scalar1=-1.0, scalar2=1.0, op0=ALU.mult, op1=ALU.add, reads=[tmp], writes=[gr])
                P.op("dve", "tensor_tensor", out=gr[:], in0=gr[:], in1=gr[:], op=ALU.mult, reads=[gr], writes=[gr])
                P.op("dve", "scalar_tensor_tensor", out=xr[:], in0=tmp[:], scalar=-1.0, in1=tmp[:], op0=ALU.mult, op1=ALU.mult, reads=[tmp], writes=[xr])
                P.op("dve", "scalar_tensor_tensor", out=tmp[:], in0=tmp[:], scalar=2.0, in1=xr[:], op0=ALU.mult, op1=ALU.add, reads=[tmp, xr], writes=[tmp])
                P.op("dve", "scalar_tensor_tensor", out=tmp[:], in0=gr[:], scalar=1.0, in1=tmp[:], op0=ALU.add, op1=ALU.mult, reads=[gr, tmp], writes=[tmp])
                P.op("dve", "tensor_tensor", out=gr[:], in0=gr[:], in1=gr[:], op=ALU.mult, reads=[gr], writes=[gr])
                P.op("dve", "scalar_tensor_tensor", out=tmp[:], in0=gr[:], scalar=1.0, in1=tmp[:], op0=ALU.add, op1=ALU.mult, reads=[gr, tmp], writes=[tmp])
                P.op("act", "activation", out=tmp[:], in_=tmp[:], func=AF.Sqrt, reads=[tmp], writes=[tmp])
                P.op("dve", "tensor_tensor", out=tmp[:], in0=tmp[:], in1=gi[:], op=ALU.mult, reads=[tmp, gi], writes=[tmp])
                P.op("dve", "tensor_tensor", out=tmp[:], in0=tmp[:], in1=xc[:], op=ALU.mult, reads=[tmp, xc], writes=[tmp])
                if n == 0:
                    dbg(g, "dbg_a%d" % d, gr, gr[:], [128, T])
                    dbg(g, "dbg_b%d" % d, tmp, tmp[:], [128, T])
                hd = hs if d == 0 else gi
                if d == 0:
                    P.op("dve", "tensor_tensor_scan", out=hd[:, :], data0=gr[:, :], data1=tmp[:, :], initial=0.0, op0=ALU.mult, op1=ALU.add,
                         reads=[gr, tmp], writes=[hd])
                else:
                    P.op("dve", "tensor_tensor_scan", out=hd[:, 0:CTX][:, ::-1], data0=gr[:, 0:CTX][:, ::-1], data1=tmp[:, 0:CTX][:, ::-1], initial=0.0,
                         op0=ALU.mult, op1=ALU.add, reads=[gr, tmp], writes=[hd])
                    P.op("dve", "scalar_tensor_tensor", out=tmp[:, T - 1:T], in0=gr[:, T - 1:T], scalar=hd[:, 0:1], in1=tmp[:, T - 1:T], op0=ALU.mult, op1=ALU.add,
                         reads=[gr, hd, tmp], writes=[tmp])
                    P.op("dve", "tensor_tensor_scan", out=hd[:, CTX:T][:, ::-1], data0=gr[:, CTX:T][:, ::-1], data1=tmp[:, CTX:T][:, ::-1], initial=0.0,
                         op0=ALU.mult, op1=ALU.add, reads=[gr, tmp, hd], writes=[hd])
                    P.op("dve", "tensor_tensor", out=hs[:], in0=hs[:], in1=hd[:], op=ALU.add, reads=[hs, hd], writes=[hs])
            if n == 0:
                dbg(g, "dbg_hs", hs, hs[:], [128, T])
            P.op("dve", "tensor_tensor", out=tmp[:], in0=gb[:], in1=gb[:], op=ALU.mult, reads=[gb], writes=[tmp])
            P.op("dve", "tensor_scalar", out=tmp[:], in0=tmp[:], scalar1=0.044715, scalar2=1.0, op0=ALU.mult, op1=ALU.add, reads=[tmp], writes=[tmp])
            P.op("dve", "tensor_tensor", out=tmp[:], in0=tmp[:], in1=gb[:], op=ALU.mult, reads=[tmp, gb], writes=[tmp])
            P.op("act", "activation", out=tmp[:], in_=tmp[:], func=AF.Sigmoid, scale=1.5957691216057308, reads=[tmp], writes=[tmp])
            P.op("dve", "tensor_tensor", out=tmp[:], in0=tmp[:], in1=gb[:], op=ALU.mult, reads=[tmp, gb], writes=[tmp])
            P.op("dve", "tensor_tensor", out=yb[:], in0=tmp[:], in1=hs[:], op=ALU.mult, reads=[tmp, hs], writes=[yb])
            P.dma(YL[n * 128:(n + 1) * 128, :], yb[:], reads=[yb], writes=[YL], q="act")


def stage_rwkv_mix(g, l):
    P = g.P
    T = g.T
    R = g.R
    F = [scratch(g, "F%d" % i, [128, T]) for i in range(NSEG)]
    omm, mud = g.s["omm"], g.s["mud"]
    with P.scope():
        fs = [P.sbuf("mxf%d" % i, [128, T], F32) for i in range(2)]
        os_ = [P.sbuf("mxo%d" % i, [128, T], F32) for i in range(2)]
        for si in range(NRW):
            c0, w = SEGS[si]
            f = fs[si % 2]
            o = os_[si % 2]
            P.dma(f[:w, :], F[si][:w, :], reads=[F[si]], writes=[f], q="sp" if si % 2 == 0 else "act")
            P.op("dve", "tensor_scalar", out=o[:w, :], in0=f[:w, :], scalar1=omm[:w, l, si:si + 1], scalar2=None, op0=ALU.mult, reads=[f, omm], writes=[o])
            fl = f[:w, CTX:T].rearrange("p (r c) -> p r c", c=GW)
            ol = o[:w, CTX:T].rearrange("p (r c) -> p r c", c=GW)
            for q in seg_dirs(si):
                sc = mud[:w, l, q, si:si + 1]
                if q == 0:
                    pairs = [(ol[:, :, 1:GW], fl[:, :, 0:GW - 1]), (o[:w, 1:CTX], f[:w, 0:CTX - 1])]
                elif q == 1:
                    pairs = [(ol[:, :, 0:GW - 1], fl[:, :, 1:GW]), (o[:w, 0:CTX - 1], f[:w, 1:CTX])]
                elif q == 2:
                    pairs = [(ol[:, 1:R, :], fl[:, 0:R - 1, :]), (o[:w, 1:CTX], f[:w, 0:CTX - 1])]
                else:
                    pairs = [(ol[:, 0:R - 1, :], fl[:, 1:R, :]), (o[:w, 0:CTX - 1], f[:w, 1:CTX])]
                for (oo, ii) in pairs:
                    P.op("dve", "scalar_tensor_tensor", out=oo, in0=ii, scalar=sc, in1=oo, op0=ALU.mult, op1=ALU.add, reads=[f, o, mud], writes=[o])
            P.dma(F[si][:w, :], o[:w, :], reads=[o], writes=[F[si]], q="act")


SCAN_F32R = os.environ.get("SCAN_F32R", "0") == "1"


def _r(ap):
    return ap.bitcast(mybir.dt.float32r) if SCAN_F32R else ap


def rwkv_stream(g, l, hp, d, bankA, bankB, blocks, tr, YS):
    P = g.P
    F = [scratch(g, "F%d" % i, [128, g.T]) for i in range(NSEG)]
    ident, blk = g.c["ident"], g.c["blk"]
    maskA, maskXY, rm = g.c["maskA"][d], g.c["maskXY"][d], g.c["rm"][d]
    s = g.s
    pA = pTM = pW0 = pSP = bankA
    pXY = pR = pU = pY = bankB
    allA = [bankA]
    A_, TM_, W0_, SP_ = bankA[:, 0:192], bankA[:, 192:384], bankA[:, 384:448], bankA[:, 448:512]
    XY_, R_, U_, Y_ = bankB[:, 0:256], bankB[:, 256:384], bankB[:, 384:448], bankB[:, 448:512]
    P.op("dve", "memset", bankB[:, 0:256], 0.0, reads=[], writes=[pXY])
    nm = "_%d_%d" % (hp, d)
    ST = P.sbuf("ST" + nm, [128, 64], F32)
    P.op("dve", "memset", ST[:], 0.0, reads=[], writes=[ST])
    wup = P.sbuf("wup" + nm, [64, 128], F32)
    aup = P.sbuf("aup" + nm, [64, 128], F32)
    P.dma(wup[:], g.d["rwkv_w_up"][l, d, :, hp * 128:(hp + 1) * 128], reads=[g.d["rwkv_w_up"]], writes=[wup])
    P.dma(aup[:], g.d["rwkv_a_up"][l, d, :, hp * 128:(hp + 1) * 128], reads=[g.d["rwkv_a_up"]], writes=[aup])
    LM = 512
    rt, al, be, kt, vv, Ep, Yb = [P.sbuf(n_ + nm, [128, LM], F32) for n_ in ("rt", "al", "be", "kt", "vv", "Ep", "Yb")]
    XYs = [P.sbuf("XYs%d" % i + nm, [128, 256], F32) for i in range(2)]
    Rs = [P.sbuf("Rs%d" % i + nm, [128, 128], F32) for i in range(2)]
    Am = P.sbuf("Am" + nm, [128, 192], F32)
    TMs = P.sbuf("TMs" + nm, [128, 192], F32)
    W0s = P.sbuf("W0s" + nm, [128, 64], F32)
    Us = P.sbuf("Us" + nm, [128, 64], F32)
    r_, k_, twd, adt, sig, Lc, t1, t2, a_, kk_, t3 = tr
    w0, a0, kkk, ka, omka = s["w0"], s["a0"], s["kk_k"], s["k_a"], s["omka"]
    H2 = (slice(0, 64), slice(64, 128))
    for (t0, Lb) in blocks:
        ts_ = slice(t0, t0 + Lb)
        P.dma(r_[:, :Lb], F[S_R + hp][:, ts_], reads=[F[S_R + hp]], writes=[r_])
        P.dma(k_[:, :Lb], F[S_K + hp][:, ts_], reads=[F[S_K + hp]], writes=[k_], q="act")
        P.dma(vv[:, :Lb], F[S_V + hp][:, ts_], reads=[F[S_V + hp]], writes=[vv])
        P.dma(twd[0:64, :Lb], F[S_WD][0:64, ts_], reads=[F[S_WD]], writes=[twd], q="act")
        P.dma(adt[0:64, :Lb], F[S_AD][0:64, ts_], reads=[F[S_AD]], writes=[adt])
        P.op("act", "activation", out=twd[0:64, :Lb], in_=twd[0:64, :Lb], func=AF.Tanh, reads=[twd], writes=[twd])
        P.op("pe", "matmul", bankA[:, :Lb], lhsT=wup[:], rhs=twd[0:64, :Lb], start=True, stop=True, reads=[wup, twd], writes=allA)
        P.op("act", "activation", out=sig[:, :Lb], in_=bankA[:, :Lb], func=AF.Sigmoid, bias=w0[:, l, d, hp:hp + 1], reads=allA + [w0], writes=[sig])
        P.op("pe", "matmul", bankA[:, :Lb], lhsT=aup[:], rhs=adt[0:64, :Lb], start=True, stop=True, reads=[aup, adt] + allA, writes=allA)
        P.op("act", "activation", out=a_[:, :Lb], in_=bankA[:, :Lb], func=AF.Sigmoid, bias=a0[:, l, d, hp:hp + 1], reads=allA + [a0], writes=[a_])
        if d == 0:
            P.op("dve", "tensor_tensor_scan", out=Lc[:, :Lb], data0=rm[:, :Lb], data1=sig[:, :Lb], initial=0.0, op0=ALU.mult, op1=ALU.add,
                 reads=[rm, sig], writes=[Lc])
        else:
            P.op("dve", "tensor_tensor_scan", out=Lc[:, :Lb][:, ::-1], data0=rm[:, :Lb][:, ::-1], data1=sig[:, :Lb][:, ::-1], initial=0.0,
                 op0=ALU.mult, op1=ALU.add, reads=[rm, sig], writes=[Lc])
        P.op("act", "activation", out=Ep[:, :Lb], in_=Lc[:, :Lb], func=AF.Exp, scale=-C0, reads=[Lc], writes=[Ep])
        P.op("act", "activation", out=t1[:, :Lb], in_=Lc[:, :Lb], func=AF.Exp, scale=C0, reads=[Lc], writes=[t1])
        P.op("dve", "tensor_tensor", out=t2[:, :Lb], in0=Lc[:, :Lb], in1=sig[:, :Lb], op=ALU.subtract, reads=[Lc, sig], writes=[t2])
        P.op("act", "activation", out=t2[:, :Lb], in_=t2[:, :Lb], func=AF.Exp, scale=-C0, reads=[t2], writes=[t2])
        P.op("dve", "tensor_scalar", out=kk_[:, :Lb], in0=k_[:, :Lb], scalar1=kkk[:, l, hp:hp + 1], scalar2=None, op0=ALU.mult, reads=[k_, kkk], writes=[kk_])
        P.op("dve", "tensor_tensor", out=t3[:, :Lb], in0=kk_[:, :Lb], in1=kk_[:, :Lb], op=ALU.mult, reads=[kk_], writes=[t3])
        P.op("pe", "matmul", bankA[:, :Lb], lhsT=blk[:], rhs=t3[:, :Lb], start=True, stop=True, reads=[blk, t3] + allA, writes=allA)
        P.op("act", "activation", out=t3[:, :Lb], in_=bankA[:, :Lb], func=AF.Sqrt, reads=allA, writes=[t3])
        P.op("dve", "tensor_scalar", out=t3[:, :Lb], in0=t3[:, :Lb], scalar1=1e-12, scalar2=None, op0=ALU.max, reads=[t3], writes=[t3])
        P.op("dve", "reciprocal", out=t3[:, :Lb], in_=t3[:, :Lb], reads=[t3], writes=[t3])
        P.op("dve", "tensor_tensor", out=kk_[:, :Lb], in0=kk_[:, :Lb], in1=t3[:, :Lb], op=ALU.mult, reads=[kk_, t3], writes=[kk_])
        P.op("dve", "tensor_scalar", out=t3[:, :Lb], in0=a_[:, :Lb], scalar1=ka[:, l, hp:hp + 1], scalar2=omka[:, l, hp:hp + 1], op0=ALU.mult, op1=ALU.add,
             reads=[a_, ka, omka], writes=[t3])
        P.op("dve", "tensor_tensor", out=t3[:, :Lb], in0=t3[:, :Lb], in1=k_[:, :Lb], op=ALU.mult, reads=[t3, k_], writes=[t3])
        P.op("dve", "tensor_tensor", out=kt[:, :Lb], in0=t3[:, :Lb], in1=t1[:, :Lb], op=ALU.mult, reads=[t3, t1], writes=[kt])
        P.op("dve", "tensor_tensor", out=rt[:, :Lb], in0=r_[:, :Lb], in1=Ep[:, :Lb], op=ALU.mult, reads=[r_, Ep], writes=[rt])
        P.op("dve", "scalar_tensor_tensor", out=al[:, :Lb], in0=kk_[:, :Lb], scalar=-1.0, in1=t2[:, :Lb], op0=ALU.mult, op1=ALU.mult, reads=[kk_, t2], writes=[al])
        P.op("dve", "tensor_tensor", out=be[:, :Lb], in0=kk_[:, :Lb], in1=a_[:, :Lb], op=ALU.mult, reads=[kk_, a_], writes=[be])
        P.op("dve", "tensor_tensor", out=be[:, :Lb], in0=be[:, :Lb], in1=t1[:, :Lb], op=ALU.mult, reads=[be, t1], writes=[be])
        if hp == 0 and d == 1 and t0 == CTX:
            for nm_, tl_ in (("dbg_sig", sig), ("dbg_L", Lc), ("dbg_Ep", Ep), ("dbg_Em", t1), ("dbg_rt", rt), ("dbg_al", al), ("dbg_be", be), ("dbg_kt", kt), ("dbg_kk", kk_)):
                dbg(g, nm_, tl_, tl_[:, :Lb], [128, Lb])
        yield
        nch = Lb // CH
        for c in (range(nch) if d == 0 else range(nch - 1, -1, -1)):
            cs = slice(c * CH, (c + 1) * CH)
            for h in range(2):
                hs = H2[h]
                for j, src in enumerate((be, kt, vv)):
                    P.op("pe", "matmul", TM_[hs, 64 * j:64 * j + 64], lhsT=_r(src[hs, cs]), rhs=_r(ident[hs, hs]), start=True, stop=True, reads=[src, ident], writes=[pTM])
            P.op("act", "copy", out=TMs[:], in_=TM_, reads=[pTM], writes=[TMs])
            for h in range(2):
                hs = H2[h]
                P.op("pe", "matmul", XY_[hs, 64 * h:64 * h + 64], lhsT=_r(be[hs, cs]), rhs=_r(al[hs, cs]), start=True, stop=True, reads=[be, al], writes=[pXY])
                P.op("pe", "matmul", XY_[hs, 128 + 64 * h:192 + 64 * h], lhsT=_r(al[hs, cs]), rhs=_r(be[hs, cs]), start=True, stop=True, reads=[be, al], writes=[pXY])
                P.op("pe", "matmul", A_[hs, 0:64], lhsT=_r(be[hs, cs]), rhs=_r(rt[hs, cs]), start=True, stop=True, reads=[be, rt], writes=[pA])
                P.op("pe", "matmul", A_[hs, 64:128], lhsT=_r(kt[hs, cs]), rhs=_r(al[hs, cs]), start=True, stop=True, reads=[kt, al], writes=[pA])
                P.op("pe", "matmul", A_[hs, 128:192], lhsT=_r(kt[hs, cs]), rhs=_r(rt[hs, cs]), start=True, stop=True, reads=[kt, rt], writes=[pA])
            cur = 0
            P.op("dve", "tensor_tensor", out=XYs[cur][:], in0=XY_, in1=maskXY[:], op=ALU.mult, reads=[pXY, maskXY], writes=[XYs[cur]])
            P.op("dve", "tensor_tensor", out=Am[:], in0=A_, in1=maskA[:], op=ALU.mult, reads=[pA, maskA], writes=[Am])
            rc = 0
            P.op("dve", "tensor_tensor", out=Rs[rc][:], in0=XYs[cur][:, 0:128], in1=ident[:], op=ALU.add, reads=[XYs[cur], ident], writes=[Rs[rc]])
            yield
            for lvl in range(5):
                X, Y = XYs[cur][:, 0:128], XYs[cur][:, 128:256]
                nxt = 1 - cur
                if lvl < 4:
                    P.op("pe", "matmul", XY_[:, 0:128], lhsT=_r(Y), rhs=_r(X), start=True, stop=True, reads=[XYs[cur]], writes=[pXY])
                P.op("pe", "matmul", XY_[:, 128:256], lhsT=_r(X), rhs=_r(Y), start=True, stop=True, reads=[XYs[cur]], writes=[pXY])
                if lvl < 4:
                    P.op("act", "copy", out=XYs[nxt][:], in_=XY_, reads=[pXY], writes=[XYs[nxt]])
                else:
                    P.op("act", "copy", out=XYs[nxt][:, 128:256], in_=XY_[:, 128:256], reads=[pXY], writes=[XYs[nxt]])
                yield
                P.op("pe", "matmul", R_, lhsT=_r(XYs[nxt][:, 128:256]), rhs=_r(Rs[rc][:]), start=True, stop=True, reads=[XYs[nxt], Rs[rc]], writes=[pR])
                P.op("dve", "tensor_tensor", out=Rs[1 - rc][:], in0=Rs[rc][:], in1=R_, op=ALU.add, reads=[Rs[rc], pR], writes=[Rs[1 - rc]])
                rc = 1 - rc
                cur = nxt
                yield
            Rf = Rs[rc]
            for h in range(2):
                hs = H2[h]
                P.op("pe", "matmul", W0_[hs, :], lhsT=_r(al[hs, cs]), rhs=_r(ST[hs, :]), start=True, stop=False, reads=[al, ST], writes=[pW0])
                P.op("pe", "matmul", W0_[hs, :], lhsT=_r(Am[hs, 64:128]), rhs=_r(TMs[hs, 128:192]), start=False, stop=True, reads=[Am, TMs], writes=[pW0])
            P.op("act", "copy", out=W0s[:], in_=W0_, reads=[pW0], writes=[W0s])
            yield
            P.op("pe", "matmul", U_, lhsT=_r(Rf[:]), rhs=_r(W0s[:]), start=True, stop=True, reads=[Rf, W0s], writes=[pU])
            P.op("dve", "tensor_copy", out=Us[:], in_=U_, reads=[pU], writes=[Us])
            yield
            for h in range(2):
                hs = H2[h]
                P.op("pe", "matmul", Y_[hs, :], lhsT=_r(ST[hs, :]), rhs=_r(rt[hs, cs]), start=True, stop=False, reads=[ST, rt], writes=[pY])
                P.op("pe", "matmul", Y_[hs, :], lhsT=_r(Us[hs, :]), rhs=_r(Am[hs, 0:64]), start=False, stop=False, reads=[Us, Am], writes=[pY])
                P.op("pe", "matmul", Y_[hs, :], lhsT=_r(TMs[hs, 128:192]), rhs=_r(Am[hs, 128:192]), start=False, stop=True, reads=[TMs, Am], writes=[pY])
            P.op("act", "copy", out=Yb[:, cs], in_=Y_, reads=[pY], writes=[Yb])
            if hp == 0 and d == 1 and t0 == CTX and c == 0:
                for nm_, tl_, w_ in (("dbg_Am", Am, 192), ("dbg_Us", Us, 64), ("dbg_TMs", TMs, 192), ("dbg_W0s", W0s, 64), ("dbg_Rf", Rf, 128), ("dbg_ST", ST, 64), ("dbg_XY", XYs[cur], 256)):
                    dbg(g, nm_, tl_, tl_[:, :w_], [128, w_])
                dbg(g, "dbg_Yb", Yb, Yb[:, 0:128], [128, 128])
            for h in range(2):
                hs = H2[h]
                P.op("pe", "matmul", SP_[hs, :], lhsT=_r(TMs[hs, 0:64]), rhs=_r(Us[hs, :]), start=True, stop=False, reads=[TMs, Us], writes=[pSP])
                P.op("pe", "matmul", SP_[hs, :], lhsT=_r(TMs[hs, 64:128]), rhs=_r(TMs[hs, 128:192]), start=False, stop=True, reads=[TMs], writes=[pSP])
            pc = Ep[:, c * CH + CH - 1:c * CH + CH] if d == 0 else Ep[:, c * CH:c * CH + 1]
            P.op("dve", "tensor_scalar", out=ST[:], in0=ST[:], scalar1=pc, scalar2=None, op0=ALU.mult, reads=[ST, Ep], writes=[ST])
            P.op("dve", "scalar_tensor_tensor", out=ST[:], in0=SP_, scalar=pc, in1=ST[:], op0=ALU.mult, op1=ALU.add, reads=[pSP, Ep, ST], writes=[ST])
            yield
        P.dma(YS[d][hp][:, ts_], Yb[:, :Lb], reads=[Yb], writes=[YS[d][hp]], q="act")
        yield


def stage_rwkv_scan(g, l):
    P = g.P
    T = g.T
    YS = [[scratch(g, "YS%d_%d" % (d, hp), [128, T]) for hp in range(8)] for d in range(2)]
    lat = [(CTX + i * 512, min(512, T - CTX - i * 512)) for i in range((T - CTX + 511) // 512)]
    bl = [[(0, CTX)] + lat, [(0, CTX)] + lat[::-1]]
    for pair in range(4):
        with P.scope():
            tr = [P.sbuf("rtr%d" % i, [128, 512], F32) for i in range(11)]
            gens = []
            k = 0
            for hp in (2 * pair, 2 * pair + 1):
                for d in range(2):
                    gens.append(rwkv_stream(g, l, hp, d, g.ps[2 * k], g.ps[2 * k + 1], bl[d], tr, YS))
                    k += 1
            STAG = int(os.environ.get("SCAN_STAG", "0"))
            start = {id(ge): i * STAG for i, ge in enumerate(gens)}
            rnd = 0
            while gens:
                for ge in list(gens):
                    if rnd < start[id(ge)]:
                        continue
                    try:
                        next(ge)
                    except StopIteration:
                        gens.remove(ge)
                rnd += 1


def stage_rwkv_out(g, l):
    P = g.P
    T = g.T
    F = [scratch(g, "F%d" % i, [128, T]) for i in range(NSEG)]
    YS = [[scratch(g, "YS%d_%d" % (d, hp), [128, T]) for hp in range(8)] for d in range(2)]
    YR = scratch(g, "YR", [RW_W, T], BF16)
    s = g.s
    blk = g.c["blk"]
    a0, ka, omka, rk, lg, lb = s["a0"], s["k_a"], s["omka"], s["r_k"], s["lnxg"], s["lnxb"]
    for hp in range(8):
        with P.scope():
            aups = []
            for d in range(2):
                t = P.sbuf("roa%d" % d, [64, 128], F32)
                P.dma(t[:], g.d["rwkv_a_up"][l, d, :, hp * 128:(hp + 1) * 128], reads=[g.d["rwkv_a_up"]], writes=[t])
                aups.append(t)
            gu0 = P.sbuf("rogu0", [128, 128], F32)
            gu1 = P.sbuf("rogu1", [32, 128], F32)
            P.dma(gu0[:], g.d["rwkv_g_up"][l, 0:128, hp * 128:(hp + 1) * 128], reads=[g.d["rwkv_g_up"]], writes=[gu0])
            P.dma(gu1[:], g.d["rwkv_g_up"][l, 128:160, hp * 128:(hp + 1) * 128], reads=[g.d["rwkv_g_up"]], writes=[gu1])
            names = ("y", "yr", "r", "k", "v", "ad", "g0", "g1", "t1", "t2", "t3")
            tl = {n_: [P.sbuf("ro_%s%d" % (n_, i), [128, 512], F32) for i in range(2)] for n_ in names}
            ob = [P.sbuf("ro_ob%d" % i, [128, 512], BF16) for i in range(2)]
            for bi, (t0, Lb) in enumerate(blocks_of(g)):
                ts_ = slice(t0, t0 + Lb)
                y, yr, r_, k_, v_, ad, g0, g1, t1, t2, t3 = [tl[n_][bi % 2] for n_ in names]
                o = ob[bi % 2]
                P.dma(y[:, :Lb], YS[0][hp][:, ts_], reads=[YS[0][hp]], writes=[y])
                P.dma(yr[:, :Lb], YS[1][hp][:, ts_], reads=[YS[1][hp]], writes=[yr], q="act")
                P.dma(r_[:, :Lb], F[S_R + hp][:, ts_], reads=[F[S_R + hp]], writes=[r_])
                P.dma(k_[:, :Lb], F[S_K + hp][:, ts_], reads=[F[S_K + hp]], writes=[k_], q="act")
                P.dma(v_[:, :Lb], F[S_V + hp][:, ts_], reads=[F[S_V + hp]], writes=[v_])
                P.dma(ad[0:64, :Lb], F[S_AD][0:64, ts_], reads=[F[S_AD]], writes=[ad], q="act")
                P.dma(g0[:, :Lb], F[S_GD][:, ts_], reads=[F[S_GD]], writes=[g0])
                P.dma(g1[0:32, :Lb], F[S_GD + 1][0:32, ts_], reads=[F[S_GD + 1]], writes=[g1], q="act")
                p1, p2, p3, p4 = [g.ps[(4 * bi + i) % 8] for i in range(4)]
                P.op("dve", "tensor_tensor", out=y[:, :Lb], in0=y[:, :Lb], in1=yr[:, :Lb], op=ALU.add, reads=[y, yr], writes=[y])
                P.op("pe", "matmul", p1[:, :Lb], lhsT=blk[:], rhs=y[:, :Lb], start=True, stop=True, reads=[blk, y], writes=[p1])
                P.op("dve", "scalar_tensor_tensor", out=y[:, :Lb], in0=p1[:, :Lb], scalar=-1.0 / 64, in1=y[:, :Lb], op0=ALU.mult, op1=ALU.add, reads=[p1, y], writes=[y])
                P.op("act", "activation", out=t1[:, :Lb], in_=y[:, :Lb], func=AF.Square, reads=[y], writes=[t1])
                P.op("pe", "matmul", p2[:, :Lb], lhsT=blk[:], rhs=t1[:, :Lb], start=True, stop=True, reads=[blk, t1], writes=[p2])
                P.op("act", "activation", out=t1[:, :Lb], in_=p2[:, :Lb], func=AF.Sqrt, bias=GN_EPS, scale=1.0 / 64, reads=[p2], writes=[t1])
                P.op("dve", "reciprocal", out=t1[:, :Lb], in_=t1[:, :Lb], reads=[t1], writes=[t1])
                P.op("dve", "tensor_tensor", out=y[:, :Lb], in0=y[:, :Lb], in1=t1[:, :Lb], op=ALU.mult, reads=[y, t1], writes=[y])
                P.op("dve", "tensor_scalar", out=y[:, :Lb], in0=y[:, :Lb], scalar1=lg[:, l, hp:hp + 1], scalar2=lb[:, l, hp:hp + 1], op0=ALU.mult, op1=ALU.add,
                     reads=[y, lg, lb], writes=[y])
                for d in range(2):
                    P.op("pe", "matmul", p3[:, :Lb], lhsT=aups[d][:], rhs=ad[0:64, :Lb], start=True, stop=True, reads=[aups[d], ad], writes=[p3])
                    tt = t2 if d == 0 else t3
                    P.op("act", "activation", out=tt[:, :Lb], in_=p3[:, :Lb], func=AF.Sigmoid, bias=a0[:, l, d, hp:hp + 1], reads=[p3, a0], writes=[tt])
                P.op("dve", "tensor_tensor", out=t2[:, :Lb], in0=t2[:, :Lb], in1=t3[:, :Lb], op=ALU.add, reads=[t2, t3], writes=[t2])
                P.op("dve", "tensor_scalar", out=t2[:, :Lb], in0=t2[:, :Lb], scalar1=ka[:, l, hp:hp + 1], scalar2=None, op0=ALU.mult, reads=[t2, ka], writes=[t2])
                P.op("dve", "tensor_scalar", out=t3[:, :Lb], in0=k_[:, :Lb], scalar1=omka[:, l, hp:hp + 1], scalar2=2.0, op0=ALU.mult, op1=ALU.mult, reads=[k_, omka], writes=[t3])
                P.op("dve", "tensor_tensor", out=t2[:, :Lb], in0=t2[:, :Lb], in1=k_[:, :Lb], op=ALU.mult, reads=[t2, k_], writes=[t2])
                P.op("dve", "tensor_tensor", out=t2[:, :Lb], in0=t2[:, :Lb], in1=t3[:, :Lb], op=ALU.add, reads=[t2, t3], writes=[t2])
                P.op("dve", "tensor_tensor", out=t2[:, :Lb], in0=t2[:, :Lb], in1=r_[:, :Lb], op=ALU.mult, reads=[t2, r_], writes=[t2])
                P.op("dve", "tensor_scalar", out=t2[:, :Lb], in0=t2[:, :Lb], scalar1=rk[:, l, hp:hp + 1], scalar2=0.5, op0=ALU.mult, op1=ALU.mult, reads=[t2, rk], writes=[t2])
                P.op("pe", "matmul", p4[:, :Lb], lhsT=blk[:], rhs=t2[:, :Lb], start=True, stop=True, reads=[blk, t2], writes=[p4])
                P.op("dve", "tensor_tensor", out=t2[:, :Lb], in0=v_[:, :Lb], in1=p4[:, :Lb], op=ALU.mult, reads=[v_, p4], writes=[t2])
                P.op("dve", "tensor_tensor", out=y[:, :Lb], in0=y[:, :Lb], in1=t2[:, :Lb], op=ALU.add, reads=[y, t2], writes=[y])
                P.op("act", "activation", out=g0[:, :Lb], in_=g0[:, :Lb], func=AF.Sigmoid, reads=[g0], writes=[g0])
                P.op("act", "activation", out=g1[0:32, :Lb], in_=g1[0:32, :Lb], func=AF.Sigmoid, reads=[g1], writes=[g1])
                P.op("pe", "matmul", p1[:, :Lb], lhsT=gu0[:], rhs=g0[:, :Lb], start=True, stop=False, reads=[gu0, g0], writes=[p1])
                P.op("pe", "matmul", p1[:, :Lb], lhsT=gu1[0:32, :], rhs=g1[0:32, :Lb], start=False, stop=True, reads=[gu1, g1], writes=[p1])
                P.op("dve", "tensor_tensor", out=o[:, :Lb], in0=y[:, :Lb], in1=p1[:, :Lb], op=ALU.mult, reads=[y, p1], writes=[o])
                P.dma(YR[hp * 128:(hp + 1) * 128, ts_], o[:, :Lb], reads=[o], writes=[YR], q="act")


def stage_mla(g, l, ctx_out):
    P = g.P
    T = g.T
    F = [scratch(g, "F%d" % i, [128, T]) for i in range(NSEG)]
    YM = scratch(g, "YM", [2048, T], BF16)
    ones, ident, Jm = g.c["ones"], g.c["ident"], g.c["Jm"]
    qng, kvg = g.s["qng"], g.s["kvg"]
    blocks = blocks_of(g)
    NKC = T // 128
    with P.scope():
        onesb = P.sbuf("onesb", [128, 128], BF16)
        P.op("dve", "memset", onesb[:], 1.0, reads=[], writes=[onesb])
        cos = load_const(g, "cos", [64, g.SEQ])
        sin = load_const(g, "sin", [64, g.SEQ], q="act")
        qn = P.sbuf("qn", [128, 4, T], BF16)
        kvn = P.sbuf("kvn", [128, 2, T], BF16)
        kr = P.sbuf("kr", [64, T], BF16)
        with P.scope():
            xqs = [P.sbuf("xq%d" % i, [128, 4, 512], F32) for i in range(2)]
            xks = [P.sbuf("xk%d" % i, [128, 2, 512], F32) for i in range(2)]
            xrs = [P.sbuf("xr%d" % i, [64, 512], F32) for i in range(2)]
            sqs = [P.sbuf("msq%d" % i, [128, 512], F32) for i in range(2)]
            rs1 = P.sbuf("mrs1", [128, 512], F32)
            rs2 = P.sbuf("mrs2", [128, 512], F32)
            m1 = P.sbuf("mm1", [64, 512], F32)
            m2 = P.sbuf("mm2", [64, 512], F32)
            for bi, (t0, n) in enumerate(blocks):
                xq, xk, xr = xqs[bi % 2], xks[bi % 2], xrs[bi % 2]
                for i in range(4):
                    P.dma(xq[:, i, :n], F[S_Q + i][:, t0:t0 + n], reads=[F[S_Q + i]], writes=[xq], q="sp" if i % 2 == 0 else "act")
                for i in range(2):
                    P.dma(xk[:, i, :n], F[S_KV + i][:, t0:t0 + n], reads=[F[S_KV + i]], writes=[xk], q="sp" if i % 2 == 0 else "act")
                P.dma(xr[0:64, :n], F[S_KR][0:64, t0:t0 + n], reads=[F[S_KR]], writes=[xr])
                pa, pb, pj = g.ps[0], g.ps[1], g.ps[2]
                for (x_, nchunk, pp, rs, gam, dst) in ((xq, 4, pa, rs1, qng, qn), (xk, 2, pb, rs2, kvg, kvn)):
                    for i in range(nchunk):
                        sq = sqs[i % 2]
                        P.op("act", "activation", out=sq[:, :n], in_=x_[:, i, :n], func=AF.Square, reads=[x_], writes=[sq])
                        P.op("pe", "matmul", pp[:, :n], lhsT=ones[:], rhs=sq[:, :n], start=(i == 0), stop=(i == nchunk - 1), reads=[ones, sq], writes=[pp])
                    P.op("act", "activation", out=rs[:, :n], in_=pp[:, :n], func=AF.Sqrt, bias=LN_EPS, scale=1.0 / (128 * nchunk), reads=[pp], writes=[rs])
                    P.op("dve", "reciprocal", out=rs[:, :n], in_=rs[:, :n], reads=[rs], writes=[rs])
                    for i in range(nchunk):
                        P.op("dve", "scalar_tensor_tensor", out=dst[:, i, t0:t0 + n], in0=x_[:, i, :n], scalar=gam[:, l, i:i + 1], in1=rs[:, :n],
                             op0=ALU.mult, op1=ALU.mult, reads=[x_, gam, rs], writes=[dst])
                if t0 < CTX:
                    P.op("act", "copy", out=kr[0:64, t0:t0 + n], in_=xr[0:64, :n], reads=[xr], writes=[kr])
                else:
                    P.op("pe", "matmul", pj[0:64, :n], lhsT=Jm[:], rhs=xr[0:64, :n], start=True, stop=True, reads=[Jm, xr], writes=[pj])
                    P.op("dve", "tensor_tensor", out=m1[:, :n], in0=xr[0:64, :n], in1=cos[:, t0 - CTX:t0 - CTX + n], op=ALU.mult, reads=[xr, cos], writes=[m1])
                    P.op("dve", "tensor_tensor", out=m2[:, :n], in0=pj[0:64, :n], in1=sin[:, t0 - CTX:t0 - CTX + n], op=ALU.mult, reads=[pj, sin], writes=[m2])
                    P.op("dve", "tensor_tensor", out=kr[0:64, t0:t0 + n], in0=m1[:, :n], in1=m2[:, :n], op=ALU.add, reads=[m1, m2], writes=[kr])
        wuq, wuk, wuv = g.d["mla_w_uq"], g.d["mla_w_uk"], g.d["mla_w_uv"]
        wqs = [P.sbuf("wq%d" % i, [128, 4, 192], BF16) for i in range(2)]
        wks = [P.sbuf("wk%d" % i, [128, 2, 128], BF16) for i in range(2)]
        wvs = [P.sbuf("wv%d" % i, [128, 2, 128], BF16) for i in range(2)]

        def load_head_w(hd_):
            P.dma(wqs[hd_ % 2][:], wuq[l, :, hd_ * 192:(hd_ + 1) * 192].rearrange("(kc p) n -> p kc n", p=128), reads=[wuq], writes=[wqs[hd_ % 2]], q="pool")
            P.dma(wks[hd_ % 2][:], wuk[l, :, hd_ * 128:(hd_ + 1) * 128].rearrange("(kc p) n -> p kc n", p=128), reads=[wuk], writes=[wks[hd_ % 2]], q="pool")
            P.dma(wvs[hd_ % 2][:], wuv[l, :, hd_ * 128:(hd_ + 1) * 128].rearrange("(kc p) n -> p kc n", p=128), reads=[wuv], writes=[wvs[hd_ % 2]], q="pool")

        load_head_w(0)
        for hd in range(16):
            with P.scope():
                wq, wk, wv = wqs[hd % 2], wks[hd % 2], wvs[hd % 2]
                if hd + 1 < 16:
                    load_head_w(hd + 1)
                Kn = P.sbuf("Kn", [128, T], BF16)
                Qn = P.sbuf("Qn", [128, T], BF16)
                Qr = P.sbuf("Qr", [64, T], BF16)
                Va = P.sbuf("Va", [128, NKC, 128], BF16)
                xq32 = P.sbuf("xq32", [64, 512], F32)
                m1 = P.sbuf("hm1", [64, 512], F32)
                m2 = P.sbuf("hm2", [64, 512], F32)
                for bi, (t0, n) in enumerate(blocks):
                    pk, pq, pr, pj = g.ps[0], g.ps[1], g.ps[2], g.ps[3]
                    for kc in range(2):
                        P.op("pe", "matmul", pk[:, :n], lhsT=wk[:, kc, :], rhs=kvn[:, kc, t0:t0 + n], start=(kc == 0), stop=(kc == 1), reads=[wk, kvn], writes=[pk])
                    P.op("act", "copy", out=Kn[:, t0:t0 + n], in_=pk[:, :n], reads=[pk], writes=[Kn])
                    if t0 < CTX and not ctx_out:
                        continue
                    for kc in range(4):
                        P.op("pe", "matmul", pq[:, :n], lhsT=wq[:, kc, 0:128], rhs=qn[:, kc, t0:t0 + n], start=(kc == 0), stop=(kc == 3), reads=[wq, qn], writes=[pq])
                    P.op("dve", "tensor_copy", out=Qn[:, t0:t0 + n], in_=pq[:, :n], reads=[pq], writes=[Qn])
                    for kc in range(4):
                        P.op("pe", "matmul", pr[0:64, :n], lhsT=wq[:, kc, 128:192], rhs=qn[:, kc, t0:t0 + n], start=(kc == 0), stop=(kc == 3), reads=[wq, qn], writes=[pr])
                    if t0 < CTX:
                        P.op("act", "copy", out=Qr[0:64, t0:t0 + n], in_=pr[0:64, :n], reads=[pr], writes=[Qr])
                    else:
                        P.op("act", "copy", out=xq32[:, :n], in_=pr[0:64, :n], reads=[pr], writes=[xq32])
                        P.op("pe", "matmul", pj[0:64, :n], lhsT=Jm[:], rhs=xq32[:, :n], start=True, stop=True, reads=[Jm, xq32], writes=[pj])
                        P.op("dve", "tensor_tensor", out=m1[:, :n], in0=xq32[:, :n], in1=cos[:, t0 - CTX:t0 - CTX + n], op=ALU.mult, reads=[xq32, cos], writes=[m1])
                        P.op("dve", "tensor_tensor", out=m2[:, :n], in0=pj[0:64, :n], in1=sin[:, t0 - CTX:t0 - CTX + n], op=ALU.mult, reads=[pj, sin], writes=[m2])
                        P.op("dve", "tensor_tensor", out=Qr[0:64, t0:t0 + n], in0=m1[:, :n], in1=m2[:, :n], op=ALU.add, reads=[m1, m2], writes=[Qr])
                for tc_ in range(NKC):
                    pv = g.ps[2 + tc_ % 2]
                    for kc in range(2):
                        P.op("pe", "matmul", pv[:, 0:128], lhsT=kvn[:, kc, tc_ * 128:(tc_ + 1) * 128], rhs=wv[:, kc, :], start=(kc == 0), stop=(kc == 1),
                             reads=[kvn, wv], writes=[pv])
                    if tc_ % 2 == 0:
                        P.op("act", "copy", out=Va[:, tc_, :], in_=pv[:, 0:128], reads=[pv], writes=[Va])
                    else:
                        P.op("dve", "tensor_copy", out=Va[:, tc_, :], in_=pv[:, 0:128], reads=[pv], writes=[Va])
                pts = [P.sbuf("PT%d" % i, [128, 512], BF16) for i in range(3)]
                stgs = [P.sbuf("ostg%d" % i, [128, 512], BF16) for i in range(2)]
                rss = [P.sbuf("ors%d" % i, [128, 512], F32) for i in range(2)]
                qblocks = [b_ for b_ in blocks if (b_[0] >= CTX or ctx_out)]
                for qi, (q0, nq) in enumerate(qblocks):
                    keys = list(range(CTX // 128)) if q0 < CTX else list(range(NKC))
                    nk = len(keys)
                    accO = g.ps[4 + 2 * (qi % 2)]
                    accS = g.ps[5 + 2 * (qi % 2)]

                    def qk(ki):
                        kc = keys[ki]
                        S = g.ps[ki % 2]
                        P.op("pe", "matmul", S[:, :nq], lhsT=Kn[:, kc * 128:(kc + 1) * 128], rhs=Qn[:, q0:q0 + nq], start=True, stop=False, reads=[Kn, Qn], writes=[S])
                        P.op("pe", "matmul", S[:, :nq], lhsT=kr[0:64, kc * 128:(kc + 1) * 128], rhs=Qr[0:64, q0:q0 + nq], start=False, stop=True, reads=[kr, Qr], writes=[S])

                    qk(0)
                    for ki in range(nk):
                        if ki + 1 < nk:
                            qk(ki + 1)
                        kc = keys[ki]
                        S = g.ps[ki % 2]
                        PT = pts[ki % 3]
                        P.op("act", "activation", out=PT[:, :nq], in_=S[:, :nq], func=AF.Exp, scale=ATT_SCALE, reads=[S], writes=[PT])
                        P.op("pe", "matmul", accO[:, :nq], lhsT=Va[:, kc, :], rhs=PT[:, :nq], start=(ki == 0), stop=(ki == nk - 1), reads=[PT, Va], writes=[accO])
                        P.op("pe", "matmul", accS[:, :nq], lhsT=onesb[:], rhs=PT[:, :nq], start=(ki == 0), stop=(ki == nk - 1), reads=[PT, onesb], writes=[accS])
                    rs, stg = rss[qi % 2], stgs[qi % 2]
                    P.op("dve", "reciprocal", out=rs[:, :nq], in_=accS[:, :nq], reads=[accS], writes=[rs])
                    P.op("dve", "tensor_tensor", out=stg[:, :nq], in0=accO[:, :nq], in1=rs[:, :nq], op=ALU.mult, reads=[accO, rs], writes=[stg])
                    P.dma(YM[hd * 128:(hd + 1) * 128, q0:q0 + nq], stg[:, :nq], reads=[stg], writes=[YM], q="sp")


def stage_c(g, l, dst, last):
    P = g.P
    T = g.T
    XL = scratch(g, "XL", [D, T])
    F = [scratch(g, "F%d" % i, [128, T]) for i in range(NSEG)]
    YR = scratch(g, "YR", [RW_W, T], BF16)
    YM = scratch(g, "YM", [2048, T], BF16)
    YL = scratch(g, "YL", [LRU_W, T], BF16)
    prw, pml, plr, wout = g.d["proj_rwkv"], g.d["proj_mla"], g.d["proj_lru"], g.d["w_out"]
    for (t0, NT, m) in token_tiles(g, with_ctx=not last):
        with P.scope():
            s = P.sbuf("cs_", [128, KC, NT], F32)
            P.dma(s[:], fview(XL)[:, :, t0:t0 + NT], reads=[XL], writes=[s])
            P.op("act", "mul", out=s[:], in_=s[:], mul=ALPHA, reads=[s], writes=[s])
            with P.scope():
                yr = P.sbuf("cyr", [128, 8, NT], BF16)
                ym = P.sbuf("cym", [128, 16, NT], BF16)
                yl = P.sbuf("cyl", [128, 8, NT], BF16)
                P.dma(yr[:], YR[:].rearrange("(kc p) t -> p kc t", p=128)[:, :, t0:t0 + NT], reads=[YR], writes=[yr])
                P.dma(ym[:], YM[:].rearrange("(kc p) t -> p kc t", p=128)[:, :, t0:t0 + NT], reads=[YM], writes=[ym], q="act")
                P.dma(yl[:], YL[:].rearrange("(kc p) t -> p kc t", p=128)[:, :, t0:t0 + NT], reads=[YL], writes=[yl])
                ybf = P.sbuf("cyb", [128, KC, NT], BF16)
                gts = [P.sbuf("cgt%d" % i, [128, 3, NT], F32) for i in range(2)]
                pws = [P.sbuf("cpw%d" % i, [128, 32, 128], BF16) for i in range(2)]
                t1 = P.sbuf("ct1", [128, NT], F32)
                t2 = P.sbuf("ct2", [128, NT], F32)
                for dc in range(KC):
                    pw, gt = pws[dc % 2], gts[dc % 2]
                    cs_ = slice(dc * 128, (dc + 1) * 128)
                    wload(g, pw[:, 0:8, :], pw, "proj_rwkv", (l,), dc * 128)
                    wload(g, pw[:, 8:24, :], pw, "proj_mla", (l,), dc * 128)
                    wload(g, pw[:, 24:32, :], pw, "proj_lru", (l,), dc * 128)
                    for bi in range(3):
                        P.dma(gt[:, bi, :], F[S_GATE + 16 * bi + dc][:, t0:t0 + NT], reads=[F[S_GATE + 16 * bi + dc]], writes=[gt], q="sp" if bi != 1 else "act")
                    p1, p2, p3 = g.ps[0], g.ps[1], g.ps[2]
                    for kc in range(8):
                        P.op("pe", "matmul", p1[:, :NT], lhsT=pw[:, kc, :], rhs=yr[:, kc, :], start=(kc == 0), stop=(kc == 7), reads=[pw, yr], writes=[p1])
                    for kc in range(16):
                        P.op("pe", "matmul", p2[:, :NT], lhsT=pw[:, 8 + kc, :], rhs=ym[:, kc, :], start=(kc == 0), stop=(kc == 15), reads=[pw, ym], writes=[p2])
                    for kc in range(8):
                        P.op("pe", "matmul", p3[:, :NT], lhsT=pw[:, 24 + kc, :], rhs=yl[:, kc, :], start=(kc == 0), stop=(kc == 7), reads=[pw, yl], writes=[p3])
                    P.op("dve", "tensor_tensor", out=t1[:], in0=gt[:, 0, :], in1=p1[:, :NT], op=ALU.mult, reads=[gt, p1], writes=[t1])
                    P.op("dve", "tensor_tensor", out=t2[:], in0=gt[:, 1, :], in1=p2[:, :NT], op=ALU.mult, reads=[gt, p2], writes=[t2])
                    P.op("dve", "tensor_tensor", out=t1[:], in0=t1[:], in1=t2[:], op=ALU.add, reads=[t1, t2], writes=[t1])
                    P.op("dve", "tensor_tensor", out=t2[:], in0=gt[:, 2, :], in1=p3[:, :NT], op=ALU.mult, reads=[gt, p3], writes=[t2])
                    P.op("dve", "tensor_tensor", out=ybf[:, dc, :], in0=t1[:], in1=t2[:], op=ALU.add, reads=[t1, t2], writes=[ybf])
                wos = [P.sbuf("cwo%d" % i, [128, KC, 128], BF16) for i in range(2)]
                for dc in range(KC):
                    wo = wos[dc % 2]
                    wload(g, wo[:], wo, "w_out", (l,), dc * 128)
                    po = g.ps[4 + dc % 2]
                    for kc in range(KC):
                        P.op("pe", "matmul", po[:, :NT], lhsT=wo[:, kc, :], rhs=ybf[:, kc, :], start=(kc == 0), stop=(kc == KC - 1), reads=[wo, ybf], writes=[po])
                    P.op("dve", "scalar_tensor_tensor", out=s[:, dc, :], in0=po[:, :NT], scalar=g.mod[:, 5 * 16 + dc, m:m + 1], in1=s[:, dc, :],
                         op0=ALU.mult, op1=ALU.add, reads=[po, s, g.mod], writes=[s])
            layer_norm(g, s, NT, l, 1)
            ffn_block(g, s, NT, m, l, 1, (6, 7, 8), 2)
            if last:
                P.dma(fview(dst)[:, :, t0 - CTX:t0 - CTX + NT], s[:], reads=[s], writes=[dst], q="act")
            else:
                P.dma(fview(dst)[:, :, t0:t0 + NT], s[:], reads=[s], writes=[dst], q="act")


SMALL_NAMES = ["adab", "lng", "lnb", "gateb", "mu", "mud", "w0", "a0", "kk_k", "k_a", "r_k", "lnxg", "lnxb", "qng", "kvg", "convw", "convb", "ba", "bx", "lam"]
CONST_NAMES = ["ident", "ones", "blk", "Jm", "maskA"]


def build_program(SEQ, L, shapes, stages=None, ext_in=(), ext_out=()):
    nc = bass.Bass("TRN2", target_bir_lowering=False)
    g = setup(nc, SEQ, L, shapes, ext_in=ext_in, ext_out=ext_out)
    load_consts(g, CONST_NAMES)
    load_small(g, SMALL_NAMES)
    init_derived(g)
    init_mod(g)
    T = g.T
    xcur = g.d["xT"]
    XN = scratch(g, "XN", [D, T])
    yT = g.P.dram("yT", [D, SEQ], F32, "ExternalOutput")
    g.d["yT"] = yT
    for l in range(L):
        precast_layer(g, l)
    for l in range(L):
        last = l == L - 1
        stage_mod(g, l)
        stage_a(g, l, xcur)
        stage_lru(g, l)
        stage_rwkv_mix(g, l)
        stage_rwkv_scan(g, l)
        stage_rwkv_out(g, l)
        stage_mla(g, l, ctx_out=not last)
        stage_c(g, l, yT if last else XN, last)
        xcur = XN
    g.P.emit()
    return nc, g


def host_inputs(inp, b, SEQ, L):
    x = np.asarray(inp["x"][b], np.float32)
    cx = np.asarray(inp["ctx"][b], np.float32)
    d = {}
    d["xT"] = np.ascontiguousarray(np.concatenate([cx, x], 0).T)
    d["cvec"] = np.ascontiguousarray(np.stack([fm(inp["c"][b]), fm(inp["c_ctx"])], -1))
    d.update(host_consts(SEQ))
    d.update(host_small(inp, L))
    for k in BIG_W:
        d[k] = np.ascontiguousarray(np.asarray(inp[k], np.float32))
    return d


_CACHE = {}


def kernel(**inputs):
    B, SEQ, _ = inputs["x"].shape
    L = inputs["w_in"].shape[0]
    per_core = [host_inputs(inputs, b, SEQ, L) for b in range(B)]
    shapes = {k: list(v.shape) for k, v in per_core[0].items()}
    key = (SEQ, L)
    if key not in _CACHE:
        _CACHE[key] = build_program(SEQ, L, shapes)
    nc, g = _CACHE[key]
    res = run_bass_kernel_spmd(nc, per_core, core_ids=list(range(B)))
    out = np.stack([np.ascontiguousarray(res.results[b]["yT"].T) for b in range(B)], 0)
    return out.astype(np.float32)
```

```python
import contextlib
import math
import numpy as np
import concourse.bass as bass
import concourse.mybir as mybir
from concourse.bass_utils import run_bass_kernel_spmd

F32 = mybir.dt.float32
BF16 = mybir.dt.bfloat16
AF = mybir.ActivationFunctionType
ALU = mybir.AluOpType

D = 2048
KC = 16
DFF = 5632
FC = 44
NMOD = 9
CTX = 256
GW = 64
RW_W = 1024
RWKV_IN = 3360
MLA_IN = 832
LRU_W = 1024
IN_COLS = 12384
ALPHA = 4.0 ** 0.25
LN_EPS = 1e-6
GN_EPS = 64e-5
C0 = math.exp(-0.5)
ATT_SCALE = 192.0 ** -0.5
CH = 64

SEGS = []
for i in range(8):
    SEGS.append((128 * i, 128))
for i in range(8):
    SEGS.append((1024 + 128 * i, 128))
for i in range(8):
    SEGS.append((2048 + 128 * i, 128))
SEGS.append((3072, 64))
SEGS.append((3136, 64))
SEGS.append((3200, 128))
SEGS.append((3328, 32))
for i in range(4):
    SEGS.append((3360 + 128 * i, 128))
for i in range(2):
    SEGS.append((3872 + 128 * i, 128))
SEGS.append((4128, 64))
for i in range(8):
    SEGS.append((4192 + 128 * i, 128))
for i in range(8):
    SEGS.append((5216 + 128 * i, 128))
for i in range(48):
    SEGS.append((6240 + 128 * i, 128))
NSEG = len(SEGS)
S_R, S_K, S_V, S_WD, S_AD, S_GD, S_Q, S_KV, S_KR, S_XR, S_GB, S_GATE = 0, 8, 16, 24, 25, 26, 28, 32, 34, 35, 43, 51
NRW = 28


class Tile:
    __slots__ = ("t", "name", "last_w", "readers", "dsem", "osem", "kind")

    def __init__(self, t, name, kind="sbuf"):
        self.t = t
        self.name = name
        self.kind = kind
        self.last_w = None
        self.readers = []
        self.dsem = None
        self.osem = None

    def __getitem__(self, idx):
        return self.t[idx]

    def sub(self, name):
        return Tile(self.t, self.name + "." + name, self.kind)


class Op:
    __slots__ = ("eng", "fn", "deps", "is_dma", "sem", "val", "need_sig", "osize")


SEM_EPOCH = 24000
SMALL_OP = 512
import os
DEBUG_INS = os.environ.get("DEBUG_INS")


class Prog:
    ENGS = ("pe", "dve", "act", "pool", "sp")

    def __init__(self, nc):
        self.nc = nc
        self.ops = []
        self.stack = contextlib.ExitStack()
        self.scopes = []
        self.free_sems = []
        self.fence = None
        self.nsem = 0
        self.dummy = Tile(self.stack.enter_context(nc.sbuf_tensor("fence_dummy", [1, 8], F32)), "dummy")

    def new_sem(self, name):
        self.nsem += 1
        return self.stack.enter_context(self.nc.semaphore(name + "_%d" % self.nsem))

    def _reg(self, tl):
        tl.last_w = self.fence
        if self.scopes:
            self.scopes[-1][1].append(tl)
        return tl

    def sbuf(self, name, shape, dtype=F32):
        st = self.scopes[-1][0] if self.scopes else self.stack
        nm = name + "_%d" % len(self.ops)
        t = st.enter_context(self.nc.sbuf_tensor(nm, list(shape), dtype))
        return self._reg(Tile(t, name))

    def psum(self, name, shape, dtype=F32):
        t = self.stack.enter_context(self.nc.psum_tensor(name, list(shape), dtype))
        return Tile(t, name, "psum")

    def dram(self, name, shape, dtype=F32, kind="Internal"):
        t = self.nc.dram_tensor(name, list(shape), dtype, kind=kind).ap()
        return Tile(t, name, "dram")

    @contextlib.contextmanager
    def scope(self):
        st = contextlib.ExitStack()
        self.scopes.append((st, []))
        try:
            yield
        finally:
            _, tiles = self.scopes.pop()
            if tiles:
                d = self.dummy
                self.fence = self.op("dve", "memset", d[:], 0.0, reads=[], writes=[d] + tiles)
                for tl in tiles:
                    for s in (tl.dsem, tl.osem):
                        if s is not None and s[1] < SEM_EPOCH:
                            self.free_sems.append(s)
            st.close()

    def _record(self, eng, fn, reads, writes, is_dma):
        i = len(self.ops)
        deps = set()
        pr = [t for t in reads if t.kind == "psum"]
        if pr:
            writes = list(writes) + [t for t in pr if t not in writes]
            reads = [t for t in reads if t.kind != "psum"]
        for t in reads:
            if t.last_w is not None:
                deps.add(t.last_w)
        for t in writes:
            if t.last_w is not None:
                deps.add(t.last_w)
            deps.update(t.readers)
        for t in reads:
            t.readers.append(i)
        for t in writes:
            t.last_w = i
            t.readers = []
        o = Op()
        o.eng = eng
        o.fn = fn
        o.deps = deps
        o.is_dma = is_dma
        o.sem = None
        o.val = 0
        o.need_sig = is_dma
        o.osize = 1 << 30
        self.ops.append(o)
        return i

    def op(self, eng, name, *args, reads=(), writes=(), **kw):
        def fn(e, name=name, args=args, kw=kw):
            return getattr(e, name)(*args, **kw)

        i = self._record(eng, fn, reads, writes, False)
        if eng != "pe":
            oap = kw.get("out", args[0] if args else None)
            try:
                sz = 1
                for v in oap.shape[1:]:
                    sz *= int(v)
                self.ops[i].osize = 0 if name == "tensor_tensor_scan" else sz
            except Exception:
                pass
        return i

    def _get_sem(self, cur, name):
        if cur is not None and cur[1] < SEM_EPOCH:
            return cur
        if self.free_sems:
            return self.free_sems.pop()
        return [self.new_sem(name), 0]

    def dma(self, out_ap, in_ap, reads=(), writes=(), q="sp", **kw):
        def fn(e, out_ap=out_ap, in_ap=in_ap, kw=kw):
            return e.dma_start(out=out_ap, in_=in_ap, **kw)

        i = self._record(q, fn, reads, writes, True)
        if writes[0].kind == "dram":
            st = reads[0]
            st.osem = self._get_sem(st.osem, "o")
            s = st.osem
        else:
            st = writes[0]
            st.dsem = self._get_sem(st.dsem, "d")
            s = st.dsem
        s[1] += 16
        self.ops[i].sem = s[0]
        self.ops[i].val = s[1]
        return i

    def dma_dd(self, out_ap, in_ap, reads=(), writes=(), q="pool", nslots=8, **kw):
        def fn(e, out_ap=out_ap, in_ap=in_ap, kw=kw):
            return e.dma_start(out=out_ap, in_=in_ap, **kw)

        if not hasattr(self, "dd_slots"):
            self.dd_slots = [[self.new_sem("dd"), 0, None] for _ in range(nslots)]
            self.dd_i = 0
        slot = self.dd_slots[self.dd_i % nslots]
        self.dd_i += 1
        if slot[1] >= SEM_EPOCH:
            slot[0] = self.new_sem("dd")
            slot[1] = 0
        i = self._record(q, fn, reads, writes, True)
        if slot[2] is not None:
            self.ops[i].deps.add(slot[2])
        slot[1] += 16
        slot[2] = i
        self.ops[i].sem = slot[0]
        self.ops[i].val = slot[1]
        return i

    def emit(self):
        nc = self.nc
        ops = self.ops
        for o in ops:
            for d in o.deps:
                od = ops[d]
                if not od.is_dma and (od.eng != o.eng or od.osize < SMALL_OP or o.is_dma):
                    od.need_sig = True
        esem = {e: [] for e in self.ENGS}
        ecnt = {e: 0 for e in self.ENGS}
        for o in ops:
            if o.is_dma or not o.need_sig:
                continue
            c = ecnt[o.eng]
            ep = c // SEM_EPOCH
            if ep >= len(esem[o.eng]):
                esem[o.eng].append(self.new_sem("e_" + o.eng))
            o.sem = esem[o.eng][ep]
            o.val = c % SEM_EPOCH + 1
            ecnt[o.eng] = c + 1
        per_eng = {e: [] for e in self.ENGS}
        for i, o in enumerate(ops):
            per_eng[o.eng].append(i)
        fw = {}
        for o in ops:
            if o.is_dma:
                k = id(o.sem)
                if k not in fw or fw[k][1] < o.val:
                    fw[k] = (o.sem, o.val)

        def run(engname, e):
            waited = {}
            for i in per_eng[engname]:
                o = ops[i]
                for d in sorted(o.deps):
                    od = ops[d]
                    if not od.is_dma and od.eng == engname and od.osize >= SMALL_OP and not o.is_dma:
                        continue
                    k = id(od.sem)
                    if waited.get(k, 0) >= od.val:
                        continue
                    e.wait_ge(od.sem, od.val)
                    waited[k] = od.val
                ins = o.fn(e)
                if DEBUG_INS is not None and DEBUG_INS in str(getattr(ins.ins, "name", "")):
                    print("DEBUG_INS", i, engname, str(ins)[:600])
                if o.need_sig:
                    ins.then_inc(o.sem, 16 if o.is_dma else 1)
            if engname == "sp":
                for s, v in fw.values():
                    e.wait_ge(s, v)

        with nc.Block() as block:
            @block.tensor
            def _(e):
                run("pe", e)

            @block.vector
            def _(e):
                run("dve", e)

            @block.scalar
            def _(e):
                run("act", e)

            @block.gpsimd
            def _(e):
                run("pool", e)

            @block.sync
            def _(e):
                run("sp", e)
        self.stack.close()


class G:
    pass


def fm(v):
    v = np.asarray(v, np.float32)
    n = v.shape[-1] // 128
    w = v.reshape(v.shape[:-1] + (n, 128))
    return np.ascontiguousarray(np.moveaxis(w, -1, 0))


def host_consts(SEQ):
    c = {}
    c["ident"] = np.eye(128, dtype=np.float32)
    c["ones"] = np.ones((128, 128), np.float32)
    blk = np.zeros((128, 128), np.float32)
    blk[:64, :64] = 1
    blk[64:, 64:] = 1
    c["blk"] = blk
    a = np.arange(64)
    strict_f = (a[:, None] < a[None, :]).astype(np.float32)
    incl_f = (a[:, None] <= a[None, :]).astype(np.float32)
    mA = np.zeros((2, 128, 192), np.float32)
    mXY = np.zeros((2, 128, 256), np.float32)
    for d in range(2):
        st = strict_f if d == 0 else strict_f.T
        inc = incl_f if d == 0 else incl_f.T
        for h in range(2):
            rs = slice(64 * h, 64 * h + 64)
            mA[d, rs, 0:64] = inc
            mA[d, rs, 64:128] = st
            mA[d, rs, 128:192] = inc
            mXY[d, rs, 64 * h:64 * h + 64] = st
            mXY[d, rs, 128 + 64 * h:128 + 64 * h + 64] = st.T
    c["maskA"] = mA
    c["maskXY"] = mXY
    Lm = 1024
    rm = np.ones((2, 128, Lm), np.float32)
    rm[0, :, 0::64] = 0
    rm[1, :, 63::64] = 0
    c["rm"] = rm
    t = np.arange(SEQ)
    row = (t // GW).astype(np.float32)
    col = (t % GW).astype(np.float32)
    inv = (10000.0 ** (-np.arange(16, dtype=np.float32) / 16)).astype(np.float32)
    ang = np.concatenate([row[:, None] * inv, col[:, None] * inv], axis=-1).astype(np.float32)
    c["cos"] = np.ascontiguousarray(np.repeat(np.cos(ang), 2, axis=1).T.astype(np.float32))
    c["sin"] = np.ascontiguousarray(np.repeat(np.sin(ang), 2, axis=1).T.astype(np.float32))
    J = np.zeros((64, 64), np.float32)
    for i in range(32):
        J[2 * i + 1, 2 * i] = -1.0
        J[2 * i, 2 * i + 1] = 1.0
    c["Jm"] = J
    return c


def seg_mu(mu_l):
    full = np.zeros((128, NRW), np.float32)
    dirm = np.zeros((128, 4, NRW), np.float32)
    for si in range(NRW):
        c0, w = SEGS[si]
        full[:w, si] = mu_l[c0:c0 + w]
        for p in range(w):
            q = (c0 + p) // 840
            dirm[p, q, si] = mu_l[c0 + p]
    return full, dirm


def seg_dirs(si):
    c0, w = SEGS[si]
    return sorted(set((c0 + p) // 840 for p in range(w)))


SMALL_SPECS = None


def host_small(inp, L):
    s = {}
    s["adab"] = np.stack([fm(inp["ada_b"][l]) for l in range(L)], 1)
    s["lng"] = np.stack([fm(inp["ln_g"][l]) for l in range(L)], 1)
    s["lnb"] = np.stack([fm(inp["ln_b"][l]) for l in range(L)], 1)
    s["gateb"] = np.stack([fm(inp["gate_b"][l]) for l in range(L)], 1)
    mus = [seg_mu(np.asarray(inp["rwkv_mu"][l], np.float32)) for l in range(L)]
    s["mu"] = np.stack([m[0] for m in mus], 1)
    s["mud"] = np.stack([m[1] for m in mus], 1)
    s["w0"] = np.stack([fm(inp["rwkv_w0"][l]) for l in range(L)], 1)
    s["a0"] = np.stack([fm(inp["rwkv_a0"][l]) for l in range(L)], 1)
    for nm, key in (("kk_k", "rwkv_k_k"), ("k_a", "rwkv_k_a"), ("r_k", "rwkv_r_k"), ("lnxg", "rwkv_lnx_g"), ("lnxb", "rwkv_lnx_b")):
        s[nm] = np.stack([fm(inp[key][l]) for l in range(L)], 1)
    s["qng"] = np.stack([fm(inp["mla_q_norm"][l]) for l in range(L)], 1)
    s["kvg"] = np.stack([fm(inp["mla_kv_norm"][l]) for l in range(L)], 1)
    s["convw"] = np.stack([fm(inp["lru_conv_w"][l]) for l in range(L)], 1)
    s["convb"] = np.stack([fm(inp["lru_conv_b"][l]) for l in range(L)], 1)
    for nm, key in (("ba", "lru_ba"), ("bx", "lru_bx"), ("lam", "lru_lambda")):
        s[nm] = np.stack([fm(inp[key][l]) for l in range(L)], 1)
    return {k: np.ascontiguousarray(v, np.float32) for k, v in s.items()}


BIG_W = ["ada_w", "ffn_w_gate", "ffn_w_up", "ffn_w_down", "w_in", "rwkv_w_up", "rwkv_a_up", "rwkv_g_up",
         "mla_w_uq", "mla_w_uk", "mla_w_uv", "lru_wa", "lru_wx", "proj_rwkv", "proj_mla", "proj_lru", "w_out"]


def setup(nc, SEQ, L, shapes, ext_in=(), ext_out=()):
    g = G()
    g.P = Prog(nc)
    g.SEQ = SEQ
    g.T = CTX + SEQ
    g.R = SEQ // GW
    g.L = L
    g.d = {}
    P = g.P
    for nm, shp in shapes.items():
        g.d[nm] = P.dram(nm, shp, F32, "ExternalInput")
    g.ps = [P.psum("ps%d" % i, [128, 512], F32) for i in range(8)]
    g.ext_in = set(ext_in)
    g.ext_out = set(ext_out)
    g.wc = {}
    return g


def scratch(g, name, shape, dtype=F32):
    if name in g.d:
        return g.d[name]
    kind = "Internal"
    if name in g.ext_in:
        kind = "ExternalInput"
    if name in g.ext_out:
        kind = "ExternalOutput"
    g.d[name] = g.P.dram(name, shape, dtype, kind)
    return g.d[name]


def dbg(g, name, tile, ap, shape):
    if name in g.ext_out:
        d = scratch(g, name, shape)
        g.P.dma(d[:], ap, reads=[tile], writes=[d])


def load_const(g, name, shape, src=None, q="sp"):
    P = g.P
    t = P.sbuf(name, shape, F32)
    src = src if src is not None else g.d[name][:]
    P.dma(t[:], src, reads=[g.d[name]], writes=[t], q=q)
    return t


def load_consts(g, which):
    P = g.P
    g.c = {}
    shp = {"ident": [128, 128], "ones": [128, 128], "blk": [128, 128], "Jm": [64, 64]}
    for nm in which:
        if nm in shp:
            g.c[nm] = load_const(g, nm, shp[nm])
    if "maskA" in which:
        g.c["maskA"] = [load_const(g, "maskA", [128, 192], g.d["maskA"][d]) for d in range(2)]
        g.c["maskXY"] = [load_const(g, "maskXY", [128, 256], g.d["maskXY"][d]) for d in range(2)]
        g.c["rm"] = [load_const(g, "rm", [128, 1024], g.d["rm"][d]) for d in range(2)]


def load_small(g, names):
    g.s = {}
    for nm in names:
        shp = list(g.d[nm].t.shape)
        g.s[nm] = load_const(g, nm, shp)


def wc_jobs(g, name, idx, K, col_list):
    W = g.d[name]
    KCn = K // 128
    sc = g.P.dram("WC_%s_%s" % (name, "_".join(map(str, idx))), [len(col_list), 128, KCn * 128], BF16)
    jobs = []
    for t, (c0, w) in enumerate(col_list):
        trk = Tile(sc.t, sc.name + "_%d" % t, "dram")
        dst = sc.t[t].rearrange("p (kc n) -> p kc n", n=128)[:, :, :w]
        src = W[tuple(idx) + (slice(None), slice(c0, c0 + w))].rearrange("(kc p) n -> p kc n", p=128)
        g.wc[(name,) + tuple(idx) + (c0,)] = (trk, dst)
        jobs.append((dst, src, W, trk))
    return jobs


def wc_emit(g, jobs):
    for (dst, src, W, trk) in jobs:
        g.P.dma_dd(dst, src, reads=[W], writes=[trk])


def wload(g, dst_ap, dst_tile, name, idx, c0):
    trk, ap = g.wc[(name,) + tuple(idx) + (c0,)]
    g.P.dma(dst_ap, ap, reads=[trk], writes=[dst_tile], q="sp")


def precast_layer(g, l):
    cols128 = lambda n: [(i * 128, 128) for i in range(n // 128)]
    for j in range(2):
        gj = wc_jobs(g, "ffn_w_gate", (l, j), D, cols128(DFF))
        uj = wc_jobs(g, "ffn_w_up", (l, j), D, cols128(DFF))
        dj = wc_jobs(g, "ffn_w_down", (l, j), DFF, cols128(D))
        ffn = [x for pair in zip(gj, uj) for x in pair] + dj
        if j == 0:
            wc_emit(g, ffn)
            wc_emit(g, wc_jobs(g, "w_in", (l,), D, list(SEGS)))
            pr = wc_jobs(g, "proj_rwkv", (l,), RW_W, cols128(D))
            pm = wc_jobs(g, "proj_mla", (l,), 2048, cols128(D))
            pl = wc_jobs(g, "proj_lru", (l,), LRU_W, cols128(D))
            wc_emit(g, [x for tr_ in zip(pr, pm, pl) for x in tr_])
            wc_emit(g, wc_jobs(g, "w_out", (l,), D, cols128(D)))
        else:
            wc_emit(g, ffn)


def stage_mod(g, l):
    P = g.P
    pm = g.ps[0]
    with P.scope():
        wts = [P.sbuf("adw%d" % i, [128, KC, 512], F32) for i in range(2)]
        aw = g.d["ada_w"]
        for grp in range(36):
            wt = wts[grp % 2]
            P.dma(wt[:], aw[l, :, grp * 512:(grp + 1) * 512].rearrange("(kc p) n -> p kc n", p=128),
                  reads=[aw], writes=[wt], q="sp" if grp % 2 == 0 else "act")
            for jj in range(4):
                j = grp * 4 + jj
                for kc in range(KC):
                    P.op("pe", "matmul", pm[:, 2 * j:2 * j + 2], lhsT=wt[:, kc, jj * 128:(jj + 1) * 128], rhs=g.cs[:, kc, :],
                         start=(kc == 0), stop=(kc == KC - 1), reads=[wt, g.cs], writes=[pm])
        adab = g.s["adab"]
        P.op("dve", "tensor_tensor", out=g.mod[:], in0=pm[:, 0:288].rearrange("p (j m) -> p j m", m=2),
             in1=adab[:, l, :].unsqueeze(2).to_broadcast([128, 144, 2]), op=ALU.add, reads=[pm, adab], writes=[g.mod])
        P.op("dve", "tensor_scalar", out=g.onep[:], in0=g.mod[:], scalar1=1.0, scalar2=None, op0=ALU.add, reads=[g.mod], writes=[g.onep])
        P.op("dve", "tensor_scalar", out=g.hg[:], in0=g.mod[:], scalar1=0.5, scalar2=None, op0=ALU.mult, reads=[g.mod], writes=[g.hg])


def init_mod(g):
    P = g.P
    g.cs = P.sbuf("cs", [128, KC, 2], F32)
    P.dma(g.cs[:], g.d["cvec"][:], reads=[g.d["cvec"]], writes=[g.cs])
    P.op("act", "activation", out=g.cs[:], in_=g.cs[:], func=AF.Silu, reads=[g.cs], writes=[g.cs])
    g.mod = P.sbuf("mod", [128, 144, 2], F32)
    g.onep = P.sbuf("onep", [128, 144, 2], F32)
    g.hg = P.sbuf("hg", [128, 144, 2], F32)


def subs_of(tile, n):
    out = []
    for i in range(n):
        t = Tile(tile.t, tile.name + ".%d" % i, tile.kind)
        t.last_w = tile.last_w
        t.readers = list(tile.readers)
        out.append(t)
    return out


def join(P, subs, tile):
    P.op("dve", "memset", P.dummy[:], 0.0, reads=list(subs), writes=[P.dummy, tile])


def rsqrt_(P, out, in_, eps, tiles, scale=1.0):
    P.op("act", "activation", out=out, in_=in_, func=AF.Sqrt, bias=eps, scale=scale, reads=tiles, writes=tiles)
    P.op("dve", "reciprocal", out=out, in_=out, reads=tiles, writes=tiles)


def layer_norm(g, z, NT, l, idx):
    P = g.P
    s1 = g.ps[6]
    s2 = g.ps[7]
    ones = g.c["ones"]
    with P.scope():
        sq = [P.sbuf("lnsq%d" % i, [128, NT], F32) for i in range(2)]
        mean = P.sbuf("lnmean", [128, NT], F32)
        rstd = P.sbuf("lnrstd", [128, NT], F32)
        for dc in range(KC):
            P.op("pe", "matmul", s1[:, :NT], lhsT=ones[:], rhs=z[:, dc, :], start=(dc == 0), stop=(dc == KC - 1), reads=[ones, z], writes=[s1])
            q = sq[dc % 2]
            P.op("act", "activation", out=q[:], in_=z[:, dc, :], func=AF.Square, reads=[z], writes=[q])
            P.op("pe", "matmul", s2[:, :NT], lhsT=ones[:], rhs=q[:], start=(dc == 0), stop=(dc == KC - 1), reads=[ones, q], writes=[s2])
        P.op("act", "mul", out=mean[:], in_=s1[:, :NT], mul=1.0 / D, reads=[s1], writes=[mean])
        P.op("dve", "tensor_tensor", out=rstd[:], in0=mean[:], in1=mean[:], op=ALU.mult, reads=[mean], writes=[rstd])
        P.op("dve", "scalar_tensor_tensor", out=rstd[:], in0=s2[:, :NT], scalar=1.0 / D, in1=rstd[:], op0=ALU.mult, op1=ALU.subtract, reads=[s2, rstd], writes=[rstd])
        rsqrt_(P, rstd[:], rstd[:], LN_EPS, [rstd])
        lng = g.s["lng"]
        lnb = g.s["lnb"]
        zs = subs_of(z, KC)
        for dc in range(KC):
            P.op("dve", "tensor_tensor", out=z[:, dc, :], in0=z[:, dc, :], in1=mean[:], op=ALU.subtract, reads=[mean], writes=[zs[dc]])
            P.op("dve", "tensor_tensor", out=z[:, dc, :], in0=z[:, dc, :], in1=rstd[:], op=ALU.mult, reads=[rstd], writes=[zs[dc]])
            P.op("act", "activation", out=z[:, dc, :], in_=z[:, dc, :], func=AF.Identity, scale=lng[:, l, idx, dc:dc + 1], bias=lnb[:, l, idx, dc:dc + 1],
                 reads=[lng, lnb], writes=[zs[dc]])
        join(P, zs, z)


def modulate(g, out, s, NT, m, sc_slot, sh_slot):
    P = g.P
    os_ = subs_of(out, KC)
    for kc in range(KC):
        if kc % 2 == 0:
            P.op("dve", "tensor_scalar", out=out[:, kc, :], in0=s[:, kc, :], scalar1=g.onep[:, sc_slot * 16 + kc, m:m + 1],
                 scalar2=g.mod[:, sh_slot * 16 + kc, m:m + 1], op0=ALU.mult, op1=ALU.add, reads=[s, g.onep, g.mod], writes=[os_[kc]])
        else:
            P.op("act", "activation", out=out[:, kc, :], in_=s[:, kc, :], func=AF.Identity, scale=g.onep[:, sc_slot * 16 + kc, m:m + 1],
                 bias=g.mod[:, sh_slot * 16 + kc, m:m + 1], reads=[s, g.onep, g.mod], writes=[os_[kc]])
    join(P, os_, out)


def ffn_block(g, s, NT, m, l, j, slots, ln_idx):
    P = g.P
    sh, sc, gt = slots
    wgd, wud, wdd = g.d["ffn_w_gate"], g.d["ffn_w_up"], g.d["ffn_w_down"]
    with P.scope():
        AT = P.sbuf("ffa", [128, FC, NT], BF16)
        with P.scope():
            h = P.sbuf("ffh", [128, KC, NT], BF16)
            modulate(g, h, s, NT, m, sc, sh)
            P.op("act", "mul", out=s[:], in_=s[:], mul=ALPHA, reads=[s], writes=[s])
            wgs = [P.sbuf("wg%d" % i, [128, KC, 128], BF16) for i in range(3)]
            wus = [P.sbuf("wu%d" % i, [128, KC, 128], BF16) for i in range(3)]
            sgs = [P.sbuf("sg%d" % i, [128, NT], F32) for i in range(2)]
            for fc in range(FC):
                wg = wgs[fc % 3]
                wu = wus[fc % 3]
                wload(g, wg[:], wg, "ffn_w_gate", (l, j), fc * 128)
                wload(g, wu[:], wu, "ffn_w_up", (l, j), fc * 128)
                pg = g.ps[(2 * fc) % 4]
                pu = g.ps[(2 * fc + 1) % 4]
                for kc in range(KC):
                    P.op("pe", "matmul", pg[:, :NT], lhsT=wg[:, kc, :], rhs=h[:, kc, :], start=(kc == 0), stop=(kc == KC - 1), reads=[wg, h], writes=[pg])
                for kc in range(KC):
                    P.op("pe", "matmul", pu[:, :NT], lhsT=wu[:, kc, :], rhs=h[:, kc, :], start=(kc == 0), stop=(kc == KC - 1), reads=[wu, h], writes=[pu])
                sg = sgs[fc % 2]
                P.op("act", "activation", out=sg[:], in_=pg[:, :NT], func=AF.Silu, reads=[pg], writes=[sg])
                P.op("dve", "tensor_tensor", out=AT[:, fc, :], in0=sg[:], in1=pu[:, :NT], op=ALU.mult, reads=[sg, pu], writes=[AT])
        with P.scope():
            wds = [P.sbuf("wd%d" % i, [128, FC, 128], BF16) for i in range(2)]
            for dc in range(KC):
                wd = wds[dc % 2]
                wload(g, wd[:], wd, "ffn_w_down", (l, j), dc * 128)
                py = g.ps[4 + dc % 2]
                for fc in range(FC):
                    P.op("pe", "matmul", py[:, :NT], lhsT=wd[:, fc, :], rhs=AT[:, fc, :], start=(fc == 0), stop=(fc == FC - 1), reads=[wd, AT], writes=[py])
                P.op("dve", "scalar_tensor_tensor", out=s[:, dc, :], in0=py[:, :NT], scalar=g.hg[:, gt * 16 + dc, m:m + 1], in1=s[:, dc, :],
                     op0=ALU.mult, op1=ALU.add, reads=[py, s, g.hg], writes=[s])
    layer_norm(g, s, NT, l, ln_idx)


def token_tiles(g, with_ctx=True):
    tl = []
    if with_ctx:
        tl.append((0, CTX, 1))
    t0 = CTX
    while t0 < g.T:
        nt = min(512, g.T - t0)
        tl.append((t0, nt, 0))
        t0 += nt
    return tl


def fview(d):
    return d[:].rearrange("(kc p) t -> p kc t", p=128)


def stage_a(g, l, src, do_ffn=True):
    P = g.P
    XL = scratch(g, "XL", [D, g.T])
    F = [scratch(g, "F%d" % i, [128, g.T]) for i in range(NSEG)]
    win = g.d["w_in"]
    gateb = g.s["gateb"]
    for (t0, NT, m) in token_tiles(g):
        with P.scope():
            s = P.sbuf("s", [128, KC, NT], F32)
            P.dma(s[:], fview(src)[:, :, t0:t0 + NT], reads=[src], writes=[s])
            if do_ffn:
                ffn_block(g, s, NT, m, l, 0, (0, 1, 2), 0)
            P.dma(fview(XL)[:, :, t0:t0 + NT], s[:], reads=[s], writes=[XL], q="act")
            with P.scope():
                hl = P.sbuf("hl", [128, KC, NT], BF16)
                modulate(g, hl, s, NT, m, 4, 3)
                wts = [P.sbuf("wi%d" % i, [128, KC, 128], BF16) for i in range(3)]
                stg = [P.sbuf("stg%d" % i, [128, NT], F32) for i in range(3)]
                for si, (c0, w) in enumerate(SEGS):
                    wt = wts[si % 3]
                    wload(g, wt[:, :, :w], wt, "w_in", (l,), c0)
                    pp = g.ps[si % 4]
                    for kc in range(KC):
                        P.op("pe", "matmul", pp[:w, :NT], lhsT=wt[:, kc, :w], rhs=hl[:, kc, :], start=(kc == 0), stop=(kc == KC - 1), reads=[wt, hl], writes=[pp])
                    st = stg[si % 3]
                    if si >= S_GATE:
                        P.op("act", "activation", out=st[:w, :], in_=pp[:w, :NT], func=AF.Sigmoid, bias=gateb[:, l, si - S_GATE:si - S_GATE + 1],
                             reads=[pp, gateb], writes=[st])
                        sq_ = "act"
                    elif si % 2 == 0:
                        P.op("act", "copy", out=st[:w, :], in_=pp[:w, :NT], reads=[pp], writes=[st])
                        sq_ = "act"
                    else:
                        P.op("dve", "tensor_copy", out=st[:w, :], in_=pp[:w, :NT], reads=[pp], writes=[st])
                        sq_ = "act"
                    P.dma(F[si][:w, t0:t0 + NT], st[:w, :], reads=[st], writes=[F[si]], q=sq_)


def init_derived(g):
    P = g.P
    s = g.s
    for nm, src in (("omm", "mu"), ("omka", "k_a")):
        t = P.sbuf(nm, list(g.d[src].t.shape), F32)
        P.op("dve", "tensor_scalar", out=t[:], in0=s[src][:], scalar1=-1.0, scalar2=1.0, op0=ALU.mult, op1=ALU.add, reads=[s[src]], writes=[t])
        s[nm] = t
    s["clam"] = make_clam(g, s["lam"], list(g.d["lam"].t.shape))
    return
    s["clam"] = t


def make_clam(g, lam, shape):
    P = g.P
    e = P.sbuf("clam_e", shape, F32)
    t = P.sbuf("clam", shape, F32)
    P.op("act", "activation", out=e[:], in_=lam[:], func=AF.Exp, scale=-1.0, reads=[lam], writes=[e])
    nt = 12
    P.op("dve", "tensor_scalar", out=t[:], in0=e[:], scalar1=((-1.0) ** (nt + 1)) / nt, scalar2=((-1.0) ** nt) / (nt - 1), op0=ALU.mult, op1=ALU.add, reads=[e], writes=[t])
    for k in range(nt - 2, 0, -1):
        P.op("dve", "tensor_tensor", out=t[:], in0=t[:], in1=e[:], op=ALU.mult, reads=[t, e], writes=[t])
        P.op("dve", "tensor_scalar", out=t[:], in0=t[:], scalar1=((-1.0) ** (k + 1)) / k, scalar2=None, op0=ALU.add, reads=[t], writes=[t])
    P.op("dve", "scalar_tensor_tensor", out=t[:], in0=t[:], scalar=-8.0, in1=e[:], op0=ALU.mult, op1=ALU.mult, reads=[t, e], writes=[t])
    return t


def blocks_of(g):
    bl = [(0, CTX)]
    t0 = CTX
    while t0 < g.T:
        n = min(512, g.T - t0)
        bl.append((t0, n))
        t0 += n
    return bl


def stage_lru(g, l):
    P = g.P
    T = g.T
    F = [scratch(g, "F%d" % i, [128, T]) for i in range(NSEG)]
    YL = scratch(g, "YL", [LRU_W, T], BF16)
    cw, cb, ba, bx, clam = g.s["convw"], g.s["convb"], g.s["ba"], g.s["bx"], g.s["clam"]
    wad, wxd = g.d["lru_wa"], g.d["lru_wx"]
    for n in range(8):
        with P.scope():
            xr = P.sbuf("lxr", [128, T], F32)
            gb = P.sbuf("lgb", [128, T], F32)
            xc = P.sbuf("lxc", [128, T], F32)
            hs = P.sbuf("lhs", [128, T], F32)
            gr = P.sbuf("lgr", [128, T], F32)
            gi = P.sbuf("lgi", [128, T], F32)
            tmp = P.sbuf("ltmp", [128, T], F32)
            yb = P.sbuf("lyb", [128, T], BF16)
            P.dma(xr[:], F[S_XR + n][:, :], reads=[F[S_XR + n]], writes=[xr])
            P.dma(gb[:], F[S_GB + n][:, :], reads=[F[S_GB + n]], writes=[gb], q="act")
            for (a, b) in ((0, CTX), (CTX, T)):
                P.op("dve", "tensor_scalar", out=xc[:, a:b], in0=xr[:, a:b], scalar1=cw[:, l, 2, n:n + 1], scalar2=cb[:, l, n:n + 1],
                     op0=ALU.mult, op1=ALU.add, reads=[xr, cw, cb], writes=[xc])
                for (tap, off) in ((0, -2), (1, -1), (3, 1)):
                    if off < 0:
                        o_, i_ = xc[:, a - off:b], xr[:, a:b + off]
                    else:
                        o_, i_ = xc[:, a:b - off], xr[:, a + off:b]
                    P.op("dve", "scalar_tensor_tensor", out=o_, in0=i_, scalar=cw[:, l, tap, n:n + 1], in1=o_, op0=ALU.mult, op1=ALU.add,
                         reads=[xr, xc, cw], writes=[xc])
            if n == 0:
                dbg(g, "dbg_xc", xc, xc[:], [128, T])
            for d in range(2):
                wa = P.sbuf("lwa%d" % d, [128, 128], F32)
                wx = P.sbuf("lwx%d" % d, [128, 128], F32)
                P.dma(wa[:], wad[l, d, n], reads=[wad], writes=[wa])
                P.dma(wx[:], wxd[l, d, n], reads=[wxd], writes=[wx], q="act")
                for bi, (t0, nb) in enumerate(blocks_of(g)):
                    p1 = g.ps[(2 * bi) % 4]
                    p2 = g.ps[(2 * bi + 1) % 4]
                    P.op("pe", "matmul", p1[:, :nb], lhsT=wa[:], rhs=xc[:, t0:t0 + nb], start=True, stop=True, reads=[wa, xc], writes=[p1])
                    P.op("pe", "matmul", p2[:, :nb], lhsT=wx[:], rhs=xc[:, t0:t0 + nb], start=True, stop=True, reads=[wx, xc], writes=[p2])
                    P.op("act", "activation", out=gr[:, t0:t0 + nb], in_=p1[:, :nb], func=AF.Sigmoid, bias=ba[:, l, d, n:n + 1], reads=[p1, ba], writes=[gr])
                    P.op("act", "activation", out=gi[:, t0:t0 + nb], in_=p2[:, :nb], func=AF.Sigmoid, bias=bx[:, l, d, n:n + 1], reads=[p2, bx], writes=[gi])
                P.op("dve", "tensor_scalar", out=gr[:], in0=gr[:], scalar1=clam[:, l, d, n:n + 1], scalar2=0.25, op0=ALU.mult, op1=ALU.mult, reads=[gr, clam], writes=[gr])
                P.op("dve", "tensor_scalar", out=tmp[:], in0=gr[:], scalar1=1.0 / 7, scalar2=1.0, op0=ALU.mult, op1=ALU.add, reads=[gr], writes=[tmp])
                for kk_ in (6, 5, 4, 3, 2):
                    P.op("dve", "tensor_tensor", out=tmp[:], in0=tmp[:], in1=gr[:], op=ALU.mult, reads=[tmp, gr], writes=[tmp])
                    P.op("dve", "tensor_scalar", out=tmp[:], in0=tmp[:], scalar1=1.0 / kk_, scalar2=1.0, op0=ALU.mult, op1=ALU.add, reads=[tmp], writes=[tmp])
                P.op("dve", "scalar_tensor_tensor", out=tmp[:], in0=tmp[:], scalar=-1.0, in1=gr[:], op0=ALU.mult, op1=ALU.mult, reads=[tmp, gr], writes=[tmp])
                P.op("dve", "tensor_scalar", out=gr[:], in0=tmp[:], scalar1=-1.0, scalar2=1.0, op0=ALU.mult, op1=ALU.add, reads=[tmp], writes=[gr])
                P.op("dve", "tensor_tensor", out=gr[:], in0=gr[:], in1=gr[:], op=ALU.mult, reads=[gr], writes=[gr])
                P.op("dve", "scalar_tensor_tensor", out=xr[:], in0=tmp[:], scalar=-1.0, in1=tmp[:], op0=ALU.mult, op1=ALU.mult, reads=[tmp], writes=[xr])
                P.op("dve", "scalar_tensor_tensor", out=tmp[:], in0=tmp[:], scalar=2.0, in1=xr[:], op0=ALU.mult, op1=ALU.add, reads=[tmp, xr], writes=[tmp])
                P.op("dve", "scalar_tensor_tensor", out=tmp[:], in0=gr[:], scalar=1.0, in1=tmp[:], op0=ALU.add, op1=ALU.mult, reads=[gr, tmp], writes=[tmp])
                P.op("dve", "tensor_tensor", out=gr[:], in0=gr[:], in1=gr[:], op=ALU.mult, reads=[gr], writes=[gr])
                P.op("dve", "scalar_tensor_tensor", out=tmp[:], in0=gr[:], scalar=1.0, in1=tmp[:], op0=ALU.add, op1=ALU.mult, reads=[gr, tmp], writes=[tmp])
                P.op("act", "activation", out=tmp[:], in_=tmp[:], func=AF.Sqrt, reads=[tmp], writes=[tmp])
                P.op("dve", "tensor_tensor", out=tmp[:], in0=tmp[:], in1=gi[:], op=ALU.mult, reads=[tmp, gi], writes=[tmp])
                P.op("dve", "tensor_tensor", out=tmp[:], in0=tmp[:], in1=xc[:], op=ALU.mult, reads=[tmp, xc], writes=[tmp])
                if n == 0:
                    dbg(g, "dbg_a%d" % d, gr, gr[:], [128, T])
                    dbg(g, "dbg_b%d" % d, tmp, tmp[:], [128, T])
                hd = hs if d == 0 else gi
                if d == 0:
                    P.op("dve", "tensor_tensor_scan", out=hd[:, :], data0=gr[:, :], data1=tmp[:, :], initial=0.0, op0=ALU.mult, op1=ALU.add,
                         reads=[gr, tmp], writes=[hd])
                else:
                    P.op("dve", "tensor_tensor_scan", out=hd[:, 0:CTX][:, ::-1], data0=gr[:, 0:CTX][:, ::-1], data1=tmp[:, 0:CTX][:, ::-1], initial=0.0,
                         op0=ALU.mult, op1=ALU.add, reads=[gr, tmp], writes=[hd])
                    P.op("dve", "scalar_tensor_tensor", out=tmp[:, T - 1:T], in0=gr[:, T - 1:T], scalar=hd[:, 0:1], in1=tmp[:, T - 1:T], op0=ALU.mult, op1=ALU.add,
                         reads=[gr, hd, tmp], writes=[tmp])
                    P.op("dve", "tensor_tensor_scan", out=hd[:, CTX:T][:, ::-1], data0=gr[:, CTX:T][:, ::-1], data1=tmp[:, CTX:T][:, ::-1], initial=0.0,
                         op0=ALU.mult, op1=ALU.add, reads=[gr, tmp, hd], writes=[hd])
                    P.op("dve", "tensor_tensor", out=hs[:], in0=hs[:], in1=hd[:], op=ALU.add, reads=[hs, hd], writes=[hs])
            if n == 0:
                dbg(g, "dbg_hs", hs, hs[:], [128, T])
            P.op("dve", "tensor_tensor", out=tmp[:], in0=gb[:], in1=gb[:], op=ALU.mult, reads=[gb], writes=[tmp])
            P.op("dve", "tensor_scalar", out=tmp[:], in0=tmp[:], scalar1=0.044715, scalar2=1.0, op0=ALU.mult, op1=ALU.add, reads=[tmp], writes=[tmp])
            P.op("dve", "tensor_tensor", out=tmp[:], in0=tmp[:], in1=gb[:], op=ALU.mult, reads=[tmp, gb], writes=[tmp])
            P.op("act", "activation", out=tmp[:], in_=tmp[:], func=AF.Sigmoid, scale=1.5957691216057308, reads=[tmp], writes=[tmp])
            P.op("dve", "tensor_tensor", out=tmp[:], in0=tmp[:], in1=gb[:], op=ALU.mult, reads=[tmp, gb], writes=[tmp])
            P.op("dve", "tensor_tensor", out=yb[:], in0=tmp[:], in1=hs[:], op=ALU.mult, reads=[tmp, hs], writes=[yb])
            P.dma(YL[n * 128:(n + 1) * 128, :], yb[:], reads=[yb], writes=[YL], q="act")


def stage_rwkv_mix(g, l):
    P = g.P
    T = g.T
    R = g.R
    F = [scratch(g, "F%d" % i, [128, T]) for i in range(NSEG)]
    omm, mud = g.s["omm"], g.s["mud"]
    with P.scope():
        fs = [P.sbuf("mxf%d" % i, [128, T], F32) for i in range(2)]
        os_ = [P.sbuf("mxo%d" % i, [128, T], F32) for i in range(2)]
        for si in range(NRW):
            c0, w = SEGS[si]
            f = fs[si % 2]
            o = os_[si % 2]
            P.dma(f[:w, :], F[si][:w, :], reads=[F[si]], writes=[f], q="sp" if si % 2 == 0 else "act")
            P.op("dve", "tensor_scalar", out=o[:w, :], in0=f[:w, :], scalar1=omm[:w, l, si:si + 1], scalar2=None, op0=ALU.mult, reads=[f, omm], writes=[o])
            fl = f[:w, CTX:T].rearrange("p (r c) -> p r c", c=GW)
            ol = o[:w, CTX:T].rearrange("p (r c) -> p r c", c=GW)
            for q in seg_dirs(si):
                sc = mud[:w, l, q, si:si + 1]
                if q == 0:
                    pairs = [(ol[:, :, 1:GW], fl[:, :, 0:GW - 1]), (o[:w, 1:CTX], f[:w, 0:CTX - 1])]
                elif q == 1:
                    pairs = [(ol[:, :, 0:GW - 1], fl[:, :, 1:GW]), (o[:w, 0:CTX - 1], f[:w, 1:CTX])]
                elif q == 2:
                    pairs = [(ol[:, 1:R, :], fl[:, 0:R - 1, :]), (o[:w, 1:CTX], f[:w, 0:CTX - 1])]
                else:
                    pairs = [(ol[:, 0:R - 1, :], fl[:, 1:R, :]), (o[:w, 0:CTX - 1], f[:w, 1:CTX])]
                for (oo, ii) in pairs:
                    P.op("dve", "scalar_tensor_tensor", out=oo, in0=ii, scalar=sc, in1=oo, op0=ALU.mult, op1=ALU.add, reads=[f, o, mud], writes=[o])
            P.dma(F[si][:w, :], o[:w, :], reads=[o], writes=[F[si]], q="act")


SCAN_F32R = os.environ.get("SCAN_F32R", "0") == "1"


def _r(ap):
    return ap.bitcast(mybir.dt.float32r) if SCAN_F32R else ap


def rwkv_stream(g, l, hp, d, bankA, bankB, blocks, tr, YS):
    P = g.P
    F = [scratch(g, "F%d" % i, [128, g.T]) for i in range(NSEG)]
    ident, blk = g.c["ident"], g.c["blk"]
    maskA, maskXY, rm = g.c["maskA"][d], g.c["maskXY"][d], g.c["rm"][d]
    s = g.s
    pA = pTM = pW0 = pSP = bankA
    pXY = pR = pU = pY = bankB
    allA = [bankA]
    A_, TM_, W0_, SP_ = bankA[:, 0:192], bankA[:, 192:384], bankA[:, 384:448], bankA[:, 448:512]
    XY_, R_, U_, Y_ = bankB[:, 0:256], bankB[:, 256:384], bankB[:, 384:448], bankB[:, 448:512]
    P.op("dve", "memset", bankB[:, 0:256], 0.0, reads=[], writes=[pXY])
    nm = "_%d_%d" % (hp, d)
    ST = P.sbuf("ST" + nm, [128, 64], F32)
    P.op("dve", "memset", ST[:], 0.0, reads=[], writes=[ST])
    wup = P.sbuf("wup" + nm, [64, 128], F32)
    aup = P.sbuf("aup" + nm, [64, 128], F32)
    P.dma(wup[:], g.d["rwkv_w_up"][l, d, :, hp * 128:(hp + 1) * 128], reads=[g.d["rwkv_w_up"]], writes=[wup])
    P.dma(aup[:], g.d["rwkv_a_up"][l, d, :, hp * 128:(hp + 1) * 128], reads=[g.d["rwkv_a_up"]], writes=[aup])
    LM = 512
    rt, al, be, kt, vv, Ep, Yb = [P.sbuf(n_ + nm, [128, LM], F32) for n_ in ("rt", "al", "be", "kt", "vv", "Ep", "Yb")]
    XYs = [P.sbuf("XYs%d" % i + nm, [128, 256], F32) for i in range(2)]
    Rs = [P.sbuf("Rs%d" % i + nm, [128, 128], F32) for i in range(2)]
    Am = P.sbuf("Am" + nm, [128, 192], F32)
    TMs = P.sbuf("TMs" + nm, [128, 192], F32)
    W0s = P.sbuf("W0s" + nm, [128, 64], F32)
    Us = P.sbuf("Us" + nm, [128, 64], F32)
    r_, k_, twd, adt, sig, Lc, t1, t2, a_, kk_, t3 = tr
    w0, a0, kkk, ka, omka = s["w0"], s["a0"], s["kk_k"], s["k_a"], s["omka"]
    H2 = (slice(0, 64), slice(64, 128))
    for (t0, Lb) in blocks:
        ts_ = slice(t0, t0 + Lb)
        P.dma(r_[:, :Lb], F[S_R + hp][:, ts_], reads=[F[S_R + hp]], writes=[r_])
        P.dma(k_[:, :Lb], F[S_K + hp][:, ts_], reads=[F[S_K + hp]], writes=[k_], q="act")
        P.dma(vv[:, :Lb], F[S_V + hp][:, ts_], reads=[F[S_V + hp]], writes=[vv])
        P.dma(twd[0:64, :Lb], F[S_WD][0:64, ts_], reads=[F[S_WD]], writes=[twd], q="act")
        P.dma(adt[0:64, :Lb], F[S_AD][0:64, ts_], reads=[F[S_AD]], writes=[adt])
        P.op("act", "activation", out=twd[0:64, :Lb], in_=twd[0:64, :Lb], func=AF.Tanh, reads=[twd], writes=[twd])
        P.op("pe", "matmul", bankA[:, :Lb], lhsT=wup[:], rhs=twd[0:64, :Lb], start=True, stop=True, reads=[wup, twd], writes=allA)
        P.op("act", "activation", out=sig[:, :Lb], in_=bankA[:, :Lb], func=AF.Sigmoid, bias=w0[:, l, d, hp:hp + 1], reads=allA + [w0], writes=[sig])
        P.op("pe", "matmul", bankA[:, :Lb], lhsT=aup[:], rhs=adt[0:64, :Lb], start=True, stop=True, reads=[aup, adt] + allA, writes=allA)
        P.op("act", "activation", out=a_[:, :Lb], in_=bankA[:, :Lb], func=AF.Sigmoid, bias=a0[:, l, d, hp:hp + 1], reads=allA + [a0], writes=[a_])
        if d == 0:
            P.op("dve", "tensor_tensor_scan", out=Lc[:, :Lb], data0=rm[:, :Lb], data1=sig[:, :Lb], initial=0.0, op0=ALU.mult, op1=ALU.add,
                 reads=[rm, sig], writes=[Lc])
        else:
            P.op("dve", "tensor_tensor_scan", out=Lc[:, :Lb][:, ::-1], data0=rm[:, :Lb][:, ::-1], data1=sig[:, :Lb][:, ::-1], initial=0.0,
                 op0=ALU.mult, op1=ALU.add, reads=[rm, sig], writes=[Lc])
        P.op("act", "activation", out=Ep[:, :Lb], in_=Lc[:, :Lb], func=AF.Exp, scale=-C0, reads=[Lc], writes=[Ep])
        P.op("act", "activation", out=t1[:, :Lb], in_=Lc[:, :Lb], func=AF.Exp, scale=C0, reads=[Lc], writes=[t1])
        P.op("dve", "tensor_tensor", out=t2[:, :Lb], in0=Lc[:, :Lb], in1=sig[:, :Lb], op=ALU.subtract, reads=[Lc, sig], writes=[t2])
        P.op("act", "activation", out=t2[:, :Lb], in_=t2[:, :Lb], func=AF.Exp, scale=-C0, reads=[t2], writes=[t2])
        P.op("dve", "tensor_scalar", out=kk_[:, :Lb], in0=k_[:, :Lb], scalar1=kkk[:, l, hp:hp + 1], scalar2=None, op0=ALU.mult, reads=[k_, kkk], writes=[kk_])
        P.op("dve", "tensor_tensor", out=t3[:, :Lb], in0=kk_[:, :Lb], in1=kk_[:, :Lb], op=ALU.mult, reads=[kk_], writes=[t3])
        P.op("pe", "matmul", bankA[:, :Lb], lhsT=blk[:], rhs=t3[:, :Lb], start=True, stop=True, reads=[blk, t3] + allA, writes=allA)
        P.op("act", "activation", out=t3[:, :Lb], in_=bankA[:, :Lb], func=AF.Sqrt, reads=allA, writes=[t3])
        P.op("dve", "tensor_scalar", out=t3[:, :Lb], in0=t3[:, :Lb], scalar1=1e-12, scalar2=None, op0=ALU.max, reads=[t3], writes=[t3])
        P.op("dve", "reciprocal", out=t3[:, :Lb], in_=t3[:, :Lb], reads=[t3], writes=[t3])
        P.op("dve", "tensor_tensor", out=kk_[:, :Lb], in0=kk_[:, :Lb], in1=t3[:, :Lb], op=ALU.mult, reads=[kk_, t3], writes=[kk_])
        P.op("dve", "tensor_scalar", out=t3[:, :Lb], in0=a_[:, :Lb], scalar1=ka[:, l, hp:hp + 1], scalar2=omka[:, l, hp:hp + 1], op0=ALU.mult, op1=ALU.add,
             reads=[a_, ka, omka], writes=[t3])
        P.op("dve", "tensor_tensor", out=t3[:, :Lb], in0=t3[:, :Lb], in1=k_[:, :Lb], op=ALU.mult, reads=[t3, k_], writes=[t3])
        P.op("dve", "tensor_tensor", out=kt[:, :Lb], in0=t3[:, :Lb], in1=t1[:, :Lb], op=ALU.mult, reads=[t3, t1], writes=[kt])
        P.op("dve", "tensor_tensor", out=rt[:, :Lb], in0=r_[:, :Lb], in1=Ep[:, :Lb], op=ALU.mult, reads=[r_, Ep], writes=[rt])
        P.op("dve", "scalar_tensor_tensor", out=al[:, :Lb], in0=kk_[:, :Lb], scalar=-1.0, in1=t2[:, :Lb], op0=ALU.mult, op1=ALU.mult, reads=[kk_, t2], writes=[al])
        P.op("dve", "tensor_tensor", out=be[:, :Lb], in0=kk_[:, :Lb], in1=a_[:, :Lb], op=ALU.mult, reads=[kk_, a_], writes=[be])
        P.op("dve", "tensor_tensor", out=be[:, :Lb], in0=be[:, :Lb], in1=t1[:, :Lb], op=ALU.mult, reads=[be, t1], writes=[be])
        if hp == 0 and d == 1 and t0 == CTX:
            for nm_, tl_ in (("dbg_sig", sig), ("dbg_L", Lc), ("dbg_Ep", Ep), ("dbg_Em", t1), ("dbg_rt", rt), ("dbg_al", al), ("dbg_be", be), ("dbg_kt", kt), ("dbg_kk", kk_)):
                dbg(g, nm_, tl_, tl_[:, :Lb], [128, Lb])
        yield
        nch = Lb // CH
        for c in (range(nch) if d == 0 else range(nch - 1, -1, -1)):
            cs = slice(c * CH, (c + 1) * CH)
            for h in range(2):
                hs = H2[h]
                for j, src in enumerate((be, kt, vv)):
                    P.op("pe", "matmul", TM_[hs, 64 * j:64 * j + 64], lhsT=_r(src[hs, cs]), rhs=_r(ident[hs, hs]), start=True, stop=True, reads=[src, ident], writes=[pTM])
            P.op("act", "copy", out=TMs[:], in_=TM_, reads=[pTM], writes=[TMs])
            for h in range(2):
                hs = H2[h]
                P.op("pe", "matmul", XY_[hs, 64 * h:64 * h + 64], lhsT=_r(be[hs, cs]), rhs=_r(al[hs, cs]), start=True, stop=True, reads=[be, al], writes=[pXY])
                P.op("pe", "matmul", XY_[hs, 128 + 64 * h:192 + 64 * h], lhsT=_r(al[hs, cs]), rhs=_r(be[hs, cs]), start=True, stop=True, reads=[be, al], writes=[pXY])
                P.op("pe", "matmul", A_[hs, 0:64], lhsT=_r(be[hs, cs]), rhs=_r(rt[hs, cs]), start=True, stop=True, reads=[be, rt], writes=[pA])
                P.op("pe", "matmul", A_[hs, 64:128], lhsT=_r(kt[hs, cs]), rhs=_r(al[hs, cs]), start=True, stop=True, reads=[kt, al], writes=[pA])
                P.op("pe", "matmul", A_[hs, 128:192], lhsT=_r(kt[hs, cs]), rhs=_r(rt[hs, cs]), start=True, stop=True, reads=[kt, rt], writes=[pA])
            cur = 0
            P.op("dve", "tensor_tensor", out=XYs[cur][:], in0=XY_, in1=maskXY[:], op=ALU.mult, reads=[pXY, maskXY], writes=[XYs[cur]])
            P.op("dve", "tensor_tensor", out=Am[:], in0=A_, in1=maskA[:], op=ALU.mult, reads=[pA, maskA], writes=[Am])
            rc = 0
            P.op("dve", "tensor_tensor", out=Rs[rc][:], in0=XYs[cur][:, 0:128], in1=ident[:], op=ALU.add, reads=[XYs[cur], ident], writes=[Rs[rc]])
            yield
            for lvl in range(5):
                X, Y = XYs[cur][:, 0:128], XYs[cur][:, 128:256]
                nxt = 1 - cur
                if lvl < 4:
                    P.op("pe", "matmul", XY_[:, 0:128], lhsT=_r(Y), rhs=_r(X), start=True, stop=True, reads=[XYs[cur]], writes=[pXY])
                P.op("pe", "matmul", XY_[:, 128:256], lhsT=_r(X), rhs=_r(Y), start=True, stop=True, reads=[XYs[cur]], writes=[pXY])
                if lvl < 4:
                    P.op("act", "copy", out=XYs[nxt][:], in_=XY_, reads=[pXY], writes=[XYs[nxt]])
                else:
                    P.op("act", "copy", out=XYs[nxt][:, 128:256], in_=XY_[:, 128:256], reads=[pXY], writes=[XYs[nxt]])
                yield
                P.op("pe", "matmul", R_, lhsT=_r(XYs[nxt][:, 128:256]), rhs=_r(Rs[rc][:]), start=True, stop=True, reads=[XYs[nxt], Rs[rc]], writes=[pR])
                P.op("dve", "tensor_tensor", out=Rs[1 - rc][:], in0=Rs[rc][:], in1=R_, op=ALU.add, reads=[Rs[rc], pR], writes=[Rs[1 - rc]])
                rc = 1 - rc
                cur = nxt
                yield
            Rf = Rs[rc]
            for h in range(2):
                hs = H2[h]
                P.op("pe", "matmul", W0_[hs, :], lhsT=_r(al[hs, cs]), rhs=_r(ST[hs, :]), start=True, stop=False, reads=[al, ST], writes=[pW0])
                P.op("pe", "matmul", W0_[hs, :], lhsT=_r(Am[hs, 64:128]), rhs=_r(TMs[hs, 128:192]), start=False, stop=True, reads=[Am, TMs], writes=[pW0])
            P.op("act", "copy", out=W0s[:], in_=W0_, reads=[pW0], writes=[W0s])
            yield
            P.op("pe", "matmul", U_, lhsT=_r(Rf[:]), rhs=_r(W0s[:]), start=True, stop=True, reads=[Rf, W0s], writes=[pU])
            P.op("dve", "tensor_copy", out=Us[:], in_=U_, reads=[pU], writes=[Us])
            yield
            for h in range(2):
                hs = H2[h]
                P.op("pe", "matmul", Y_[hs, :], lhsT=_r(ST[hs, :]), rhs=_r(rt[hs, cs]), start=True, stop=False, reads=[ST, rt], writes=[pY])
                P.op("pe", "matmul", Y_[hs, :], lhsT=_r(Us[hs, :]), rhs=_r(Am[hs, 0:64]), start=False, stop=False, reads=[Us, Am], writes=[pY])
                P.op("pe", "matmul", Y_[hs, :], lhsT=_r(TMs[hs, 128:192]), rhs=_r(Am[hs, 128:192]), start=False, stop=True, reads=[TMs, Am], writes=[pY])
            P.op("act", "copy", out=Yb[:, cs], in_=Y_, reads=[pY], writes=[Yb])
            if hp == 0 and d == 1 and t0 == CTX and c == 0:
                for nm_, tl_, w_ in (("dbg_Am", Am, 192), ("dbg_Us", Us, 64), ("dbg_TMs", TMs, 192), ("dbg_W0s", W0s, 64), ("dbg_Rf", Rf, 128), ("dbg_ST", ST, 64), ("dbg_XY", XYs[cur], 256)):
                    dbg(g, nm_, tl_, tl_[:, :w_], [128, w_])
                dbg(g, "dbg_Yb", Yb, Yb[:, 0:128], [128, 128])
            for h in range(2):
                hs = H2[h]
                P.op("pe", "matmul", SP_[hs, :], lhsT=_r(TMs[hs, 0:64]), rhs=_r(Us[hs, :]), start=True, stop=False, reads=[TMs, Us], writes=[pSP])
                P.op("pe", "matmul", SP_[hs, :], lhsT=_r(TMs[hs, 64:128]), rhs=_r(TMs[hs, 128:192]), start=False, stop=True, reads=[TMs], writes=[pSP])
            pc = Ep[:, c * CH + CH - 1:c * CH + CH] if d == 0 else Ep[:, c * CH:c * CH + 1]
            P.op("dve", "tensor_scalar", out=ST[:], in0=ST[:], scalar1=pc, scalar2=None, op0=ALU.mult, reads=[ST, Ep], writes=[ST])
            P.op("dve", "scalar_tensor_tensor", out=ST[:], in0=SP_, scalar=pc, in1=ST[:], op0=ALU.mult, op1=ALU.add, reads=[pSP, Ep, ST], writes=[ST])
            yield
        P.dma(YS[d][hp][:, ts_], Yb[:, :Lb], reads=[Yb], writes=[YS[d][hp]], q="act")
        yield


def stage_rwkv_scan(g, l):
    P = g.P
    T = g.T
    YS = [[scratch(g, "YS%d_%d" % (d, hp), [128, T]) for hp in range(8)] for d in range(2)]
    lat = [(CTX + i * 512, min(512, T - CTX - i * 512)) for i in range((T - CTX + 511) // 512)]
    bl = [[(0, CTX)] + lat, [(0, CTX)] + lat[::-1]]
    for pair in range(4):
        with P.scope():
            tr = [P.sbuf("rtr%d" % i, [128, 512], F32) for i in range(11)]
            gens = []
            k = 0
            for hp in (2 * pair, 2 * pair + 1):
                for d in range(2):
                    gens.append(rwkv_stream(g, l, hp, d, g.ps[2 * k], g.ps[2 * k + 1], bl[d], tr, YS))
                    k += 1
            STAG = int(os.environ.get("SCAN_STAG", "0"))
            start = {id(ge): i * STAG for i, ge in enumerate(gens)}
            rnd = 0
            while gens:
                for ge in list(gens):
                    if rnd < start[id(ge)]:
                        continue
                    try:
                        next(ge)
                    except StopIteration:
                        gens.remove(ge)
                rnd += 1


def stage_rwkv_out(g, l):
    P = g.P
    T = g.T
    F = [scratch(g, "F%d" % i, [128, T]) for i in range(NSEG)]
    YS = [[scratch(g, "YS%d_%d" % (d, hp), [128, T]) for hp in range(8)] for d in range(2)]
    YR = scratch(g, "YR", [RW_W, T], BF16)
    s = g.s
    blk = g.c["blk"]
    a0, ka, omka, rk, lg, lb = s["a0"], s["k_a"], s["omka"], s["r_k"], s["lnxg"], s["lnxb"]
    for hp in range(8):
        with P.scope():
            aups = []
            for d in range(2):
                t = P.sbuf("roa%d" % d, [64, 128], F32)
                P.dma(t[:], g.d["rwkv_a_up"][l, d, :, hp * 128:(hp + 1) * 128], reads=[g.d["rwkv_a_up"]], writes=[t])
                aups.append(t)
            gu0 = P.sbuf("rogu0", [128, 128], F32)
            gu1 = P.sbuf("rogu1", [32, 128], F32)
            P.dma(gu0[:], g.d["rwkv_g_up"][l, 0:128, hp * 128:(hp + 1) * 128], reads=[g.d["rwkv_g_up"]], writes=[gu0])
            P.dma(gu1[:], g.d["rwkv_g_up"][l, 128:160, hp * 128:(hp + 1) * 128], reads=[g.d["rwkv_g_up"]], writes=[gu1])
            names = ("y", "yr", "r", "k", "v", "ad", "g0", "g1", "t1", "t2", "t3")
            tl = {n_: [P.sbuf("ro_%s%d" % (n_, i), [128, 512], F32) for i in range(2)] for n_ in names}
            ob = [P.sbuf("ro_ob%d" % i, [128, 512], BF16) for i in range(2)]
            for bi, (t0, Lb) in enumerate(blocks_of(g)):
                ts_ = slice(t0, t0 + Lb)
                y, yr, r_, k_, v_, ad, g0, g1, t1, t2, t3 = [tl[n_][bi % 2] for n_ in names]
                o = ob[bi % 2]
                P.dma(y[:, :Lb], YS[0][hp][:, ts_], reads=[YS[0][hp]], writes=[y])
                P.dma(yr[:, :Lb], YS[1][hp][:, ts_], reads=[YS[1][hp]], writes=[yr], q="act")
                P.dma(r_[:, :Lb], F[S_R + hp][:, ts_], reads=[F[S_R + hp]], writes=[r_])
                P.dma(k_[:, :Lb], F[S_K + hp][:, ts_], reads=[F[S_K + hp]], writes=[k_], q="act")
                P.dma(v_[:, :Lb], F[S_V + hp][:, ts_], reads=[F[S_V + hp]], writes=[v_])
                P.dma(ad[0:64, :Lb], F[S_AD][0:64, ts_], reads=[F[S_AD]], writes=[ad], q="act")
                P.dma(g0[:, :Lb], F[S_GD][:, ts_], reads=[F[S_GD]], writes=[g0])
                P.dma(g1[0:32, :Lb], F[S_GD + 1][0:32, ts_], reads=[F[S_GD + 1]], writes=[g1], q="act")
                p1, p2, p3, p4 = [g.ps[(4 * bi + i) % 8] for i in range(4)]
                P.op("dve", "tensor_tensor", out=y[:, :Lb], in0=y[:, :Lb], in1=yr[:, :Lb], op=ALU.add, reads=[y, yr], writes=[y])
                P.op("pe", "matmul", p1[:, :Lb], lhsT=blk[:], rhs=y[:, :Lb], start=True, stop=True, reads=[blk, y], writes=[p1])
                P.op("dve", "scalar_tensor_tensor", out=y[:, :Lb], in0=p1[:, :Lb], scalar=-1.0 / 64, in1=y[:, :Lb], op0=ALU.mult, op1=ALU.add, reads=[p1, y], writes=[y])
                P.op("act", "activation", out=t1[:, :Lb], in_=y[:, :Lb], func=AF.Square, reads=[y], writes=[t1])
                P.op("pe", "matmul", p2[:, :Lb], lhsT=blk[:], rhs=t1[:, :Lb], start=True, stop=True, reads=[blk, t1], writes=[p2])
                P.op("act", "activation", out=t1[:, :Lb], in_=p2[:, :Lb], func=AF.Sqrt, bias=GN_EPS, scale=1.0 / 64, reads=[p2], writes=[t1])
                P.op("dve", "reciprocal", out=t1[:, :Lb], in_=t1[:, :Lb], reads=[t1], writes=[t1])
                P.op("dve", "tensor_tensor", out=y[:, :Lb], in0=y[:, :Lb], in1=t1[:, :Lb], op=ALU.mult, reads=[y, t1], writes=[y])
                P.op("dve", "tensor_scalar", out=y[:, :Lb], in0=y[:, :Lb], scalar1=lg[:, l, hp:hp + 1], scalar2=lb[:, l, hp:hp + 1], op0=ALU.mult, op1=ALU.add,
                     reads=[y, lg, lb], writes=[y])
                for d in range(2):
                    P.op("pe", "matmul", p3[:, :Lb], lhsT=aups[d][:], rhs=ad[0:64, :Lb], start=True, stop=True, reads=[aups[d], ad], writes=[p3])
                    tt = t2 if d == 0 else t3
                    P.op("act", "activation", out=tt[:, :Lb], in_=p3[:, :Lb], func=AF.Sigmoid, bias=a0[:, l, d, hp:hp + 1], reads=[p3, a0], writes=[tt])
                P.op("dve", "tensor_tensor", out=t2[:, :Lb], in0=t2[:, :Lb], in1=t3[:, :Lb], op=ALU.add, reads=[t2, t3], writes=[t2])
                P.op("dve", "tensor_scalar", out=t2[:, :Lb], in0=t2[:, :Lb], scalar1=ka[:, l, hp:hp + 1], scalar2=None, op0=ALU.mult, reads=[t2, ka], writes=[t2])
                P.op("dve", "tensor_scalar", out=t3[:, :Lb], in0=k_[:, :Lb], scalar1=omka[:, l, hp:hp + 1], scalar2=2.0, op0=ALU.mult, op1=ALU.mult, reads=[k_, omka], writes=[t3])
                P.op("dve", "tensor_tensor", out=t2[:, :Lb], in0=t2[:, :Lb], in1=k_[:, :Lb], op=ALU.mult, reads=[t2, k_], writes=[t2])
                P.op("dve", "tensor_tensor", out=t2[:, :Lb], in0=t2[:, :Lb], in1=t3[:, :Lb], op=ALU.add, reads=[t2, t3], writes=[t2])
                P.op("dve", "tensor_tensor", out=t2[:, :Lb], in0=t2[:, :Lb], in1=r_[:, :Lb], op=ALU.mult, reads=[t2, r_], writes=[t2])
                P.op("dve", "tensor_scalar", out=t2[:, :Lb], in0=t2[:, :Lb], scalar1=rk[:, l, hp:hp + 1], scalar2=0.5, op0=ALU.mult, op1=ALU.mult, reads=[t2, rk], writes=[t2])
                P.op("pe", "matmul", p4[:, :Lb], lhsT=blk[:], rhs=t2[:, :Lb], start=True, stop=True, reads=[blk, t2], writes=[p4])
                P.op("dve", "tensor_tensor", out=t2[:, :Lb], in0=v_[:, :Lb], in1=p4[:, :Lb], op=ALU.mult, reads=[v_, p4], writes=[t2])
                P.op("dve", "tensor_tensor", out=y[:, :Lb], in0=y[:, :Lb], in1=t2[:, :Lb], op=ALU.add, reads=[y, t2], writes=[y])
                P.op("act", "activation", out=g0[:, :Lb], in_=g0[:, :Lb], func=AF.Sigmoid, reads=[g0], writes=[g0])
                P.op("act", "activation", out=g1[0:32, :Lb], in_=g1[0:32, :Lb], func=AF.Sigmoid, reads=[g1], writes=[g1])
                P.op("pe", "matmul", p1[:, :Lb], lhsT=gu0[:], rhs=g0[:, :Lb], start=True, stop=False, reads=[gu0, g0], writes=[p1])
                P.op("pe", "matmul", p1[:, :Lb], lhsT=gu1[0:32, :], rhs=g1[0:32, :Lb], start=False, stop=True, reads=[gu1, g1], writes=[p1])
                P.op("dve", "tensor_tensor", out=o[:, :Lb], in0=y[:, :Lb], in1=p1[:, :Lb], op=ALU.mult, reads=[y, p1], writes=[o])
                P.dma(YR[hp * 128:(hp + 1) * 128, ts_], o[:, :Lb], reads=[o], writes=[YR], q="act")


def stage_mla(g, l, ctx_out):
    P = g.P
    T = g.T
    F = [scratch(g, "F%d" % i, [128, T]) for i in range(NSEG)]
    YM = scratch(g, "YM", [2048, T], BF16)
    ones, ident, Jm = g.c["ones"], g.c["ident"], g.c["Jm"]
    qng, kvg = g.s["qng"], g.s["kvg"]
    blocks = blocks_of(g)
    NKC = T // 128
    with P.scope():
        onesb = P.sbuf("onesb", [128, 128], BF16)
        P.op("dve", "memset", onesb[:], 1.0, reads=[], writes=[onesb])
        cos = load_const(g, "cos", [64, g.SEQ])
        sin = load_const(g, "sin", [64, g.SEQ], q="act")
        qn = P.sbuf("qn", [128, 4, T], BF16)
        kvn = P.sbuf("kvn", [128, 2, T], BF16)
        kr = P.sbuf("kr", [64, T], BF16)
        with P.scope():
            xqs = [P.sbuf("xq%d" % i, [128, 4, 512], F32) for i in range(2)]
            xks = [P.sbuf("xk%d" % i, [128, 2, 512], F32) for i in range(2)]
            xrs = [P.sbuf("xr%d" % i, [64, 512], F32) for i in range(2)]
            sqs = [P.sbuf("msq%d" % i, [128, 512], F32) for i in range(2)]
            rs1 = P.sbuf("mrs1", [128, 512], F32)
            rs2 = P.sbuf("mrs2", [128, 512], F32)
            m1 = P.sbuf("mm1", [64, 512], F32)
            m2 = P.sbuf("mm2", [64, 512], F32)
            for bi, (t0, n) in enumerate(blocks):
                xq, xk, xr = xqs[bi % 2], xks[bi % 2], xrs[bi % 2]
                for i in range(4):
                    P.dma(xq[:, i, :n], F[S_Q + i][:, t0:t0 + n], reads=[F[S_Q + i]], writes=[xq], q="sp" if i % 2 == 0 else "act")
                for i in range(2):
                    P.dma(xk[:, i, :n], F[S_KV + i][:, t0:t0 + n], reads=[F[S_KV + i]], writes=[xk], q="sp" if i % 2 == 0 else "act")
                P.dma(xr[0:64, :n], F[S_KR][0:64, t0:t0 + n], reads=[F[S_KR]], writes=[xr])
                pa, pb, pj = g.ps[0], g.ps[1], g.ps[2]
                for (x_, nchunk, pp, rs, gam, dst) in ((xq, 4, pa, rs1, qng, qn), (xk, 2, pb, rs2, kvg, kvn)):
                    for i in range(nchunk):
                        sq = sqs[i % 2]
                        P.op("act", "activation", out=sq[:, :n], in_=x_[:, i, :n], func=AF.Square, reads=[x_], writes=[sq])
                        P.op("pe", "matmul", pp[:, :n], lhsT=ones[:], rhs=sq[:, :n], start=(i == 0), stop=(i == nchunk - 1), reads=[ones, sq], writes=[pp])
                    P.op("act", "activation", out=rs[:, :n], in_=pp[:, :n], func=AF.Sqrt, bias=LN_EPS, scale=1.0 / (128 * nchunk), reads=[pp], writes=[rs])
                    P.op("dve", "reciprocal", out=rs[:, :n], in_=rs[:, :n], reads=[rs], writes=[rs])
                    for i in range(nchunk):
                        P.op("dve", "scalar_tensor_tensor", out=dst[:, i, t0:t0 + n], in0=x_[:, i, :n], scalar=gam[:, l, i:i + 1], in1=rs[:, :n],
                             op0=ALU.mult, op1=ALU.mult, reads=[x_, gam, rs], writes=[dst])
                if t0 < CTX:
                    P.op("act", "copy", out=kr[0:64, t0:t0 + n], in_=xr[0:64, :n], reads=[xr], writes=[kr])
                else:
                    P.op("pe", "matmul", pj[0:64, :n], lhsT=Jm[:], rhs=xr[0:64, :n], start=True, stop=True, reads=[Jm, xr], writes=[pj])
                    P.op("dve", "tensor_tensor", out=m1[:, :n], in0=xr[0:64, :n], in1=cos[:, t0 - CTX:t0 - CTX + n], op=ALU.mult, reads=[xr, cos], writes=[m1])
                    P.op("dve", "tensor_tensor", out=m2[:, :n], in0=pj[0:64, :n], in1=sin[:, t0 - CTX:t0 - CTX + n], op=ALU.mult, reads=[pj, sin], writes=[m2])
                    P.op("dve", "tensor_tensor", out=kr[0:64, t0:t0 + n], in0=m1[:, :n], in1=m2[:, :n], op=ALU.add, reads=[m1, m2], writes=[kr])
        wuq, wuk, wuv = g.d["mla_w_uq"], g.d["mla_w_uk"], g.d["mla_w_uv"]
        wqs = [P.sbuf("wq%d" % i, [128, 4, 192], BF16) for i in range(2)]
        wks = [P.sbuf("wk%d" % i, [128, 2, 128], BF16) for i in range(2)]
        wvs = [P.sbuf("wv%d" % i, [128, 2, 128], BF16) for i in range(2)]

        def load_head_w(hd_):
            P.dma(wqs[hd_ % 2][:], wuq[l, :, hd_ * 192:(hd_ + 1) * 192].rearrange("(kc p) n -> p kc n", p=128), reads=[wuq], writes=[wqs[hd_ % 2]], q="pool")
            P.dma(wks[hd_ % 2][:], wuk[l, :, hd_ * 128:(hd_ + 1) * 128].rearrange("(kc p) n -> p kc n", p=128), reads=[wuk], writes=[wks[hd_ % 2]], q="pool")
            P.dma(wvs[hd_ % 2][:], wuv[l, :, hd_ * 128:(hd_ + 1) * 128].rearrange("(kc p) n -> p kc n", p=128), reads=[wuv], writes=[wvs[hd_ % 2]], q="pool")

        load_head_w(0)
        for hd in range(16):
            with P.scope():
                wq, wk, wv = wqs[hd % 2], wks[hd % 2], wvs[hd % 2]
                if hd + 1 < 16:
                    load_head_w(hd + 1)
                Kn = P.sbuf("Kn", [128, T], BF16)
                Qn = P.sbuf("Qn", [128, T], BF16)
                Qr = P.sbuf("Qr", [64, T], BF16)
                Va = P.sbuf("Va", [128, NKC, 128], BF16)
                xq32 = P.sbuf("xq32", [64, 512], F32)
                m1 = P.sbuf("hm1", [64, 512], F32)
                m2 = P.sbuf("hm2", [64, 512], F32)
                for bi, (t0, n) in enumerate(blocks):
                    pk, pq, pr, pj = g.ps[0], g.ps[1], g.ps[2], g.ps[3]
                    for kc in range(2):
                        P.op("pe", "matmul", pk[:, :n], lhsT=wk[:, kc, :], rhs=kvn[:, kc, t0:t0 + n], start=(kc == 0), stop=(kc == 1), reads=[wk, kvn], writes=[pk])
                    P.op("act", "copy", out=Kn[:, t0:t0 + n], in_=pk[:, :n], reads=[pk], writes=[Kn])
                    if t0 < CTX and not ctx_out:
                        continue
                    for kc in range(4):
                        P.op("pe", "matmul", pq[:, :n], lhsT=wq[:, kc, 0:128], rhs=qn[:, kc, t0:t0 + n], start=(kc == 0), stop=(kc == 3), reads=[wq, qn], writes=[pq])
                    P.op("dve", "tensor_copy", out=Qn[:, t0:t0 + n], in_=pq[:, :n], reads=[pq], writes=[Qn])
                    for kc in range(4):
                        P.op("pe", "matmul", pr[0:64, :n], lhsT=wq[:, kc, 128:192], rhs=qn[:, kc, t0:t0 + n], start=(kc == 0), stop=(kc == 3), reads=[wq, qn], writes=[pr])
                    if t0 < CTX:
                        P.op("act", "copy", out=Qr[0:64, t0:t0 + n], in_=pr[0:64, :n], reads=[pr], writes=[Qr])
                    else:
                        P.op("act", "copy", out=xq32[:, :n], in_=pr[0:64, :n], reads=[pr], writes=[xq32])
                        P.op("pe", "matmul", pj[0:64, :n], lhsT=Jm[:], rhs=xq32[:, :n], start=True, stop=True, reads=[Jm, xq32], writes=[pj])
                        P.op("dve", "tensor_tensor", out=m1[:, :n], in0=xq32[:, :n], in1=cos[:, t0 - CTX:t0 - CTX + n], op=ALU.mult, reads=[xq32, cos], writes=[m1])
                        P.op("dve", "tensor_tensor", out=m2[:, :n], in0=pj[0:64, :n], in1=sin[:, t0 - CTX:t0 - CTX + n], op=ALU.mult, reads=[pj, sin], writes=[m2])
                        P.op("dve", "tensor_tensor", out=Qr[0:64, t0:t0 + n], in0=m1[:, :n], in1=m2[:, :n], op=ALU.add, reads=[m1, m2], writes=[Qr])
                for tc_ in range(NKC):
                    pv = g.ps[2 + tc_ % 2]
                    for kc in range(2):
                        P.op("pe", "matmul", pv[:, 0:128], lhsT=kvn[:, kc, tc_ * 128:(tc_ + 1) * 128], rhs=wv[:, kc, :], start=(kc == 0), stop=(kc == 1),
                             reads=[kvn, wv], writes=[pv])
                    if tc_ % 2 == 0:
                        P.op("act", "copy", out=Va[:, tc_, :], in_=pv[:, 0:128], reads=[pv], writes=[Va])
                    else:
                        P.op("dve", "tensor_copy", out=Va[:, tc_, :], in_=pv[:, 0:128], reads=[pv], writes=[Va])
                pts = [P.sbuf("PT%d" % i, [128, 512], BF16) for i in range(4)]
                stgs = [P.sbuf("ostg%d" % i, [128, 512], BF16) for i in range(2)]
                rss = [P.sbuf("ors%d" % i, [128, 512], F32) for i in range(2)]
                qblocks = [b_ for b_ in blocks if (b_[0] >= CTX or ctx_out)]
                for qi, (q0, nq) in enumerate(qblocks):
                    keys = list(range(CTX // 128)) if q0 < CTX else list(range(NKC))
                    nk = len(keys)
                    accO = g.ps[4 + 2 * (qi % 2)]
                    accS = g.ps[5 + 2 * (qi % 2)]

                    def qk(ki):
                        kc = keys[ki]
                        S = g.ps[ki % 3]
                        P.op("pe", "matmul", S[:, :nq], lhsT=Kn[:, kc * 128:(kc + 1) * 128], rhs=Qn[:, q0:q0 + nq], start=True, stop=False, reads=[Kn, Qn], writes=[S])
                        P.op("pe", "matmul", S[:, :nq], lhsT=kr[0:64, kc * 128:(kc + 1) * 128], rhs=Qr[0:64, q0:q0 + nq], start=False, stop=True, reads=[kr, Qr], writes=[S])

                    qk(0)
                    if nk > 1:
                        qk(1)
                    for ki in range(nk):
                        if ki + 2 < nk:
                            qk(ki + 2)
                        kc = keys[ki]
                        S = g.ps[ki % 3]
                        PT = pts[ki % 4]
                        P.op("act", "activation", out=PT[:, :nq], in_=S[:, :nq], func=AF.Exp, scale=ATT_SCALE, reads=[S], writes=[PT])
                        P.op("pe", "matmul", accO[:, :nq], lhsT=Va[:, kc, :], rhs=PT[:, :nq], start=(ki == 0), stop=(ki == nk - 1), reads=[PT, Va], writes=[accO])
                        P.op("pe", "matmul", accS[:, :nq], lhsT=onesb[:], rhs=PT[:, :nq], start=(ki == 0), stop=(ki == nk - 1), reads=[PT, onesb], writes=[accS])
                    rs, stg = rss[qi % 2], stgs[qi % 2]
                    P.op("dve", "reciprocal", out=rs[:, :nq], in_=accS[:, :nq], reads=[accS], writes=[rs])
                    P.op("dve", "tensor_tensor", out=stg[:, :nq], in0=accO[:, :nq], in1=rs[:, :nq], op=ALU.mult, reads=[accO, rs], writes=[stg])
                    P.dma(YM[hd * 128:(hd + 1) * 128, q0:q0 + nq], stg[:, :nq], reads=[stg], writes=[YM], q="sp")


def stage_c(g, l, dst, last):
    P = g.P
    T = g.T
    XL = scratch(g, "XL", [D, T])
    F = [scratch(g, "F%d" % i, [128, T]) for i in range(NSEG)]
    YR = scratch(g, "YR", [RW_W, T], BF16)
    YM = scratch(g, "YM", [2048, T], BF16)
    YL = scratch(g, "YL", [LRU_W, T], BF16)
    prw, pml, plr, wout = g.d["proj_rwkv"], g.d["proj_mla"], g.d["proj_lru"], g.d["w_out"]
    for (t0, NT, m) in token_tiles(g, with_ctx=not last):
        with P.scope():
            s = P.sbuf("cs_", [128, KC, NT], F32)
            P.dma(s[:], fview(XL)[:, :, t0:t0 + NT], reads=[XL], writes=[s])
            P.op("act", "mul", out=s[:], in_=s[:], mul=ALPHA, reads=[s], writes=[s])
            with P.scope():
                yr = P.sbuf("cyr", [128, 8, NT], BF16)
                ym = P.sbuf("cym", [128, 16, NT], BF16)
                yl = P.sbuf("cyl", [128, 8, NT], BF16)
                P.dma(yr[:], YR[:].rearrange("(kc p) t -> p kc t", p=128)[:, :, t0:t0 + NT], reads=[YR], writes=[yr])
                P.dma(ym[:], YM[:].rearrange("(kc p) t -> p kc t", p=128)[:, :, t0:t0 + NT], reads=[YM], writes=[ym], q="act")
                P.dma(yl[:], YL[:].rearrange("(kc p) t -> p kc t", p=128)[:, :, t0:t0 + NT], reads=[YL], writes=[yl])
                ybf = P.sbuf("cyb", [128, KC, NT], BF16)
                gts = [P.sbuf("cgt%d" % i, [128, 3, NT], F32) for i in range(2)]
                pws = [P.sbuf("cpw%d" % i, [128, 32, 128], BF16) for i in range(2)]
                t1 = P.sbuf("ct1", [128, NT], F32)
                t2 = P.sbuf("ct2", [128, NT], F32)
                for dc in range(KC):
                    pw, gt = pws[dc % 2], gts[dc % 2]
                    cs_ = slice(dc * 128, (dc + 1) * 128)
                    wload(g, pw[:, 0:8, :], pw, "proj_rwkv", (l,), dc * 128)
                    wload(g, pw[:, 8:24, :], pw, "proj_mla", (l,), dc * 128)
                    wload(g, pw[:, 24:32, :], pw, "proj_lru", (l,), dc * 128)
                    for bi in range(3):
                        P.dma(gt[:, bi, :], F[S_GATE + 16 * bi + dc][:, t0:t0 + NT], reads=[F[S_GATE + 16 * bi + dc]], writes=[gt], q="sp" if bi != 1 else "act")
                    p1, p2, p3 = g.ps[0], g.ps[1], g.ps[2]
                    for kc in range(8):
                        P.op("pe", "matmul", p1[:, :NT], lhsT=pw[:, kc, :], rhs=yr[:, kc, :], start=(kc == 0), stop=(kc == 7), reads=[pw, yr], writes=[p1])
                    for kc in range(16):
                        P.op("pe", "matmul", p2[:, :NT], lhsT=pw[:, 8 + kc, :], rhs=ym[:, kc, :], start=(kc == 0), stop=(kc == 15), reads=[pw, ym], writes=[p2])
                    for kc in range(8):
                        P.op("pe", "matmul", p3[:, :NT], lhsT=pw[:, 24 + kc, :], rhs=yl[:, kc, :], start=(kc == 0), stop=(kc == 7), reads=[pw, yl], writes=[p3])
                    P.op("dve", "tensor_tensor", out=t1[:], in0=gt[:, 0, :], in1=p1[:, :NT], op=ALU.mult, reads=[gt, p1], writes=[t1])
                    P.op("dve", "tensor_tensor", out=t2[:], in0=gt[:, 1, :], in1=p2[:, :NT], op=ALU.mult, reads=[gt, p2], writes=[t2])
                    P.op("dve", "tensor_tensor", out=t1[:], in0=t1[:], in1=t2[:], op=ALU.add, reads=[t1, t2], writes=[t1])
                    P.op("dve", "tensor_tensor", out=t2[:], in0=gt[:, 2, :], in1=p3[:, :NT], op=ALU.mult, reads=[gt, p3], writes=[t2])
                    P.op("dve", "tensor_tensor", out=ybf[:, dc, :], in0=t1[:], in1=t2[:], op=ALU.add, reads=[t1, t2], writes=[ybf])
                wos = [P.sbuf("cwo%d" % i, [128, KC, 128], BF16) for i in range(2)]
                for dc in range(KC):
                    wo = wos[dc % 2]
                    wload(g, wo[:], wo, "w_out", (l,), dc * 128)
                    po = g.ps[4 + dc % 2]
                    for kc in range(KC):
                        P.op("pe", "matmul", po[:, :NT], lhsT=wo[:, kc, :], rhs=ybf[:, kc, :], start=(kc == 0), stop=(kc == KC - 1), reads=[wo, ybf], writes=[po])
                    P.op("dve", "scalar_tensor_tensor", out=s[:, dc, :], in0=po[:, :NT], scalar=g.mod[:, 5 * 16 + dc, m:m + 1], in1=s[:, dc, :],
                         op0=ALU.mult, op1=ALU.add, reads=[po, s, g.mod], writes=[s])
            layer_norm(g, s, NT, l, 1)
            ffn_block(g, s, NT, m, l, 1, (6, 7, 8), 2)
            if last:
                P.dma(fview(dst)[:, :, t0 - CTX:t0 - CTX + NT], s[:], reads=[s], writes=[dst], q="act")
            else:
                P.dma(fview(dst)[:, :, t0:t0 + NT], s[:], reads=[s], writes=[dst], q="act")


SMALL_NAMES = ["adab", "lng", "lnb", "gateb", "mu", "mud", "w0", "a0", "kk_k", "k_a", "r_k", "lnxg", "lnxb", "qng", "kvg", "convw", "convb", "ba", "bx", "lam"]
CONST_NAMES = ["ident", "ones", "blk", "Jm", "maskA"]


def build_program(SEQ, L, shapes, stages=None, ext_in=(), ext_out=()):
    nc = bass.Bass("TRN2", target_bir_lowering=False)
    g = setup(nc, SEQ, L, shapes, ext_in=ext_in, ext_out=ext_out)
    load_consts(g, CONST_NAMES)
    load_small(g, SMALL_NAMES)
    init_derived(g)
    init_mod(g)
    T = g.T
    xcur = g.d["xT"]
    XN = scratch(g, "XN", [D, T])
    yT = g.P.dram("yT", [D, SEQ], F32, "ExternalOutput")
    g.d["yT"] = yT
    for l in range(L):
        precast_layer(g, l)
    for l in range(L):
        last = l == L - 1
        stage_mod(g, l)
        stage_a(g, l, xcur)
        stage_lru(g, l)
        stage_rwkv_mix(g, l)
        stage_rwkv_scan(g, l)
        stage_rwkv_out(g, l)
        stage_mla(g, l, ctx_out=not last)
        stage_c(g, l, yT if last else XN, last)
        xcur = XN
    g.P.emit()
    return nc, g


def host_inputs(inp, b, SEQ, L):
    x = np.asarray(inp["x"][b], np.float32)
    cx = np.asarray(inp["ctx"][b], np.float32)
    d = {}
    d["xT"] = np.ascontiguousarray(np.concatenate([cx, x], 0).T)
    d["cvec"] = np.ascontiguousarray(np.stack([fm(inp["c"][b]), fm(inp["c_ctx"])], -1))
    d.update(host_consts(SEQ))
    d.update(host_small(inp, L))
    for k in BIG_W:
        d[k] = np.ascontiguousarray(np.asarray(inp[k], np.float32))
    return d


_CACHE = {}


def kernel(**inputs):
    B, SEQ, _ = inputs["x"].shape
    L = inputs["w_in"].shape[0]
    per_core = [host_inputs(inputs, b, SEQ, L) for b in range(B)]
    shapes = {k: list(v.shape) for k, v in per_core[0].items()}
    key = (SEQ, L)
    if key not in _CACHE:
        _CACHE[key] = build_program(SEQ, L, shapes)
    nc, g = _CACHE[key]
    res = run_bass_kernel_spmd(nc, per_core, core_ids=list(range(B)))
    out = np.stack([np.ascontiguousarray(res.results[b]["yT"].T) for b in range(B)], 0)
    return out.astype(np.float32)
```
